# Optimizing a Trainium2 kernel written in Bass

```python
import jax, jax.numpy as jnp
from jax import lax
import numpy as np

D_MODEL = 1024
BATCH = 4
SEQ = 8192
DEPTH = 1
DEC_BATCH = 8
DEC_SEQ = 64
PAST_LEN = 2048

CHUNK = 64
HEAD_DIM = 64
RWKV_HEADS = 8
FOX_HEADS = 8
RWKV_WIDTH = RWKV_HEADS * HEAD_DIM
FOX_WIDTH = FOX_HEADS * HEAD_DIM
MIX_WIDTH = RWKV_WIDTH + FOX_WIDTH
DECAY_LORA = 64
AAA_LORA = 64
GATE_LORA = 128
RWKV_COLS = 3 * RWKV_WIDTH + DECAY_LORA + AAA_LORA + GATE_LORA
FOX_COLS = 3 * FOX_WIDTH + FOX_HEADS
IN_COLS = RWKV_COLS + FOX_COLS
RWKV_SPLITS = (RWKV_WIDTH, 2 * RWKV_WIDTH, 3 * RWKV_WIDTH,
               3 * RWKV_WIDTH + DECAY_LORA, 3 * RWKV_WIDTH + DECAY_LORA + AAA_LORA)
FOX_SPLITS = (FOX_WIDTH, 2 * FOX_WIDTH, 3 * FOX_WIDTH)
D_FF = -(-8 * D_MODEL // (3 * 256)) * 256
Q_BLOCK = 128
NORM_EPS = 1e-6
GN_EPS = 64e-5
FORGET_BIAS_INIT = 2.0

kernel_name = 'hymba_rwkv7_fox_adaln_stream_step'


def _rms_norm(x, g):
    x32 = x.astype(jnp.float32)
    y = x32 * lax.rsqrt(jnp.mean(x32 * x32, axis=-1, keepdims=True) + NORM_EPS)
    return (y * g.astype(jnp.float32)).astype(x.dtype)


def _heads(z, n_heads):
    return z.reshape(z.shape[:-1] + (n_heads, HEAD_DIM))


def _rwkv_scan(r, decay, k, v, kk, a, s0):
    def step(s, inp):
        r_t, w_t, k_t, v_t, kk_t, a_t = inp
        sa = jnp.einsum('bhvk,bhk->bhv', s, -kk_t)
        s = (s * w_t[:, :, None, :]
             + sa[..., None] * (kk_t * a_t)[:, :, None, :]
             + v_t[..., None] * k_t[:, :, None, :])
        return s, jnp.einsum('bhvk,bhk->bhv', s, r_t)
    xs = tuple(jnp.moveaxis(z, 1, 0) for z in (r, decay, k, v, kk, a))
    s_final, ys = lax.scan(step, s0, xs)
    return jnp.moveaxis(ys, 0, 1), s_final


def _fox_attend(q, k, v, lc_q, lc_k, pos_q, pos_k):
    s = jnp.einsum('bqhd,bshd->bhqs', q, k, preferred_element_type=jnp.float32) * (HEAD_DIM ** -0.5)
    decay_bias = jnp.transpose(lc_q, (0, 2, 1))[..., :, None] - jnp.transpose(lc_k, (0, 2, 1))[..., None, :]
    causal = pos_k[None, :] <= pos_q[:, None]
    p = jax.nn.softmax(jnp.where(causal, s + decay_bias, -jnp.inf), axis=-1)
    return jnp.einsum('bhqs,bshd->bqhd', p.astype(v.dtype), v)


def _fox_mixer(q, k_all, v_all, lc_all, past):
    b, t = q.shape[:2]
    lc_q = lc_all[:, past:]
    pos_k = jnp.arange(k_all.shape[1])
    pos_q = past + jnp.arange(t)
    if t <= Q_BLOCK:
        return _fox_attend(q, k_all, v_all, lc_q, lc_all, pos_q, pos_k)
    nb = t // Q_BLOCK
    qb = jnp.moveaxis(q.reshape(b, nb, Q_BLOCK, FOX_HEADS, HEAD_DIM), 1, 0)
    lb = jnp.moveaxis(lc_q.reshape(b, nb, Q_BLOCK, FOX_HEADS), 1, 0)
    pb = pos_q.reshape(nb, Q_BLOCK)
    out = lax.map(lambda blk: _fox_attend(blk[0], k_all, v_all, blk[1], lc_all, blk[2], pos_k), (qb, lb, pb))
    return jnp.moveaxis(out, 0, 1).reshape(q.shape)


def _layer(x, c, shift_prev, s_prev, k_past, v_past, logf_past,
           norm1_g, w_ada, b_ada, w_in, shift_mu, w0, w_decay_up, a0, w_aaa_up, w_gate_up,
           k_k, k_a, r_k, gn_g, gn_b, fox_q_g, fox_k_g, fox_f_b, w_out,
           norm2_g, w_ffn_gate, w_ffn_up, w_ffn_down):
    f32 = jnp.float32
    b, t, _ = x.shape
    past = k_past.shape[1]
    mod = jax.nn.silu(c) @ w_ada + b_ada
    sh1, sc1, gt1, sh2, sc2, gt2 = jnp.split(mod, 6, axis=-1)
    h = _rms_norm(x, norm1_g) * (1 + sc1[:, None]) + sh1[:, None]
    proj = h @ w_in
    p_rwkv = proj[..., :RWKV_COLS]
    p_fox = proj[..., RWKV_COLS:]

    prev = jnp.concatenate([shift_prev.astype(p_rwkv.dtype), p_rwkv[:, :-1]], axis=1)
    z = (p_rwkv + (prev - p_rwkv) * shift_mu).astype(f32)
    new_shift = p_rwkv[:, -1:]
    r, k, v, dw, da, dg = jnp.split(z, list(RWKV_SPLITS), axis=-1)
    w_log = -jax.nn.softplus(-(w0 + jnp.tanh(dw) @ w_decay_up)) - 0.5
    decay = jnp.exp(-jnp.exp(w_log))
    a = jax.nn.sigmoid(a0 + da @ w_aaa_up)
    g = jax.nn.sigmoid(dg) @ w_gate_up
    kk = _heads(k * k_k, RWKV_HEADS)
    kk = kk / jnp.maximum(jnp.linalg.norm(kk, axis=-1, keepdims=True), 1e-12)
    k = k * (1 + (a - 1) * k_a)
    rh, kh, vh = _heads(r, RWKV_HEADS), _heads(k, RWKV_HEADS), _heads(v, RWKV_HEADS)
    y, s_new = _rwkv_scan(rh, _heads(decay, RWKV_HEADS), kh, vh, kk, _heads(a, RWKV_HEADS),
                          s_prev.astype(f32))
    mu = jnp.mean(y, axis=-1, keepdims=True)
    var = jnp.mean(jnp.square(y - mu), axis=-1, keepdims=True)
    y = ((y - mu) * lax.rsqrt(var + GN_EPS)).reshape(b, t, RWKV_WIDTH) * gn_g + gn_b
    bonus = (jnp.sum(rh * kh * r_k, axis=-1, keepdims=True) * vh).reshape(b, t, RWKV_WIDTH)
    y_rwkv = ((y + bonus) * g).astype(x.dtype)

    q, kf, vf, fl = jnp.split(p_fox, list(FOX_SPLITS), axis=-1)
    q = _rms_norm(_heads(q, FOX_HEADS), fox_q_g)
    kf = _rms_norm(_heads(kf, FOX_HEADS), fox_k_g)
    vf = _heads(vf, FOX_HEADS)
    logf = jax.nn.log_sigmoid((fl + fox_f_b).astype(f32))
    k_all = jnp.concatenate([k_past.astype(kf.dtype), kf], axis=1)
    v_all = jnp.concatenate([v_past.astype(vf.dtype), vf], axis=1)
    lc_all = jnp.cumsum(jnp.concatenate([logf_past.astype(f32), logf], axis=1), axis=1)
    y_fox = _fox_mixer(q, k_all, v_all, lc_all, past).reshape(b, t, FOX_WIDTH).astype(x.dtype)

    mix = jnp.concatenate([y_rwkv, y_fox], axis=-1) @ w_out
    x = x + gt1[:, None] * mix
    h2 = _rms_norm(x, norm2_g) * (1 + sc2[:, None]) + sh2[:, None]
    ffn = (jax.nn.silu(h2 @ w_ffn_gate) * (h2 @ w_ffn_up)) @ w_ffn_down
    x = x + gt2[:, None] * ffn
    return (x, s_new.astype(s_prev.dtype), new_shift.astype(shift_prev.dtype), kf, vf,
            logf.astype(logf_past.dtype))


def setup_inputs(seed: int = 0) -> dict:
    key = jax.random.key(seed)
    ks = jax.random.split(key, 32)
    f32 = jnp.float32

    def nrm(i, shape, s=1.0):
        return s * jax.random.normal(ks[i], shape, f32)

    L = DEPTH
    return {
        'x_prompt': nrm(0, (BATCH, SEQ, D_MODEL)),
        'x_sample': nrm(1, (DEC_BATCH, DEC_SEQ, D_MODEL)),
        'cache_fox_k': nrm(2, (L, DEC_BATCH, PAST_LEN, FOX_HEADS, HEAD_DIM)),
        'cache_fox_v': nrm(3, (L, DEC_BATCH, PAST_LEN, FOX_HEADS, HEAD_DIM)),
        'cache_fox_logf': jax.nn.log_sigmoid(FORGET_BIAS_INIT + nrm(4, (L, DEC_BATCH, PAST_LEN, FOX_HEADS))),
        'state_rwkv': nrm(5, (L, DEC_BATCH, RWKV_HEADS, HEAD_DIM, HEAD_DIM), 0.5),
        'state_rwkv_shift': nrm(6, (L, DEC_BATCH, 1, RWKV_COLS)),
        'c_prompt': nrm(7, (BATCH, D_MODEL)),
        'c_sample': nrm(8, (DEC_BATCH, D_MODEL)),
        'norm1_g': 1.0 + nrm(9, (L, D_MODEL), 0.1),
        'w_ada': nrm(10, (L, D_MODEL, 6 * D_MODEL), 0.5 * D_MODEL ** -0.5),
        'b_ada': nrm(11, (L, 6 * D_MODEL), 0.01),
        'w_in': nrm(12, (L, D_MODEL, IN_COLS), D_MODEL ** -0.5),
        'shift_mu': jax.random.uniform(ks[13], (L, RWKV_COLS), f32),
        'w0': nrm(14, (L, RWKV_WIDTH), 0.5),
        'w_decay_up': nrm(15, (L, DECAY_LORA, RWKV_WIDTH), 0.5 * DECAY_LORA ** -0.5),
        'a0': nrm(16, (L, RWKV_WIDTH), 0.5),
        'w_aaa_up': nrm(17, (L, AAA_LORA, RWKV_WIDTH), AAA_LORA ** -0.5),
        'w_gate_up': nrm(18, (L, GATE_LORA, RWKV_WIDTH), GATE_LORA ** -0.5),
        'k_k': 1.0 + nrm(19, (L, RWKV_WIDTH), 0.1),
        'k_a': 1.0 + nrm(20, (L, RWKV_WIDTH), 0.1),
        'r_k': nrm(21, (L, RWKV_HEADS, HEAD_DIM), 0.1),
        'gn_g': 1.0 + nrm(22, (L, RWKV_WIDTH), 0.1),
        'gn_b': nrm(23, (L, RWKV_WIDTH), 0.01),
        'fox_q_g': 1.0 + nrm(24, (L, HEAD_DIM), 0.1),
        'fox_k_g': 1.0 + nrm(25, (L, HEAD_DIM), 0.1),
        'fox_f_b': FORGET_BIAS_INIT + nrm(26, (L, FOX_HEADS), 0.5),
        'w_out': nrm(27, (L, MIX_WIDTH, D_MODEL), MIX_WIDTH ** -0.5),
        'norm2_g': 1.0 + nrm(28, (L, D_MODEL), 0.1),
        'w_ffn_gate': nrm(29, (L, D_MODEL, D_FF), D_MODEL ** -0.5),
        'w_ffn_up': nrm(30, (L, D_MODEL, D_FF), D_MODEL ** -0.5),
        'w_ffn_down': nrm(31, (L, D_FF, D_MODEL), D_FF ** -0.5),
    }


def reference(x_prompt, x_sample, cache_fox_k, cache_fox_v, cache_fox_logf, state_rwkv, state_rwkv_shift,
              c_prompt, c_sample, norm1_g, w_ada, b_ada, w_in, shift_mu, w0, w_decay_up, a0, w_aaa_up,
              w_gate_up, k_k, k_a, r_k, gn_g, gn_b, fox_q_g, fox_k_g, fox_f_b, w_out, norm2_g,
              w_ffn_gate, w_ffn_up, w_ffn_down):
    assert x_sample.shape[1] <= CHUNK
    bp = x_prompt.shape[0]
    zero_shift = jnp.zeros((bp, 1, RWKV_COLS), state_rwkv_shift.dtype)
    zero_state = jnp.zeros((bp, RWKV_HEADS, HEAD_DIM, HEAD_DIM), state_rwkv.dtype)
    zero_kv = jnp.zeros((bp, 0, FOX_HEADS, HEAD_DIM), cache_fox_k.dtype)
    zero_logf = jnp.zeros((bp, 0, FOX_HEADS), cache_fox_logf.dtype)
    hp, hs = x_prompt, x_sample
    rs_p, sh_p, k_p, v_p, lf_p = [], [], [], [], []
    rs_s, sh_s, k_s, v_s, lf_s = [], [], [], [], []
    for l in range(DEPTH):
        lw = (norm1_g[l], w_ada[l], b_ada[l], w_in[l], shift_mu[l], w0[l], w_decay_up[l], a0[l],
              w_aaa_up[l], w_gate_up[l], k_k[l], k_a[l], r_k[l], gn_g[l], gn_b[l], fox_q_g[l],
              fox_k_g[l], fox_f_b[l], w_out[l], norm2_g[l], w_ffn_gate[l], w_ffn_up[l], w_ffn_down[l])
        hp, s1, s2, s3, s4, s5 = _layer(hp, c_prompt, zero_shift, zero_state, zero_kv, zero_kv, zero_logf, *lw)
        rs_p.append(s1); sh_p.append(s2); k_p.append(s3); v_p.append(s4); lf_p.append(s5)
        hs, u1, u2, u3, u4, u5 = _layer(hs, c_sample, state_rwkv_shift[l], state_rwkv[l], cache_fox_k[l],
                                        cache_fox_v[l], cache_fox_logf[l], *lw)
        rs_s.append(u1); sh_s.append(u2); k_s.append(u3); v_s.append(u4); lf_s.append(u5)
    y_prompt, y_sample = hp, hs
    rwkv_state_p, rwkv_shift_p = jnp.stack(rs_p), jnp.stack(sh_p)
    fox_k_p, fox_v_p, fox_logf_p = jnp.stack(k_p), jnp.stack(v_p), jnp.stack(lf_p)
    rwkv_state_s, rwkv_shift_s = jnp.stack(rs_s), jnp.stack(sh_s)
    fox_k_s, fox_v_s, fox_logf_s = jnp.stack(k_s), jnp.stack(v_s), jnp.stack(lf_s)
    return (y_prompt, y_sample, rwkv_state_p, rwkv_shift_p, fox_k_p, fox_v_p, fox_logf_p,
            rwkv_state_s, rwkv_shift_s, fox_k_s, fox_v_s, fox_logf_s)
```

```python
import contextlib
import math
import numpy as np
import concourse.bass as bass
import concourse.mybir as mybir
from concourse.bass_utils import run_bass_kernel_spmd

F32 = mybir.dt.float32
BF16 = mybir.dt.bfloat16
AF = mybir.ActivationFunctionType
ALU = mybir.AluOpType
AX = mybir.AxisListType

ENGS = ('pe', 'dve', 'act', 'pool', 'sp')

D = 1024
HD = 64
NH = 8
RW = 512
RCOLS = 1792
INC = 3336
DFF = 2816
NFB = DFF // 128
EPS = 1e-6
GN_EPS = 64e-5
C0 = math.exp(-0.5)
NEG = -30000.0


class Buf:
    __slots__ = ('last_write', 'readers')

    def __init__(self):
        self.last_write = None
        self.readers = []


class Prog:
    def __init__(self, nc, st, n_dma_sems=10):
        self.nc = nc
        self.q = {e: [] for e in ENGS}
        self.count = {e: 0 for e in ENGS}
        self.seen = {e: {} for e in ENGS}
        self.n_dma_sems = n_dma_sems
        self.dma_next = {e: 0 for e in ENGS}
        self.dma_val = {}
        names = list(ENGS)
        for e in ('sp', 'pool', 'act'):
            for i in range(n_dma_sems):
                k = f'd_{e}_{i}'
                names.append(k)
                self.dma_val[k] = 0
        self.sems = {n: st.enter_context(nc.semaphore(n)) for n in names}
        self.n_ops = 0
        self.noself = ()
        self.wswap = False

    def _deps(self, q, reads, writes, extra=()):
        need = {}

        def add(tok):
            if tok is None:
                return
            k, v = tok
            if need.get(k, 0) < v:
                need[k] = v
        for b in reads:
            add(b.last_write)
        for b in writes:
            add(b.last_write)
            for r in b.readers:
                add(r)
        for t in extra:
            add(t)
        waits = []
        for k, v in need.items():
            if k == q and (q == 'pe' or q in self.noself):
                continue
            if self.seen[q].get(k, 0) >= v:
                continue
            self.seen[q][k] = v
            waits.append((k, v))
        waits.sort(key=lambda kv: kv[0] == q)
        return waits

    def _mark(self, tok, reads, writes):
        for b in reads:
            if len(b.readers) > 6:
                m = {}
                for k, v in b.readers:
                    if m.get(k, 0) < v:
                        m[k] = v
                b.readers = list(m.items())
            b.readers.append(tok)
        for b in writes:
            b.last_write = tok
            b.readers = []

    def op(self, q, fn, reads=(), writes=()):
        waits = self._deps(q, reads, writes)
        self.count[q] += 1
        tok = (q, self.count[q])
        self.q[q].append((waits, fn, (q, 1)))
        self._mark(tok, reads, writes)
        self.n_ops += 1
        return tok

    def dma(self, q, out, in_, reads=(), writes=(), **kw):
        i = self.dma_next[q]
        self.dma_next[q] = (i + 1) % self.n_dma_sems
        k = f'd_{q}_{i}'
        prev = self.dma_val[k]
        ex = [(k, prev)] if prev > 0 else []
        waits = self._deps(q, reads, writes, ex)
        self.dma_val[k] = prev + 16
        tok = (k, prev + 16)
        self.q[q].append((waits, lambda e: e.dma_start(out=out, in_=in_, **kw), (k, 16)))
        self._mark(tok, reads, writes)
        self.n_ops += 1
        return tok

    def dma_fn(self, q, fn, reads=(), writes=()):
        i = self.dma_next[q]
        self.dma_next[q] = (i + 1) % self.n_dma_sems
        k = f'd_{q}_{i}'
        prev = self.dma_val[k]
        ex = [(k, prev)] if prev > 0 else []
        waits = self._deps(q, reads, writes, ex)
        self.dma_val[k] = prev + 16
        tok = (k, prev + 16)
        self.q[q].append((waits, fn, (k, 16)))
        self._mark(tok, reads, writes)
        self.n_ops += 1
        return tok

    def wait_all_dma(self, q):
        toks = [(k, v) for k, v in self.dma_val.items() if v > 0]
        waits = self._deps(q, (), (), toks)
        self.q[q].append((waits, None, None))

    def emit(self):
        nc = self.nc
        sems = self.sems
        with nc.Block() as block:
            handles = {'pe': block.tensor, 'dve': block.vector, 'act': block.scalar,
                       'pool': block.gpsimd, 'sp': block.sync}
            for e in ENGS:
                ops = self.q[e]
                if not ops:
                    continue

                def body(eng, ops=ops):
                    for waits, fn, inc in ops:
                        if self.wswap:
                            waits = list(reversed(waits))
                        for k, v in waits:
                            eng.wait_ge(sems[k], v)
                        if fn is not None:
                            ins = fn(eng)
                            if inc is not None:
                                ins.then_inc(sems[inc[0]], inc[1])
                handles[e](body)
        self.q = {e: [] for e in ENGS}


class TB:
    def __init__(self, t, n=1):
        self.t = t
        self.bs = [Buf() for _ in range(n)]

    @property
    def b(self):
        return self.bs[0]

    def __getitem__(self, k):
        return self.t[k]


class KB:
    def __init__(self, SEQ, PAST, NS, debug=False, phases=(1, 2, 3), sub=None):
        self.SEQ, self.PAST, self.NS, self.debug = SEQ, PAST, NS, debug
        self.phases = phases
        self.sub = sub or {}
        self.nc = bass.Bass("TRN2", target_bir_lowering=False)
        self.uid = 0

    def bb(self):
        self._bb = (getattr(self, '_bb', -1) + 1) % 4
        return self.pA[2 + self._bb]

    def mm(self, out, lhsT, rhs, start=True, stop=True, R=(), W=()):
        return self.P.op('pe', lambda e: e.matmul(out, lhsT, rhs, start=start, stop=stop), R, W)

    def tr(self, out, in_, ident, R=(), W=()):
        return self.P.op('pe', lambda e: e.transpose(out, in_, ident), R, W)

    def act(self, out, in_, func, bias=None, scale=None, accum=None, R=(), W=()):
        kw = {}
        if bias is not None:
            kw['bias'] = bias
        if scale is not None:
            kw['scale'] = scale
        if accum is not None:
            kw['accum_out'] = accum
        return self.P.op('act', lambda e: e.activation(out, in_, func, **kw), R, W)

    def ts(self, eng, out, in0, s1, s2=None, op0=ALU.mult, op1=None, R=(), W=()):
        if op1 is None:
            return self.P.op(eng, lambda e: e.tensor_scalar(out, in0, s1, None, op0), R, W)
        return self.P.op(eng, lambda e: e.tensor_scalar(out, in0, s1, s2, op0, op1), R, W)

    def tt(self, eng, out, in0, in1, op, R=(), W=()):
        return self.P.op(eng, lambda e: e.tensor_tensor(out, in0, in1, op=op), R, W)

    def stt(self, out, in0, scalar, in1, op0, op1, R=(), W=()):
        return self.P.op('dve', lambda e: e.scalar_tensor_tensor(out, in0, scalar, in1, op0, op1), R, W)

    def cp(self, eng, out, in_, R=(), W=()):
        if eng == 'act':
            return self.P.op('act', lambda e: e.activation(out, in_, AF.Copy), R, W)
        return self.P.op(eng, lambda e: e.tensor_copy(out, in_), R, W)

    def red(self, out, in_, op=ALU.add, R=(), W=()):
        return self.P.op('dve', lambda e: e.tensor_reduce(out, in_, AX.X, op), R, W)

    def recip(self, out, in_, R=(), W=()):
        return self.P.op('dve', lambda e: e.reciprocal(out, in_), R, W)

    def memset(self, eng, ap, val, W=()):
        return self.P.op(eng, lambda e: e.memset(ap, val), (), W)

    def dma(self, q, out, in_, R=(), W=(), **kw):
        return self.P.dma(q, out, in_, R, W, **kw)

    def sb(self, st, shape, dt, n=1, name=None):
        self.uid += 1
        t = st.enter_context(self.nc.sbuf_tensor(f"{name or 's'}_{self.uid}", list(shape), dt))
        return TB(t, n)

    def ps(self, st, shape, dt, n=1, name=None):
        self.uid += 1
        t = st.enter_context(self.nc.psum_tensor(f"{name or 'p'}_{self.uid}", list(shape), dt))
        return TB(t, n)

    def din(self, name, shape, dt=F32):
        return self.nc.dram_tensor(name, list(shape), dt, kind="ExternalInput").ap()

    def dout(self, name, shape, dt=F32):
        return self.nc.dram_tensor(name, list(shape), dt, kind="ExternalOutput").ap()

    def dscr(self, name, shape, dt=BF16):
        kind = "ExternalOutput" if self.debug else "Internal"
        return self.nc.dram_tensor(name, list(shape), dt, kind=kind).ap()

    def build(self):
        nc = self.nc
        SEQ, PAST, NS = self.SEQ, self.PAST, self.NS
        I = {}
        self.I = I
        I['half'] = self.din('half', (1, 1), mybir.dt.int32)
        I['xp3'] = self.din('xp3', (SEQ // 2, D))
        for nm, shp in [('xp', (SEQ, D)), ('xs', (NS, D)), ('cvec', (128, 8, 2)),
                        ('ck', (PAST, 512)), ('cv', (PAST, 512)), ('clf', (PAST, 8)),
                        ('st0', (64, 8, 64)), ('shp0', (128, 14)),
                        ('w_ada', (D, 6 * D)), ('b_ada', (128, 48)), ('w_in', (D, INC)),
                        ('w_out', (D, D)), ('w_g', (D, DFF)), ('w_u', (D, DFF)), ('w_d', (DFF, D)),
                        ('n1g', (128, 8)), ('n2g', (128, 8)), ('mu', (128, 14)),
                        ('w0', (128, 4)), ('a0', (128, 4)), ('k_k', (128, 4)), ('k_a', (128, 4)),
                        ('r_k', (128, 4)), ('gn_g', (128, 4)), ('gn_b', (128, 4)),
                        ('w_dec', (64, 512)), ('w_aaa', (64, 512)), ('w_gup', (128, 512)),
                        ('gqk', (128, 1024)), ('f_b', (128, 8)),
                        ('c_ident', (128, 128)), ('c_triu', (128, 128)), ('c_ones', (128, 128)),
                        ('c_blk', (128, 128)), ('c_maskA', (128, 256)), ('c_maskC', (128, 128)),
                        ('c_maskD', (128, 128)), ('c_reset', (128, 1024)), ('c_ipair', (128, 64))]:
            I[nm] = self.din(nm, shp)
        self.I = I
        O = {}
        for nm, shp in [('y_p', (SEQ // 2, D)), ('y_s', (NS, D)), ('st_p', (8, 64, 64)), ('sh_p', (128, 14)),
                        ('k_p', (SEQ, 512)), ('v_p', (SEQ, 512)), ('lf_p', (SEQ, 8)),
                        ('st_s', (8, 64, 64)), ('sh_s', (128, 14)),
                        ('k_s', (NS, 512)), ('v_s', (NS, 512)), ('lf_s', (NS, 8))]:
            O[nm] = self.dout(nm, shp)
        self.O = O
        self.G = []
        for gi, (T, past) in enumerate([(SEQ, 0), (NS, PAST)]):
            tot = past + T
            nkb = (tot + 127) // 128
            g = dict(gi=gi, T=T, past=past, tot=tot, nkb=nkb,
                     Qs=self.dscr(f'Qs{gi}', (8, 67, T)), Ks=self.dscr(f'Ks{gi}', (8, 67, tot)),
                     Vs=self.dscr(f'Vs{gi}', (8, 128, nkb, 64)), Ys=self.dscr(f'Ys{gi}', (D, T)),
                     x=I['xp'] if gi == 0 else I['xs'])
            g['o_y'], g['o_st'], g['o_sh'], g['o_k'], g['o_v'], g['o_lf'] = (
                (O['y_p'], O['st_p'], O['sh_p'], O['k_p'], O['v_p'], O['lf_p']) if gi == 0 else
                (O['y_s'], O['st_s'], O['sh_s'], O['k_s'], O['v_s'], O['lf_s']))
            self.G.append(g)

        with contextlib.ExitStack() as st:
            self.P = Prog(nc, st)
            self.P.noself = tuple(self.sub.get('noself', ()))
            self.persistent(st)
            ph = self.phases
            with contextlib.ExitStack() as s0:
                self.phase0(s0)
                self.P.wait_all_dma('sp')
                self.P.emit()
            if 1 in ph:
                with contextlib.ExitStack() as s1:
                    self.phase1(s1)
                    self.P.wait_all_dma('sp')
                    self.P.emit()
            if 2 in ph:
                with contextlib.ExitStack() as s2:
                    self.phase2(s2)
                    self.P.wait_all_dma('sp')
                    self.P.emit()
            with contextlib.ExitStack() as s3:
                if 3 in ph:
                    self.phase3(s3)
                self.P.wait_all_dma('sp')
                self.P.emit()
        return nc

    def persistent(self, st):
        I = self.I
        c = {}
        def ld(nm, shape, dt=F32, q='sp'):
            t = self.sb(st, shape, dt, name=nm)
            self.dma('pool' if dt == BF16 else q, t[:], I[nm], W=[t.b])
            return t
        c['ident_f'] = ld('c_ident', (128, 128))
        c['triu_f'] = ld('c_triu', (128, 128))
        c['ones_f'] = ld('c_ones', (128, 128))
        c['maskA'] = ld('c_maskA', (128, 256))
        c['maskC'] = ld('c_maskC', (128, 128))
        c['reset'] = ld('c_reset', (128, 1024))
        c['ipair'] = ld('c_ipair', (128, 64))
        c['ident_b'] = self.sb(st, (128, 128), BF16, name='identb')
        self.dma('pool', c['ident_b'][:], I['c_ident'], W=[c['ident_b'].b])
        c['blk_b'] = self.sb(st, (128, 128), BF16, name='blkb')
        self.dma('pool', c['blk_b'][:], I['c_blk'], W=[c['blk_b'].b])
        c['maskD_b'] = self.sb(st, (128, 128), BF16, name='maskDb')
        self.dma('pool', c['maskD_b'][:], I['c_maskD'], W=[c['maskD_b'].b])
        for nm in ['n1g', 'n2g']:
            c[nm] = ld(nm, (128, 8))
        c['mu'] = ld('mu', (128, 14))
        for nm in ['w0', 'a0', 'k_k', 'k_a', 'r_k', 'gn_g', 'gn_b']:
            c[nm] = ld(nm, (128, 4))
        c['f_b'] = ld('f_b', (128, 8))
        c['eps'] = self.sb(st, (128, 1), F32, name='eps')
        self.memset('dve', c['eps'][:], EPS, W=[c['eps'].b])
        c['eps64'] = self.sb(st, (128, 1), F32, name='eps64')
        self.memset('dve', c['eps64'][:], 64 * EPS, W=[c['eps64'].b])
        c['gneps'] = self.sb(st, (128, 1), F32, name='gneps')
        self.memset('dve', c['gneps'][:], GN_EPS, W=[c['gneps'].b])
        c['omu'] = self.sb(st, (128, 14), F32, name='omu')
        self.ts('dve', c['omu'][:], c['mu'][:], -1.0, 1.0, ALU.mult, ALU.add, R=[c['mu'].b], W=[c['omu'].b])
        c['omka'] = self.sb(st, (128, 4), F32, name='omka')
        self.ts('dve', c['omka'][:], c['k_a'][:], -1.0, 1.0, ALU.mult, ALU.add, R=[c['k_a'].b], W=[c['omka'].b])
        c['mod'] = self.sb(st, (128, 48, 2), F32, name='mod')
        self.c = c
        for g in self.G:
            g['gm1'] = self.sb(st, (128, 8), F32, name='gm1')
            g['gm2'] = self.sb(st, (128, 8), F32, name='gm2')
            g['sh1'] = self.sb(st, (128, 8), F32, name='sh1')
            g['sh2'] = self.sb(st, (128, 8), F32, name='sh2')
            g['neglc'] = self.sb(st, (128, g['nkb'], 8), F32, name='neglc')
        self.pT = [self.ps(st, (128, 1024), BF16, name='pT') for _ in range(2)]
        self.pA = [self.ps(st, (128, 512), F32, name='pA') for _ in range(6)]

    def phase0(self, st):
        I, c = self.I, self.c
        cv = self.sb(st, (128, 8, 2), F32)
        self.dma('sp', cv[:], I['cvec'], W=[cv.b])
        cs = self.sb(st, (128, 8, 2), F32)
        self.act(cs[:], cv[:], AF.Silu, R=[cv.b], W=[cs.b])
        bada = self.sb(st, (128, 48), F32)
        self.dma('sp', bada[:], I['b_ada'], W=[bada.b])
        wa = [self.sb(st, (128, 8, 512), F32) for _ in range(2)]
        wsrc = I['w_ada'].rearrange("(k p) c -> p k c", p=128)
        pm = self.pA[0]
        for ch in range(12):
            w = wa[ch % 2]
            self.dma('sp', w[:], wsrc[:, :, ch * 512:(ch + 1) * 512], W=[w.b])
            for cbl in range(4):
                gb = ch * 4 + cbl
                for k in range(8):
                    self.mm(pm[:, gb * 2:gb * 2 + 2], w[:, k, cbl * 128:(cbl + 1) * 128], cs[:, k, :],
                            start=(k == 0), stop=(k == 7), R=[w.b, cs.b], W=[pm.b])
        mod = c['mod']
        self.tt('dve', mod[:], pm[:, 0:96].rearrange("p (a b) -> p a b", b=2),
                bada[:].unsqueeze(2).to_broadcast([128, 48, 2]), ALU.add, R=[pm.b, bada.b], W=[mod.b])
        tmp = self.sb(st, (128, 8), F32)
        for g in self.G:
            gi = g['gi']
            for (dst, blk0, ng) in [(g['gm1'], 8, c['n1g']), (g['gm2'], 32, c['n2g'])]:
                self.ts('dve', tmp[:], mod[:, blk0:blk0 + 8, gi], 1.0, None, ALU.add, R=[mod.b], W=[tmp.b])
                self.tt('dve', dst[:], tmp[:], ng[:], ALU.mult, R=[tmp.b, ng.b], W=[dst.b])
            self.cp('dve', g['sh1'][:], mod[:, 0:8, gi], R=[mod.b], W=[g['sh1'].b])
            self.cp('dve', g['sh2'][:], mod[:, 24:32, gi], R=[mod.b], W=[g['sh2'].b])

    def build_gates(self, g, gt1, gt2, tl):
        c = self.c
        mod = c['mod']
        gi = g['gi']
        n = 0
        for (dst, blk0) in [(gt1, 16), (gt2, 40)]:
            for half in range(2):
                pg = self.pA[4 + (n % 2)]
                n += 1
                for j in range(4):
                    blk = half * 4 + j
                    t = tl[blk % 2]
                    self.ts('dve', t[:, 0:128], c['ones_f'][:], mod[:, blk0 + blk, gi:gi + 1], None, ALU.mult,
                            R=[c['ones_f'].b, mod.b], W=[t.b])
                    self.mm(pg[:, j * 128:(j + 1) * 128], t[:, 0:128], c['ident_f'][:], R=[t.b, c['ident_f'].b], W=[pg.b])
                self.cp('act', dst[:, half * 512:(half + 1) * 512], pg[:], R=[pg.b], W=[dst.b])

    def norm_transpose(self, xt, nb, bs, gm, sh, xsb, hT, junk, ss, rstd, pT_sel, alt=False):
        c = self.c
        for blk in range(nb):
            self.act(junk[0:bs, :], xt[0:bs, blk, :], AF.Square, accum=ss[0:bs, blk:blk + 1],
                     R=[xt.b], W=[junk.b, ss.b])
        self.act(rstd[0:bs, 0:nb], ss[0:bs, 0:nb], AF.Sqrt, bias=c['eps'][0:bs, :], scale=1.0 / D,
                 R=[ss.b, c['eps'].b], W=[rstd.b])
        self.recip(rstd[0:bs, 0:nb], rstd[0:bs, 0:nb], R=[rstd.b], W=[rstd.b])
        for blk in range(nb):
            self.act(xsb[0:bs, blk, :], xt[0:bs, blk, :], AF.Copy, scale=rstd[0:bs, blk:blk + 1],
                     R=[xt.b, rstd.b], W=[xsb.b])
        nt = (nb - 1) * 128 + bs
        for k in range(8):
            pt = self.pT[(pT_sel + k) % 2] if alt else self.pT[pT_sel]
            for blk in range(nb):
                self.tr(pt[:, blk * 128:blk * 128 + bs], xsb[0:bs, blk, k * 128:(k + 1) * 128],
                        c['ident_b'][0:bs, 0:bs], R=[xsb.b, c['ident_b'].b], W=[pt.b])
            self.ts('dve', hT[:, k, 0:nt], pt[:, 0:nt], gm[:, k:k + 1], sh[:, k:k + 1], ALU.mult, ALU.add,
                    R=[pt.b, gm.b, sh.b], W=[hT.b])

    def phase1(self, st):
        I, c = self.I, self.c
        win = self.sb(st, (128, 8, INC), BF16, name='win')
        wsrc = I['w_in'].rearrange("(k p) c -> p k c", p=128)
        for k in range(8):
            for (a, b) in [(0, 1792), (1792, INC)]:
                self.dma('pool', win[:, k, a:b], wsrc[:, k, a:b], W=[win.b])
        wdec = self.sb(st, (64, 512), BF16, name='wdec')
        self.dma('pool', wdec[:], I['w_dec'], W=[wdec.b])
        waaa = self.sb(st, (128, 512), BF16, name='waaa')
        self.dma('pool', waaa[64:128, :], I['w_aaa'], W=[waaa.b])
        wgup = self.sb(st, (128, 512), BF16, name='wgup')
        self.dma('pool', wgup[:], I['w_gup'], W=[wgup.b])
        gqk = self.sb(st, (128, 1024), F32, name='gqk')
        self.dma('sp', gqk[:], I['gqk'], W=[gqk.b])
        self.win, self.wdec, self.waaa, self.wgup, self.gqk = win, wdec, waaa, wgup, gqk

        NT = 128
        self.NT1 = NT
        W = {}
        W['xt'] = [self.sb(st, (128, 1, D), F32, name='xt') for _ in range(2)]
        W['xsb'] = self.sb(st, (128, 1, D), BF16, name='xsb')
        W['hT'] = self.sb(st, (128, 8, NT), BF16, name='hT')
        W['junk'] = self.sb(st, (128, D), BF16, name='junk')
        W['ss'] = self.sb(st, (128, 4), F32)
        W['rstd'] = self.sb(st, (128, 4), F32)
        W['sq'] = self.sb(st, (128, 1024), F32, name='sq')
        W['ss16'] = self.sb(st, (128, 16), F32)
        W['rs16'] = self.sb(st, (128, 16), F32)
        W['qkn'] = self.sb(st, (128, 1024), F32, name='qkn')
        W['kout'] = [self.sb(st, (128, 512), F32, name='kout') for _ in range(2)]
        W['qkb'] = self.sb(st, (128, 1024), BF16, name='qkb')
        W['QKt'] = [self.sb(st, (128, 8, NT), BF16, name='QKt') for _ in range(2)]
        W['v32'] = [self.sb(st, (128, 512), F32, name='v32') for _ in range(2)]
        W['vb'] = [self.sb(st, (128, 1, 512), BF16, name='vb') for _ in range(2)]
        W['lf'] = [self.sb(st, (128, 8), F32, name='lf') for _ in range(2)]
        W['lft'] = self.sb(st, (128, 8), F32)
        W['lcT'] = self.sb(st, (8, NT), F32)
        W['lcr'] = self.sb(st, (8, NT), F32)
        W['lcs'] = [self.sb(st, (8, 3, NT), BF16) for _ in range(2)]
        W['lchf'] = self.sb(st, (8, NT), F32)
        W['R'] = self.sb(st, (128, 8), F32, name='Racc')
        W['onesb'] = self.sb(st, (3, 2112), BF16, name='onesb')
        self.memset('pool', W['onesb'][:], 1.0, W=[W['onesb'].b])
        for nm in ['z']:
            W[nm] = self.sb(st, (128, 14, NT), F32, name=nm)
        W['t1'] = [self.sb(st, (128, NT), F32, name='t1') for _ in range(2)]
        W['carry'] = self.sb(st, (128, 14), F32, name='carry')
        for nm in ['esig', 'aa', 'kk', 'kp', 'cs', 'gx', 'tmpf', 'beta']:
            W[nm] = self.sb(st, (128, 4, NT), F32, name=nm)
        for nm in ['sqb', 'rkb']:
            W[nm] = self.sb(st, (128, 4, NT), BF16, name=nm)
        W['HO'] = []
        for _ in range(2):
            ho = {}
            for nm in ['BtT', 'KtT', 'BgT', 'KgT', 'vTb', 'gT', 'bonus']:
                ho[nm] = self.sb(st, (128, 4, NT), BF16, name=nm)
            ho['gam'] = self.sb(st, (128, 4, NT), F32, name='gam')
            ho['AR'] = self.sb(st, (128, 4, 1, 2, 128), BF16, name='AR')
            W['HO'].append(ho)
        W['tdw'] = self.sb(st, (128, NT), BF16, name='tdw')
        W['dab'] = self.sb(st, (128, NT), BF16, name='dab')
        W['sg'] = self.sb(st, (128, NT), BF16, name='sg')
        W['nb16'] = self.sb(st, (128, 4, 1), F32, name='nb16')
        W['Bgt'] = self.sb(st, (128, 512), BF16, name='Bgt')
        W['Kgt'] = self.sb(st, (128, 512), BF16, name='Kgt')
        W['Vt'] = self.sb(st, (128, 512), BF16, name='Vt')
        W['MLt'] = self.sb(st, (128, 8, 256), BF16, name='MLt')
        W['MKt'] = self.sb(st, (128, 8, 256), BF16, name='MKt')
        W['Lc'] = [self.sb(st, (128, 8, 128), BF16, name='Lc') for _ in range(2)]
        W['Mc'] = [self.sb(st, (128, 8, 128), BF16, name='Mc') for _ in range(2)]
        W['Xf'] = self.sb(st, (128, 8, 128), F32, name='Xf')
        W['Xb'] = self.sb(st, (128, 8, 128), BF16, name='Xb')
        W['GT'] = self.sb(st, (128, 4, 64), BF16, name='GT')
        W['Hs'] = self.sb(st, (128, 4, 64), F32, name='Hs')
        W['RAT'] = self.sb(st, (128, 4, 128), BF16, name='RAT')
        W['Sf'] = self.sb(st, (128, 4, 64), F32, name='Sf')
        W['Sb'] = self.sb(st, (128, 4, 64), BF16, name='Sb')
        W['ysb'] = self.sb(st, (128, 8, 64), F32, name='ysb')
        W['ysq'] = self.sb(st, (128, 8, 64), F32, name='ysq')
        W['yh'] = self.sb(st, (128, 512), BF16, name='yh')
        W['st8'] = [self.sb(st, (128, 8), F32) for _ in range(4)]
        W['yT1'] = self.sb(st, (128, 4, 128), F32, name='yT1')
        W['yr'] = [self.sb(st, (128, 4, 128), BF16, name='yr') for _ in range(2)]
        self.nyr = 0
        W['Sv'] = self.sb(st, (64, 8, 64), F32, name='Sv')
        W['So'] = self.sb(st, (64, 4, 128), F32, name='So')
        self.W = W

        for g in self.G:
            self.phase1_group(g, NT)

    def phase1_group(self, g, NT):
        I, c, W = self.I, self.c, self.W
        gi, T, past = g['gi'], g['T'], g['past']
        for h in range(NH):
            for a in range(0, g['tot'], 2112):
                b = min(g['tot'], a + 2112)
                self.dma('sp', g['Ks'][h, 64:67, a:b], W['onesb'][:, 0:b - a], R=[W['onesb'].b])
        self.memset('dve', W['R'][:], 0.0, W=[W['R'].b])
        npb = past // 128
        for pb in range(0 if self.sub.get('skip_past') else npb):
            kc = W['kout'][pb % 2]
            self.dma('sp', kc[:], I['ck'][pb * 128:(pb + 1) * 128, :], W=[kc.b])
            self.cp('pool', W['qkb'][:, 512:1024], kc[:], R=[kc.b], W=[W['qkb'].b])
            self.k_transposes(g, pb * 128, 128, 0, only_k=True, blk=pb)
            self.flush_qk(g, pb * 128, 128, only_k=True, blk=pb)
            vc = W['v32'][pb % 2]
            self.dma('sp', vc[:], I['cv'][pb * 128:(pb + 1) * 128, :], W=[vc.b])
            vb = W['vb'][pb % 2]
            self.cp('pool', vb[:, 0, :], vc[:], R=[vc.b], W=[vb.b])
            self.dma('sp', g['Vs'][:, :, pb, :].rearrange("h p d -> p h d"),
                     vb[:, 0, :].rearrange("p (h d) -> p h d", d=64), R=[vb.b])
            lf = W['lf'][pb % 2]
            self.dma('sp', lf[:], I['clf'][pb * 128:(pb + 1) * 128, :], W=[lf.b])
            self.lc_block(g, lf, 128, pb, None, 0)
        if gi == 0:
            self.memset('dve', W['Sf'][:], 0.0, W=[W['Sf'].b])
            self.memset('pool', W['Sb'][:], 0.0, W=[W['Sb'].b])
            self.memset('dve', W['carry'][:], 0.0, W=[W['carry'].b])
        else:
            self.dma('sp', W['carry'][:], I['shp0'], W=[W['carry'].b])
            self.dma('sp', W['Sv'][:], I['st0'], W=[W['Sv'].b])
            for cb in range(4):
                pa = self.pA[cb % 2]
                self.tr(pa[:, 0:64], W['Sv'][:, 2 * cb:2 * cb + 2, :], c['ident_f'][0:64, 0:64],
                        R=[W['Sv'].b, c['ident_f'].b], W=[pa.b])
                self.cp('dve', W['Sf'][:, cb, :], pa[:, 0:64], R=[pa.b], W=[W['Sf'].b])
            self.cp('pool', W['Sb'][:], W['Sf'][:], R=[W['Sf'].b], W=[W['Sb'].b])
        ntile = (T + NT - 1) // NT
        def run(gens):
            gens = [x for x in gens if x is not None]
            while gens:
                for x in list(gens):
                    try:
                        next(x)
                    except StopIteration:
                        gens.remove(x)
        genB = None
        for ti in range(ntile):
            nt = min(NT, T - ti * NT)
            genA = self.phase1_tile(g, ti, ti * NT, nt)
            run([genA, genB])
            C_ = min(128, nt)
            genB = self.rwkv_chunk(g, ti, ti * NT, 0, C_, nt) if not self.sub.get('skip_rwkv') else None
        run([genB])
        if self.sub.get('skip_final'):
            return
        self.dma('sp', g['o_sh'], W['carry'][:], R=[W['carry'].b])
        for cb in range(4):
            pa = self.pA[cb % 2]
            self.tr(pa[0:64, 0:128], W['Sf'][:, cb, :], c['ident_f'][:], R=[W['Sf'].b, c['ident_f'].b], W=[pa.b])
            self.cp('dve', W['So'][:, cb, :], pa[0:64, 0:128], R=[pa.b], W=[W['So'].b])
        self.dma('sp', g['o_st'].rearrange("(cb hh) v k -> v cb hh k", hh=2),
                 W['So'][:].rearrange("v cb (hh k) -> v cb hh k", hh=2), R=[W['So'].b])

    def k_transposes(self, g, tok0, bs, col0, only_k, blk):
        c, W = self.c, self.W
        QKt = W['QKt'][g.get('qkt_sel', 0)]
        for j in range(4 if only_k else 8):
            jj = j + 4 if only_k else j
            pt = self.pT[0]
            self.tr(pt[:, 0:bs], W['qkb'][0:bs, jj * 128:(jj + 1) * 128], c['ident_b'][0:bs, 0:bs],
                    R=[W['qkb'].b, c['ident_b'].b], W=[pt.b])
            self.cp('act', QKt[:, jj, col0:col0 + bs], pt[:, 0:bs], R=[pt.b], W=[QKt.b])

    def flush_qk(self, g, tok0, n, only_k, blk=None, qtok0=None):
        W = self.W
        QKt = W['QKt'][g.get('qkt_sel', 0)]
        for h in range(NH):
            pb = 64 * (h % 2)
            self.dma('sp', g['Ks'][h, 0:64, tok0:tok0 + n], QKt[pb:pb + 64, 4 + h // 2, 0:n], R=[QKt.b])
            if not only_k:
                self.dma('sp', g['Qs'][h, 0:64, qtok0:qtok0 + n], QKt[pb:pb + 64, h // 2, 0:n], R=[QKt.b])
        g['qkt_sel'] = 1 - g.get('qkt_sel', 0)

    def lc_block(self, g, lf, bs, kb, lcT_cols, col0):
        c, W = self.c, self.W
        R = W['R']
        pa = self.pA[1]
        self.mm(pa[0:bs, 0:8], c['triu_f'][0:bs, 0:bs], lf[0:bs, :], start=True, stop=False,
                R=[c['triu_f'].b, lf.b], W=[pa.b])
        self.mm(pa[0:bs, 0:8], c['ones_f'][:, 0:bs], R[:, :], start=False, stop=True, R=[c['ones_f'].b, R.b], W=[pa.b])
        self.ts('dve', g['neglc'][0:bs, kb, :], pa[0:bs, 0:8], -1.0, None, ALU.mult, R=[pa.b], W=[g['neglc'].b])
        if lcT_cols is not None:
            o = 16 + col0
            self.mm(pa[0:8, o:o + bs], lf[0:bs, :], c['triu_f'][0:bs, 0:bs], start=True, stop=False,
                    R=[lf.b, c['triu_f'].b], W=[pa.b])
            self.mm(pa[0:8, o:o + bs], R[:, :], c['ones_f'][:, 0:bs], start=False, stop=True,
                    R=[R.b, c['ones_f'].b], W=[pa.b])
            self.cp('dve', W['lcT'][:, col0:col0 + bs], pa[0:8, o:o + bs], R=[pa.b], W=[W['lcT'].b])
        self.tt('dve', R[0:bs, :], R[0:bs, :], lf[0:bs, :], ALU.add, R=[R.b, lf.b], W=[R.b])

    def phase1_tile(self, g, ti, tok0, nt):
        I, c, W = self.I, self.c, self.W
        gi, past = g['gi'], g['past']
        nb = (nt + 127) // 128
        bs = min(128, nt)
        xt = W['xt'][ti % 2]
        self.dma('sp', xt[0:bs, 0:nb, :], g['x'][tok0:tok0 + nt, :].rearrange("(b p) d -> p b d", p=bs), W=[xt.b])
        hT = W['hT']
        self.norm_transpose(xt, nb, bs, g['gm1'], g['sh1'], W['xsb'], hT, W['junk'], W['ss'], W['rstd'], 0)
        win = self.win
        yield
        for blk in range(0 if self.sub.get('skip_fox') else nb):
            kb = (past + tok0) // 128 + blk
            pq, pk, pv, pf = self.pA[0], self.pA[1], self.pA[0], self.pA[1]
            def proj(pp, c0, wdt):
                for k in range(8):
                    self.mm(pp[0:bs, 0:wdt], hT[:, k, blk * 128:blk * 128 + bs], win[:, k, c0:c0 + wdt],
                            start=(k == 0), stop=(k == 7), R=[hT.b, win.b], W=[pp.b])
            proj(pq, 1792, 512)
            proj(pk, 2304, 512)
            yield
            sq, ss16, rs16, qkn = W['sq'], W['ss16'], W['rs16'], W['qkn']
            self.act(sq[0:bs, 0:512], pq[0:bs, :], AF.Square, R=[pq.b], W=[sq.b])
            self.act(sq[0:bs, 512:1024], pk[0:bs, :], AF.Square, R=[pk.b], W=[sq.b])
            self.red(ss16[0:bs, :], sq[0:bs, :].rearrange("p (g d) -> p g d", d=64), R=[sq.b], W=[ss16.b])
            self.act(rs16[0:bs, 0:8], ss16[0:bs, 0:8], AF.Sqrt, bias=c['eps64'][0:bs, :], scale=1.0,
                     R=[ss16.b, c['eps64'].b], W=[rs16.b])
            self.act(rs16[0:bs, 8:16], ss16[0:bs, 8:16], AF.Sqrt, bias=c['eps'][0:bs, :], scale=1.0 / 64,
                     R=[ss16.b, c['eps'].b], W=[rs16.b])
            self.recip(rs16[0:bs, :], rs16[0:bs, :], R=[rs16.b], W=[rs16.b])
            self.tt('dve', qkn[0:bs, 0:512].rearrange("p (g d) -> p g d", d=64),
                    pq[0:bs, :].rearrange("p (g d) -> p g d", d=64),
                    rs16[0:bs, 0:8].unsqueeze(2).to_broadcast([bs, 8, 64]), ALU.mult, R=[pq.b, rs16.b], W=[qkn.b])
            self.tt('dve', qkn[0:bs, 512:1024].rearrange("p (g d) -> p g d", d=64),
                    pk[0:bs, :].rearrange("p (g d) -> p g d", d=64),
                    rs16[0:bs, 8:16].unsqueeze(2).to_broadcast([bs, 8, 64]), ALU.mult, R=[pk.b, rs16.b], W=[qkn.b])
            kout = W['kout'][blk % 2]
            self.tt('dve', W['qkb'][0:bs, 0:512], qkn[0:bs, 0:512], self.gqk[0:bs, 0:512], ALU.mult,
                    R=[qkn.b, self.gqk.b], W=[W['qkb'].b])
            self.tt('dve', kout[0:bs, :], qkn[0:bs, 512:1024], self.gqk[0:bs, 512:1024], ALU.mult,
                    R=[qkn.b, self.gqk.b], W=[kout.b])
            self.dma('sp', g['o_k'][tok0 + blk * 128:tok0 + blk * 128 + bs, :], kout[0:bs, :], R=[kout.b])
            self.cp('pool', W['qkb'][0:bs, 512:1024], kout[0:bs, :], R=[kout.b], W=[W['qkb'].b])
            self.k_transposes(g, tok0, bs, blk * 128, only_k=False, blk=blk)
            yield
            proj(pv, 2816, 512)
            proj(pf, 3328, 8)
            v32 = W['v32'][blk % 2]
            self.cp('act', v32[0:bs, :], pv[0:bs, :], R=[pv.b], W=[v32.b])
            self.dma('sp', g['o_v'][tok0 + blk * 128:tok0 + blk * 128 + bs, :], v32[0:bs, :], R=[v32.b])
            vb = W['vb'][ti % 2]
            self.cp('pool', vb[0:bs, blk, :], v32[0:bs, :], R=[v32.b], W=[vb.b])
            yield
            lf = W['lf'][blk % 2]
            self.tt('dve', W['lft'][0:bs, :], pf[0:bs, 0:8], c['f_b'][0:bs, :], ALU.add, R=[pf.b, c['f_b'].b], W=[W['lft'].b])
            self.act(W['lft'][0:bs, :], W['lft'][0:bs, :], AF.Exp, scale=-1.0, R=[W['lft'].b], W=[W['lft'].b])
            self.act(W['lft'][0:bs, :], W['lft'][0:bs, :], AF.Ln, bias=1.0, scale=1.0, R=[W['lft'].b], W=[W['lft'].b])
            self.ts('dve', lf[0:bs, :], W['lft'][0:bs, :], -1.0, None, ALU.mult, R=[W['lft'].b], W=[lf.b])
            self.dma('sp', g['o_lf'][tok0 + blk * 128:tok0 + blk * 128 + bs, :], lf[0:bs, :], R=[lf.b])
            self.lc_block(g, lf, bs, kb, True, blk * 128)
        if self.sub.get('skip_fox'):
            if not self.sub.get('skip_rwkv'):
                yield from self.rwkv_tile(g, ti, tok0, nt)
            return
        self.flush_qk(g, past + tok0, nt, only_k=False, qtok0=tok0)
        vb = W['vb'][ti % 2]
        kb0 = (past + tok0) // 128
        self.dma('sp', g['Vs'][:, 0:bs, kb0:kb0 + nb, :].rearrange("h p b d -> p h b d"),
                 vb[0:bs, 0:nb, :].rearrange("p b (h d) -> p h b d", d=64), R=[vb.b])
        yield
        lcs = W['lcs'][ti % 2]
        lcT, lcr, lchf = W['lcT'], W['lcr'], W['lchf']
        self.cp('dve', lcs[:, 0, 0:nt], lcT[:, 0:nt], R=[lcT.b], W=[lcs.b])
        self.tt('dve', lcr[:, 0:nt], lcT[:, 0:nt], lcs[:, 0, 0:nt], ALU.subtract, R=[lcT.b, lcs.b], W=[lcr.b])
        self.cp('dve', lcs[:, 1, 0:nt], lcr[:, 0:nt], R=[lcr.b], W=[lcs.b])
        self.tt('dve', lchf[:, 0:nt], lcr[:, 0:nt], lcs[:, 1, 0:nt], ALU.subtract, R=[lcr.b, lcs.b], W=[lchf.b])
        self.cp('dve', lcs[:, 2, 0:nt], lchf[:, 0:nt], R=[lchf.b], W=[lcs.b])
        self.dma('sp', g['Qs'][:, 64:67, tok0:tok0 + nt], lcs[:, :, 0:nt], R=[lcs.b])
        if not self.sub.get('skip_rwkv'):
            yield from self.rwkv_tile(g, ti, tok0, nt)
        yield

    def rwkv_tile(self, g, ti, tok0, nt):
        I, c, W = self.I, self.c, self.W
        ho = W['HO'][ti % 2]
        win, hT = self.win, W['hT']
        C = min(128, nt)
        nch = nt // C
        z = W['z']
        carry = W['carry']
        for cb in range(14):
            pp = self.pA[cb % 2]
            for k in range(8):
                self.mm(pp[:, 0:nt], win[:, k, cb * 128:(cb + 1) * 128], hT[:, k, 0:nt],
                        start=(k == 0), stop=(k == 7), R=[win.b, hT.b], W=[pp.b])
            t1 = W['t1'][cb % 2]
            v = self.sub.get('v1', 15)
            if v & 1:
                self.act(t1[:, 1:nt], pp[:, 0:nt - 1], AF.Copy, scale=c['mu'][:, cb:cb + 1], R=[pp.b, c['mu'].b], W=[t1.b])
            if v & 2:
                self.ts('dve' if v & 16 else 'pool', t1[:, 0:1], carry[:, cb:cb + 1], c['mu'][:, cb:cb + 1], None, ALU.mult,
                        R=[carry.b, c['mu'].b], W=[t1.b])
            if v & 4:
                self.stt(z[:, cb, 0:nt], pp[:, 0:nt], c['omu'][:, cb:cb + 1], t1[:, 0:nt], ALU.mult, ALU.add,
                         R=[pp.b, c['omu'].b, t1.b], W=[z.b])
            if v & 8:
                self.cp('act', carry[:, cb:cb + 1], pp[:, nt - 1:nt], R=[pp.b], W=[carry.b])
            if cb % 2 == 1:
                yield
        if self.sub.get('rstop', 99) <= 1:
            return
        zr, zk, zv = z[:, 0:4, 0:nt], z[:, 4:8, 0:nt], z[:, 8:12, 0:nt]
        tdw, dab, sg = W['tdw'], W['dab'], W['sg']
        self.act(tdw[0:64, 0:nt], z[0:64, 12, 0:nt], AF.Tanh, R=[z.b], W=[tdw.b])
        self.cp('pool', dab[64:128, 0:nt], z[64:128, 12, 0:nt], R=[z.b], W=[dab.b])
        self.act(sg[:, 0:nt], z[:, 13, 0:nt], AF.Sigmoid, R=[z.b], W=[sg.b])
        esig, aa, gT = W['esig'], W['aa'], ho['gT']
        for cb in range(4):
            p1, p2, p3 = self.pA[0], self.pA[1], self.pA[0]
            o = (cb % 2) * 256
            self.mm(p1[:, o:o + nt], self.wdec[0:64, cb * 128:(cb + 1) * 128], tdw[0:64, 0:nt], R=[self.wdec.b, tdw.b], W=[p1.b])
            self.act(esig[:, cb, 0:nt], p1[:, o:o + nt], AF.Sigmoid, bias=c['w0'][:, cb:cb + 1], R=[p1.b, c['w0'].b], W=[esig.b])
            self.mm(p2[:, o:o + nt], self.waaa[64:128, cb * 128:(cb + 1) * 128], dab[64:128, 0:nt], R=[self.waaa.b, dab.b], W=[p2.b])
            self.act(aa[:, cb, 0:nt], p2[:, o:o + nt], AF.Sigmoid, bias=c['a0'][:, cb:cb + 1], R=[p2.b, c['a0'].b], W=[aa.b])
            self.mm(p3[:, o:o + nt], self.wgup[:, cb * 128:(cb + 1) * 128], sg[:, 0:nt], R=[self.wgup.b, sg.b], W=[p3.b])
            self.cp('dve', gT[:, cb, 0:nt], p3[:, o:o + nt], R=[p3.b], W=[gT.b])
        if self.sub.get('rstop', 99) <= 2:
            return
        yield
        kk, kp, sqb, tmpf = W['kk'], W['kp'], W['sqb'], W['tmpf']
        for cb in range(4):
            self.ts('dve', kk[:, cb, 0:nt], z[:, 4 + cb, 0:nt], c['k_k'][:, cb:cb + 1], None, ALU.mult, R=[z.b, c['k_k'].b], W=[kk.b])
        self.act(sqb[:, :, 0:nt], kk[:, :, 0:nt], AF.Square, R=[kk.b], W=[sqb.b])
        for cb in range(4):
            pp = self.pA[cb // 2]
            o = (cb % 2) * 256
            self.mm(pp[:, o:o + nt], c['blk_b'][:], sqb[:, cb, 0:nt], R=[c['blk_b'].b, sqb.b], W=[pp.b])
            self.act(tmpf[:, cb, 0:nt], pp[:, o:o + nt], AF.Sqrt, R=[pp.b], W=[tmpf.b])
        self.ts('dve', tmpf[:, :, 0:nt], tmpf[:, :, 0:nt], 1e-12, None, ALU.max, R=[tmpf.b], W=[tmpf.b])
        self.recip(tmpf[:, :, 0:nt], tmpf[:, :, 0:nt], R=[tmpf.b], W=[tmpf.b])
        self.tt('dve', kk[:, :, 0:nt], kk[:, :, 0:nt], tmpf[:, :, 0:nt], ALU.mult, R=[kk.b, tmpf.b], W=[kk.b])
        yield
        for cb in range(4):
            self.ts('dve', tmpf[:, cb, 0:nt], aa[:, cb, 0:nt], c['k_a'][:, cb:cb + 1], c['omka'][:, cb:cb + 1], ALU.mult, ALU.add,
                    R=[aa.b, c['k_a'].b, c['omka'].b], W=[tmpf.b])
        self.tt('dve', kp[:, :, 0:nt], zk, tmpf[:, :, 0:nt], ALU.mult, R=[z.b, tmpf.b], W=[kp.b])
        yield
        bonus, rkb = ho['bonus'], W['rkb']
        self.tt('dve', tmpf[:, :, 0:nt], zr, kp[:, :, 0:nt], ALU.mult, R=[z.b, kp.b], W=[tmpf.b])
        for cb in range(4):
            self.ts('dve', rkb[:, cb, 0:nt], tmpf[:, cb, 0:nt], c['r_k'][:, cb:cb + 1], None, ALU.mult, R=[tmpf.b, c['r_k'].b], W=[rkb.b])
        for cb in range(4):
            pp = self.pA[cb // 2]
            o = (cb % 2) * 256
            self.mm(pp[:, o:o + nt], c['blk_b'][:], rkb[:, cb, 0:nt], R=[c['blk_b'].b, rkb.b], W=[pp.b])
            self.tt('dve', bonus[:, cb, 0:nt], pp[:, o:o + nt], z[:, 8 + cb, 0:nt], ALU.mult, R=[pp.b, z.b], W=[bonus.b])
        if self.sub.get('rstop', 99) <= 3:
            return
        yield
        cs, gam, gx, beta = W['cs'], ho['gam'], W['gx'], W['beta']
        NTf = self.NT1
        if nt == NTf:
            self.P.op('dve', lambda e: e.tensor_tensor_scan(cs[:].rearrange("p a t -> p (a t)"), c['reset'][:, 0:4 * nt],
                                                            esig[:].rearrange("p a t -> p (a t)"), 0.0, op0=ALU.mult, op1=ALU.add),
                      [c['reset'].b, esig.b], [cs.b])
        else:
            for cb in range(4):
                self.P.op('dve', lambda e, cb=cb: e.tensor_tensor_scan(cs[:, cb, 0:nt], c['reset'][:, 0:nt], esig[:, cb, 0:nt], 0.0,
                                                                      op0=ALU.mult, op1=ALU.add),
                          [c['reset'].b, esig.b], [cs.b])
        self.act(gam[:, :, 0:nt], cs[:, :, 0:nt], AF.Exp, scale=-C0, R=[cs.b], W=[gam.b])
        AR, BtT, KtT, BgT, KgT, vTb = ho['AR'], ho['BtT'], ho['KtT'], ho['BgT'], ho['KgT'], ho['vTb']
        def chv(ap):
            return ap.rearrange("p a (n c) -> p a n c", c=C)
        self.tt('dve', AR[:, :, 0:nch, 1, 0:C], chv(zr), chv(gam[:, :, 0:nt]), ALU.mult, R=[z.b, gam.b], W=[AR.b])
        self.tt('dve', beta[:, :, 0:nt], kk[:, :, 0:nt], aa[:, :, 0:nt], ALU.mult, R=[kk.b, aa.b], W=[beta.b])
        yield
        self.act(gx[:, :, 0:nt], cs[:, :, 0:nt], AF.Exp, scale=C0, R=[cs.b], W=[gx.b])
        self.tt('dve', BtT[:, :, 0:nt], beta[:, :, 0:nt], gx[:, :, 0:nt], ALU.mult, R=[beta.b, gx.b], W=[BtT.b])
        self.tt('dve', KtT[:, :, 0:nt], kp[:, :, 0:nt], gx[:, :, 0:nt], ALU.mult, R=[kp.b, gx.b], W=[KtT.b])
        yield
        self.tt('dve', tmpf[:, :, 0:nt], cs[:, :, 0:nt], esig[:, :, 0:nt], ALU.subtract, R=[cs.b, esig.b], W=[tmpf.b])
        self.act(gx[:, :, 0:nt], tmpf[:, :, 0:nt], AF.Exp, scale=-C0, R=[tmpf.b], W=[gx.b])
        self.tt('dve', AR[:, :, 0:nch, 0, 0:C], chv(kk[:, :, 0:nt]), chv(gx[:, :, 0:nt]), ALU.mult, R=[kk.b, gx.b], W=[AR.b])
        yield
        nb16 = W['nb16']
        self.ts('dve', nb16[:, :, 0:nch], cs[:, :, C - 1:nt:C], -C0, None, ALU.mult, R=[cs.b], W=[nb16.b])
        for cb in range(4):
            for ch in range(nch):
                self.act(gx[:, cb, ch * C:(ch + 1) * C], cs[:, cb, ch * C:(ch + 1) * C], AF.Exp, bias=nb16[:, cb, ch:ch + 1], scale=C0,
                         R=[cs.b, nb16.b], W=[gx.b])
        self.tt('dve', BgT[:, :, 0:nt], beta[:, :, 0:nt], gx[:, :, 0:nt], ALU.mult, R=[beta.b, gx.b], W=[BgT.b])
        self.tt('dve', KgT[:, :, 0:nt], kp[:, :, 0:nt], gx[:, :, 0:nt], ALU.mult, R=[kp.b, gx.b], W=[KgT.b])
        self.cp('pool', vTb[:, :, 0:nt], zv, R=[z.b], W=[vTb.b])
        if self.sub.get('rstop', 99) <= 4:
            return

    def rwkv_chunk(self, g, ti, tok0, ch, C, nt):
        c, W = self.c, self.W
        ho = W['HO'][ti % 2]
        AR, BtT, KtT, BgT, KgT, vTb = ho['AR'], ho['BtT'], ho['KtT'], ho['BgT'], ho['KgT'], ho['vTb']
        Bgt, Kgt, Vt, MLt, MKt, Xf, Xb = W['Bgt'], W['Kgt'], W['Vt'], W['MLt'], W['MKt'], W['Xf'], W['Xb']
        sl = slice(ch * C, (ch + 1) * C)
        idb = c['ident_b']
        pt = self.pT[1]
        for cb in range(4):
            self.tr(pt[0:C, cb * 128:(cb + 1) * 128], AR[:, cb, ch, 0, 0:C], idb[:], R=[AR.b, idb.b], W=[pt.b])
        self.ts('dve', Xb[0:C, :, 0:64], pt[0:C, 0:512].rearrange("p (h d) -> p h d", d=64), -1.0, None, ALU.mult, R=[pt.b], W=[Xb.b])
        for (src, dst, eng, pi) in [(BgT, Bgt, 'act', 1), (KgT, Kgt, 'act', 0), (vTb, Vt, 'act', 1)]:
            pt = self.pT[1]
            for cb in range(4):
                self.tr(pt[0:C, cb * 128:(cb + 1) * 128], src[:, cb, sl], idb[:], R=[src.b, idb.b], W=[pt.b])
            self.cp(eng, dst[0:C, :], pt[0:C, 0:512], R=[pt.b], W=[dst.b])
        if self.sub.get('rstop', 99) <= 5:
            return
        yield
        mA = c['maskA'][0:C, :].rearrange("p (a c) -> p a c", a=2)[:, :, 0:C]
        Lc0 = W['Lc'][0]
        for par in range(2):
            pb_ = 64 * par
            for half in range(2):
                pA_, pB_, pc = self.bb(), self.bb(), self.bb()
                for j in range(2):
                    cb = 2 * half + j
                    ar = AR[pb_:pb_ + 64, cb, ch, :, 0:C]
                    o = j * 256
                    self.mm(pA_[0:C, o:o + 2 * C].rearrange("p (a c) -> p a c", a=2), BtT[pb_:pb_ + 64, cb, sl], ar,
                            R=[BtT.b, AR.b], W=[pA_.b])
                    self.mm(pB_[0:C, o:o + 2 * C].rearrange("p (a c) -> p a c", a=2), KtT[pb_:pb_ + 64, cb, sl], ar,
                            R=[KtT.b, AR.b], W=[pB_.b])
                    self.mm(pc[0:C, j * 128:j * 128 + C], AR[pb_:pb_ + 64, cb, ch, 0, 0:C], BtT[pb_:pb_ + 64, cb, sl],
                            R=[AR.b, BtT.b], W=[pc.b])
                for j in range(2):
                    cb = 2 * half + j
                    h = 2 * cb + par
                    o = j * 256
                    self.tt('dve', MLt[0:C, h, :].rearrange("p (a c) -> p a c", a=2)[:, :, 0:C],
                            pA_[0:C, o:o + 2 * C].rearrange("p (a c) -> p a c", a=2), mA, ALU.mult,
                            R=[pA_.b, c['maskA'].b], W=[MLt.b])
                    self.tt('dve', MKt[0:C, h, :].rearrange("p (a c) -> p a c", a=2)[:, :, 0:C],
                            pB_[0:C, o:o + 2 * C].rearrange("p (a c) -> p a c", a=2), mA, ALU.mult,
                            R=[pB_.b, c['maskA'].b], W=[MKt.b])
                    self.tt('dve', Lc0[0:C, h, 0:C], pc[0:C, j * 128:j * 128 + C], c['maskC'][0:C, 0:C], ALU.mult,
                            R=[pc.b, c['maskC'].b], W=[Lc0.b])
                yield
        if self.sub.get('rstop', 99) <= 6:
            return
        yield
        pl = self.bb()
        for h in range(NH):
            self.mm(pl[0:C, h * 64:(h + 1) * 64], MKt[0:C, h, 0:C], Vt[0:C, h * 64:(h + 1) * 64], R=[MKt.b, Vt.b], W=[pl.b])
        self.cp('act', Xb[0:C, :, 64:128], pl[0:C, :].rearrange("p (h d) -> p h d", d=64), R=[pl.b], W=[Xb.b])
        if self.sub.get('rstop', 99) <= 7:
            return
        yield
        nlev = int(round(math.log2(C)))
        Lc, Mc = W['Lc'], W['Mc']
        for lev in range(nlev):
            Lcur = Lc[lev % 2]
            Lnx = Lc[(lev + 1) % 2]
            Mnx = Mc[(lev + 1) % 2]
            def Mcur(h):
                return MLt[0:C, h, 0:C] if lev == 0 else Mc[lev % 2][0:C, h, 0:C]
            Mb = MLt.b if lev == 0 else Mc[lev % 2].b
            for half in range(2):
                yield
                px = self.bb()
                for hh in range(4):
                    h = 4 * half + hh
                    self.mm(px[0:C, hh * 128:hh * 128 + 128], Mcur(h), Xb[0:C, h, :], R=[Mb, Xb.b], W=[px.b])
                if lev < nlev - 1:
                    pm_, pl_ = self.bb(), self.bb()
                    for hh in range(4):
                        h = 4 * half + hh
                        self.mm(pm_[0:C, hh * 128:hh * 128 + C], Lcur[0:C, h, 0:C], Mcur(h), R=[Lcur.b, Mb], W=[pm_.b])
                        self.mm(pl_[0:C, hh * 128:hh * 128 + C], Mcur(h), Lcur[0:C, h, 0:C], R=[Lcur.b, Mb], W=[pl_.b])
                self.tt('dve', Xb[0:C, 4 * half:4 * half + 4, :], Xb[0:C, 4 * half:4 * half + 4, :],
                        px[0:C, :].rearrange("p (h d) -> p h d", d=128), ALU.add, R=[px.b, Xb.b], W=[Xb.b])
                if lev < nlev - 1:
                    self.cp('act', Mnx[0:C, 4 * half:4 * half + 4, 0:C],
                            pm_[0:C, :].rearrange("p (h d) -> p h d", d=128)[:, :, 0:C], R=[pm_.b], W=[Mnx.b])
                    self.cp('act', Lnx[0:C, 4 * half:4 * half + 4, 0:C],
                            pl_[0:C, :].rearrange("p (h d) -> p h d", d=128)[:, :, 0:C], R=[pl_.b], W=[Lnx.b])
        if self.sub.get('rstop', 99) <= 8:
            return
        yield
        GT, Hs, RAT, Sf, Sb = W['GT'], W['Hs'], W['RAT'], W['Sf'], W['Sb']
        gam = ho['gam']
        for par in range(2):
            pb_ = 64 * par
            pg, ph, pr = self.bb(), self.bb(), self.bb()
            for cb in range(4):
                h = 2 * cb + par
                self.mm(pg[pb_:pb_ + 64, cb * 64:(cb + 1) * 64], Xb[0:C, h, 0:64], Bgt[0:C, h * 64:(h + 1) * 64], R=[Xb.b, Bgt.b], W=[pg.b])
                self.mm(ph[pb_:pb_ + 64, cb * 64:(cb + 1) * 64], Bgt[0:C, h * 64:(h + 1) * 64], Xb[0:C, h, 64:128], start=True, stop=False,
                        R=[Xb.b, Bgt.b], W=[ph.b])
                self.mm(ph[pb_:pb_ + 64, cb * 64:(cb + 1) * 64], Kgt[0:C, h * 64:(h + 1) * 64], Vt[0:C, h * 64:(h + 1) * 64], start=False, stop=True,
                        R=[Kgt.b, Vt.b], W=[ph.b])
                self.mm(pr[pb_:pb_ + 64, cb * 128:cb * 128 + C], Xb[0:C, h, 0:64], MLt[0:C, h, 128:128 + C], R=[Xb.b, MLt.b], W=[pr.b])
            for cb in range(4):
                gC = gam[pb_:pb_ + 64, cb, ch * C + C - 1:ch * C + C]
                self.stt(GT[pb_:pb_ + 64, cb, :], c['ipair'][pb_:pb_ + 64, :], gC, pg[pb_:pb_ + 64, cb * 64:(cb + 1) * 64], ALU.mult, ALU.add,
                         R=[c['ipair'].b, gam.b, pg.b], W=[GT.b])
            self.cp('act', Hs[pb_:pb_ + 64, :, :], ph[pb_:pb_ + 64, 0:256].rearrange("p (a d) -> p a d", d=64), R=[ph.b], W=[Hs.b])
            self.tt('dve', RAT[pb_:pb_ + 64, :, 0:C], pr[pb_:pb_ + 64, :].rearrange("p (a d) -> p a d", d=128)[:, :, 0:C],
                    AR[pb_:pb_ + 64, :, ch, 1, 0:C], ALU.add, R=[pr.b, AR.b], W=[RAT.b])
        if self.sub.get('rstop', 99) <= 9:
            return
        yield
        ysb, ysq = W['ysb'], W['ysq']
        s10 = self.sub.get('s10', 3)
        if s10 & 1:
            for par in range(2):
                pb_ = 64 * par
                py = self.bb()
                py2 = self.bb() if C != 128 else None
                for cb in range(4):
                    h = 2 * cb + par
                    o = cb * 64
                    self.mm(py[0:C, o:o + 64], MLt[0:C, h, 128:128 + C], Xb[0:C, h, 64:128], start=True, stop=False, R=[MLt.b, Xb.b], W=[py.b])
                    if C == 128:
                        self.mm(py[0:C, o:o + 64], MKt[0:C, h, 128:128 + C], Vt[0:C, h * 64:(h + 1) * 64], start=False, stop=False, R=[MKt.b, Vt.b], W=[py.b])
                        self.mm(py[0:C, o:o + 64], RAT[pb_:pb_ + 64, cb, 0:C], Sb[pb_:pb_ + 64, cb, :], start=False, stop=True, R=[RAT.b, Sb.b], W=[py.b])
                    else:
                        self.mm(py[0:C, o:o + 64], MKt[0:C, h, 128:128 + C], Vt[0:C, h * 64:(h + 1) * 64], start=False, stop=True, R=[MKt.b, Vt.b], W=[py.b])
                        self.mm(py2[0:C, o:o + 64], RAT[pb_:pb_ + 64, cb, 0:C], Sb[pb_:pb_ + 64, cb, :], start=True, stop=True, R=[RAT.b, Sb.b], W=[py2.b])
                if C != 128:
                    self.cp('act', ysq[0:C, 0:4, :], py2[0:C, 0:256].rearrange("p (a d) -> p a d", d=64), R=[py2.b], W=[ysq.b])
                    self.tt('dve', ysb[0:C, par:8:2, :], py[0:C, 0:256].rearrange("p (a d) -> p a d", d=64), ysq[0:C, 0:4, :], ALU.add,
                            R=[py.b, ysq.b], W=[ysb.b])
                    continue
                if s10 & 4:
                    self.cp('dve', ysb[0:C, par:8:2, :], py[0:C, 0:256].rearrange("p (a d) -> p a d", d=64), R=[py.b], W=[ysb.b])
                elif s10 & 8:
                    pass
                else:
                    self.cp('act', ysb[0:C, par:8:2, :], py[0:C, 0:256].rearrange("p (a d) -> p a d", d=64), R=[py.b], W=[ysb.b])
        if s10 & 2:
            pSs = [self.bb(), self.bb()]
            for par in range(2):
                pb_ = 64 * par
                pS = pSs[par]
                for cb in range(4):
                    self.mm(pS[pb_:pb_ + 64, cb * 64:(cb + 1) * 64], GT[pb_:pb_ + 64, cb, :], Sb[pb_:pb_ + 64, cb, :], R=[GT.b, Sb.b], W=[pS.b])
            for par in range(2):
                pb_ = 64 * par
                pS = pSs[par]
                self.tt('dve', Sf[pb_:pb_ + 64, :, :], pS[pb_:pb_ + 64, 0:256].rearrange("p (a d) -> p a d", d=64), Hs[pb_:pb_ + 64, :, :], ALU.add,
                        R=[pS.b, Hs.b], W=[Sf.b])
            self.cp('pool', Sb[:], Sf[:], R=[Sf.b], W=[Sb.b])
        if self.sub.get('rstop', 99) <= 10:
            return
        yield
        s8 = W['st8']
        self.red(s8[0][0:C, :], ysb[0:C, :, :], R=[ysb.b], W=[s8[0].b])
        self.act(ysq[0:C, :, :], ysb[0:C, :, :], AF.Square, R=[ysb.b], W=[ysq.b])
        self.red(s8[1][0:C, :], ysq[0:C, :, :], R=[ysq.b], W=[s8[1].b])
        self.ts('dve', s8[0][0:C, :], s8[0][0:C, :], 1.0 / 64, None, ALU.mult, R=[s8[0].b], W=[s8[0].b])
        self.tt('dve', s8[2][0:C, :], s8[0][0:C, :], s8[0][0:C, :], ALU.mult, R=[s8[0].b], W=[s8[2].b])
        self.stt(s8[1][0:C, :], s8[1][0:C, :], 1.0 / 64, s8[2][0:C, :], ALU.mult, ALU.subtract, R=[s8[1].b, s8[2].b], W=[s8[1].b])
        self.act(s8[1][0:C, :], s8[1][0:C, :], AF.Sqrt, bias=c['gneps'][0:C, :], scale=1.0, R=[s8[1].b, c['gneps'].b], W=[s8[1].b])
        self.recip(s8[1][0:C, :], s8[1][0:C, :], R=[s8[1].b], W=[s8[1].b])
        self.tt('dve', ysb[0:C, :, :], ysb[0:C, :, :], s8[0][0:C, :].unsqueeze(2).to_broadcast([C, 8, 64]), ALU.subtract, R=[ysb.b, s8[0].b], W=[ysb.b])
        yh = W['yh']
        self.tt('dve', yh[0:C, :].rearrange("p (h d) -> p h d", d=64), ysb[0:C, :, :], s8[1][0:C, :].unsqueeze(2).to_broadcast([C, 8, 64]), ALU.mult,
                R=[ysb.b, s8[1].b], W=[yh.b])
        pt = self.pT[1]
        for cb in range(4):
            self.tr(pt[:, cb * 128:cb * 128 + C], yh[0:C, cb * 128:(cb + 1) * 128], idb[0:C, 0:C], R=[yh.b, idb.b], W=[pt.b])
        yT1, yr = W['yT1'], W['yr'][self.nyr % 2]
        self.nyr += 1
        bonus, gT = ho['bonus'], ho['gT']
        for cb in range(4):
            self.ts('dve', yT1[:, cb, 0:C], pt[:, cb * 128:cb * 128 + C], c['gn_g'][:, cb:cb + 1], c['gn_b'][:, cb:cb + 1], ALU.mult, ALU.add,
                    R=[pt.b, c['gn_g'].b, c['gn_b'].b], W=[yT1.b])
        self.tt('dve', yT1[:, :, 0:C], yT1[:, :, 0:C], bonus[:, :, sl], ALU.add, R=[yT1.b, bonus.b], W=[yT1.b])
        self.tt('dve', yr[:, :, 0:C], yT1[:, :, 0:C], gT[:, :, sl], ALU.mult, R=[yT1.b, gT.b], W=[yr.b])
        t0 = tok0 + ch * C
        self.dma('sp', g['Ys'][0:512, t0:t0 + C].rearrange("(a p) t -> p a t", p=128), yr[:, :, 0:C], R=[yr.b])

    def phase2(self, st):
        c = self.c
        maxtot = max(g['tot'] for g in self.G)
        maxT = max(g['T'] for g in self.G)
        maxkb = max(g['nkb'] for g in self.G)
        Kt = [self.sb(st, (67, maxtot), BF16, name='Kt') for _ in range(2)]
        Qt = [self.sb(st, (67, maxT), BF16, name='Qt') for _ in range(2)]
        Vh = [self.sb(st, (128, maxkb, 128), BF16, name='Vh') for _ in range(2)]
        for v in Vh:
            self.memset('pool', v[:, :, 64:128], 1.0, W=[v.b])
        Pt = [self.sb(st, (128, 512), BF16, name='Pt') for _ in range(3)]
        rl = [self.sb(st, (64, 512), F32, name='rl') for _ in range(2)]
        yo = [self.sb(st, (64, 512), BF16, name='yo') for _ in range(2)]
        NPS = 4
        LA = 3
        pS = [self.pA[0], self.pA[1], self.pA[2], self.pA[3]]
        pO = [self.pA[4], self.pA[5]]
        Pt = Pt + [self.sb(st, (128, 512), BF16, name='Pt')]
        heads = [(g, h) for g in self.G for h in range(NH)]
        def load(idx):
            g, h = heads[idx]
            T, tot = g['T'], g['tot']
            kt, qt, vh = Kt[idx % 2], Qt[idx % 2], Vh[idx % 2]
            self.dma('sp', kt[0:64, 0:tot], g['Ks'][h, 0:64, :], W=[kt.b])
            self.dma('sp', kt[64:67, 0:tot], g['Ks'][h, 64:67, :], W=[kt.b])
            self.dma('sp', qt[0:64, 0:T], g['Qs'][h, 0:64, :], W=[qt.b])
            self.dma('sp', qt[64:67, 0:T], g['Qs'][h, 64:67, :], W=[qt.b])
            nfull = tot // 128
            if nfull > 0:
                self.dma('sp', vh[:, 0:nfull, 0:64], g['Vs'][h, :, 0:nfull, :], W=[vh.b])
            rem = tot - nfull * 128
            if rem:
                self.dma('sp', vh[0:rem, nfull, 0:64], g['Vs'][h, 0:rem, nfull, :], W=[vh.b])
        blocks = []
        nq = 0
        for idx, (g, h) in enumerate(heads):
            T, past, tot = g['T'], g['past'], g['tot']
            QT = min(512, T)
            for qi in range(T // QT):
                q0 = qi * QT
                qpos0 = past + q0
                nblk = (qpos0 + QT - 1) // 128 + 1
                for j in range(nblk):
                    k0 = j * 128
                    rows = min(128, tot - k0)
                    if k0 + rows - 1 <= qpos0:
                        c0, diag = 0, False
                    else:
                        c0, diag = k0 - qpos0, True
                    blocks.append(dict(idx=idx, g=g, h=h, q0=q0, QT=QT, j=j, k0=k0, rows=rows, c0=c0, diag=diag,
                                       first=(j == 0), last=(j == nblk - 1), nq=nq, newhead=(qi == 0 and j == 0)))
                nq += 1

        def emit_S(n):
            bl = blocks[n]
            g, h, idx = bl['g'], bl['h'], bl['idx']
            kt, qt = Kt[idx % 2], Qt[idx % 2]
            rows, c0, QT, q0, k0, j = bl['rows'], bl['c0'], bl['QT'], bl['q0'], bl['k0'], bl['j']
            ps_, pt_ = pS[n % NPS], Pt[n % NPS]
            self.mm(ps_[0:rows, c0:QT], kt[:, k0:k0 + rows], qt[:, q0 + c0:q0 + QT], start=True, stop=not bl['diag'],
                    R=[kt.b, qt.b], W=[ps_.b])
            if bl['diag']:
                self.mm(ps_[0:rows, c0:c0 + rows], c['ident_b'][0:rows, 0:rows], c['maskD_b'][0:rows, 0:rows], start=False, stop=True,
                        R=[c['ident_b'].b, c['maskD_b'].b], W=[ps_.b])
            self.act(pt_[0:rows, c0:QT], ps_[0:rows, c0:QT], AF.Exp, bias=g['neglc'][0:rows, j, h:h + 1], scale=1.0,
                     R=[ps_.b, g['neglc'].b], W=[pt_.b])

        def emit_PV(n):
            bl = blocks[n]
            g, h, idx = bl['g'], bl['h'], bl['idx']
            vh = Vh[idx % 2]
            rows, c0, QT, q0, j = bl['rows'], bl['c0'], bl['QT'], bl['q0'], bl['j']
            pt_ = Pt[n % NPS]
            po = pO[bl['nq'] % 2]
            self.mm(po[:, c0:QT], vh[0:rows, j, :], pt_[0:rows, c0:QT], start=bl['first'], stop=bl['last'],
                    R=[vh.b, pt_.b], W=[po.b])
            if bl['last']:
                rlt, yot = rl[bl['nq'] % 2], yo[bl['nq'] % 2]
                self.recip(rlt[:, 0:QT], po[64:128, 0:QT], R=[po.b], W=[rlt.b])
                self.tt('dve', yot[:, 0:QT], po[0:64, 0:QT], rlt[:, 0:QT], ALU.mult, R=[po.b, rlt.b], W=[yot.b])
                self.dma('sp', g['Ys'][512 + h * 64:512 + (h + 1) * 64, q0:q0 + QT], yot[:, 0:QT], R=[yot.b])

        load(0)
        if len(heads) > 1:
            load(1)
        nb_ = len(blocks)
        first_block = {}
        for n, bl in enumerate(blocks):
            first_block.setdefault(bl['idx'], n)
        load_at = {first_block[idx] + LA: idx + 1 for idx in range(1, len(heads) - 1)}
        for n in range(nb_ + LA):
            if n in load_at:
                load(load_at[n])
            if n < nb_:
                emit_S(n)
            if n - LA >= 0:
                emit_PV(n - LA)

    def phase3(self, st):
        I, c = self.I, self.c
        wout = self.sb(st, (128, 8, D), BF16, name='wout')
        wg = self.sb(st, (128, 8, DFF), BF16, name='wg')
        wu = self.sb(st, (128, 8, DFF), BF16, name='wu')
        wd = self.sb(st, (128, NFB, D), BF16, name='wd')
        for (dst, src, kk_, ncol) in [(wout, I['w_out'], 8, D), (wg, I['w_g'], 8, DFF), (wu, I['w_u'], 8, DFF), (wd, I['w_d'], NFB, D)]:
            s3 = src.rearrange("(k p) c -> p k c", p=128)
            for k in range(kk_):
                for a in range(0, ncol, 1408 if ncol == DFF else 1024):
                    b = min(ncol, a + (1408 if ncol == DFF else 1024))
                    self.dma('pool', dst[:, k, a:b], s3[:, k, a:b], W=[dst.b])
        NT = 256
        xt = [self.sb(st, (128, 2, D), F32, name='x3')]
        yT = [self.sb(st, (128, 8, NT), BF16, name='yT')]
        xsb = self.sb(st, (128, 2, D), BF16, name='xsb3')
        h2 = self.sb(st, (128, 8, NT), BF16, name='h2')
        junk = self.sb(st, (128, D), BF16, name='junk3')
        ss = self.sb(st, (128, 4), F32)
        rstd = self.sb(st, (128, 4), F32)
        actT = self.sb(st, (128, NFB, NT), BF16, name='actT')
        sil = [self.sb(st, (128, NT), F32, name='sil') for _ in range(2)]
        yo = [self.sb(st, (128, D), F32, name='yo3')]
        gt1 = self.sb(st, (128, D), F32, name='gt1')
        gt2 = self.sb(st, (128, D), F32, name='gt2')
        n = 0
        hflag = self.sb(st, (1, 1), mybir.dt.int32, name='hflag')
        self.dma('sp', hflag[:], I['half'], W=[hflag.b])
        r_base = st.enter_context(self.nc.gpsimd.register("r_base"))
        r_off = st.enter_context(self.nc.gpsimd.register("r_off"))
        HALF = self.SEQ // 2
        def init_reg(e):
            e.reg_load(r_base, hflag[0:1, 0:1])
            e.reg_mul(r_base, r_base, HALF)
            return e.nop()
        self.P.op('pool', init_reg, [hflag.b], [])
        for g in self.G:
            T = g['T'] if g['gi'] == 1 else HALF
            xsrc = g['x'] if g['gi'] == 1 else I['xp3']
            self.build_gates(g, gt1, gt2, sil)
            ntile = (T + NT - 1) // NT
            for ti in range(ntile):
                tok0 = ti * NT
                nt = min(NT, T - tok0)
                nb = (nt + 127) // 128
                bs = min(128, nt)
                x = xt[n % len(xt)]
                y_ = yT[n % len(yT)]
                n += 1
                self.dma('sp', x[0:bs, 0:nb, :], xsrc[tok0:tok0 + nt, :].rearrange("(b p) d -> p b d", p=bs), W=[x.b])
                if g['gi'] == 1:
                    self.dma('sp', y_[:, :, 0:nt], g['Ys'][:, tok0:tok0 + nt].rearrange("(k p) t -> p k t", p=128), W=[y_.b])
                else:
                    ys = g['Ys']
                    SEQ_ = self.SEQ
                    def dyn(e, y_=y_, tok0=tok0, nt=nt, ys=ys, SEQ_=SEQ_):
                        e.reg_add(r_off, r_base, tok0)
                        src = bass.AP(ys.tensor, r_off, [[SEQ_, 128], [128 * SEQ_, 8], [1, nt]])
                        return e.dma_start(out=y_[:, :, 0:nt], in_=src)
                    self.P.dma_fn('pool', dyn, (), [y_.b])
                tmpm = yo[0]
                for blk in range(nb):
                    for half in range(2):
                        pp = self.pA[half]
                        for k in range(8):
                            self.mm(pp[0:bs, :], y_[:, k, blk * 128:blk * 128 + bs], wout[:, k, half * 512:(half + 1) * 512],
                                    start=(k == 0), stop=(k == 7), R=[y_.b, wout.b], W=[pp.b])
                        self.tt('dve', tmpm[0:bs, half * 512:(half + 1) * 512], pp[0:bs, :], gt1[0:bs, half * 512:(half + 1) * 512], ALU.mult,
                                R=[pp.b, gt1.b], W=[tmpm.b])
                    self.tt('pool', x[0:bs, blk, :], tmpm[0:bs, :], x[0:bs, blk, :], ALU.add, R=[tmpm.b, x.b], W=[x.b])
                x1 = x
                self.norm_transpose(x1, nb, bs, g['gm2'], g['sh2'], xsb, h2, junk, ss, rstd, 0, alt=True)
                for fb in range(NFB):
                    pg, pu = self.pA[2 + (fb % 2)], self.pA[4 + (fb % 2)]
                    for k in range(8):
                        self.mm(pg[:, 0:nt], wg[:, k, fb * 128:(fb + 1) * 128], h2[:, k, 0:nt], start=(k == 0), stop=(k == 7), R=[wg.b, h2.b], W=[pg.b])
                    for k in range(8):
                        self.mm(pu[:, 0:nt], wu[:, k, fb * 128:(fb + 1) * 128], h2[:, k, 0:nt], start=(k == 0), stop=(k == 7), R=[wu.b, h2.b], W=[pu.b])
                    s_ = sil[fb % 2]
                    self.act(s_[:, 0:nt], pg[:, 0:nt], AF.Silu, R=[pg.b], W=[s_.b])
                    self.tt('dve', actT[:, fb, 0:nt], s_[:, 0:nt], pu[:, 0:nt], ALU.mult, R=[s_.b, pu.b], W=[actT.b])
                for blk in range(nb):
                    yo_ = yo[0]
                    for half in range(2):
                        pp = self.pA[half]
                        for fb in range(NFB):
                            self.mm(pp[0:bs, :], actT[:, fb, blk * 128:blk * 128 + bs], wd[:, fb, half * 512:(half + 1) * 512],
                                    start=(fb == 0), stop=(fb == NFB - 1), R=[actT.b, wd.b], W=[pp.b])
                        self.tt('dve', yo_[0:bs, half * 512:(half + 1) * 512], pp[0:bs, :], gt2[0:bs, half * 512:(half + 1) * 512], ALU.mult,
                                R=[pp.b, gt2.b], W=[yo_.b])
                    self.tt('pool', yo_[0:bs, :], yo_[0:bs, :], x1[0:bs, blk, :], ALU.add, R=[yo_.b, x1.b], W=[yo_.b])
                    self.dma('sp', g['o_y'][tok0 + blk * 128:tok0 + blk * 128 + bs, :], yo_[0:bs, :], R=[yo_.b])


def _consts():
    i = np.arange(128)
    s, t = i[:, None], i[None, :]
    cst = {}
    cst['c_ident'] = np.eye(128, dtype=np.float32)
    cst['c_triu'] = (s <= t).astype(np.float32)
    cst['c_ones'] = np.ones((128, 128), np.float32)
    cst['c_blk'] = ((s // 64) == (t // 64)).astype(np.float32)
    cst['c_maskA'] = np.concatenate([-(s < t).astype(np.float32), (s <= t).astype(np.float32)], axis=1)
    cst['c_maskC'] = -(s > t).astype(np.float32)
    cst['c_maskD'] = np.where(s <= t, 0.0, NEG).astype(np.float32)
    r = np.ones((128, 1024), np.float32)
    r[:, ::128] = 0.0
    cst['c_reset'] = r
    cst['c_ipair'] = ((i[:, None] % 64) == np.arange(64)[None, :]).astype(np.float32)
    return cst


def _pk(v, nblk):
    return np.ascontiguousarray(np.asarray(v, np.float32).reshape(nblk, 128).T)


_NC_CACHE = {}


def _get_nc(SEQ, PAST, NS, debug=False, phases=(1, 2, 3), sub=None):
    key = (SEQ, PAST, NS, debug, tuple(phases), str(sub))
    if key not in _NC_CACHE:
        _NC_CACHE[key] = KB(SEQ, PAST, NS, debug, phases, sub).build()
    return _NC_CACHE[key]


def make_in_maps(inp, n_cores=8):
    f = lambda a: np.ascontiguousarray(np.asarray(a, dtype=np.float32))
    xp, xs = f(inp['x_prompt']), f(inp['x_sample'])
    BP = xp.shape[0]
    cst = _consts()
    shared = dict(cst)
    L = 0
    shared['w_ada'] = f(inp['w_ada'][L]); shared['b_ada'] = _pk(inp['b_ada'][L], 48)
    shared['w_in'] = f(inp['w_in'][L]); shared['w_out'] = f(inp['w_out'][L])
    shared['w_g'] = f(inp['w_ffn_gate'][L]); shared['w_u'] = f(inp['w_ffn_up'][L]); shared['w_d'] = f(inp['w_ffn_down'][L])
    shared['n1g'] = _pk(inp['norm1_g'][L], 8); shared['n2g'] = _pk(inp['norm2_g'][L], 8)
    shared['mu'] = _pk(inp['shift_mu'][L], 14)
    for nm, src in [('w0', 'w0'), ('a0', 'a0'), ('k_k', 'k_k'), ('k_a', 'k_a'), ('gn_g', 'gn_g'), ('gn_b', 'gn_b')]:
        shared[nm] = _pk(inp[src][L], 4)
    shared['r_k'] = _pk(np.asarray(inp['r_k'][L]).reshape(512), 4)
    shared['w_dec'] = f(inp['w_decay_up'][L]); shared['w_aaa'] = f(inp['w_aaa_up'][L]); shared['w_gup'] = f(inp['w_gate_up'][L])
    gq = np.tile(np.asarray(inp['fox_q_g'][L], np.float32), 8)
    gk = np.tile(np.asarray(inp['fox_k_g'][L], np.float32), 8)
    shared['gqk'] = np.ascontiguousarray(np.broadcast_to(np.concatenate([gq, gk])[None, :], (128, 1024)))
    shared['f_b'] = np.ascontiguousarray(np.broadcast_to(np.asarray(inp['fox_f_b'][L], np.float32)[None, :], (128, 8)))
    maps = []
    for cidx in range(n_cores):
        b = cidx % BP
        m = dict(shared)
        m['xp'] = xp[b]
        hf = cidx // BP
        H2 = xp.shape[1] // 2
        m['half'] = np.array([[hf]], np.int32)
        m['xp3'] = np.ascontiguousarray(xp[b, hf * H2:(hf + 1) * H2])
        m['xs'] = xs[cidx]
        cv = np.stack([np.asarray(inp['c_prompt'][b], np.float32), np.asarray(inp['c_sample'][cidx], np.float32)], axis=-1)
        m['cvec'] = np.ascontiguousarray(cv.reshape(8, 128, 2).transpose(1, 0, 2))
        m['ck'] = f(inp['cache_fox_k'][L, cidx]).reshape(-1, 512)
        m['cv'] = f(inp['cache_fox_v'][L, cidx]).reshape(-1, 512)
        m['clf'] = f(inp['cache_fox_logf'][L, cidx])
        m['st0'] = np.ascontiguousarray(f(inp['state_rwkv'][L, cidx]).transpose(1, 0, 2))
        m['shp0'] = _pk(inp['state_rwkv_shift'][L, cidx, 0], 14)
        maps.append(m)
    return maps


def assemble(res, BP, SEQ, NSEQ, NS):
    r = res
    def upk(a):
        return np.ascontiguousarray(a.T).reshape(-1)
    y_p = np.stack([np.concatenate([r[b]['y_p'], r[b + BP]['y_p']], axis=0) for b in range(BP)])
    y_s = np.stack([r[c]['y_s'] for c in range(NSEQ)])
    st_p = np.stack([r[b]['st_p'] for b in range(BP)])[None]
    sh_p = np.stack([upk(r[b]['sh_p'])[None, :] for b in range(BP)])[None]
    k_p = np.stack([r[b]['k_p'].reshape(SEQ, 8, 64) for b in range(BP)])[None]
    v_p = np.stack([r[b]['v_p'].reshape(SEQ, 8, 64) for b in range(BP)])[None]
    lf_p = np.stack([r[b]['lf_p'] for b in range(BP)])[None]
    st_s = np.stack([r[c]['st_s'] for c in range(NSEQ)])[None]
    sh_s = np.stack([upk(r[c]['sh_s'])[None, :] for c in range(NSEQ)])[None]
    k_s = np.stack([r[c]['k_s'].reshape(NS, 8, 64) for c in range(NSEQ)])[None]
    v_s = np.stack([r[c]['v_s'].reshape(NS, 8, 64) for c in range(NSEQ)])[None]
    lf_s = np.stack([r[c]['lf_s'] for c in range(NSEQ)])[None]
    outs = (y_p, y_s, st_p, sh_p, k_p, v_p, lf_p, st_s, sh_s, k_s, v_s, lf_s)
    return tuple(np.ascontiguousarray(o, dtype=np.float32) for o in outs)


def kernel(**inputs):
    xp = np.asarray(inputs['x_prompt'])
    xs = np.asarray(inputs['x_sample'])
    BP, SEQ, _ = xp.shape
    NSEQ, NS, _ = xs.shape
    PAST = np.asarray(inputs['cache_fox_k']).shape[2]
    nc = _get_nc(SEQ, PAST, NS)
    maps = make_in_maps(inputs, 8)
    res = run_bass_kernel_spmd(nc, maps, core_ids=list(range(8)))
    return assemble(res.results, BP, SEQ, NSEQ, NS)
```

```python
import contextlib
import math
import numpy as np
import concourse.bass as bass
import concourse.mybir as mybir
from concourse.bass_utils import run_bass_kernel_spmd

F32 = mybir.dt.float32
BF16 = mybir.dt.bfloat16
AF = mybir.ActivationFunctionType
ALU = mybir.AluOpType
AX = mybir.AxisListType

ENGS = ('pe', 'dve', 'act', 'pool', 'sp')

D = 1024
HD = 64
NH = 8
RW = 512
RCOLS = 1792
INC = 3336
DFF = 2816
NFB = DFF // 128
EPS = 1e-6
GN_EPS = 64e-5
C0 = math.exp(-0.5)
NEG = -30000.0


class Buf:
    __slots__ = ('last_write', 'readers')

    def __init__(self):
        self.last_write = None
        self.readers = []


class Prog:
    def __init__(self, nc, st, n_dma_sems=10):
        self.nc = nc
        self.q = {e: [] for e in ENGS}
        self.count = {e: 0 for e in ENGS}
        self.seen = {e: {} for e in ENGS}
        self.n_dma_sems = n_dma_sems
        self.dma_next = {e: 0 for e in ENGS}
        self.dma_val = {}
        names = list(ENGS)
        for e in ('sp', 'pool', 'act'):
            for i in range(n_dma_sems):
                k = f'd_{e}_{i}'
                names.append(k)
                self.dma_val[k] = 0
        self.sems = {n: st.enter_context(nc.semaphore(n)) for n in names}
        self.n_ops = 0
        self.noself = ()
        self.wswap = False

    def _deps(self, q, reads, writes, extra=()):
        need = {}

        def add(tok):
            if tok is None:
                return
            k, v = tok
            if need.get(k, 0) < v:
                need[k] = v
        for b in reads:
            add(b.last_write)
        for b in writes:
            add(b.last_write)
            for r in b.readers:
                add(r)
        for t in extra:
            add(t)
        waits = []
        for k, v in need.items():
            if k == q and (q == 'pe' or q in self.noself):
                continue
            if self.seen[q].get(k, 0) >= v:
                continue
            self.seen[q][k] = v
            waits.append((k, v))
        waits.sort(key=lambda kv: kv[0] == q)
        return waits

    def _mark(self, tok, reads, writes):
        for b in reads:
            if len(b.readers) > 6:
                m = {}
                for k, v in b.readers:
                    if m.get(k, 0) < v:
                        m[k] = v
                b.readers = list(m.items())
            b.readers.append(tok)
        for b in writes:
            b.last_write = tok
            b.readers = []

    def op(self, q, fn, reads=(), writes=()):
        waits = self._deps(q, reads, writes)
        self.count[q] += 1
        tok = (q, self.count[q])
        self.q[q].append((waits, fn, (q, 1)))
        self._mark(tok, reads, writes)
        self.n_ops += 1
        return tok

    def dma(self, q, out, in_, reads=(), writes=(), **kw):
        i = self.dma_next[q]
        self.dma_next[q] = (i + 1) % self.n_dma_sems
        k = f'd_{q}_{i}'
        prev = self.dma_val[k]
        ex = [(k, prev)] if prev > 0 else []
        waits = self._deps(q, reads, writes, ex)
        self.dma_val[k] = prev + 16
        tok = (k, prev + 16)
        self.q[q].append((waits, lambda e: e.dma_start(out=out, in_=in_, **kw), (k, 16)))
        self._mark(tok, reads, writes)
        self.n_ops += 1
        return tok

    def dma_fn(self, q, fn, reads=(), writes=()):
        i = self.dma_next[q]
        self.dma_next[q] = (i + 1) % self.n_dma_sems
        k = f'd_{q}_{i}'
        prev = self.dma_val[k]
        ex = [(k, prev)] if prev > 0 else []
        waits = self._deps(q, reads, writes, ex)
        self.dma_val[k] = prev + 16
        tok = (k, prev + 16)
        self.q[q].append((waits, fn, (k, 16)))
        self._mark(tok, reads, writes)
        self.n_ops += 1
        return tok

    def wait_all_dma(self, q):
        toks = [(k, v) for k, v in self.dma_val.items() if v > 0]
        waits = self._deps(q, (), (), toks)
        self.q[q].append((waits, None, None))

    def emit(self):
        nc = self.nc
        sems = self.sems
        with nc.Block() as block:
            handles = {'pe': block.tensor, 'dve': block.vector, 'act': block.scalar,
                       'pool': block.gpsimd, 'sp': block.sync}
            for e in ENGS:
                ops = self.q[e]
                if not ops:
                    continue

                def body(eng, ops=ops):
                    for waits, fn, inc in ops:
                        if self.wswap:
                            waits = list(reversed(waits))
                        for k, v in waits:
                            eng.wait_ge(sems[k], v)
                        if fn is not None:
                            ins = fn(eng)
                            if inc is not None:
                                ins.then_inc(sems[inc[0]], inc[1])
                handles[e](body)
        self.q = {e: [] for e in ENGS}


class TB:
    def __init__(self, t, n=1):
        self.t = t
        self.bs = [Buf() for _ in range(n)]

    @property
    def b(self):
        return self.bs[0]

    def __getitem__(self, k):
        return self.t[k]


class KB:
    def __init__(self, SEQ, PAST, NS, debug=False, phases=(1, 2, 3), sub=None):
        self.SEQ, self.PAST, self.NS, self.debug = SEQ, PAST, NS, debug
        self.phases = phases
        self.sub = sub or {}
        self.nc = bass.Bass("TRN2", target_bir_lowering=False)
        self.uid = 0

    def bb(self):
        self._bb = (getattr(self, '_bb', -1) + 1) % 4
        return self.pA[2 + self._bb]

    def mm(self, out, lhsT, rhs, start=True, stop=True, R=(), W=()):
        return self.P.op('pe', lambda e: e.matmul(out, lhsT, rhs, start=start, stop=stop), R, W)

    def tr(self, out, in_, ident, R=(), W=()):
        return self.P.op('pe', lambda e: e.transpose(out, in_, ident), R, W)

    def act(self, out, in_, func, bias=None, scale=None, accum=None, R=(), W=()):
        kw = {}
        if bias is not None:
            kw['bias'] = bias
        if scale is not None:
            kw['scale'] = scale
        if accum is not None:
            kw['accum_out'] = accum
        return self.P.op('act', lambda e: e.activation(out, in_, func, **kw), R, W)

    def ts(self, eng, out, in0, s1, s2=None, op0=ALU.mult, op1=None, R=(), W=()):
        if op1 is None:
            return self.P.op(eng, lambda e: e.tensor_scalar(out, in0, s1, None, op0), R, W)
        return self.P.op(eng, lambda e: e.tensor_scalar(out, in0, s1, s2, op0, op1), R, W)

    def tt(self, eng, out, in0, in1, op, R=(), W=()):
        return self.P.op(eng, lambda e: e.tensor_tensor(out, in0, in1, op=op), R, W)

    def stt(self, out, in0, scalar, in1, op0, op1, R=(), W=()):
        return self.P.op('dve', lambda e: e.scalar_tensor_tensor(out, in0, scalar, in1, op0, op1), R, W)

    def cp(self, eng, out, in_, R=(), W=()):
        if eng == 'act':
            return self.P.op('act', lambda e: e.activation(out, in_, AF.Copy), R, W)
        return self.P.op(eng, lambda e: e.tensor_copy(out, in_), R, W)

    def red(self, out, in_, op=ALU.add, R=(), W=()):
        return self.P.op('dve', lambda e: e.tensor_reduce(out, in_, AX.X, op), R, W)

    def recip(self, out, in_, R=(), W=()):
        return self.P.op('dve', lambda e: e.reciprocal(out, in_), R, W)

    def memset(self, eng, ap, val, W=()):
        return self.P.op(eng, lambda e: e.memset(ap, val), (), W)

    def dma(self, q, out, in_, R=(), W=(), **kw):
        return self.P.dma(q, out, in_, R, W, **kw)

    def sb(self, st, shape, dt, n=1, name=None):
        self.uid += 1
        t = st.enter_context(self.nc.sbuf_tensor(f"{name or 's'}_{self.uid}", list(shape), dt))
        return TB(t, n)

    def ps(self, st, shape, dt, n=1, name=None):
        self.uid += 1
        t = st.enter_context(self.nc.psum_tensor(f"{name or 'p'}_{self.uid}", list(shape), dt))
        return TB(t, n)

    def din(self, name, shape, dt=F32):
        return self.nc.dram_tensor(name, list(shape), dt, kind="ExternalInput").ap()

    def dout(self, name, shape, dt=F32):
        return self.nc.dram_tensor(name, list(shape), dt, kind="ExternalOutput").ap()

    def dscr(self, name, shape, dt=BF16):
        kind = "ExternalOutput" if self.debug else "Internal"
        return self.nc.dram_tensor(name, list(shape), dt, kind=kind).ap()

    def build(self):
        nc = self.nc
        SEQ, PAST, NS = self.SEQ, self.PAST, self.NS
        I = {}
        self.I = I
        I['half'] = self.din('half', (1, 1), mybir.dt.int32)
        I['xp3'] = self.din('xp3', (SEQ // 2, D))
        for nm, shp in [('xp', (SEQ, D)), ('xs', (NS, D)), ('cvec', (128, 8, 2)),
                        ('ck', (PAST, 512)), ('cv', (PAST, 512)), ('clf', (PAST, 8)),
                        ('st0', (64, 8, 64)), ('shp0', (128, 14)),
                        ('w_ada', (D, 6 * D)), ('b_ada', (128, 48)), ('w_in', (D, INC)),
                        ('w_out', (D, D)), ('w_g', (D, DFF)), ('w_u', (D, DFF)), ('w_d', (DFF, D)),
                        ('n1g', (128, 8)), ('n2g', (128, 8)), ('mu', (128, 14)),
                        ('w0', (128, 4)), ('a0', (128, 4)), ('k_k', (128, 4)), ('k_a', (128, 4)),
                        ('r_k', (128, 4)), ('gn_g', (128, 4)), ('gn_b', (128, 4)),
                        ('w_dec', (64, 512)), ('w_aaa', (64, 512)), ('w_gup', (128, 512)),
                        ('gqk', (128, 1024)), ('f_b', (128, 8)),
                        ('c_ident', (128, 128)), ('c_triu', (128, 128)), ('c_ones', (128, 128)),
                        ('c_blk', (128, 128)), ('c_maskA', (128, 256)), ('c_maskC', (128, 128)),
                        ('c_maskD', (128, 128)), ('c_reset', (128, 1024)), ('c_ipair', (128, 64))]:
            I[nm] = self.din(nm, shp)
        self.I = I
        O = {}
        for nm, shp in [('y_p', (SEQ // 2, D)), ('y_s', (NS, D)), ('st_p', (8, 64, 64)), ('sh_p', (128, 14)),
                        ('k_p', (SEQ, 512)), ('v_p', (SEQ, 512)), ('lf_p', (SEQ, 8)),
                        ('st_s', (8, 64, 64)), ('sh_s', (128, 14)),
                        ('k_s', (NS, 512)), ('v_s', (NS, 512)), ('lf_s', (NS, 8))]:
            O[nm] = self.dout(nm, shp)
        self.O = O
        self.G = []
        for gi, (T, past) in enumerate([(SEQ, 0), (NS, PAST)]):
            tot = past + T
            nkb = (tot + 127) // 128
            g = dict(gi=gi, T=T, past=past, tot=tot, nkb=nkb,
                     Qs=self.dscr(f'Qs{gi}', (8, 67, T)), Ks=self.dscr(f'Ks{gi}', (8, 67, tot)),
                     Vs=self.dscr(f'Vs{gi}', (8, 128, nkb, 64)), Ys=self.dscr(f'Ys{gi}', (D, T)),
                     x=I['xp'] if gi == 0 else I['xs'])
            g['o_y'], g['o_st'], g['o_sh'], g['o_k'], g['o_v'], g['o_lf'] = (
                (O['y_p'], O['st_p'], O['sh_p'], O['k_p'], O['v_p'], O['lf_p']) if gi == 0 else
                (O['y_s'], O['st_s'], O['sh_s'], O['k_s'], O['v_s'], O['lf_s']))
            self.G.append(g)

        with contextlib.ExitStack() as st:
            self.P = Prog(nc, st)
            self.P.noself = tuple(self.sub.get('noself', ()))
            self.persistent(st)
            ph = self.phases
            with contextlib.ExitStack() as sw1:
                if 1 in ph:
                    self.load_win(sw1)
                with contextlib.ExitStack() as s0:
                    self.phase0(s0)
                    self.P.wait_all_dma('sp')
                    self.P.emit()
                if 1 in ph:
                    with contextlib.ExitStack() as s1:
                        self.phase1(s1)
                        self.P.wait_all_dma('sp')
                        self.P.emit()
            with contextlib.ExitStack() as sw3:
                if 3 in ph:
                    self.load_wgu(sw3)
                if 2 in ph:
                    with contextlib.ExitStack() as s2:
                        self.phase2(s2)
                        self.P.wait_all_dma('sp')
                        self.P.emit()
                with contextlib.ExitStack() as s3:
                    if 3 in ph:
                        self.phase3(s3)
                    self.P.wait_all_dma('sp')
                    self.P.emit()
        return nc

    def persistent(self, st):
        I = self.I
        c = {}
        def ld(nm, shape, dt=F32, q='sp'):
            t = self.sb(st, shape, dt, name=nm)
            self.dma('pool' if dt == BF16 else q, t[:], I[nm], W=[t.b])
            return t
        c['ident_f'] = ld('c_ident', (128, 128))
        c['triu_f'] = ld('c_triu', (128, 128))
        c['ones_f'] = ld('c_ones', (128, 128))
        c['maskA'] = ld('c_maskA', (128, 256))
        c['maskC'] = ld('c_maskC', (128, 128))
        c['reset'] = ld('c_reset', (128, 1024))
        c['ipair'] = ld('c_ipair', (128, 64))
        c['ident_b'] = self.sb(st, (128, 128), BF16, name='identb')
        self.dma('pool', c['ident_b'][:], I['c_ident'], W=[c['ident_b'].b])
        c['blk_b'] = self.sb(st, (128, 128), BF16, name='blkb')
        self.dma('pool', c['blk_b'][:], I['c_blk'], W=[c['blk_b'].b])
        c['maskD_b'] = self.sb(st, (128, 128), BF16, name='maskDb')
        self.dma('pool', c['maskD_b'][:], I['c_maskD'], W=[c['maskD_b'].b])
        for nm in ['n1g', 'n2g']:
            c[nm] = ld(nm, (128, 8))
        c['mu'] = ld('mu', (128, 14))
        for nm in ['w0', 'a0', 'k_k', 'k_a', 'r_k', 'gn_g', 'gn_b']:
            c[nm] = ld(nm, (128, 4))
        c['f_b'] = ld('f_b', (128, 8))
        c['eps'] = self.sb(st, (128, 1), F32, name='eps')
        self.memset('dve', c['eps'][:], EPS, W=[c['eps'].b])
        c['eps64'] = self.sb(st, (128, 1), F32, name='eps64')
        self.memset('dve', c['eps64'][:], 64 * EPS, W=[c['eps64'].b])
        c['gneps'] = self.sb(st, (128, 1), F32, name='gneps')
        self.memset('dve', c['gneps'][:], GN_EPS, W=[c['gneps'].b])
        c['omu'] = self.sb(st, (128, 14), F32, name='omu')
        self.ts('dve', c['omu'][:], c['mu'][:], -1.0, 1.0, ALU.mult, ALU.add, R=[c['mu'].b], W=[c['omu'].b])
        c['omka'] = self.sb(st, (128, 4), F32, name='omka')
        self.ts('dve', c['omka'][:], c['k_a'][:], -1.0, 1.0, ALU.mult, ALU.add, R=[c['k_a'].b], W=[c['omka'].b])
        c['mod'] = self.sb(st, (128, 48, 2), F32, name='mod')
        self.c = c
        for g in self.G:
            g['gm1'] = self.sb(st, (128, 8), F32, name='gm1')
            g['gm2'] = self.sb(st, (128, 8), F32, name='gm2')
            g['sh1'] = self.sb(st, (128, 8), F32, name='sh1')
            g['sh2'] = self.sb(st, (128, 8), F32, name='sh2')
            g['neglc'] = self.sb(st, (128, g['nkb'], 8), F32, name='neglc')
        self.pT = [self.ps(st, (128, 1024), BF16, name='pT') for _ in range(2)]
        self.pA = [self.ps(st, (128, 512), F32, name='pA') for _ in range(6)]

    def load_win(self, st):
        I = self.I
        win = self.sb(st, (128, 8, INC), BF16, name='win')
        wsrc = I['w_in'].rearrange("(k p) c -> p k c", p=128)
        for k in range(8):
            for (a, b) in [(0, 1792), (1792, INC)]:
                self.dma('pool', win[:, k, a:b], wsrc[:, k, a:b], W=[win.b])
        self.win = win

    def load_wgu(self, st):
        I = self.I
        self.wg = self.sb(st, (128, 8, DFF), BF16, name='wg')
        self.wu = self.sb(st, (128, 8, DFF), BF16, name='wu')
        for (dst, src) in [(self.wg, I['w_g']), (self.wu, I['w_u'])]:
            s3 = src.rearrange("(k p) c -> p k c", p=128)
            for k in range(8):
                for a in range(0, DFF, 1408):
                    self.dma('pool', dst[:, k, a:a + 1408], s3[:, k, a:a + 1408], W=[dst.b])

    def phase0(self, st):
        I, c = self.I, self.c
        cv = self.sb(st, (128, 8, 2), F32)
        self.dma('sp', cv[:], I['cvec'], W=[cv.b])
        cs = self.sb(st, (128, 8, 2), F32)
        self.act(cs[:], cv[:], AF.Silu, R=[cv.b], W=[cs.b])
        bada = self.sb(st, (128, 48), F32)
        self.dma('sp', bada[:], I['b_ada'], W=[bada.b])
        wa = [self.sb(st, (128, 8, 512), F32) for _ in range(2)]
        wsrc = I['w_ada'].rearrange("(k p) c -> p k c", p=128)
        pm = self.pA[0]
        for ch in range(12):
            w = wa[ch % 2]
            self.dma('sp', w[:], wsrc[:, :, ch * 512:(ch + 1) * 512], W=[w.b])
            for cbl in range(4):
                gb = ch * 4 + cbl
                for k in range(8):
                    self.mm(pm[:, gb * 2:gb * 2 + 2], w[:, k, cbl * 128:(cbl + 1) * 128], cs[:, k, :],
                            start=(k == 0), stop=(k == 7), R=[w.b, cs.b], W=[pm.b])
        mod = c['mod']
        self.tt('dve', mod[:], pm[:, 0:96].rearrange("p (a b) -> p a b", b=2),
                bada[:].unsqueeze(2).to_broadcast([128, 48, 2]), ALU.add, R=[pm.b, bada.b], W=[mod.b])
        tmp = self.sb(st, (128, 8), F32)
        for g in self.G:
            gi = g['gi']
            for (dst, blk0, ng) in [(g['gm1'], 8, c['n1g']), (g['gm2'], 32, c['n2g'])]:
                self.ts('dve', tmp[:], mod[:, blk0:blk0 + 8, gi], 1.0, None, ALU.add, R=[mod.b], W=[tmp.b])
                self.tt('dve', dst[:], tmp[:], ng[:], ALU.mult, R=[tmp.b, ng.b], W=[dst.b])
            self.cp('dve', g['sh1'][:], mod[:, 0:8, gi], R=[mod.b], W=[g['sh1'].b])
            self.cp('dve', g['sh2'][:], mod[:, 24:32, gi], R=[mod.b], W=[g['sh2'].b])

    def build_gates(self, g, gt1, gt2, tl):
        c = self.c
        mod = c['mod']
        gi = g['gi']
        n = 0
        for (dst, blk0) in [(gt1, 16), (gt2, 40)]:
            for half in range(2):
                pg = self.pA[4 + (n % 2)]
                n += 1
                for j in range(4):
                    blk = half * 4 + j
                    t = tl[blk % 2]
                    self.ts('dve', t[:, 0:128], c['ones_f'][:], mod[:, blk0 + blk, gi:gi + 1], None, ALU.mult,
                            R=[c['ones_f'].b, mod.b], W=[t.b])
                    self.mm(pg[:, j * 128:(j + 1) * 128], t[:, 0:128], c['ident_f'][:], R=[t.b, c['ident_f'].b], W=[pg.b])
                self.cp('act', dst[:, half * 512:(half + 1) * 512], pg[:], R=[pg.b], W=[dst.b])

    def norm_transpose(self, xt, nb, bs, gm, sh, xsb, hT, junk, ss, rstd, pT_sel, alt=False):
        c = self.c
        for blk in range(nb):
            self.act(junk[0:bs, :], xt[0:bs, blk, :], AF.Square, accum=ss[0:bs, blk:blk + 1],
                     R=[xt.b], W=[junk.b, ss.b])
        self.act(rstd[0:bs, 0:nb], ss[0:bs, 0:nb], AF.Sqrt, bias=c['eps'][0:bs, :], scale=1.0 / D,
                 R=[ss.b, c['eps'].b], W=[rstd.b])
        self.recip(rstd[0:bs, 0:nb], rstd[0:bs, 0:nb], R=[rstd.b], W=[rstd.b])
        for blk in range(nb):
            self.act(xsb[0:bs, blk, :], xt[0:bs, blk, :], AF.Copy, scale=rstd[0:bs, blk:blk + 1],
                     R=[xt.b, rstd.b], W=[xsb.b])
        nt = (nb - 1) * 128 + bs
        for k in range(8):
            pt = self.pT[(pT_sel + k) % 2] if alt else self.pT[pT_sel]
            for blk in range(nb):
                self.tr(pt[:, blk * 128:blk * 128 + bs], xsb[0:bs, blk, k * 128:(k + 1) * 128],
                        c['ident_b'][0:bs, 0:bs], R=[xsb.b, c['ident_b'].b], W=[pt.b])
            self.ts('dve', hT[:, k, 0:nt], pt[:, 0:nt], gm[:, k:k + 1], sh[:, k:k + 1], ALU.mult, ALU.add,
                    R=[pt.b, gm.b, sh.b], W=[hT.b])

    def phase1(self, st):
        I, c = self.I, self.c
        win = self.win
        wdec = self.sb(st, (64, 512), BF16, name='wdec')
        self.dma('pool', wdec[:], I['w_dec'], W=[wdec.b])
        waaa = self.sb(st, (128, 512), BF16, name='waaa')
        self.dma('pool', waaa[64:128, :], I['w_aaa'], W=[waaa.b])
        wgup = self.sb(st, (128, 512), BF16, name='wgup')
        self.dma('pool', wgup[:], I['w_gup'], W=[wgup.b])
        gqk = self.sb(st, (128, 1024), F32, name='gqk')
        self.dma('sp', gqk[:], I['gqk'], W=[gqk.b])
        self.win, self.wdec, self.waaa, self.wgup, self.gqk = win, wdec, waaa, wgup, gqk

        NT = 128
        self.NT1 = NT
        W = {}
        W['xt'] = [self.sb(st, (128, 1, D), F32, name='xt') for _ in range(2)]
        W['xsb'] = self.sb(st, (128, 1, D), BF16, name='xsb')
        W['hT'] = self.sb(st, (128, 8, NT), BF16, name='hT')
        W['junk'] = self.sb(st, (128, D), BF16, name='junk')
        W['ss'] = self.sb(st, (128, 4), F32)
        W['rstd'] = self.sb(st, (128, 4), F32)
        W['sq'] = self.sb(st, (128, 1024), F32, name='sq')
        W['ss16'] = self.sb(st, (128, 16), F32)
        W['rs16'] = self.sb(st, (128, 16), F32)
        W['qkn'] = self.sb(st, (128, 1024), F32, name='qkn')
        W['kout'] = [self.sb(st, (128, 512), F32, name='kout') for _ in range(2)]
        W['qkb'] = self.sb(st, (128, 1024), BF16, name='qkb')
        W['QKt'] = [self.sb(st, (128, 8, NT), BF16, name='QKt') for _ in range(2)]
        W['v32'] = [self.sb(st, (128, 512), F32, name='v32') for _ in range(2)]
        W['vb'] = [self.sb(st, (128, 1, 512), BF16, name='vb') for _ in range(2)]
        W['lf'] = [self.sb(st, (128, 8), F32, name='lf') for _ in range(2)]
        W['lft'] = self.sb(st, (128, 8), F32)
        W['lcT'] = self.sb(st, (8, NT), F32)
        W['lcr'] = self.sb(st, (8, NT), F32)
        W['lcs'] = [self.sb(st, (8, 3, NT), BF16) for _ in range(2)]
        W['lchf'] = self.sb(st, (8, NT), F32)
        W['R'] = self.sb(st, (128, 8), F32, name='Racc')
        W['onesb'] = self.sb(st, (3, 2112), BF16, name='onesb')
        self.memset('pool', W['onesb'][:], 1.0, W=[W['onesb'].b])
        for nm in ['z']:
            W[nm] = self.sb(st, (128, 14, NT), F32, name=nm)
        W['t1'] = [self.sb(st, (128, NT), F32, name='t1') for _ in range(2)]
        W['carry'] = self.sb(st, (128, 14), F32, name='carry')
        for nm in ['esig', 'aa', 'kk', 'kp', 'cs', 'gx', 'tmpf', 'beta']:
            W[nm] = self.sb(st, (128, 4, NT), F32, name=nm)
        for nm in ['sqb', 'rkb']:
            W[nm] = self.sb(st, (128, 4, NT), BF16, name=nm)
        W['HO'] = []
        for _ in range(2):
            ho = {}
            for nm in ['BtT', 'KtT', 'BgT', 'KgT', 'vTb', 'gT', 'bonus']:
                ho[nm] = self.sb(st, (128, 4, NT), BF16, name=nm)
            ho['gam'] = self.sb(st, (128, 4, NT), F32, name='gam')
            ho['AR'] = self.sb(st, (128, 4, 1, 2, 128), BF16, name='AR')
            W['HO'].append(ho)
        W['tdw'] = self.sb(st, (128, NT), BF16, name='tdw')
        W['dab'] = self.sb(st, (128, NT), BF16, name='dab')
        W['sg'] = self.sb(st, (128, NT), BF16, name='sg')
        W['nb16'] = self.sb(st, (128, 4, 1), F32, name='nb16')
        W['Bgt'] = self.sb(st, (128, 512), BF16, name='Bgt')
        W['Kgt'] = self.sb(st, (128, 512), BF16, name='Kgt')
        W['Vt'] = self.sb(st, (128, 512), BF16, name='Vt')
        W['MLt'] = self.sb(st, (128, 8, 256), BF16, name='MLt')
        W['MKt'] = self.sb(st, (128, 8, 256), BF16, name='MKt')
        W['Lc'] = [self.sb(st, (128, 8, 128), BF16, name='Lc') for _ in range(2)]
        W['Mc'] = [self.sb(st, (128, 8, 128), BF16, name='Mc') for _ in range(2)]
        W['Xf'] = self.sb(st, (128, 8, 128), F32, name='Xf')
        W['Xb'] = self.sb(st, (128, 8, 128), BF16, name='Xb')
        W['GT'] = self.sb(st, (128, 4, 64), BF16, name='GT')
        W['Hs'] = self.sb(st, (128, 4, 64), F32, name='Hs')
        W['RAT'] = self.sb(st, (128, 4, 128), BF16, name='RAT')
        W['Sf'] = self.sb(st, (128, 4, 64), F32, name='Sf')
        W['Sb'] = self.sb(st, (128, 4, 64), BF16, name='Sb')
        W['ysb'] = self.sb(st, (128, 8, 64), F32, name='ysb')
        W['ysq'] = self.sb(st, (128, 8, 64), F32, name='ysq')
        W['yh'] = self.sb(st, (128, 512), BF16, name='yh')
        W['st8'] = [self.sb(st, (128, 8), F32) for _ in range(4)]
        W['yT1'] = self.sb(st, (128, 4, 128), F32, name='yT1')
        W['yr'] = [self.sb(st, (128, 4, 128), BF16, name='yr') for _ in range(2)]
        self.nyr = 0
        W['Sv'] = self.sb(st, (64, 8, 64), F32, name='Sv')
        W['So'] = self.sb(st, (64, 4, 128), F32, name='So')
        self.W = W

        for g in self.G:
            self.phase1_group(g, NT)

    def phase1_group(self, g, NT):
        I, c, W = self.I, self.c, self.W
        gi, T, past = g['gi'], g['T'], g['past']
        for h in range(NH):
            for a in range(0, g['tot'], 2112):
                b = min(g['tot'], a + 2112)
                self.dma('sp', g['Ks'][h, 64:67, a:b], W['onesb'][:, 0:b - a], R=[W['onesb'].b])
        self.memset('dve', W['R'][:], 0.0, W=[W['R'].b])
        npb = past // 128
        for pb in range(0 if self.sub.get('skip_past') else npb):
            kc = W['kout'][pb % 2]
            self.dma('sp', kc[:], I['ck'][pb * 128:(pb + 1) * 128, :], W=[kc.b])
            self.cp('pool', W['qkb'][:, 512:1024], kc[:], R=[kc.b], W=[W['qkb'].b])
            self.k_transposes(g, pb * 128, 128, 0, only_k=True, blk=pb)
            self.flush_qk(g, pb * 128, 128, only_k=True, blk=pb)
            vc = W['v32'][pb % 2]
            self.dma('sp', vc[:], I['cv'][pb * 128:(pb + 1) * 128, :], W=[vc.b])
            vb = W['vb'][pb % 2]
            self.cp('pool', vb[:, 0, :], vc[:], R=[vc.b], W=[vb.b])
            self.dma('sp', g['Vs'][:, :, pb, :].rearrange("h p d -> p h d"),
                     vb[:, 0, :].rearrange("p (h d) -> p h d", d=64), R=[vb.b])
            lf = W['lf'][pb % 2]
            self.dma('sp', lf[:], I['clf'][pb * 128:(pb + 1) * 128, :], W=[lf.b])
            self.lc_block(g, lf, 128, pb, None, 0)
        if gi == 0:
            self.memset('dve', W['Sf'][:], 0.0, W=[W['Sf'].b])
            self.memset('pool', W['Sb'][:], 0.0, W=[W['Sb'].b])
            self.memset('dve', W['carry'][:], 0.0, W=[W['carry'].b])
        else:
            self.dma('sp', W['carry'][:], I['shp0'], W=[W['carry'].b])
            self.dma('sp', W['Sv'][:], I['st0'], W=[W['Sv'].b])
            for cb in range(4):
                pa = self.pA[cb % 2]
                self.tr(pa[:, 0:64], W['Sv'][:, 2 * cb:2 * cb + 2, :], c['ident_f'][0:64, 0:64],
                        R=[W['Sv'].b, c['ident_f'].b], W=[pa.b])
                self.cp('dve', W['Sf'][:, cb, :], pa[:, 0:64], R=[pa.b], W=[W['Sf'].b])
            self.cp('pool', W['Sb'][:], W['Sf'][:], R=[W['Sf'].b], W=[W['Sb'].b])
        ntile = (T + NT - 1) // NT
        def run(gens):
            gens = [x for x in gens if x is not None]
            while gens:
                for x in list(gens):
                    try:
                        next(x)
                    except StopIteration:
                        gens.remove(x)
        genB = None
        for ti in range(ntile):
            nt = min(NT, T - ti * NT)
            genA = self.phase1_tile(g, ti, ti * NT, nt)
            run([genA, genB])
            C_ = min(128, nt)
            genB = self.rwkv_chunk(g, ti, ti * NT, 0, C_, nt) if not self.sub.get('skip_rwkv') else None
        run([genB])
        if self.sub.get('skip_final'):
            return
        self.dma('sp', g['o_sh'], W['carry'][:], R=[W['carry'].b])
        for cb in range(4):
            pa = self.pA[cb % 2]
            self.tr(pa[0:64, 0:128], W['Sf'][:, cb, :], c['ident_f'][:], R=[W['Sf'].b, c['ident_f'].b], W=[pa.b])
            self.cp('dve', W['So'][:, cb, :], pa[0:64, 0:128], R=[pa.b], W=[W['So'].b])
        self.dma('sp', g['o_st'].rearrange("(cb hh) v k -> v cb hh k", hh=2),
                 W['So'][:].rearrange("v cb (hh k) -> v cb hh k", hh=2), R=[W['So'].b])

    def k_transposes(self, g, tok0, bs, col0, only_k, blk):
        c, W = self.c, self.W
        QKt = W['QKt'][g.get('qkt_sel', 0)]
        for j in range(4 if only_k else 8):
            jj = j + 4 if only_k else j
            pt = self.pT[0]
            self.tr(pt[:, 0:bs], W['qkb'][0:bs, jj * 128:(jj + 1) * 128], c['ident_b'][0:bs, 0:bs],
                    R=[W['qkb'].b, c['ident_b'].b], W=[pt.b])
            self.cp('act', QKt[:, jj, col0:col0 + bs], pt[:, 0:bs], R=[pt.b], W=[QKt.b])

    def flush_qk(self, g, tok0, n, only_k, blk=None, qtok0=None):
        W = self.W
        QKt = W['QKt'][g.get('qkt_sel', 0)]
        for h in range(NH):
            pb = 64 * (h % 2)
            self.dma('sp', g['Ks'][h, 0:64, tok0:tok0 + n], QKt[pb:pb + 64, 4 + h // 2, 0:n], R=[QKt.b])
            if not only_k:
                self.dma('sp', g['Qs'][h, 0:64, qtok0:qtok0 + n], QKt[pb:pb + 64, h // 2, 0:n], R=[QKt.b])
        g['qkt_sel'] = 1 - g.get('qkt_sel', 0)

    def lc_block(self, g, lf, bs, kb, lcT_cols, col0):
        c, W = self.c, self.W
        R = W['R']
        pa = self.pA[1]
        self.mm(pa[0:bs, 0:8], c['triu_f'][0:bs, 0:bs], lf[0:bs, :], start=True, stop=False,
                R=[c['triu_f'].b, lf.b], W=[pa.b])
        self.mm(pa[0:bs, 0:8], c['ones_f'][:, 0:bs], R[:, :], start=False, stop=True, R=[c['ones_f'].b, R.b], W=[pa.b])
        self.ts('dve', g['neglc'][0:bs, kb, :], pa[0:bs, 0:8], -1.0, None, ALU.mult, R=[pa.b], W=[g['neglc'].b])
        if lcT_cols is not None:
            o = 16 + col0
            self.mm(pa[0:8, o:o + bs], lf[0:bs, :], c['triu_f'][0:bs, 0:bs], start=True, stop=False,
                    R=[lf.b, c['triu_f'].b], W=[pa.b])
            self.mm(pa[0:8, o:o + bs], R[:, :], c['ones_f'][:, 0:bs], start=False, stop=True,
                    R=[R.b, c['ones_f'].b], W=[pa.b])
            self.cp('dve', W['lcT'][:, col0:col0 + bs], pa[0:8, o:o + bs], R=[pa.b], W=[W['lcT'].b])
        self.tt('dve', R[0:bs, :], R[0:bs, :], lf[0:bs, :], ALU.add, R=[R.b, lf.b], W=[R.b])

    def phase1_tile(self, g, ti, tok0, nt):
        I, c, W = self.I, self.c, self.W
        gi, past = g['gi'], g['past']
        nb = (nt + 127) // 128
        bs = min(128, nt)
        xt = W['xt'][ti % 2]
        self.dma('sp', xt[0:bs, 0:nb, :], g['x'][tok0:tok0 + nt, :].rearrange("(b p) d -> p b d", p=bs), W=[xt.b])
        hT = W['hT']
        self.norm_transpose(xt, nb, bs, g['gm1'], g['sh1'], W['xsb'], hT, W['junk'], W['ss'], W['rstd'], 0)
        win = self.win
        yield
        for blk in range(0 if self.sub.get('skip_fox') else nb):
            kb = (past + tok0) // 128 + blk
            pq, pk, pv, pf = self.pA[0], self.pA[1], self.pA[0], self.pA[1]
            def proj(pp, c0, wdt):
                for k in range(8):
                    self.mm(pp[0:bs, 0:wdt], hT[:, k, blk * 128:blk * 128 + bs], win[:, k, c0:c0 + wdt],
                            start=(k == 0), stop=(k == 7), R=[hT.b, win.b], W=[pp.b])
            proj(pq, 1792, 512)
            proj(pk, 2304, 512)
            yield
            sq, ss16, rs16, qkn = W['sq'], W['ss16'], W['rs16'], W['qkn']
            self.act(sq[0:bs, 0:512], pq[0:bs, :], AF.Square, R=[pq.b], W=[sq.b])
            self.act(sq[0:bs, 512:1024], pk[0:bs, :], AF.Square, R=[pk.b], W=[sq.b])
            self.red(ss16[0:bs, :], sq[0:bs, :].rearrange("p (g d) -> p g d", d=64), R=[sq.b], W=[ss16.b])
            self.act(rs16[0:bs, 0:8], ss16[0:bs, 0:8], AF.Sqrt, bias=c['eps64'][0:bs, :], scale=1.0,
                     R=[ss16.b, c['eps64'].b], W=[rs16.b])
            self.act(rs16[0:bs, 8:16], ss16[0:bs, 8:16], AF.Sqrt, bias=c['eps'][0:bs, :], scale=1.0 / 64,
                     R=[ss16.b, c['eps'].b], W=[rs16.b])
            self.recip(rs16[0:bs, :], rs16[0:bs, :], R=[rs16.b], W=[rs16.b])
            self.tt('dve', qkn[0:bs, 0:512].rearrange("p (g d) -> p g d", d=64),
                    pq[0:bs, :].rearrange("p (g d) -> p g d", d=64),
                    rs16[0:bs, 0:8].unsqueeze(2).to_broadcast([bs, 8, 64]), ALU.mult, R=[pq.b, rs16.b], W=[qkn.b])
            self.tt('dve', qkn[0:bs, 512:1024].rearrange("p (g d) -> p g d", d=64),
                    pk[0:bs, :].rearrange("p (g d) -> p g d", d=64),
                    rs16[0:bs, 8:16].unsqueeze(2).to_broadcast([bs, 8, 64]), ALU.mult, R=[pk.b, rs16.b], W=[qkn.b])
            kout = W['kout'][blk % 2]
            self.tt('dve', W['qkb'][0:bs, 0:512], qkn[0:bs, 0:512], self.gqk[0:bs, 0:512], ALU.mult,
                    R=[qkn.b, self.gqk.b], W=[W['qkb'].b])
            self.tt('dve', kout[0:bs, :], qkn[0:bs, 512:1024], self.gqk[0:bs, 512:1024], ALU.mult,
                    R=[qkn.b, self.gqk.b], W=[kout.b])
            self.dma('sp', g['o_k'][tok0 + blk * 128:tok0 + blk * 128 + bs, :], kout[0:bs, :], R=[kout.b])
            self.cp('pool', W['qkb'][0:bs, 512:1024], kout[0:bs, :], R=[kout.b], W=[W['qkb'].b])
            self.k_transposes(g, tok0, bs, blk * 128, only_k=False, blk=blk)
            yield
            proj(pv, 2816, 512)
            proj(pf, 3328, 8)
            v32 = W['v32'][blk % 2]
            self.cp('act', v32[0:bs, :], pv[0:bs, :], R=[pv.b], W=[v32.b])
            self.dma('sp', g['o_v'][tok0 + blk * 128:tok0 + blk * 128 + bs, :], v32[0:bs, :], R=[v32.b])
            vb = W['vb'][ti % 2]
            self.cp('pool', vb[0:bs, blk, :], v32[0:bs, :], R=[v32.b], W=[vb.b])
            yield
            lf = W['lf'][blk % 2]
            self.tt('dve', W['lft'][0:bs, :], pf[0:bs, 0:8], c['f_b'][0:bs, :], ALU.add, R=[pf.b, c['f_b'].b], W=[W['lft'].b])
            self.act(W['lft'][0:bs, :], W['lft'][0:bs, :], AF.Exp, scale=-1.0, R=[W['lft'].b], W=[W['lft'].b])
            self.act(W['lft'][0:bs, :], W['lft'][0:bs, :], AF.Ln, bias=1.0, scale=1.0, R=[W['lft'].b], W=[W['lft'].b])
            self.ts('dve', lf[0:bs, :], W['lft'][0:bs, :], -1.0, None, ALU.mult, R=[W['lft'].b], W=[lf.b])
            self.dma('sp', g['o_lf'][tok0 + blk * 128:tok0 + blk * 128 + bs, :], lf[0:bs, :], R=[lf.b])
            self.lc_block(g, lf, bs, kb, True, blk * 128)
        if self.sub.get('skip_fox'):
            if not self.sub.get('skip_rwkv'):
                yield from self.rwkv_tile(g, ti, tok0, nt)
            return
        self.flush_qk(g, past + tok0, nt, only_k=False, qtok0=tok0)
        vb = W['vb'][ti % 2]
        kb0 = (past + tok0) // 128
        self.dma('sp', g['Vs'][:, 0:bs, kb0:kb0 + nb, :].rearrange("h p b d -> p h b d"),
                 vb[0:bs, 0:nb, :].rearrange("p b (h d) -> p h b d", d=64), R=[vb.b])
        yield
        lcs = W['lcs'][ti % 2]
        lcT, lcr, lchf = W['lcT'], W['lcr'], W['lchf']
        self.cp('dve', lcs[:, 0, 0:nt], lcT[:, 0:nt], R=[lcT.b], W=[lcs.b])
        self.tt('dve', lcr[:, 0:nt], lcT[:, 0:nt], lcs[:, 0, 0:nt], ALU.subtract, R=[lcT.b, lcs.b], W=[lcr.b])
        self.cp('dve', lcs[:, 1, 0:nt], lcr[:, 0:nt], R=[lcr.b], W=[lcs.b])
        self.tt('dve', lchf[:, 0:nt], lcr[:, 0:nt], lcs[:, 1, 0:nt], ALU.subtract, R=[lcr.b, lcs.b], W=[lchf.b])
        self.cp('dve', lcs[:, 2, 0:nt], lchf[:, 0:nt], R=[lchf.b], W=[lcs.b])
        self.dma('sp', g['Qs'][:, 64:67, tok0:tok0 + nt], lcs[:, :, 0:nt], R=[lcs.b])
        if not self.sub.get('skip_rwkv'):
            yield from self.rwkv_tile(g, ti, tok0, nt)
        yield

    def rwkv_tile(self, g, ti, tok0, nt):
        I, c, W = self.I, self.c, self.W
        ho = W['HO'][ti % 2]
        win, hT = self.win, W['hT']
        C = min(128, nt)
        nch = nt // C
        z = W['z']
        carry = W['carry']
        for cb in range(14):
            pp = self.pA[cb % 2]
            for k in range(8):
                self.mm(pp[:, 0:nt], win[:, k, cb * 128:(cb + 1) * 128], hT[:, k, 0:nt],
                        start=(k == 0), stop=(k == 7), R=[win.b, hT.b], W=[pp.b])
            t1 = W['t1'][cb % 2]
            v = self.sub.get('v1', 15)
            if v & 1:
                self.act(t1[:, 1:nt], pp[:, 0:nt - 1], AF.Copy, scale=c['mu'][:, cb:cb + 1], R=[pp.b, c['mu'].b], W=[t1.b])
            if v & 2:
                self.ts('dve' if v & 16 else 'pool', t1[:, 0:1], carry[:, cb:cb + 1], c['mu'][:, cb:cb + 1], None, ALU.mult,
                        R=[carry.b, c['mu'].b], W=[t1.b])
            if v & 4:
                self.stt(z[:, cb, 0:nt], pp[:, 0:nt], c['omu'][:, cb:cb + 1], t1[:, 0:nt], ALU.mult, ALU.add,
                         R=[pp.b, c['omu'].b, t1.b], W=[z.b])
            if v & 8:
                self.cp('act', carry[:, cb:cb + 1], pp[:, nt - 1:nt], R=[pp.b], W=[carry.b])
            if cb % 2 == 1:
                yield
        if self.sub.get('rstop', 99) <= 1:
            return
        zr, zk, zv = z[:, 0:4, 0:nt], z[:, 4:8, 0:nt], z[:, 8:12, 0:nt]
        tdw, dab, sg = W['tdw'], W['dab'], W['sg']
        self.act(tdw[0:64, 0:nt], z[0:64, 12, 0:nt], AF.Tanh, R=[z.b], W=[tdw.b])
        self.cp('pool', dab[64:128, 0:nt], z[64:128, 12, 0:nt], R=[z.b], W=[dab.b])
        self.act(sg[:, 0:nt], z[:, 13, 0:nt], AF.Sigmoid, R=[z.b], W=[sg.b])
        esig, aa, gT = W['esig'], W['aa'], ho['gT']
        for cb in range(4):
            p1, p2, p3 = self.pA[0], self.pA[1], self.pA[0]
            o = (cb % 2) * 256
            self.mm(p1[:, o:o + nt], self.wdec[0:64, cb * 128:(cb + 1) * 128], tdw[0:64, 0:nt], R=[self.wdec.b, tdw.b], W=[p1.b])
            self.act(esig[:, cb, 0:nt], p1[:, o:o + nt], AF.Sigmoid, bias=c['w0'][:, cb:cb + 1], R=[p1.b, c['w0'].b], W=[esig.b])
            self.mm(p2[:, o:o + nt], self.waaa[64:128, cb * 128:(cb + 1) * 128], dab[64:128, 0:nt], R=[self.waaa.b, dab.b], W=[p2.b])
            self.act(aa[:, cb, 0:nt], p2[:, o:o + nt], AF.Sigmoid, bias=c['a0'][:, cb:cb + 1], R=[p2.b, c['a0'].b], W=[aa.b])
            self.mm(p3[:, o:o + nt], self.wgup[:, cb * 128:(cb + 1) * 128], sg[:, 0:nt], R=[self.wgup.b, sg.b], W=[p3.b])
            self.cp('dve', gT[:, cb, 0:nt], p3[:, o:o + nt], R=[p3.b], W=[gT.b])
        if self.sub.get('rstop', 99) <= 2:
            return
        yield
        kk, kp, sqb, tmpf = W['kk'], W['kp'], W['sqb'], W['tmpf']
        for cb in range(4):
            self.ts('dve', kk[:, cb, 0:nt], z[:, 4 + cb, 0:nt], c['k_k'][:, cb:cb + 1], None, ALU.mult, R=[z.b, c['k_k'].b], W=[kk.b])
        self.act(sqb[:, :, 0:nt], kk[:, :, 0:nt], AF.Square, R=[kk.b], W=[sqb.b])
        for cb in range(4):
            pp = self.pA[cb // 2]
            o = (cb % 2) * 256
            self.mm(pp[:, o:o + nt], c['blk_b'][:], sqb[:, cb, 0:nt], R=[c['blk_b'].b, sqb.b], W=[pp.b])
            self.act(tmpf[:, cb, 0:nt], pp[:, o:o + nt], AF.Sqrt, R=[pp.b], W=[tmpf.b])
        self.ts('dve', tmpf[:, :, 0:nt], tmpf[:, :, 0:nt], 1e-12, None, ALU.max, R=[tmpf.b], W=[tmpf.b])
        self.recip(tmpf[:, :, 0:nt], tmpf[:, :, 0:nt], R=[tmpf.b], W=[tmpf.b])
        self.tt('dve', kk[:, :, 0:nt], kk[:, :, 0:nt], tmpf[:, :, 0:nt], ALU.mult, R=[kk.b, tmpf.b], W=[kk.b])
        yield
        for cb in range(4):
            self.ts('dve', tmpf[:, cb, 0:nt], aa[:, cb, 0:nt], c['k_a'][:, cb:cb + 1], c['omka'][:, cb:cb + 1], ALU.mult, ALU.add,
                    R=[aa.b, c['k_a'].b, c['omka'].b], W=[tmpf.b])
        self.tt('dve', kp[:, :, 0:nt], zk, tmpf[:, :, 0:nt], ALU.mult, R=[z.b, tmpf.b], W=[kp.b])
        yield
        bonus, rkb = ho['bonus'], W['rkb']
        self.tt('dve', tmpf[:, :, 0:nt], zr, kp[:, :, 0:nt], ALU.mult, R=[z.b, kp.b], W=[tmpf.b])
        for cb in range(4):
            self.ts('dve', rkb[:, cb, 0:nt], tmpf[:, cb, 0:nt], c['r_k'][:, cb:cb + 1], None, ALU.mult, R=[tmpf.b, c['r_k'].b], W=[rkb.b])
        for cb in range(4):
            pp = self.pA[cb // 2]
            o = (cb % 2) * 256
            self.mm(pp[:, o:o + nt], c['blk_b'][:], rkb[:, cb, 0:nt], R=[c['blk_b'].b, rkb.b], W=[pp.b])
            self.tt('dve', bonus[:, cb, 0:nt], pp[:, o:o + nt], z[:, 8 + cb, 0:nt], ALU.mult, R=[pp.b, z.b], W=[bonus.b])
        if self.sub.get('rstop', 99) <= 3:
            return
        yield
        cs, gam, gx, beta = W['cs'], ho['gam'], W['gx'], W['beta']
        NTf = self.NT1
        if nt == NTf:
            self.P.op('dve', lambda e: e.tensor_tensor_scan(cs[:].rearrange("p a t -> p (a t)"), c['reset'][:, 0:4 * nt],
                                                            esig[:].rearrange("p a t -> p (a t)"), 0.0, op0=ALU.mult, op1=ALU.add),
                      [c['reset'].b, esig.b], [cs.b])
        else:
            for cb in range(4):
                self.P.op('dve', lambda e, cb=cb: e.tensor_tensor_scan(cs[:, cb, 0:nt], c['reset'][:, 0:nt], esig[:, cb, 0:nt], 0.0,
                                                                      op0=ALU.mult, op1=ALU.add),
                          [c['reset'].b, esig.b], [cs.b])
        self.act(gam[:, :, 0:nt], cs[:, :, 0:nt], AF.Exp, scale=-C0, R=[cs.b], W=[gam.b])
        AR, BtT, KtT, BgT, KgT, vTb = ho['AR'], ho['BtT'], ho['KtT'], ho['BgT'], ho['KgT'], ho['vTb']
        def chv(ap):
            return ap.rearrange("p a (n c) -> p a n c", c=C)
        self.tt('dve', AR[:, :, 0:nch, 1, 0:C], chv(zr), chv(gam[:, :, 0:nt]), ALU.mult, R=[z.b, gam.b], W=[AR.b])
        self.tt('dve', beta[:, :, 0:nt], kk[:, :, 0:nt], aa[:, :, 0:nt], ALU.mult, R=[kk.b, aa.b], W=[beta.b])
        yield
        self.act(gx[:, :, 0:nt], cs[:, :, 0:nt], AF.Exp, scale=C0, R=[cs.b], W=[gx.b])
        self.tt('dve', BtT[:, :, 0:nt], beta[:, :, 0:nt], gx[:, :, 0:nt], ALU.mult, R=[beta.b, gx.b], W=[BtT.b])
        self.tt('dve', KtT[:, :, 0:nt], kp[:, :, 0:nt], gx[:, :, 0:nt], ALU.mult, R=[kp.b, gx.b], W=[KtT.b])
        yield
        self.tt('dve', tmpf[:, :, 0:nt], cs[:, :, 0:nt], esig[:, :, 0:nt], ALU.subtract, R=[cs.b, esig.b], W=[tmpf.b])
        self.act(gx[:, :, 0:nt], tmpf[:, :, 0:nt], AF.Exp, scale=-C0, R=[tmpf.b], W=[gx.b])
        self.tt('dve', AR[:, :, 0:nch, 0, 0:C], chv(kk[:, :, 0:nt]), chv(gx[:, :, 0:nt]), ALU.mult, R=[kk.b, gx.b], W=[AR.b])
        yield
        nb16 = W['nb16']
        self.ts('dve', nb16[:, :, 0:nch], cs[:, :, C - 1:nt:C], -C0, None, ALU.mult, R=[cs.b], W=[nb16.b])
        for cb in range(4):
            for ch in range(nch):
                self.act(gx[:, cb, ch * C:(ch + 1) * C], cs[:, cb, ch * C:(ch + 1) * C], AF.Exp, bias=nb16[:, cb, ch:ch + 1], scale=C0,
                         R=[cs.b, nb16.b], W=[gx.b])
        self.tt('dve', BgT[:, :, 0:nt], beta[:, :, 0:nt], gx[:, :, 0:nt], ALU.mult, R=[beta.b, gx.b], W=[BgT.b])
        self.tt('dve', KgT[:, :, 0:nt], kp[:, :, 0:nt], gx[:, :, 0:nt], ALU.mult, R=[kp.b, gx.b], W=[KgT.b])
        self.cp('pool', vTb[:, :, 0:nt], zv, R=[z.b], W=[vTb.b])
        if self.sub.get('rstop', 99) <= 4:
            return

    def rwkv_chunk(self, g, ti, tok0, ch, C, nt):
        c, W = self.c, self.W
        ho = W['HO'][ti % 2]
        AR, BtT, KtT, BgT, KgT, vTb = ho['AR'], ho['BtT'], ho['KtT'], ho['BgT'], ho['KgT'], ho['vTb']
        Bgt, Kgt, Vt, MLt, MKt, Xf, Xb = W['Bgt'], W['Kgt'], W['Vt'], W['MLt'], W['MKt'], W['Xf'], W['Xb']
        sl = slice(ch * C, (ch + 1) * C)
        idb = c['ident_b']
        pt = self.pT[1]
        for cb in range(4):
            self.tr(pt[0:C, cb * 128:(cb + 1) * 128], AR[:, cb, ch, 0, 0:C], idb[:], R=[AR.b, idb.b], W=[pt.b])
        self.ts('dve', Xb[0:C, :, 0:64], pt[0:C, 0:512].rearrange("p (h d) -> p h d", d=64), -1.0, None, ALU.mult, R=[pt.b], W=[Xb.b])
        for (src, dst, eng, pi) in [(BgT, Bgt, 'act', 1), (KgT, Kgt, 'act', 0), (vTb, Vt, 'act', 1)]:
            pt = self.pT[1]
            for cb in range(4):
                self.tr(pt[0:C, cb * 128:(cb + 1) * 128], src[:, cb, sl], idb[:], R=[src.b, idb.b], W=[pt.b])
            self.cp(eng, dst[0:C, :], pt[0:C, 0:512], R=[pt.b], W=[dst.b])
        if self.sub.get('rstop', 99) <= 5:
            return
        yield
        mA = c['maskA'][0:C, :].rearrange("p (a c) -> p a c", a=2)[:, :, 0:C]
        Lc0 = W['Lc'][0]
        for par in range(2):
            pb_ = 64 * par
            for half in range(2):
                pA_, pB_, pc = self.bb(), self.bb(), self.bb()
                for j in range(2):
                    cb = 2 * half + j
                    ar = AR[pb_:pb_ + 64, cb, ch, :, 0:C]
                    o = j * 256
                    self.mm(pA_[0:C, o:o + 2 * C].rearrange("p (a c) -> p a c", a=2), BtT[pb_:pb_ + 64, cb, sl], ar,
                            R=[BtT.b, AR.b], W=[pA_.b])
                    self.mm(pB_[0:C, o:o + 2 * C].rearrange("p (a c) -> p a c", a=2), KtT[pb_:pb_ + 64, cb, sl], ar,
                            R=[KtT.b, AR.b], W=[pB_.b])
                    self.mm(pc[0:C, j * 128:j * 128 + C], AR[pb_:pb_ + 64, cb, ch, 0, 0:C], BtT[pb_:pb_ + 64, cb, sl],
                            R=[AR.b, BtT.b], W=[pc.b])
                for j in range(2):
                    cb = 2 * half + j
                    h = 2 * cb + par
                    o = j * 256
                    self.tt('dve', MLt[0:C, h, :].rearrange("p (a c) -> p a c", a=2)[:, :, 0:C],
                            pA_[0:C, o:o + 2 * C].rearrange("p (a c) -> p a c", a=2), mA, ALU.mult,
                            R=[pA_.b, c['maskA'].b], W=[MLt.b])
                    self.tt('dve', MKt[0:C, h, :].rearrange("p (a c) -> p a c", a=2)[:, :, 0:C],
                            pB_[0:C, o:o + 2 * C].rearrange("p (a c) -> p a c", a=2), mA, ALU.mult,
                            R=[pB_.b, c['maskA'].b], W=[MKt.b])
                    self.tt('dve', Lc0[0:C, h, 0:C], pc[0:C, j * 128:j * 128 + C], c['maskC'][0:C, 0:C], ALU.mult,
                            R=[pc.b, c['maskC'].b], W=[Lc0.b])
                yield
        if self.sub.get('rstop', 99) <= 6:
            return
        yield
        pl = self.bb()
        for h in range(NH):
            self.mm(pl[0:C, h * 64:(h + 1) * 64], MKt[0:C, h, 0:C], Vt[0:C, h * 64:(h + 1) * 64], R=[MKt.b, Vt.b], W=[pl.b])
        self.cp('act', Xb[0:C, :, 64:128], pl[0:C, :].rearrange("p (h d) -> p h d", d=64), R=[pl.b], W=[Xb.b])
        if self.sub.get('rstop', 99) <= 7:
            return
        yield
        nlev = int(round(math.log2(C)))
        Lc, Mc = W['Lc'], W['Mc']
        for lev in range(nlev):
            Lcur = Lc[lev % 2]
            Lnx = Lc[(lev + 1) % 2]
            Mnx = Mc[(lev + 1) % 2]
            def Mcur(h):
                return MLt[0:C, h, 0:C] if lev == 0 else Mc[lev % 2][0:C, h, 0:C]
            Mb = MLt.b if lev == 0 else Mc[lev % 2].b
            for half in range(2):
                yield
                px = self.bb()
                for hh in range(4):
                    h = 4 * half + hh
                    self.mm(px[0:C, hh * 128:hh * 128 + 128], Mcur(h), Xb[0:C, h, :], R=[Mb, Xb.b], W=[px.b])
                if lev < nlev - 1:
                    pm_, pl_ = self.bb(), self.bb()
                    for hh in range(4):
                        h = 4 * half + hh
                        self.mm(pm_[0:C, hh * 128:hh * 128 + C], Lcur[0:C, h, 0:C], Mcur(h), R=[Lcur.b, Mb], W=[pm_.b])
                        self.mm(pl_[0:C, hh * 128:hh * 128 + C], Mcur(h), Lcur[0:C, h, 0:C], R=[Lcur.b, Mb], W=[pl_.b])
                self.tt('dve', Xb[0:C, 4 * half:4 * half + 4, :], Xb[0:C, 4 * half:4 * half + 4, :],
                        px[0:C, :].rearrange("p (h d) -> p h d", d=128), ALU.add, R=[px.b, Xb.b], W=[Xb.b])
                if lev < nlev - 1:
                    self.cp('act', Mnx[0:C, 4 * half:4 * half + 4, 0:C],
                            pm_[0:C, :].rearrange("p (h d) -> p h d", d=128)[:, :, 0:C], R=[pm_.b], W=[Mnx.b])
                    self.cp('act', Lnx[0:C, 4 * half:4 * half + 4, 0:C],
                            pl_[0:C, :].rearrange("p (h d) -> p h d", d=128)[:, :, 0:C], R=[pl_.b], W=[Lnx.b])
        if self.sub.get('rstop', 99) <= 8:
            return
        yield
        GT, Hs, RAT, Sf, Sb = W['GT'], W['Hs'], W['RAT'], W['Sf'], W['Sb']
        gam = ho['gam']
        for par in range(2):
            pb_ = 64 * par
            pg, ph, pr = self.bb(), self.bb(), self.bb()
            for cb in range(4):
                h = 2 * cb + par
                self.mm(pg[pb_:pb_ + 64, cb * 64:(cb + 1) * 64], Xb[0:C, h, 0:64], Bgt[0:C, h * 64:(h + 1) * 64], R=[Xb.b, Bgt.b], W=[pg.b])
                self.mm(ph[pb_:pb_ + 64, cb * 64:(cb + 1) * 64], Bgt[0:C, h * 64:(h + 1) * 64], Xb[0:C, h, 64:128], start=True, stop=False,
                        R=[Xb.b, Bgt.b], W=[ph.b])
                self.mm(ph[pb_:pb_ + 64, cb * 64:(cb + 1) * 64], Kgt[0:C, h * 64:(h + 1) * 64], Vt[0:C, h * 64:(h + 1) * 64], start=False, stop=True,
                        R=[Kgt.b, Vt.b], W=[ph.b])
                self.mm(pr[pb_:pb_ + 64, cb * 128:cb * 128 + C], Xb[0:C, h, 0:64], MLt[0:C, h, 128:128 + C], R=[Xb.b, MLt.b], W=[pr.b])
            for cb in range(4):
                gC = gam[pb_:pb_ + 64, cb, ch * C + C - 1:ch * C + C]
                self.stt(GT[pb_:pb_ + 64, cb, :], c['ipair'][pb_:pb_ + 64, :], gC, pg[pb_:pb_ + 64, cb * 64:(cb + 1) * 64], ALU.mult, ALU.add,
                         R=[c['ipair'].b, gam.b, pg.b], W=[GT.b])
            self.cp('act', Hs[pb_:pb_ + 64, :, :], ph[pb_:pb_ + 64, 0:256].rearrange("p (a d) -> p a d", d=64), R=[ph.b], W=[Hs.b])
            self.tt('dve', RAT[pb_:pb_ + 64, :, 0:C], pr[pb_:pb_ + 64, :].rearrange("p (a d) -> p a d", d=128)[:, :, 0:C],
                    AR[pb_:pb_ + 64, :, ch, 1, 0:C], ALU.add, R=[pr.b, AR.b], W=[RAT.b])
        if self.sub.get('rstop', 99) <= 9:
            return
        yield
        ysb, ysq = W['ysb'], W['ysq']
        s10 = self.sub.get('s10', 3)
        if s10 & 1:
            for par in range(2):
                pb_ = 64 * par
                py = self.bb()
                py2 = self.bb() if C != 128 else None
                for cb in range(4):
                    h = 2 * cb + par
                    o = cb * 64
                    self.mm(py[0:C, o:o + 64], MLt[0:C, h, 128:128 + C], Xb[0:C, h, 64:128], start=True, stop=False, R=[MLt.b, Xb.b], W=[py.b])
                    if C == 128:
                        self.mm(py[0:C, o:o + 64], MKt[0:C, h, 128:128 + C], Vt[0:C, h * 64:(h + 1) * 64], start=False, stop=False, R=[MKt.b, Vt.b], W=[py.b])
                        self.mm(py[0:C, o:o + 64], RAT[pb_:pb_ + 64, cb, 0:C], Sb[pb_:pb_ + 64, cb, :], start=False, stop=True, R=[RAT.b, Sb.b], W=[py.b])
                    else:
                        self.mm(py[0:C, o:o + 64], MKt[0:C, h, 128:128 + C], Vt[0:C, h * 64:(h + 1) * 64], start=False, stop=True, R=[MKt.b, Vt.b], W=[py.b])
                        self.mm(py2[0:C, o:o + 64], RAT[pb_:pb_ + 64, cb, 0:C], Sb[pb_:pb_ + 64, cb, :], start=True, stop=True, R=[RAT.b, Sb.b], W=[py2.b])
                if C != 128:
                    self.cp('act', ysq[0:C, 0:4, :], py2[0:C, 0:256].rearrange("p (a d) -> p a d", d=64), R=[py2.b], W=[ysq.b])
                    self.tt('dve', ysb[0:C, par:8:2, :], py[0:C, 0:256].rearrange("p (a d) -> p a d", d=64), ysq[0:C, 0:4, :], ALU.add,
                            R=[py.b, ysq.b], W=[ysb.b])
                    continue
                if s10 & 4:
                    self.cp('dve', ysb[0:C, par:8:2, :], py[0:C, 0:256].rearrange("p (a d) -> p a d", d=64), R=[py.b], W=[ysb.b])
                elif s10 & 8:
                    pass
                else:
                    self.cp('act', ysb[0:C, par:8:2, :], py[0:C, 0:256].rearrange("p (a d) -> p a d", d=64), R=[py.b], W=[ysb.b])
        if s10 & 2:
            pSs = [self.bb(), self.bb()]
            for par in range(2):
                pb_ = 64 * par
                pS = pSs[par]
                for cb in range(4):
                    self.mm(pS[pb_:pb_ + 64, cb * 64:(cb + 1) * 64], GT[pb_:pb_ + 64, cb, :], Sb[pb_:pb_ + 64, cb, :], R=[GT.b, Sb.b], W=[pS.b])
            for par in range(2):
                pb_ = 64 * par
                pS = pSs[par]
                self.tt('dve', Sf[pb_:pb_ + 64, :, :], pS[pb_:pb_ + 64, 0:256].rearrange("p (a d) -> p a d", d=64), Hs[pb_:pb_ + 64, :, :], ALU.add,
                        R=[pS.b, Hs.b], W=[Sf.b])
            self.cp('pool', Sb[:], Sf[:], R=[Sf.b], W=[Sb.b])
        if self.sub.get('rstop', 99) <= 10:
            return
        yield
        s8 = W['st8']
        self.red(s8[0][0:C, :], ysb[0:C, :, :], R=[ysb.b], W=[s8[0].b])
        self.act(ysq[0:C, :, :], ysb[0:C, :, :], AF.Square, R=[ysb.b], W=[ysq.b])
        self.red(s8[1][0:C, :], ysq[0:C, :, :], R=[ysq.b], W=[s8[1].b])
        self.ts('dve', s8[0][0:C, :], s8[0][0:C, :], 1.0 / 64, None, ALU.mult, R=[s8[0].b], W=[s8[0].b])
        self.tt('dve', s8[2][0:C, :], s8[0][0:C, :], s8[0][0:C, :], ALU.mult, R=[s8[0].b], W=[s8[2].b])
        self.stt(s8[1][0:C, :], s8[1][0:C, :], 1.0 / 64, s8[2][0:C, :], ALU.mult, ALU.subtract, R=[s8[1].b, s8[2].b], W=[s8[1].b])
        self.act(s8[1][0:C, :], s8[1][0:C, :], AF.Sqrt, bias=c['gneps'][0:C, :], scale=1.0, R=[s8[1].b, c['gneps'].b], W=[s8[1].b])
        self.recip(s8[1][0:C, :], s8[1][0:C, :], R=[s8[1].b], W=[s8[1].b])
        self.tt('dve', ysb[0:C, :, :], ysb[0:C, :, :], s8[0][0:C, :].unsqueeze(2).to_broadcast([C, 8, 64]), ALU.subtract, R=[ysb.b, s8[0].b], W=[ysb.b])
        yh = W['yh']
        self.tt('dve', yh[0:C, :].rearrange("p (h d) -> p h d", d=64), ysb[0:C, :, :], s8[1][0:C, :].unsqueeze(2).to_broadcast([C, 8, 64]), ALU.mult,
                R=[ysb.b, s8[1].b], W=[yh.b])
        pt = self.pT[1]
        for cb in range(4):
            self.tr(pt[:, cb * 128:cb * 128 + C], yh[0:C, cb * 128:(cb + 1) * 128], idb[0:C, 0:C], R=[yh.b, idb.b], W=[pt.b])
        yT1, yr = W['yT1'], W['yr'][self.nyr % 2]
        self.nyr += 1
        bonus, gT = ho['bonus'], ho['gT']
        for cb in range(4):
            self.ts('dve', yT1[:, cb, 0:C], pt[:, cb * 128:cb * 128 + C], c['gn_g'][:, cb:cb + 1], c['gn_b'][:, cb:cb + 1], ALU.mult, ALU.add,
                    R=[pt.b, c['gn_g'].b, c['gn_b'].b], W=[yT1.b])
        self.tt('dve', yT1[:, :, 0:C], yT1[:, :, 0:C], bonus[:, :, sl], ALU.add, R=[yT1.b, bonus.b], W=[yT1.b])
        self.tt('dve', yr[:, :, 0:C], yT1[:, :, 0:C], gT[:, :, sl], ALU.mult, R=[yT1.b, gT.b], W=[yr.b])
        t0 = tok0 + ch * C
        self.dma('sp', g['Ys'][0:512, t0:t0 + C].rearrange("(a p) t -> p a t", p=128), yr[:, :, 0:C], R=[yr.b])

    def phase2(self, st):
        c = self.c
        maxtot = max(g['tot'] for g in self.G)
        maxT = max(g['T'] for g in self.G)
        maxkb = max(g['nkb'] for g in self.G)
        Kt = [self.sb(st, (67, maxtot), BF16, name='Kt') for _ in range(2)]
        Qt = [self.sb(st, (67, maxT), BF16, name='Qt') for _ in range(2)]
        Vh = [self.sb(st, (128, maxkb, 128), BF16, name='Vh') for _ in range(2)]
        for v in Vh:
            self.memset('pool', v[:, :, 64:128], 1.0, W=[v.b])
        Pt = [self.sb(st, (128, 512), BF16, name='Pt') for _ in range(3)]
        rl = [self.sb(st, (64, 512), F32, name='rl') for _ in range(2)]
        yo = [self.sb(st, (64, 512), BF16, name='yo') for _ in range(2)]
        NPS = 4
        LA = 3
        pS = [self.pA[0], self.pA[1], self.pA[2], self.pA[3]]
        pO = [self.pA[4], self.pA[5]]
        Pt = Pt + [self.sb(st, (128, 512), BF16, name='Pt')]
        heads = [(g, h) for g in self.G for h in range(NH)]
        def load(idx):
            g, h = heads[idx]
            T, tot = g['T'], g['tot']
            kt, qt, vh = Kt[idx % 2], Qt[idx % 2], Vh[idx % 2]
            self.dma('sp', kt[0:64, 0:tot], g['Ks'][h, 0:64, :], W=[kt.b])
            self.dma('sp', kt[64:67, 0:tot], g['Ks'][h, 64:67, :], W=[kt.b])
            self.dma('sp', qt[0:64, 0:T], g['Qs'][h, 0:64, :], W=[qt.b])
            self.dma('sp', qt[64:67, 0:T], g['Qs'][h, 64:67, :], W=[qt.b])
            nfull = tot // 128
            if nfull > 0:
                self.dma('sp', vh[:, 0:nfull, 0:64], g['Vs'][h, :, 0:nfull, :], W=[vh.b])
            rem = tot - nfull * 128
            if rem:
                self.dma('sp', vh[0:rem, nfull, 0:64], g['Vs'][h, 0:rem, nfull, :], W=[vh.b])
        blocks = []
        nq = 0
        for idx, (g, h) in enumerate(heads):
            T, past, tot = g['T'], g['past'], g['tot']
            QT = min(512, T)
            for qi in range(T // QT):
                q0 = qi * QT
                qpos0 = past + q0
                nblk = (qpos0 + QT - 1) // 128 + 1
                for j in range(nblk):
                    k0 = j * 128
                    rows = min(128, tot - k0)
                    if k0 + rows - 1 <= qpos0:
                        c0, diag = 0, False
                    else:
                        c0, diag = k0 - qpos0, True
                    blocks.append(dict(idx=idx, g=g, h=h, q0=q0, QT=QT, j=j, k0=k0, rows=rows, c0=c0, diag=diag,
                                       first=(j == 0), last=(j == nblk - 1), nq=nq, newhead=(qi == 0 and j == 0)))
                nq += 1

        def emit_S(n):
            bl = blocks[n]
            g, h, idx = bl['g'], bl['h'], bl['idx']
            kt, qt = Kt[idx % 2], Qt[idx % 2]
            rows, c0, QT, q0, k0, j = bl['rows'], bl['c0'], bl['QT'], bl['q0'], bl['k0'], bl['j']
            ps_, pt_ = pS[n % NPS], Pt[n % NPS]
            self.mm(ps_[0:rows, c0:QT], kt[:, k0:k0 + rows], qt[:, q0 + c0:q0 + QT], start=True, stop=not bl['diag'],
                    R=[kt.b, qt.b], W=[ps_.b])
            if bl['diag']:
                self.mm(ps_[0:rows, c0:c0 + rows], c['ident_b'][0:rows, 0:rows], c['maskD_b'][0:rows, 0:rows], start=False, stop=True,
                        R=[c['ident_b'].b, c['maskD_b'].b], W=[ps_.b])
            self.act(pt_[0:rows, c0:QT], ps_[0:rows, c0:QT], AF.Exp, bias=g['neglc'][0:rows, j, h:h + 1], scale=1.0,
                     R=[ps_.b, g['neglc'].b], W=[pt_.b])

        def emit_PV(n):
            bl = blocks[n]
            g, h, idx = bl['g'], bl['h'], bl['idx']
            vh = Vh[idx % 2]
            rows, c0, QT, q0, j = bl['rows'], bl['c0'], bl['QT'], bl['q0'], bl['j']
            pt_ = Pt[n % NPS]
            po = pO[bl['nq'] % 2]
            self.mm(po[:, c0:QT], vh[0:rows, j, :], pt_[0:rows, c0:QT], start=bl['first'], stop=bl['last'],
                    R=[vh.b, pt_.b], W=[po.b])
            if bl['last']:
                rlt, yot = rl[bl['nq'] % 2], yo[bl['nq'] % 2]
                self.recip(rlt[:, 0:QT], po[64:128, 0:QT], R=[po.b], W=[rlt.b])
                self.tt('dve', yot[:, 0:QT], po[0:64, 0:QT], rlt[:, 0:QT], ALU.mult, R=[po.b, rlt.b], W=[yot.b])
                self.dma('sp', g['Ys'][512 + h * 64:512 + (h + 1) * 64, q0:q0 + QT], yot[:, 0:QT], R=[yot.b])

        load(0)
        if len(heads) > 1:
            load(1)
        nb_ = len(blocks)
        first_block = {}
        for n, bl in enumerate(blocks):
            first_block.setdefault(bl['idx'], n)
        load_at = {first_block[idx] + LA: idx + 1 for idx in range(1, len(heads) - 1)}
        for n in range(nb_ + LA):
            if n in load_at:
                load(load_at[n])
            if n < nb_:
                emit_S(n)
            if n - LA >= 0:
                emit_PV(n - LA)

    def phase3(self, st):
        I, c = self.I, self.c
        wout = self.sb(st, (128, 8, D), BF16, name='wout')
        wg, wu = self.wg, self.wu
        wd = self.sb(st, (128, NFB, D), BF16, name='wd')
        for (dst, src, kk_, ncol) in [(wout, I['w_out'], 8, D), (wd, I['w_d'], NFB, D)]:
            s3 = src.rearrange("(k p) c -> p k c", p=128)
            for k in range(kk_):
                for a in range(0, ncol, 1408 if ncol == DFF else 1024):
                    b = min(ncol, a + (1408 if ncol == DFF else 1024))
                    self.dma('pool', dst[:, k, a:b], s3[:, k, a:b], W=[dst.b])
        NT = 256
        xt = [self.sb(st, (128, 2, D), F32, name='x3')]
        yT = [self.sb(st, (128, 8, NT), BF16, name='yT')]
        xsb = self.sb(st, (128, 2, D), BF16, name='xsb3')
        h2 = self.sb(st, (128, 8, NT), BF16, name='h2')
        junk = self.sb(st, (128, D), BF16, name='junk3')
        ss = self.sb(st, (128, 4), F32)
        rstd = self.sb(st, (128, 4), F32)
        actT = self.sb(st, (128, NFB, NT), BF16, name='actT')
        sil = [self.sb(st, (128, NT), F32, name='sil') for _ in range(2)]
        yo = [self.sb(st, (128, D), F32, name='yo3')]
        gt1 = self.sb(st, (128, D), F32, name='gt1')
        gt2 = self.sb(st, (128, D), F32, name='gt2')
        n = 0
        hflag = self.sb(st, (1, 1), mybir.dt.int32, name='hflag')
        self.dma('sp', hflag[:], I['half'], W=[hflag.b])
        r_base = st.enter_context(self.nc.gpsimd.register("r_base"))
        r_off = st.enter_context(self.nc.gpsimd.register("r_off"))
        HALF = self.SEQ // 2
        def init_reg(e):
            e.reg_load(r_base, hflag[0:1, 0:1])
            e.reg_mul(r_base, r_base, HALF)
            return e.nop()
        self.P.op('pool', init_reg, [hflag.b], [])
        for g in self.G:
            T = g['T'] if g['gi'] == 1 else HALF
            xsrc = g['x'] if g['gi'] == 1 else I['xp3']
            self.build_gates(g, gt1, gt2, sil)
            ntile = (T + NT - 1) // NT
            for ti in range(ntile):
                tok0 = ti * NT
                nt = min(NT, T - tok0)
                nb = (nt + 127) // 128
                bs = min(128, nt)
                x = xt[n % len(xt)]
                y_ = yT[n % len(yT)]
                n += 1
                self.dma('sp', x[0:bs, 0:nb, :], xsrc[tok0:tok0 + nt, :].rearrange("(b p) d -> p b d", p=bs), W=[x.b])
                if g['gi'] == 1:
                    self.dma('sp', y_[:, :, 0:nt], g['Ys'][:, tok0:tok0 + nt].rearrange("(k p) t -> p k t", p=128), W=[y_.b])
                else:
                    ys = g['Ys']
                    SEQ_ = self.SEQ
                    def dyn(e, y_=y_, tok0=tok0, nt=nt, ys=ys, SEQ_=SEQ_):
                        e.reg_add(r_off, r_base, tok0)
                        src = bass.AP(ys.tensor, r_off, [[SEQ_, 128], [128 * SEQ_, 8], [1, nt]])
                        return e.dma_start(out=y_[:, :, 0:nt], in_=src)
                    self.P.dma_fn('pool', dyn, (), [y_.b])
                tmpm = yo[0]
                for blk in range(nb):
                    for half in range(2):
                        pp = self.pA[half]
                        for k in range(8):
                            self.mm(pp[0:bs, :], y_[:, k, blk * 128:blk * 128 + bs], wout[:, k, half * 512:(half + 1) * 512],
                                    start=(k == 0), stop=(k == 7), R=[y_.b, wout.b], W=[pp.b])
                        self.tt('dve', tmpm[0:bs, half * 512:(half + 1) * 512], pp[0:bs, :], gt1[0:bs, half * 512:(half + 1) * 512], ALU.mult,
                                R=[pp.b, gt1.b], W=[tmpm.b])
                    self.tt('pool', x[0:bs, blk, :], tmpm[0:bs, :], x[0:bs, blk, :], ALU.add, R=[tmpm.b, x.b], W=[x.b])
                x1 = x
                self.norm_transpose(x1, nb, bs, g['gm2'], g['sh2'], xsb, h2, junk, ss, rstd, 0, alt=True)
                for fb in range(NFB):
                    pg, pu = self.pA[2 + (fb % 2)], self.pA[4 + (fb % 2)]
                    for k in range(8):
                        self.mm(pg[:, 0:nt], wg[:, k, fb * 128:(fb + 1) * 128], h2[:, k, 0:nt], start=(k == 0), stop=(k == 7), R=[wg.b, h2.b], W=[pg.b])
                    for k in range(8):
                        self.mm(pu[:, 0:nt], wu[:, k, fb * 128:(fb + 1) * 128], h2[:, k, 0:nt], start=(k == 0), stop=(k == 7), R=[wu.b, h2.b], W=[pu.b])
                    s_ = sil[fb % 2]
                    self.act(s_[:, 0:nt], pg[:, 0:nt], AF.Silu, R=[pg.b], W=[s_.b])
                    self.tt('dve', actT[:, fb, 0:nt], s_[:, 0:nt], pu[:, 0:nt], ALU.mult, R=[s_.b, pu.b], W=[actT.b])
                for blk in range(nb):
                    yo_ = yo[0]
                    for half in range(2):
                        pp = self.pA[half]
                        for fb in range(NFB):
                            self.mm(pp[0:bs, :], actT[:, fb, blk * 128:blk * 128 + bs], wd[:, fb, half * 512:(half + 1) * 512],
                                    start=(fb == 0), stop=(fb == NFB - 1), R=[actT.b, wd.b], W=[pp.b])
                        self.tt('dve', yo_[0:bs, half * 512:(half + 1) * 512], pp[0:bs, :], gt2[0:bs, half * 512:(half + 1) * 512], ALU.mult,
                                R=[pp.b, gt2.b], W=[yo_.b])
                    self.tt('pool', yo_[0:bs, :], yo_[0:bs, :], x1[0:bs, blk, :], ALU.add, R=[yo_.b, x1.b], W=[yo_.b])
                    self.dma('sp', g['o_y'][tok0 + blk * 128:tok0 + blk * 128 + bs, :], yo_[0:bs, :], R=[yo_.b])


def _consts():
    i = np.arange(128)
    s, t = i[:, None], i[None, :]
    cst = {}
    cst['c_ident'] = np.eye(128, dtype=np.float32)
    cst['c_triu'] = (s <= t).astype(np.float32)
    cst['c_ones'] = np.ones((128, 128), np.float32)
    cst['c_blk'] = ((s // 64) == (t // 64)).astype(np.float32)
    cst['c_maskA'] = np.concatenate([-(s < t).astype(np.float32), (s <= t).astype(np.float32)], axis=1)
    cst['c_maskC'] = -(s > t).astype(np.float32)
    cst['c_maskD'] = np.where(s <= t, 0.0, NEG).astype(np.float32)
    r = np.ones((128, 1024), np.float32)
    r[:, ::128] = 0.0
    cst['c_reset'] = r
    cst['c_ipair'] = ((i[:, None] % 64) == np.arange(64)[None, :]).astype(np.float32)
    return cst


def _pk(v, nblk):
    return np.ascontiguousarray(np.asarray(v, np.float32).reshape(nblk, 128).T)


_NC_CACHE = {}


def _get_nc(SEQ, PAST, NS, debug=False, phases=(1, 2, 3), sub=None):
    key = (SEQ, PAST, NS, debug, tuple(phases), str(sub))
    if key not in _NC_CACHE:
        _NC_CACHE[key] = KB(SEQ, PAST, NS, debug, phases, sub).build()
    return _NC_CACHE[key]


def make_in_maps(inp, n_cores=8):
    f = lambda a: np.ascontiguousarray(np.asarray(a, dtype=np.float32))
    xp, xs = f(inp['x_prompt']), f(inp['x_sample'])
    BP = xp.shape[0]
    cst = _consts()
    shared = dict(cst)
    L = 0
    shared['w_ada'] = f(inp['w_ada'][L]); shared['b_ada'] = _pk(inp['b_ada'][L], 48)
    shared['w_in'] = f(inp['w_in'][L]); shared['w_out'] = f(inp['w_out'][L])
    shared['w_g'] = f(inp['w_ffn_gate'][L]); shared['w_u'] = f(inp['w_ffn_up'][L]); shared['w_d'] = f(inp['w_ffn_down'][L])
    shared['n1g'] = _pk(inp['norm1_g'][L], 8); shared['n2g'] = _pk(inp['norm2_g'][L], 8)
    shared['mu'] = _pk(inp['shift_mu'][L], 14)
    for nm, src in [('w0', 'w0'), ('a0', 'a0'), ('k_k', 'k_k'), ('k_a', 'k_a'), ('gn_g', 'gn_g'), ('gn_b', 'gn_b')]:
        shared[nm] = _pk(inp[src][L], 4)
    shared['r_k'] = _pk(np.asarray(inp['r_k'][L]).reshape(512), 4)
    shared['w_dec'] = f(inp['w_decay_up'][L]); shared['w_aaa'] = f(inp['w_aaa_up'][L]); shared['w_gup'] = f(inp['w_gate_up'][L])
    gq = np.tile(np.asarray(inp['fox_q_g'][L], np.float32), 8)
    gk = np.tile(np.asarray(inp['fox_k_g'][L], np.float32), 8)
    shared['gqk'] = np.ascontiguousarray(np.broadcast_to(np.concatenate([gq, gk])[None, :], (128, 1024)))
    shared['f_b'] = np.ascontiguousarray(np.broadcast_to(np.asarray(inp['fox_f_b'][L], np.float32)[None, :], (128, 8)))
    maps = []
    for cidx in range(n_cores):
        b = cidx % BP
        m = dict(shared)
        m['xp'] = xp[b]
        hf = cidx // BP
        H2 = xp.shape[1] // 2
        m['half'] = np.array([[hf]], np.int32)
        m['xp3'] = np.ascontiguousarray(xp[b, hf * H2:(hf + 1) * H2])
        m['xs'] = xs[cidx]
        cv = np.stack([np.asarray(inp['c_prompt'][b], np.float32), np.asarray(inp['c_sample'][cidx], np.float32)], axis=-1)
        m['cvec'] = np.ascontiguousarray(cv.reshape(8, 128, 2).transpose(1, 0, 2))
        m['ck'] = f(inp['cache_fox_k'][L, cidx]).reshape(-1, 512)
        m['cv'] = f(inp['cache_fox_v'][L, cidx]).reshape(-1, 512)
        m['clf'] = f(inp['cache_fox_logf'][L, cidx])
        m['st0'] = np.ascontiguousarray(f(inp['state_rwkv'][L, cidx]).transpose(1, 0, 2))
        m['shp0'] = _pk(inp['state_rwkv_shift'][L, cidx, 0], 14)
        maps.append(m)
    return maps


def assemble(res, BP, SEQ, NSEQ, NS):
    r = res
    def upk(a):
        return np.ascontiguousarray(a.T).reshape(-1)
    y_p = np.stack([np.concatenate([r[b]['y_p'], r[b + BP]['y_p']], axis=0) for b in range(BP)])
    y_s = np.stack([r[c]['y_s'] for c in range(NSEQ)])
    st_p = np.stack([r[b]['st_p'] for b in range(BP)])[None]
    sh_p = np.stack([upk(r[b]['sh_p'])[None, :] for b in range(BP)])[None]
    k_p = np.stack([r[b]['k_p'].reshape(SEQ, 8, 64) for b in range(BP)])[None]
    v_p = np.stack([r[b]['v_p'].reshape(SEQ, 8, 64) for b in range(BP)])[None]
    lf_p = np.stack([r[b]['lf_p'] for b in range(BP)])[None]
    st_s = np.stack([r[c]['st_s'] for c in range(NSEQ)])[None]
    sh_s = np.stack([upk(r[c]['sh_s'])[None, :] for c in range(NSEQ)])[None]
    k_s = np.stack([r[c]['k_s'].reshape(NS, 8, 64) for c in range(NSEQ)])[None]
    v_s = np.stack([r[c]['v_s'].reshape(NS, 8, 64) for c in range(NSEQ)])[None]
    lf_s = np.stack([r[c]['lf_s'] for c in range(NSEQ)])[None]
    outs = (y_p, y_s, st_p, sh_p, k_p, v_p, lf_p, st_s, sh_s, k_s, v_s, lf_s)
    return tuple(np.ascontiguousarray(o, dtype=np.float32) for o in outs)


def kernel(**inputs):
    xp = np.asarray(inputs['x_prompt'])
    xs = np.asarray(inputs['x_sample'])
    BP, SEQ, _ = xp.shape
    NSEQ, NS, _ = xs.shape
    PAST = np.asarray(inputs['cache_fox_k']).shape[2]
    nc = _get_nc(SEQ, PAST, NS)
    maps = make_in_maps(inputs, 8)
    res = run_bass_kernel_spmd(nc, maps, core_ids=list(range(8)))
    return assemble(res.results, BP, SEQ, NSEQ, NS)
```

```python
import contextlib
import math
import numpy as np
import concourse.bass as bass
import concourse.mybir as mybir
from concourse.bass_utils import run_bass_kernel_spmd

F32 = mybir.dt.float32
BF16 = mybir.dt.bfloat16
AF = mybir.ActivationFunctionType
ALU = mybir.AluOpType
AX = mybir.AxisListType

ENGS = ('pe', 'dve', 'act', 'pool', 'sp')

D = 1024
HD = 64
NH = 8
RW = 512
RCOLS = 1792
INC = 3336
DFF = 2816
NFB = DFF // 128
EPS = 1e-6
GN_EPS = 64e-5
C0 = math.exp(-0.5)
NEG = -30000.0


class Buf:
    __slots__ = ('last_write', 'readers')

    def __init__(self):
        self.last_write = None
        self.readers = []


class Prog:
    def __init__(self, nc, st, n_dma_sems=10):
        self.nc = nc
        self.q = {e: [] for e in ENGS}
        self.count = {e: 0 for e in ENGS}
        self.seen = {e: {} for e in ENGS}
        self.n_dma_sems = n_dma_sems
        self.dma_next = {e: 0 for e in ENGS}
        self.dma_val = {}
        names = list(ENGS)
        for e in ('sp', 'pool', 'act'):
            for i in range(n_dma_sems):
                k = f'd_{e}_{i}'
                names.append(k)
                self.dma_val[k] = 0
        self.sems = {n: st.enter_context(nc.semaphore(n)) for n in names}
        self.n_ops = 0
        self.noself = ()
        self.wswap = False

    def _deps(self, q, reads, writes, extra=()):
        need = {}

        def add(tok):
            if tok is None:
                return
            k, v = tok
            if need.get(k, 0) < v:
                need[k] = v
        for b in reads:
            add(b.last_write)
        for b in writes:
            add(b.last_write)
            for r in b.readers:
                add(r)
        for t in extra:
            add(t)
        waits = []
        for k, v in need.items():
            if k == q and (q == 'pe' or q in self.noself):
                continue
            if self.seen[q].get(k, 0) >= v:
                continue
            self.seen[q][k] = v
            waits.append((k, v))
        waits.sort(key=lambda kv: kv[0] == q)
        return waits

    def _mark(self, tok, reads, writes):
        for b in reads:
            if len(b.readers) > 6:
                m = {}
                for k, v in b.readers:
                    if m.get(k, 0) < v:
                        m[k] = v
                b.readers = list(m.items())
            b.readers.append(tok)
        for b in writes:
            b.last_write = tok
            b.readers = []

    def op(self, q, fn, reads=(), writes=()):
        waits = self._deps(q, reads, writes)
        self.count[q] += 1
        tok = (q, self.count[q])
        self.q[q].append((waits, fn, (q, 1)))
        self._mark(tok, reads, writes)
        self.n_ops += 1
        return tok

    def dma(self, q, out, in_, reads=(), writes=(), **kw):
        i = self.dma_next[q]
        self.dma_next[q] = (i + 1) % self.n_dma_sems
        k = f'd_{q}_{i}'
        prev = self.dma_val[k]
        ex = [(k, prev)] if prev > 0 else []
        waits = self._deps(q, reads, writes, ex)
        self.dma_val[k] = prev + 16
        tok = (k, prev + 16)
        self.q[q].append((waits, lambda e: e.dma_start(out=out, in_=in_, **kw), (k, 16)))
        self._mark(tok, reads, writes)
        self.n_ops += 1
        return tok

    def dma_fn(self, q, fn, reads=(), writes=()):
        i = self.dma_next[q]
        self.dma_next[q] = (i + 1) % self.n_dma_sems
        k = f'd_{q}_{i}'
        prev = self.dma_val[k]
        ex = [(k, prev)] if prev > 0 else []
        waits = self._deps(q, reads, writes, ex)
        self.dma_val[k] = prev + 16
        tok = (k, prev + 16)
        self.q[q].append((waits, fn, (k, 16)))
        self._mark(tok, reads, writes)
        self.n_ops += 1
        return tok

    def wait_all_dma(self, q):
        toks = [(k, v) for k, v in self.dma_val.items() if v > 0]
        waits = self._deps(q, (), (), toks)
        self.q[q].append((waits, None, None))

    def emit(self):
        nc = self.nc
        sems = self.sems
        with nc.Block() as block:
            handles = {'pe': block.tensor, 'dve': block.vector, 'act': block.scalar,
                       'pool': block.gpsimd, 'sp': block.sync}
            for e in ENGS:
                ops = self.q[e]
                if not ops:
                    continue

                def body(eng, ops=ops):
                    for waits, fn, inc in ops:
                        if self.wswap:
                            waits = list(reversed(waits))
                        for k, v in waits:
                            eng.wait_ge(sems[k], v)
                        if fn is not None:
                            ins = fn(eng)
                            if inc is not None:
                                ins.then_inc(sems[inc[0]], inc[1])
                handles[e](body)
        self.q = {e: [] for e in ENGS}


class TB:
    def __init__(self, t, n=1):
        self.t = t
        self.bs = [Buf() for _ in range(n)]

    @property
    def b(self):
        return self.bs[0]

    def __getitem__(self, k):
        return self.t[k]


class KB:
    def __init__(self, SEQ, PAST, NS, debug=False, phases=(1, 2, 3), sub=None):
        self.SEQ, self.PAST, self.NS, self.debug = SEQ, PAST, NS, debug
        self.phases = phases
        self.sub = sub or {}
        self.nc = bass.Bass("TRN2", target_bir_lowering=False)
        self.uid = 0

    def bb(self):
        self._bb = (getattr(self, '_bb', -1) + 1) % 4
        return self.pA[2 + self._bb]

    def mm(self, out, lhsT, rhs, start=True, stop=True, R=(), W=()):
        return self.P.op('pe', lambda e: e.matmul(out, lhsT, rhs, start=start, stop=stop), R, W)

    def tr(self, out, in_, ident, R=(), W=()):
        return self.P.op('pe', lambda e: e.transpose(out, in_, ident), R, W)

    def act(self, out, in_, func, bias=None, scale=None, accum=None, R=(), W=()):
        kw = {}
        if bias is not None:
            kw['bias'] = bias
        if scale is not None:
            kw['scale'] = scale
        if accum is not None:
            kw['accum_out'] = accum
        return self.P.op('act', lambda e: e.activation(out, in_, func, **kw), R, W)

    def ts(self, eng, out, in0, s1, s2=None, op0=ALU.mult, op1=None, R=(), W=()):
        if op1 is None:
            return self.P.op(eng, lambda e: e.tensor_scalar(out, in0, s1, None, op0), R, W)
        return self.P.op(eng, lambda e: e.tensor_scalar(out, in0, s1, s2, op0, op1), R, W)

    def tt(self, eng, out, in0, in1, op, R=(), W=()):
        return self.P.op(eng, lambda e: e.tensor_tensor(out, in0, in1, op=op), R, W)

    def stt(self, out, in0, scalar, in1, op0, op1, R=(), W=()):
        return self.P.op('dve', lambda e: e.scalar_tensor_tensor(out, in0, scalar, in1, op0, op1), R, W)

    def cp(self, eng, out, in_, R=(), W=()):
        if eng == 'act':
            return self.P.op('act', lambda e: e.activation(out, in_, AF.Copy), R, W)
        return self.P.op(eng, lambda e: e.tensor_copy(out, in_), R, W)

    def red(self, out, in_, op=ALU.add, R=(), W=()):
        return self.P.op('dve', lambda e: e.tensor_reduce(out, in_, AX.X, op), R, W)

    def recip(self, out, in_, R=(), W=()):
        return self.P.op('dve', lambda e: e.reciprocal(out, in_), R, W)

    def memset(self, eng, ap, val, W=()):
        return self.P.op(eng, lambda e: e.memset(ap, val), (), W)

    def dma(self, q, out, in_, R=(), W=(), **kw):
        return self.P.dma(q, out, in_, R, W, **kw)

    def sb(self, st, shape, dt, n=1, name=None):
        self.uid += 1
        t = st.enter_context(self.nc.sbuf_tensor(f"{name or 's'}_{self.uid}", list(shape), dt))
        return TB(t, n)

    def ps(self, st, shape, dt, n=1, name=None):
        self.uid += 1
        t = st.enter_context(self.nc.psum_tensor(f"{name or 'p'}_{self.uid}", list(shape), dt))
        return TB(t, n)

    def din(self, name, shape, dt=F32):
        return self.nc.dram_tensor(name, list(shape), dt, kind="ExternalInput").ap()

    def dout(self, name, shape, dt=F32):
        return self.nc.dram_tensor(name, list(shape), dt, kind="ExternalOutput").ap()

    def dscr(self, name, shape, dt=BF16):
        kind = "ExternalOutput" if self.debug else "Internal"
        return self.nc.dram_tensor(name, list(shape), dt, kind=kind).ap()

    def build(self):
        nc = self.nc
        SEQ, PAST, NS = self.SEQ, self.PAST, self.NS
        I = {}
        self.I = I
        I['half'] = self.din('half', (1, 1), mybir.dt.int32)
        I['xp3'] = self.din('xp3', (SEQ // 2, D))
        for nm, shp in [('xp', (SEQ, D)), ('xs', (NS, D)), ('cvec', (128, 8, 2)),
                        ('ck', (PAST, 512)), ('cv', (PAST, 512)), ('clf', (PAST, 8)),
                        ('st0', (64, 8, 64)), ('shp0', (128, 14)),
                        ('w_ada', (D, 6 * D)), ('b_ada', (128, 48)), ('w_in', (D, INC)),
                        ('w_out', (D, D)), ('w_g', (D, DFF)), ('w_u', (D, DFF)), ('w_d', (DFF, D)),
                        ('n1g', (128, 8)), ('n2g', (128, 8)), ('mu', (128, 14)),
                        ('w0', (128, 4)), ('a0', (128, 4)), ('k_k', (128, 4)), ('k_a', (128, 4)),
                        ('r_k', (128, 4)), ('gn_g', (128, 4)), ('gn_b', (128, 4)),
                        ('w_dec', (64, 512)), ('w_aaa', (64, 512)), ('w_gup', (128, 512)),
                        ('gqk', (128, 1024)), ('f_b', (128, 8)),
                        ('c_ident', (128, 128)), ('c_triu', (128, 128)), ('c_ones', (128, 128)),
                        ('c_blk', (128, 128)), ('c_maskA', (128, 256)), ('c_maskC', (128, 128)),
                        ('c_maskD', (128, 128)), ('c_reset', (128, 1024)), ('c_ipair', (128, 64))]:
            I[nm] = self.din(nm, shp)
        self.I = I
        O = {}
        for nm, shp in [('y_p', (SEQ // 2, D)), ('y_s', (NS, D)), ('st_p', (8, 64, 64)), ('sh_p', (128, 14)),
                        ('k_p', (SEQ, 512)), ('v_p', (SEQ, 512)), ('lf_p', (SEQ, 8)),
                        ('st_s', (8, 64, 64)), ('sh_s', (128, 14)),
                        ('k_s', (NS, 512)), ('v_s', (NS, 512)), ('lf_s', (NS, 8))]:
            O[nm] = self.dout(nm, shp)
        self.O = O
        self.G = []
        for gi, (T, past) in enumerate([(SEQ, 0), (NS, PAST)]):
            tot = past + T
            nkb = (tot + 127) // 128
            g = dict(gi=gi, T=T, past=past, tot=tot, nkb=nkb,
                     Qs=self.dscr(f'Qs{gi}', (8, 67, T)), Ks=self.dscr(f'Ks{gi}', (8, 67, tot)),
                     Vs=self.dscr(f'Vs{gi}', (8, 128, nkb, 64)), Ys=self.dscr(f'Ys{gi}', (D, T)),
                     x=I['xp'] if gi == 0 else I['xs'])
            g['o_y'], g['o_st'], g['o_sh'], g['o_k'], g['o_v'], g['o_lf'] = (
                (O['y_p'], O['st_p'], O['sh_p'], O['k_p'], O['v_p'], O['lf_p']) if gi == 0 else
                (O['y_s'], O['st_s'], O['sh_s'], O['k_s'], O['v_s'], O['lf_s']))
            self.G.append(g)

        with contextlib.ExitStack() as st:
            self.P = Prog(nc, st)
            self.P.noself = tuple(self.sub.get('noself', ()))
            self.persistent(st)
            ph = self.phases
            with contextlib.ExitStack() as sw1:
                if 1 in ph:
                    self.load_win(sw1)
                with contextlib.ExitStack() as s0:
                    self.phase0(s0)
                    self.P.wait_all_dma('sp')
                    self.P.emit()
                if 1 in ph:
                    with contextlib.ExitStack() as s1:
                        self.phase1(s1)
                        self.P.wait_all_dma('sp')
                        self.P.emit()
            with contextlib.ExitStack() as sw3:
                if 3 in ph:
                    self.load_wgu(sw3)
                if 2 in ph:
                    with contextlib.ExitStack() as s2:
                        self.phase2(s2)
                        self.P.wait_all_dma('sp')
                        self.P.emit()
                with contextlib.ExitStack() as s3:
                    if 3 in ph:
                        self.phase3(s3)
                    self.P.wait_all_dma('sp')
                    self.P.emit()
        return nc

    def persistent(self, st):
        I = self.I
        c = {}
        def ld(nm, shape, dt=F32, q='sp'):
            t = self.sb(st, shape, dt, name=nm)
            self.dma('pool' if dt == BF16 else q, t[:], I[nm], W=[t.b])
            return t
        c['ident_f'] = ld('c_ident', (128, 128))
        c['triu_f'] = ld('c_triu', (128, 128))
        c['ones_f'] = ld('c_ones', (128, 128))
        c['maskA'] = ld('c_maskA', (128, 256))
        c['maskC'] = ld('c_maskC', (128, 128))
        c['reset'] = ld('c_reset', (128, 1024))
        c['ipair'] = ld('c_ipair', (128, 64))
        c['ident_b'] = self.sb(st, (128, 128), BF16, name='identb')
        self.dma('pool', c['ident_b'][:], I['c_ident'], W=[c['ident_b'].b])
        c['blk_b'] = self.sb(st, (128, 128), BF16, name='blkb')
        self.dma('pool', c['blk_b'][:], I['c_blk'], W=[c['blk_b'].b])
        c['maskD_b'] = self.sb(st, (128, 128), BF16, name='maskDb')
        self.dma('pool', c['maskD_b'][:], I['c_maskD'], W=[c['maskD_b'].b])
        for nm in ['n1g', 'n2g']:
            c[nm] = ld(nm, (128, 8))
        c['mu'] = ld('mu', (128, 14))
        for nm in ['w0', 'a0', 'k_k', 'k_a', 'r_k', 'gn_g', 'gn_b']:
            c[nm] = ld(nm, (128, 4))
        c['f_b'] = ld('f_b', (128, 8))
        c['eps'] = self.sb(st, (128, 1), F32, name='eps')
        self.memset('dve', c['eps'][:], EPS, W=[c['eps'].b])
        c['eps64'] = self.sb(st, (128, 1), F32, name='eps64')
        self.memset('dve', c['eps64'][:], 64 * EPS, W=[c['eps64'].b])
        c['gneps'] = self.sb(st, (128, 1), F32, name='gneps')
        self.memset('dve', c['gneps'][:], GN_EPS, W=[c['gneps'].b])
        c['omu'] = self.sb(st, (128, 14), F32, name='omu')
        self.ts('dve', c['omu'][:], c['mu'][:], -1.0, 1.0, ALU.mult, ALU.add, R=[c['mu'].b], W=[c['omu'].b])
        c['omka'] = self.sb(st, (128, 4), F32, name='omka')
        self.ts('dve', c['omka'][:], c['k_a'][:], -1.0, 1.0, ALU.mult, ALU.add, R=[c['k_a'].b], W=[c['omka'].b])
        c['mod'] = self.sb(st, (128, 48, 2), F32, name='mod')
        self.c = c
        for g in self.G:
            g['gm1'] = self.sb(st, (128, 8), F32, name='gm1')
            g['gm2'] = self.sb(st, (128, 8), F32, name='gm2')
            g['sh1'] = self.sb(st, (128, 8), F32, name='sh1')
            g['sh2'] = self.sb(st, (128, 8), F32, name='sh2')
            g['neglc'] = self.sb(st, (128, g['nkb'], 8), F32, name='neglc')
        self.pT = [self.ps(st, (128, 1024), BF16, name='pT') for _ in range(2)]
        self.pA = [self.ps(st, (128, 512), F32, name='pA') for _ in range(6)]

    def load_win(self, st):
        I = self.I
        win = self.sb(st, (128, 8, INC), BF16, name='win')
        wsrc = I['w_in'].rearrange("(k p) c -> p k c", p=128)
        for k in range(8):
            for (a, b) in [(0, 1792), (1792, INC)]:
                self.dma('pool', win[:, k, a:b], wsrc[:, k, a:b], W=[win.b])
        self.win = win

    def load_wgu(self, st):
        I = self.I
        self.wg = self.sb(st, (128, 8, DFF), BF16, name='wg')
        self.wu = self.sb(st, (128, 8, DFF), BF16, name='wu')
        for (dst, src) in [(self.wg, I['w_g']), (self.wu, I['w_u'])]:
            s3 = src.rearrange("(k p) c -> p k c", p=128)
            for k in range(8):
                for a in range(0, DFF, 1408):
                    self.dma('pool', dst[:, k, a:a + 1408], s3[:, k, a:a + 1408], W=[dst.b])

    def phase0(self, st):
        I, c = self.I, self.c
        cv = self.sb(st, (128, 8, 2), F32)
        self.dma('sp', cv[:], I['cvec'], W=[cv.b])
        cs = self.sb(st, (128, 8, 2), F32)
        self.act(cs[:], cv[:], AF.Silu, R=[cv.b], W=[cs.b])
        bada = self.sb(st, (128, 48), F32)
        self.dma('sp', bada[:], I['b_ada'], W=[bada.b])
        wa = [self.sb(st, (128, 8, 512), F32) for _ in range(2)]
        wsrc = I['w_ada'].rearrange("(k p) c -> p k c", p=128)
        pm = self.pA[0]
        for ch in range(12):
            w = wa[ch % 2]
            self.dma('sp', w[:], wsrc[:, :, ch * 512:(ch + 1) * 512], W=[w.b])
            for cbl in range(4):
                gb = ch * 4 + cbl
                for k in range(8):
                    self.mm(pm[:, gb * 2:gb * 2 + 2], w[:, k, cbl * 128:(cbl + 1) * 128], cs[:, k, :],
                            start=(k == 0), stop=(k == 7), R=[w.b, cs.b], W=[pm.b])
        mod = c['mod']
        self.tt('dve', mod[:], pm[:, 0:96].rearrange("p (a b) -> p a b", b=2),
                bada[:].unsqueeze(2).to_broadcast([128, 48, 2]), ALU.add, R=[pm.b, bada.b], W=[mod.b])
        tmp = self.sb(st, (128, 8), F32)
        for g in self.G:
            gi = g['gi']
            for (dst, blk0, ng) in [(g['gm1'], 8, c['n1g']), (g['gm2'], 32, c['n2g'])]:
                self.ts('dve', tmp[:], mod[:, blk0:blk0 + 8, gi], 1.0, None, ALU.add, R=[mod.b], W=[tmp.b])
                self.tt('dve', dst[:], tmp[:], ng[:], ALU.mult, R=[tmp.b, ng.b], W=[dst.b])
            self.cp('dve', g['sh1'][:], mod[:, 0:8, gi], R=[mod.b], W=[g['sh1'].b])
            self.cp('dve', g['sh2'][:], mod[:, 24:32, gi], R=[mod.b], W=[g['sh2'].b])

    def build_gates(self, g, gt1, gt2, tl):
        c = self.c
        mod = c['mod']
        gi = g['gi']
        n = 0
        for (dst, blk0) in [(gt1, 16), (gt2, 40)]:
            for half in range(2):
                pg = self.pA[4 + (n % 2)]
                n += 1
                for j in range(4):
                    blk = half * 4 + j
                    t = tl[blk % 2]
                    self.ts('dve', t[:, 0:128], c['ones_f'][:], mod[:, blk0 + blk, gi:gi + 1], None, ALU.mult,
                            R=[c['ones_f'].b, mod.b], W=[t.b])
                    self.mm(pg[:, j * 128:(j + 1) * 128], t[:, 0:128], c['ident_f'][:], R=[t.b, c['ident_f'].b], W=[pg.b])
                self.cp('act', dst[:, half * 512:(half + 1) * 512], pg[:], R=[pg.b], W=[dst.b])

    def norm_transpose(self, xt, nb, bs, gm, sh, xsb, hT, junk, ss, rstd, pT_sel, alt=False):
        c = self.c
        for blk in range(nb):
            self.act(junk[0:bs, :], xt[0:bs, blk, :], AF.Square, accum=ss[0:bs, blk:blk + 1],
                     R=[xt.b], W=[junk.b, ss.b])
        self.act(rstd[0:bs, 0:nb], ss[0:bs, 0:nb], AF.Sqrt, bias=c['eps'][0:bs, :], scale=1.0 / D,
                 R=[ss.b, c['eps'].b], W=[rstd.b])
        self.recip(rstd[0:bs, 0:nb], rstd[0:bs, 0:nb], R=[rstd.b], W=[rstd.b])
        for blk in range(nb):
            self.act(xsb[0:bs, blk, :], xt[0:bs, blk, :], AF.Copy, scale=rstd[0:bs, blk:blk + 1],
                     R=[xt.b, rstd.b], W=[xsb.b])
        nt = (nb - 1) * 128 + bs
        for k in range(8):
            pt = self.pT[(pT_sel + k) % 2] if alt else self.pT[pT_sel]
            for blk in range(nb):
                self.tr(pt[:, blk * 128:blk * 128 + bs], xsb[0:bs, blk, k * 128:(k + 1) * 128],
                        c['ident_b'][0:bs, 0:bs], R=[xsb.b, c['ident_b'].b], W=[pt.b])
            self.ts('dve', hT[:, k, 0:nt], pt[:, 0:nt], gm[:, k:k + 1], sh[:, k:k + 1], ALU.mult, ALU.add,
                    R=[pt.b, gm.b, sh.b], W=[hT.b])

    def phase1(self, st):
        I, c = self.I, self.c
        win = self.win
        wdec = self.sb(st, (64, 512), BF16, name='wdec')
        self.dma('pool', wdec[:], I['w_dec'], W=[wdec.b])
        waaa = self.sb(st, (128, 512), BF16, name='waaa')
        self.dma('pool', waaa[64:128, :], I['w_aaa'], W=[waaa.b])
        wgup = self.sb(st, (128, 512), BF16, name='wgup')
        self.dma('pool', wgup[:], I['w_gup'], W=[wgup.b])
        gqk = self.sb(st, (128, 1024), F32, name='gqk')
        self.dma('sp', gqk[:], I['gqk'], W=[gqk.b])
        self.win, self.wdec, self.waaa, self.wgup, self.gqk = win, wdec, waaa, wgup, gqk

        NT = 128
        self.NT1 = NT
        W = {}
        W['xt'] = [self.sb(st, (128, 1, D), F32, name='xt') for _ in range(2)]
        W['xsb'] = self.sb(st, (128, 1, D), BF16, name='xsb')
        W['hT'] = self.sb(st, (128, 8, NT), BF16, name='hT')
        W['junk'] = self.sb(st, (128, D), BF16, name='junk')
        W['ss'] = self.sb(st, (128, 4), F32)
        W['rstd'] = self.sb(st, (128, 4), F32)
        W['sq'] = self.sb(st, (128, 1024), F32, name='sq')
        W['ss16'] = self.sb(st, (128, 16), F32)
        W['rs16'] = self.sb(st, (128, 16), F32)
        W['qkn'] = self.sb(st, (128, 1024), F32, name='qkn')
        W['kout'] = [self.sb(st, (128, 512), F32, name='kout') for _ in range(2)]
        W['qkb'] = self.sb(st, (128, 1024), BF16, name='qkb')
        W['QKt'] = [self.sb(st, (128, 8, NT), BF16, name='QKt') for _ in range(2)]
        W['v32'] = [self.sb(st, (128, 512), F32, name='v32') for _ in range(2)]
        W['vb'] = [self.sb(st, (128, 1, 512), BF16, name='vb') for _ in range(2)]
        W['lf'] = [self.sb(st, (128, 8), F32, name='lf') for _ in range(2)]
        W['lft'] = self.sb(st, (128, 8), F32)
        W['lcT'] = self.sb(st, (8, NT), F32)
        W['lcr'] = self.sb(st, (8, NT), F32)
        W['lcs'] = [self.sb(st, (8, 3, NT), BF16) for _ in range(2)]
        W['lchf'] = self.sb(st, (8, NT), F32)
        W['R'] = self.sb(st, (128, 8), F32, name='Racc')
        W['onesb'] = self.sb(st, (3, 2112), BF16, name='onesb')
        self.memset('pool', W['onesb'][:], 1.0, W=[W['onesb'].b])
        for nm in ['z']:
            W[nm] = self.sb(st, (128, 14, NT), F32, name=nm)
        W['t1'] = [self.sb(st, (128, NT), F32, name='t1') for _ in range(2)]
        W['carry'] = self.sb(st, (128, 14), F32, name='carry')
        for nm in ['esig', 'aa', 'kk', 'kp', 'cs', 'gx', 'tmpf', 'beta']:
            W[nm] = self.sb(st, (128, 4, NT), F32, name=nm)
        for nm in ['sqb', 'rkb']:
            W[nm] = self.sb(st, (128, 4, NT), BF16, name=nm)
        W['HO'] = []
        for _ in range(2):
            ho = {}
            for nm in ['BtT', 'KtT', 'BgT', 'KgT', 'vTb', 'gT', 'bonus']:
                ho[nm] = self.sb(st, (128, 4, NT), BF16, name=nm)
            ho['gam'] = self.sb(st, (128, 4, NT), F32, name='gam')
            ho['AR'] = self.sb(st, (128, 4, 1, 2, 128), BF16, name='AR')
            W['HO'].append(ho)
        W['tdw'] = self.sb(st, (128, NT), BF16, name='tdw')
        W['dab'] = self.sb(st, (128, NT), BF16, name='dab')
        W['sg'] = self.sb(st, (128, NT), BF16, name='sg')
        W['nb16'] = self.sb(st, (128, 4, 1), F32, name='nb16')
        W['Bgt'] = self.sb(st, (128, 512), BF16, name='Bgt')
        W['Kgt'] = self.sb(st, (128, 512), BF16, name='Kgt')
        W['Vt'] = self.sb(st, (128, 512), BF16, name='Vt')
        W['MLt'] = self.sb(st, (128, 8, 256), BF16, name='MLt')
        W['MKt'] = self.sb(st, (128, 8, 256), BF16, name='MKt')
        W['Lc'] = [self.sb(st, (128, 8, 128), BF16, name='Lc') for _ in range(2)]
        W['Mc'] = [self.sb(st, (128, 8, 128), BF16, name='Mc') for _ in range(2)]
        W['Xf'] = self.sb(st, (128, 8, 128), F32, name='Xf')
        W['Xb'] = self.sb(st, (128, 8, 128), BF16, name='Xb')
        W['GT'] = self.sb(st, (128, 4, 64), BF16, name='GT')
        W['Hs'] = self.sb(st, (128, 4, 64), F32, name='Hs')
        W['RAT'] = self.sb(st, (128, 4, 128), BF16, name='RAT')
        W['Sf'] = self.sb(st, (128, 4, 64), F32, name='Sf')
        W['Sb'] = self.sb(st, (128, 4, 64), BF16, name='Sb')
        W['ysb'] = self.sb(st, (128, 8, 64), F32, name='ysb')
        W['ysq'] = self.sb(st, (128, 8, 64), F32, name='ysq')
        W['yh'] = self.sb(st, (128, 512), BF16, name='yh')
        W['st8'] = [self.sb(st, (128, 8), F32) for _ in range(4)]
        W['yT1'] = self.sb(st, (128, 4, 128), F32, name='yT1')
        W['yr'] = [self.sb(st, (128, 4, 128), BF16, name='yr') for _ in range(2)]
        self.nyr = 0
        W['Sv'] = self.sb(st, (64, 8, 64), F32, name='Sv')
        W['So'] = self.sb(st, (64, 4, 128), F32, name='So')
        self.W = W

        for g in self.G:
            self.phase1_group(g, NT)

    def phase1_group(self, g, NT):
        I, c, W = self.I, self.c, self.W
        gi, T, past = g['gi'], g['T'], g['past']
        for h in range(NH):
            for a in range(0, g['tot'], 2112):
                b = min(g['tot'], a + 2112)
                self.dma('sp', g['Ks'][h, 64:67, a:b], W['onesb'][:, 0:b - a], R=[W['onesb'].b])
        self.memset('dve', W['R'][:], 0.0, W=[W['R'].b])
        npb = past // 128
        for pb in range(0 if self.sub.get('skip_past') else npb):
            kc = W['kout'][pb % 2]
            self.dma('sp', kc[:], I['ck'][pb * 128:(pb + 1) * 128, :], W=[kc.b])
            self.cp('pool', W['qkb'][:, 512:1024], kc[:], R=[kc.b], W=[W['qkb'].b])
            self.k_transposes(g, pb * 128, 128, 0, only_k=True, blk=pb)
            self.flush_qk(g, pb * 128, 128, only_k=True, blk=pb)
            vc = W['v32'][pb % 2]
            self.dma('sp', vc[:], I['cv'][pb * 128:(pb + 1) * 128, :], W=[vc.b])
            vb = W['vb'][pb % 2]
            self.cp('pool', vb[:, 0, :], vc[:], R=[vc.b], W=[vb.b])
            self.dma('sp', g['Vs'][:, :, pb, :].rearrange("h p d -> p h d"),
                     vb[:, 0, :].rearrange("p (h d) -> p h d", d=64), R=[vb.b])
            lf = W['lf'][pb % 2]
            self.dma('sp', lf[:], I['clf'][pb * 128:(pb + 1) * 128, :], W=[lf.b])
            self.lc_block(g, lf, 128, pb, None, 0)
        if gi == 0:
            self.memset('dve', W['Sf'][:], 0.0, W=[W['Sf'].b])
            self.memset('pool', W['Sb'][:], 0.0, W=[W['Sb'].b])
            self.memset('dve', W['carry'][:], 0.0, W=[W['carry'].b])
        else:
            self.dma('sp', W['carry'][:], I['shp0'], W=[W['carry'].b])
            self.dma('sp', W['Sv'][:], I['st0'], W=[W['Sv'].b])
            for cb in range(4):
                pa = self.pA[cb % 2]
                self.tr(pa[:, 0:64], W['Sv'][:, 2 * cb:2 * cb + 2, :], c['ident_f'][0:64, 0:64],
                        R=[W['Sv'].b, c['ident_f'].b], W=[pa.b])
                self.cp('dve', W['Sf'][:, cb, :], pa[:, 0:64], R=[pa.b], W=[W['Sf'].b])
            self.cp('pool', W['Sb'][:], W['Sf'][:], R=[W['Sf'].b], W=[W['Sb'].b])
        ntile = (T + NT - 1) // NT
        def run(gens):
            gens = [x for x in gens if x[0] is not None]
            while gens:
                for x in list(gens):
                    for _ in range(x[1]):
                        try:
                            next(x[0])
                        except StopIteration:
                            gens.remove(x)
                            break
        genB = None
        for ti in range(ntile):
            nt = min(NT, T - ti * NT)
            genA = self.phase1_tile(g, ti, ti * NT, nt)
            run([(genB, self.sub.get('nb', 1)), (genA, self.sub.get('na', 1))])
            C_ = min(128, nt)
            genB = self.rwkv_chunk(g, ti, ti * NT, 0, C_, nt) if not self.sub.get('skip_rwkv') else None
        run([(genB, 1)])
        if self.sub.get('skip_final'):
            return
        self.dma('sp', g['o_sh'], W['carry'][:], R=[W['carry'].b])
        for cb in range(4):
            pa = self.pA[cb % 2]
            self.tr(pa[0:64, 0:128], W['Sf'][:, cb, :], c['ident_f'][:], R=[W['Sf'].b, c['ident_f'].b], W=[pa.b])
            self.cp('dve', W['So'][:, cb, :], pa[0:64, 0:128], R=[pa.b], W=[W['So'].b])
        self.dma('sp', g['o_st'].rearrange("(cb hh) v k -> v cb hh k", hh=2),
                 W['So'][:].rearrange("v cb (hh k) -> v cb hh k", hh=2), R=[W['So'].b])

    def k_transposes(self, g, tok0, bs, col0, only_k, blk):
        c, W = self.c, self.W
        QKt = W['QKt'][g.get('qkt_sel', 0)]
        for j in range(4 if only_k else 8):
            jj = j + 4 if only_k else j
            pt = self.pT[0]
            self.tr(pt[:, 0:bs], W['qkb'][0:bs, jj * 128:(jj + 1) * 128], c['ident_b'][0:bs, 0:bs],
                    R=[W['qkb'].b, c['ident_b'].b], W=[pt.b])
            self.cp('act', QKt[:, jj, col0:col0 + bs], pt[:, 0:bs], R=[pt.b], W=[QKt.b])

    def flush_qk(self, g, tok0, n, only_k, blk=None, qtok0=None):
        W = self.W
        QKt = W['QKt'][g.get('qkt_sel', 0)]
        for h in range(NH):
            pb = 64 * (h % 2)
            self.dma('sp', g['Ks'][h, 0:64, tok0:tok0 + n], QKt[pb:pb + 64, 4 + h // 2, 0:n], R=[QKt.b])
            if not only_k:
                self.dma('sp', g['Qs'][h, 0:64, qtok0:qtok0 + n], QKt[pb:pb + 64, h // 2, 0:n], R=[QKt.b])
        g['qkt_sel'] = 1 - g.get('qkt_sel', 0)

    def lc_block(self, g, lf, bs, kb, lcT_cols, col0):
        c, W = self.c, self.W
        R = W['R']
        pa = self.pA[1]
        self.mm(pa[0:bs, 0:8], c['triu_f'][0:bs, 0:bs], lf[0:bs, :], start=True, stop=False,
                R=[c['triu_f'].b, lf.b], W=[pa.b])
        self.mm(pa[0:bs, 0:8], c['ones_f'][:, 0:bs], R[:, :], start=False, stop=True, R=[c['ones_f'].b, R.b], W=[pa.b])
        self.ts('dve', g['neglc'][0:bs, kb, :], pa[0:bs, 0:8], -1.0, None, ALU.mult, R=[pa.b], W=[g['neglc'].b])
        if lcT_cols is not None:
            o = 16 + col0
            self.mm(pa[0:8, o:o + bs], lf[0:bs, :], c['triu_f'][0:bs, 0:bs], start=True, stop=False,
                    R=[lf.b, c['triu_f'].b], W=[pa.b])
            self.mm(pa[0:8, o:o + bs], R[:, :], c['ones_f'][:, 0:bs], start=False, stop=True,
                    R=[R.b, c['ones_f'].b], W=[pa.b])
            self.cp('dve', W['lcT'][:, col0:col0 + bs], pa[0:8, o:o + bs], R=[pa.b], W=[W['lcT'].b])
        self.tt('dve', R[0:bs, :], R[0:bs, :], lf[0:bs, :], ALU.add, R=[R.b, lf.b], W=[R.b])

    def phase1_tile(self, g, ti, tok0, nt):
        I, c, W = self.I, self.c, self.W
        gi, past = g['gi'], g['past']
        nb = (nt + 127) // 128
        bs = min(128, nt)
        xt = W['xt'][ti % 2]
        self.dma('sp', xt[0:bs, 0:nb, :], g['x'][tok0:tok0 + nt, :].rearrange("(b p) d -> p b d", p=bs), W=[xt.b])
        hT = W['hT']
        self.norm_transpose(xt, nb, bs, g['gm1'], g['sh1'], W['xsb'], hT, W['junk'], W['ss'], W['rstd'], 0)
        win = self.win
        yield
        for blk in range(0 if self.sub.get('skip_fox') else nb):
            kb = (past + tok0) // 128 + blk
            pq, pk, pv, pf = self.pA[0], self.pA[1], self.pA[0], self.pA[1]
            def proj(pp, c0, wdt):
                for k in range(8):
                    self.mm(pp[0:bs, 0:wdt], hT[:, k, blk * 128:blk * 128 + bs], win[:, k, c0:c0 + wdt],
                            start=(k == 0), stop=(k == 7), R=[hT.b, win.b], W=[pp.b])
            proj(pq, 1792, 512)
            proj(pk, 2304, 512)
            yield
            sq, ss16, rs16, qkn = W['sq'], W['ss16'], W['rs16'], W['qkn']
            self.act(sq[0:bs, 0:512], pq[0:bs, :], AF.Square, R=[pq.b], W=[sq.b])
            self.act(sq[0:bs, 512:1024], pk[0:bs, :], AF.Square, R=[pk.b], W=[sq.b])
            self.red(ss16[0:bs, :], sq[0:bs, :].rearrange("p (g d) -> p g d", d=64), R=[sq.b], W=[ss16.b])
            self.act(rs16[0:bs, 0:8], ss16[0:bs, 0:8], AF.Sqrt, bias=c['eps64'][0:bs, :], scale=1.0,
                     R=[ss16.b, c['eps64'].b], W=[rs16.b])
            self.act(rs16[0:bs, 8:16], ss16[0:bs, 8:16], AF.Sqrt, bias=c['eps'][0:bs, :], scale=1.0 / 64,
                     R=[ss16.b, c['eps'].b], W=[rs16.b])
            self.recip(rs16[0:bs, :], rs16[0:bs, :], R=[rs16.b], W=[rs16.b])
            self.tt('dve', qkn[0:bs, 0:512].rearrange("p (g d) -> p g d", d=64),
                    pq[0:bs, :].rearrange("p (g d) -> p g d", d=64),
                    rs16[0:bs, 0:8].unsqueeze(2).to_broadcast([bs, 8, 64]), ALU.mult, R=[pq.b, rs16.b], W=[qkn.b])
            self.tt('dve', qkn[0:bs, 512:1024].rearrange("p (g d) -> p g d", d=64),
                    pk[0:bs, :].rearrange("p (g d) -> p g d", d=64),
                    rs16[0:bs, 8:16].unsqueeze(2).to_broadcast([bs, 8, 64]), ALU.mult, R=[pk.b, rs16.b], W=[qkn.b])
            kout = W['kout'][blk % 2]
            self.tt('dve', W['qkb'][0:bs, 0:512], qkn[0:bs, 0:512], self.gqk[0:bs, 0:512], ALU.mult,
                    R=[qkn.b, self.gqk.b], W=[W['qkb'].b])
            self.tt('dve', kout[0:bs, :], qkn[0:bs, 512:1024], self.gqk[0:bs, 512:1024], ALU.mult,
                    R=[qkn.b, self.gqk.b], W=[kout.b])
            self.dma('sp', g['o_k'][tok0 + blk * 128:tok0 + blk * 128 + bs, :], kout[0:bs, :], R=[kout.b])
            self.cp('pool', W['qkb'][0:bs, 512:1024], kout[0:bs, :], R=[kout.b], W=[W['qkb'].b])
            self.k_transposes(g, tok0, bs, blk * 128, only_k=False, blk=blk)
            yield
            proj(pv, 2816, 512)
            proj(pf, 3328, 8)
            v32 = W['v32'][blk % 2]
            self.cp('act', v32[0:bs, :], pv[0:bs, :], R=[pv.b], W=[v32.b])
            self.dma('sp', g['o_v'][tok0 + blk * 128:tok0 + blk * 128 + bs, :], v32[0:bs, :], R=[v32.b])
            vb = W['vb'][ti % 2]
            self.cp('pool', vb[0:bs, blk, :], v32[0:bs, :], R=[v32.b], W=[vb.b])
            yield
            lf = W['lf'][blk % 2]
            self.tt('dve', W['lft'][0:bs, :], pf[0:bs, 0:8], c['f_b'][0:bs, :], ALU.add, R=[pf.b, c['f_b'].b], W=[W['lft'].b])
            self.act(W['lft'][0:bs, :], W['lft'][0:bs, :], AF.Exp, scale=-1.0, R=[W['lft'].b], W=[W['lft'].b])
            self.act(W['lft'][0:bs, :], W['lft'][0:bs, :], AF.Ln, bias=1.0, scale=1.0, R=[W['lft'].b], W=[W['lft'].b])
            self.ts('dve', lf[0:bs, :], W['lft'][0:bs, :], -1.0, None, ALU.mult, R=[W['lft'].b], W=[lf.b])
            self.dma('sp', g['o_lf'][tok0 + blk * 128:tok0 + blk * 128 + bs, :], lf[0:bs, :], R=[lf.b])
            self.lc_block(g, lf, bs, kb, True, blk * 128)
        if self.sub.get('skip_fox'):
            if not self.sub.get('skip_rwkv'):
                yield from self.rwkv_tile(g, ti, tok0, nt)
            return
        self.flush_qk(g, past + tok0, nt, only_k=False, qtok0=tok0)
        vb = W['vb'][ti % 2]
        kb0 = (past + tok0) // 128
        self.dma('sp', g['Vs'][:, 0:bs, kb0:kb0 + nb, :].rearrange("h p b d -> p h b d"),
                 vb[0:bs, 0:nb, :].rearrange("p b (h d) -> p h b d", d=64), R=[vb.b])
        yield
        lcs = W['lcs'][ti % 2]
        lcT, lcr, lchf = W['lcT'], W['lcr'], W['lchf']
        self.cp('dve', lcs[:, 0, 0:nt], lcT[:, 0:nt], R=[lcT.b], W=[lcs.b])
        self.tt('dve', lcr[:, 0:nt], lcT[:, 0:nt], lcs[:, 0, 0:nt], ALU.subtract, R=[lcT.b, lcs.b], W=[lcr.b])
        self.cp('dve', lcs[:, 1, 0:nt], lcr[:, 0:nt], R=[lcr.b], W=[lcs.b])
        self.tt('dve', lchf[:, 0:nt], lcr[:, 0:nt], lcs[:, 1, 0:nt], ALU.subtract, R=[lcr.b, lcs.b], W=[lchf.b])
        self.cp('dve', lcs[:, 2, 0:nt], lchf[:, 0:nt], R=[lchf.b], W=[lcs.b])
        self.dma('sp', g['Qs'][:, 64:67, tok0:tok0 + nt], lcs[:, :, 0:nt], R=[lcs.b])
        if not self.sub.get('skip_rwkv'):
            yield from self.rwkv_tile(g, ti, tok0, nt)
        yield

    def rwkv_tile(self, g, ti, tok0, nt):
        I, c, W = self.I, self.c, self.W
        ho = W['HO'][ti % 2]
        win, hT = self.win, W['hT']
        C = min(128, nt)
        nch = nt // C
        z = W['z']
        carry = W['carry']
        for cb in range(14):
            pp = self.pA[cb % 2]
            for k in range(8):
                self.mm(pp[:, 0:nt], win[:, k, cb * 128:(cb + 1) * 128], hT[:, k, 0:nt],
                        start=(k == 0), stop=(k == 7), R=[win.b, hT.b], W=[pp.b])
            t1 = W['t1'][cb % 2]
            v = self.sub.get('v1', 15)
            if v & 1:
                self.act(t1[:, 1:nt], pp[:, 0:nt - 1], AF.Copy, scale=c['mu'][:, cb:cb + 1], R=[pp.b, c['mu'].b], W=[t1.b])
            if v & 2:
                self.ts('dve' if v & 16 else 'pool', t1[:, 0:1], carry[:, cb:cb + 1], c['mu'][:, cb:cb + 1], None, ALU.mult,
                        R=[carry.b, c['mu'].b], W=[t1.b])
            if v & 4:
                self.stt(z[:, cb, 0:nt], pp[:, 0:nt], c['omu'][:, cb:cb + 1], t1[:, 0:nt], ALU.mult, ALU.add,
                         R=[pp.b, c['omu'].b, t1.b], W=[z.b])
            if v & 8:
                self.cp('act', carry[:, cb:cb + 1], pp[:, nt - 1:nt], R=[pp.b], W=[carry.b])
            if cb % 2 == 1:
                yield
        if self.sub.get('rstop', 99) <= 1:
            return
        zr, zk, zv = z[:, 0:4, 0:nt], z[:, 4:8, 0:nt], z[:, 8:12, 0:nt]
        tdw, dab, sg = W['tdw'], W['dab'], W['sg']
        self.act(tdw[0:64, 0:nt], z[0:64, 12, 0:nt], AF.Tanh, R=[z.b], W=[tdw.b])
        self.cp('pool', dab[64:128, 0:nt], z[64:128, 12, 0:nt], R=[z.b], W=[dab.b])
        self.act(sg[:, 0:nt], z[:, 13, 0:nt], AF.Sigmoid, R=[z.b], W=[sg.b])
        esig, aa, gT = W['esig'], W['aa'], ho['gT']
        for cb in range(4):
            p1, p2, p3 = self.pA[0], self.pA[1], self.pA[0]
            o = (cb % 2) * 256
            self.mm(p1[:, o:o + nt], self.wdec[0:64, cb * 128:(cb + 1) * 128], tdw[0:64, 0:nt], R=[self.wdec.b, tdw.b], W=[p1.b])
            self.act(esig[:, cb, 0:nt], p1[:, o:o + nt], AF.Sigmoid, bias=c['w0'][:, cb:cb + 1], R=[p1.b, c['w0'].b], W=[esig.b])
            self.mm(p2[:, o:o + nt], self.waaa[64:128, cb * 128:(cb + 1) * 128], dab[64:128, 0:nt], R=[self.waaa.b, dab.b], W=[p2.b])
            self.act(aa[:, cb, 0:nt], p2[:, o:o + nt], AF.Sigmoid, bias=c['a0'][:, cb:cb + 1], R=[p2.b, c['a0'].b], W=[aa.b])
            self.mm(p3[:, o:o + nt], self.wgup[:, cb * 128:(cb + 1) * 128], sg[:, 0:nt], R=[self.wgup.b, sg.b], W=[p3.b])
            self.cp('dve', gT[:, cb, 0:nt], p3[:, o:o + nt], R=[p3.b], W=[gT.b])
        if self.sub.get('rstop', 99) <= 2:
            return
        yield
        kk, kp, sqb, tmpf = W['kk'], W['kp'], W['sqb'], W['tmpf']
        for cb in range(4):
            self.ts('dve', kk[:, cb, 0:nt], z[:, 4 + cb, 0:nt], c['k_k'][:, cb:cb + 1], None, ALU.mult, R=[z.b, c['k_k'].b], W=[kk.b])
        self.act(sqb[:, :, 0:nt], kk[:, :, 0:nt], AF.Square, R=[kk.b], W=[sqb.b])
        for cb in range(4):
            pp = self.pA[cb // 2]
            o = (cb % 2) * 256
            self.mm(pp[:, o:o + nt], c['blk_b'][:], sqb[:, cb, 0:nt], R=[c['blk_b'].b, sqb.b], W=[pp.b])
            self.act(tmpf[:, cb, 0:nt], pp[:, o:o + nt], AF.Sqrt, R=[pp.b], W=[tmpf.b])
        self.ts('dve', tmpf[:, :, 0:nt], tmpf[:, :, 0:nt], 1e-12, None, ALU.max, R=[tmpf.b], W=[tmpf.b])
        self.recip(tmpf[:, :, 0:nt], tmpf[:, :, 0:nt], R=[tmpf.b], W=[tmpf.b])
        self.tt('dve', kk[:, :, 0:nt], kk[:, :, 0:nt], tmpf[:, :, 0:nt], ALU.mult, R=[kk.b, tmpf.b], W=[kk.b])
        yield
        for cb in range(4):
            self.ts('dve', tmpf[:, cb, 0:nt], aa[:, cb, 0:nt], c['k_a'][:, cb:cb + 1], c['omka'][:, cb:cb + 1], ALU.mult, ALU.add,
                    R=[aa.b, c['k_a'].b, c['omka'].b], W=[tmpf.b])
        self.tt('dve', kp[:, :, 0:nt], zk, tmpf[:, :, 0:nt], ALU.mult, R=[z.b, tmpf.b], W=[kp.b])
        yield
        bonus, rkb = ho['bonus'], W['rkb']
        self.tt('dve', tmpf[:, :, 0:nt], zr, kp[:, :, 0:nt], ALU.mult, R=[z.b, kp.b], W=[tmpf.b])
        for cb in range(4):
            self.ts('dve', rkb[:, cb, 0:nt], tmpf[:, cb, 0:nt], c['r_k'][:, cb:cb + 1], None, ALU.mult, R=[tmpf.b, c['r_k'].b], W=[rkb.b])
        for cb in range(4):
            pp = self.pA[cb // 2]
            o = (cb % 2) * 256
            self.mm(pp[:, o:o + nt], c['blk_b'][:], rkb[:, cb, 0:nt], R=[c['blk_b'].b, rkb.b], W=[pp.b])
            self.tt('dve', bonus[:, cb, 0:nt], pp[:, o:o + nt], z[:, 8 + cb, 0:nt], ALU.mult, R=[pp.b, z.b], W=[bonus.b])
        if self.sub.get('rstop', 99) <= 3:
            return
        yield
        cs, gam, gx, beta = W['cs'], ho['gam'], W['gx'], W['beta']
        NTf = self.NT1
        if nt == NTf:
            self.P.op('dve', lambda e: e.tensor_tensor_scan(cs[:].rearrange("p a t -> p (a t)"), c['reset'][:, 0:4 * nt],
                                                            esig[:].rearrange("p a t -> p (a t)"), 0.0, op0=ALU.mult, op1=ALU.add),
                      [c['reset'].b, esig.b], [cs.b])
        else:
            for cb in range(4):
                self.P.op('dve', lambda e, cb=cb: e.tensor_tensor_scan(cs[:, cb, 0:nt], c['reset'][:, 0:nt], esig[:, cb, 0:nt], 0.0,
                                                                      op0=ALU.mult, op1=ALU.add),
                          [c['reset'].b, esig.b], [cs.b])
        self.act(gam[:, :, 0:nt], cs[:, :, 0:nt], AF.Exp, scale=-C0, R=[cs.b], W=[gam.b])
        AR, BtT, KtT, BgT, KgT, vTb = ho['AR'], ho['BtT'], ho['KtT'], ho['BgT'], ho['KgT'], ho['vTb']
        def chv(ap):
            return ap.rearrange("p a (n c) -> p a n c", c=C)
        self.tt('dve', AR[:, :, 0:nch, 1, 0:C], chv(zr), chv(gam[:, :, 0:nt]), ALU.mult, R=[z.b, gam.b], W=[AR.b])
        self.tt('dve', beta[:, :, 0:nt], kk[:, :, 0:nt], aa[:, :, 0:nt], ALU.mult, R=[kk.b, aa.b], W=[beta.b])
        yield
        self.act(gx[:, :, 0:nt], cs[:, :, 0:nt], AF.Exp, scale=C0, R=[cs.b], W=[gx.b])
        self.tt('dve', BtT[:, :, 0:nt], beta[:, :, 0:nt], gx[:, :, 0:nt], ALU.mult, R=[beta.b, gx.b], W=[BtT.b])
        self.tt('dve', KtT[:, :, 0:nt], kp[:, :, 0:nt], gx[:, :, 0:nt], ALU.mult, R=[kp.b, gx.b], W=[KtT.b])
        yield
        self.tt('dve', tmpf[:, :, 0:nt], cs[:, :, 0:nt], esig[:, :, 0:nt], ALU.subtract, R=[cs.b, esig.b], W=[tmpf.b])
        self.act(gx[:, :, 0:nt], tmpf[:, :, 0:nt], AF.Exp, scale=-C0, R=[tmpf.b], W=[gx.b])
        self.tt('dve', AR[:, :, 0:nch, 0, 0:C], chv(kk[:, :, 0:nt]), chv(gx[:, :, 0:nt]), ALU.mult, R=[kk.b, gx.b], W=[AR.b])
        yield
        nb16 = W['nb16']
        self.ts('dve', nb16[:, :, 0:nch], cs[:, :, C - 1:nt:C], -C0, None, ALU.mult, R=[cs.b], W=[nb16.b])
        for cb in range(4):
            for ch in range(nch):
                self.act(gx[:, cb, ch * C:(ch + 1) * C], cs[:, cb, ch * C:(ch + 1) * C], AF.Exp, bias=nb16[:, cb, ch:ch + 1], scale=C0,
                         R=[cs.b, nb16.b], W=[gx.b])
        self.tt('dve', BgT[:, :, 0:nt], beta[:, :, 0:nt], gx[:, :, 0:nt], ALU.mult, R=[beta.b, gx.b], W=[BgT.b])
        self.tt('dve', KgT[:, :, 0:nt], kp[:, :, 0:nt], gx[:, :, 0:nt], ALU.mult, R=[kp.b, gx.b], W=[KgT.b])
        self.cp('pool', vTb[:, :, 0:nt], zv, R=[z.b], W=[vTb.b])
        if self.sub.get('rstop', 99) <= 4:
            return

    def rwkv_chunk(self, g, ti, tok0, ch, C, nt):
        c, W = self.c, self.W
        ho = W['HO'][ti % 2]
        AR, BtT, KtT, BgT, KgT, vTb = ho['AR'], ho['BtT'], ho['KtT'], ho['BgT'], ho['KgT'], ho['vTb']
        Bgt, Kgt, Vt, MLt, MKt, Xf, Xb = W['Bgt'], W['Kgt'], W['Vt'], W['MLt'], W['MKt'], W['Xf'], W['Xb']
        sl = slice(ch * C, (ch + 1) * C)
        idb = c['ident_b']
        pt = self.pT[1]
        for cb in range(4):
            self.tr(pt[0:C, cb * 128:(cb + 1) * 128], AR[:, cb, ch, 0, 0:C], idb[:], R=[AR.b, idb.b], W=[pt.b])
        self.ts('dve', Xb[0:C, :, 0:64], pt[0:C, 0:512].rearrange("p (h d) -> p h d", d=64), -1.0, None, ALU.mult, R=[pt.b], W=[Xb.b])
        for (src, dst, eng, pi) in [(BgT, Bgt, 'act', 1), (KgT, Kgt, 'act', 0), (vTb, Vt, 'act', 1)]:
            pt = self.pT[1]
            for cb in range(4):
                self.tr(pt[0:C, cb * 128:(cb + 1) * 128], src[:, cb, sl], idb[:], R=[src.b, idb.b], W=[pt.b])
            self.cp(eng, dst[0:C, :], pt[0:C, 0:512], R=[pt.b], W=[dst.b])
        if self.sub.get('rstop', 99) <= 5:
            return
        yield
        mA = c['maskA'][0:C, :].rearrange("p (a c) -> p a c", a=2)[:, :, 0:C]
        Lc0 = W['Lc'][0]
        for par in range(2):
            pb_ = 64 * par
            for half in range(2):
                pA_, pB_, pc = self.bb(), self.bb(), self.bb()
                for j in range(2):
                    cb = 2 * half + j
                    ar = AR[pb_:pb_ + 64, cb, ch, :, 0:C]
                    o = j * 256
                    self.mm(pA_[0:C, o:o + 2 * C].rearrange("p (a c) -> p a c", a=2), BtT[pb_:pb_ + 64, cb, sl], ar,
                            R=[BtT.b, AR.b], W=[pA_.b])
                    self.mm(pB_[0:C, o:o + 2 * C].rearrange("p (a c) -> p a c", a=2), KtT[pb_:pb_ + 64, cb, sl], ar,
                            R=[KtT.b, AR.b], W=[pB_.b])
                    self.mm(pc[0:C, j * 128:j * 128 + C], AR[pb_:pb_ + 64, cb, ch, 0, 0:C], BtT[pb_:pb_ + 64, cb, sl],
                            R=[AR.b, BtT.b], W=[pc.b])
                for j in range(2):
                    cb = 2 * half + j
                    h = 2 * cb + par
                    o = j * 256
                    self.tt('dve', MLt[0:C, h, :].rearrange("p (a c) -> p a c", a=2)[:, :, 0:C],
                            pA_[0:C, o:o + 2 * C].rearrange("p (a c) -> p a c", a=2), mA, ALU.mult,
                            R=[pA_.b, c['maskA'].b], W=[MLt.b])
                    self.tt('dve', MKt[0:C, h, :].rearrange("p (a c) -> p a c", a=2)[:, :, 0:C],
                            pB_[0:C, o:o + 2 * C].rearrange("p (a c) -> p a c", a=2), mA, ALU.mult,
                            R=[pB_.b, c['maskA'].b], W=[MKt.b])
                    self.tt('dve', Lc0[0:C, h, 0:C], pc[0:C, j * 128:j * 128 + C], c['maskC'][0:C, 0:C], ALU.mult,
                            R=[pc.b, c['maskC'].b], W=[Lc0.b])
                yield
        if self.sub.get('rstop', 99) <= 6:
            return
        yield
        pl = self.bb()
        for h in range(NH):
            self.mm(pl[0:C, h * 64:(h + 1) * 64], MKt[0:C, h, 0:C], Vt[0:C, h * 64:(h + 1) * 64], R=[MKt.b, Vt.b], W=[pl.b])
        self.cp('act', Xb[0:C, :, 64:128], pl[0:C, :].rearrange("p (h d) -> p h d", d=64), R=[pl.b], W=[Xb.b])
        if self.sub.get('rstop', 99) <= 7:
            return
        yield
        nlev = int(round(math.log2(C)))
        Lc, Mc = W['Lc'], W['Mc']
        for lev in range(nlev):
            Lcur = Lc[lev % 2]
            Lnx = Lc[(lev + 1) % 2]
            Mnx = Mc[(lev + 1) % 2]
            def Mcur(h):
                return MLt[0:C, h, 0:C] if lev == 0 else Mc[lev % 2][0:C, h, 0:C]
            Mb = MLt.b if lev == 0 else Mc[lev % 2].b
            for half in range(2):
                yield
                px = self.bb()
                for hh in range(4):
                    h = 4 * half + hh
                    self.mm(px[0:C, hh * 128:hh * 128 + 128], Mcur(h), Xb[0:C, h, :], R=[Mb, Xb.b], W=[px.b])
                if lev < nlev - 1:
                    pm_, pl_ = self.bb(), self.bb()
                    for hh in range(4):
                        h = 4 * half + hh
                        self.mm(pm_[0:C, hh * 128:hh * 128 + C], Lcur[0:C, h, 0:C], Mcur(h), R=[Lcur.b, Mb], W=[pm_.b])
                        self.mm(pl_[0:C, hh * 128:hh * 128 + C], Mcur(h), Lcur[0:C, h, 0:C], R=[Lcur.b, Mb], W=[pl_.b])
                self.tt('dve', Xb[0:C, 4 * half:4 * half + 4, :], Xb[0:C, 4 * half:4 * half + 4, :],
                        px[0:C, :].rearrange("p (h d) -> p h d", d=128), ALU.add, R=[px.b, Xb.b], W=[Xb.b])
                if lev < nlev - 1:
                    self.cp('act', Mnx[0:C, 4 * half:4 * half + 4, 0:C],
                            pm_[0:C, :].rearrange("p (h d) -> p h d", d=128)[:, :, 0:C], R=[pm_.b], W=[Mnx.b])
                    self.cp('act', Lnx[0:C, 4 * half:4 * half + 4, 0:C],
                            pl_[0:C, :].rearrange("p (h d) -> p h d", d=128)[:, :, 0:C], R=[pl_.b], W=[Lnx.b])
        if self.sub.get('rstop', 99) <= 8:
            return
        yield
        GT, Hs, RAT, Sf, Sb = W['GT'], W['Hs'], W['RAT'], W['Sf'], W['Sb']
        gam = ho['gam']
        for par in range(2):
            pb_ = 64 * par
            pg, ph, pr = self.bb(), self.bb(), self.bb()
            for cb in range(4):
                h = 2 * cb + par
                self.mm(pg[pb_:pb_ + 64, cb * 64:(cb + 1) * 64], Xb[0:C, h, 0:64], Bgt[0:C, h * 64:(h + 1) * 64], R=[Xb.b, Bgt.b], W=[pg.b])
                self.mm(ph[pb_:pb_ + 64, cb * 64:(cb + 1) * 64], Bgt[0:C, h * 64:(h + 1) * 64], Xb[0:C, h, 64:128], start=True, stop=False,
                        R=[Xb.b, Bgt.b], W=[ph.b])
                self.mm(ph[pb_:pb_ + 64, cb * 64:(cb + 1) * 64], Kgt[0:C, h * 64:(h + 1) * 64], Vt[0:C, h * 64:(h + 1) * 64], start=False, stop=True,
                        R=[Kgt.b, Vt.b], W=[ph.b])
                self.mm(pr[pb_:pb_ + 64, cb * 128:cb * 128 + C], Xb[0:C, h, 0:64], MLt[0:C, h, 128:128 + C], R=[Xb.b, MLt.b], W=[pr.b])
            for cb in range(4):
                gC = gam[pb_:pb_ + 64, cb, ch * C + C - 1:ch * C + C]
                self.stt(GT[pb_:pb_ + 64, cb, :], c['ipair'][pb_:pb_ + 64, :], gC, pg[pb_:pb_ + 64, cb * 64:(cb + 1) * 64], ALU.mult, ALU.add,
                         R=[c['ipair'].b, gam.b, pg.b], W=[GT.b])
            self.cp('act', Hs[pb_:pb_ + 64, :, :], ph[pb_:pb_ + 64, 0:256].rearrange("p (a d) -> p a d", d=64), R=[ph.b], W=[Hs.b])
            self.tt('dve', RAT[pb_:pb_ + 64, :, 0:C], pr[pb_:pb_ + 64, :].rearrange("p (a d) -> p a d", d=128)[:, :, 0:C],
                    AR[pb_:pb_ + 64, :, ch, 1, 0:C], ALU.add, R=[pr.b, AR.b], W=[RAT.b])
        if self.sub.get('rstop', 99) <= 9:
            return
        yield
        ysb, ysq = W['ysb'], W['ysq']
        s10 = self.sub.get('s10', 3)
        if s10 & 1:
            for par in range(2):
                pb_ = 64 * par
                py = self.bb()
                py2 = self.bb() if C != 128 else None
                for cb in range(4):
                    h = 2 * cb + par
                    o = cb * 64
                    self.mm(py[0:C, o:o + 64], MLt[0:C, h, 128:128 + C], Xb[0:C, h, 64:128], start=True, stop=False, R=[MLt.b, Xb.b], W=[py.b])
                    if C == 128:
                        self.mm(py[0:C, o:o + 64], MKt[0:C, h, 128:128 + C], Vt[0:C, h * 64:(h + 1) * 64], start=False, stop=False, R=[MKt.b, Vt.b], W=[py.b])
                        self.mm(py[0:C, o:o + 64], RAT[pb_:pb_ + 64, cb, 0:C], Sb[pb_:pb_ + 64, cb, :], start=False, stop=True, R=[RAT.b, Sb.b], W=[py.b])
                    else:
                        self.mm(py[0:C, o:o + 64], MKt[0:C, h, 128:128 + C], Vt[0:C, h * 64:(h + 1) * 64], start=False, stop=True, R=[MKt.b, Vt.b], W=[py.b])
                        self.mm(py2[0:C, o:o + 64], RAT[pb_:pb_ + 64, cb, 0:C], Sb[pb_:pb_ + 64, cb, :], start=True, stop=True, R=[RAT.b, Sb.b], W=[py2.b])
                if C != 128:
                    self.cp('act', ysq[0:C, 0:4, :], py2[0:C, 0:256].rearrange("p (a d) -> p a d", d=64), R=[py2.b], W=[ysq.b])
                    self.tt('dve', ysb[0:C, par:8:2, :], py[0:C, 0:256].rearrange("p (a d) -> p a d", d=64), ysq[0:C, 0:4, :], ALU.add,
                            R=[py.b, ysq.b], W=[ysb.b])
                    continue
                if s10 & 4:
                    self.cp('dve', ysb[0:C, par:8:2, :], py[0:C, 0:256].rearrange("p (a d) -> p a d", d=64), R=[py.b], W=[ysb.b])
                elif s10 & 8:
                    pass
                else:
                    self.cp('act', ysb[0:C, par:8:2, :], py[0:C, 0:256].rearrange("p (a d) -> p a d", d=64), R=[py.b], W=[ysb.b])
        if s10 & 2:
            pSs = [self.bb(), self.bb()]
            for par in range(2):
                pb_ = 64 * par
                pS = pSs[par]
                for cb in range(4):
                    self.mm(pS[pb_:pb_ + 64, cb * 64:(cb + 1) * 64], GT[pb_:pb_ + 64, cb, :], Sb[pb_:pb_ + 64, cb, :], R=[GT.b, Sb.b], W=[pS.b])
            for par in range(2):
                pb_ = 64 * par
                pS = pSs[par]
                self.tt('dve', Sf[pb_:pb_ + 64, :, :], pS[pb_:pb_ + 64, 0:256].rearrange("p (a d) -> p a d", d=64), Hs[pb_:pb_ + 64, :, :], ALU.add,
                        R=[pS.b, Hs.b], W=[Sf.b])
            self.cp('pool', Sb[:], Sf[:], R=[Sf.b], W=[Sb.b])
        if self.sub.get('rstop', 99) <= 10:
            return
        yield
        s8 = W['st8']
        self.red(s8[0][0:C, :], ysb[0:C, :, :], R=[ysb.b], W=[s8[0].b])
        self.act(ysq[0:C, :, :], ysb[0:C, :, :], AF.Square, R=[ysb.b], W=[ysq.b])
        self.red(s8[1][0:C, :], ysq[0:C, :, :], R=[ysq.b], W=[s8[1].b])
        self.ts('dve', s8[0][0:C, :], s8[0][0:C, :], 1.0 / 64, None, ALU.mult, R=[s8[0].b], W=[s8[0].b])
        self.tt('dve', s8[2][0:C, :], s8[0][0:C, :], s8[0][0:C, :], ALU.mult, R=[s8[0].b], W=[s8[2].b])
        self.stt(s8[1][0:C, :], s8[1][0:C, :], 1.0 / 64, s8[2][0:C, :], ALU.mult, ALU.subtract, R=[s8[1].b, s8[2].b], W=[s8[1].b])
        self.act(s8[1][0:C, :], s8[1][0:C, :], AF.Sqrt, bias=c['gneps'][0:C, :], scale=1.0, R=[s8[1].b, c['gneps'].b], W=[s8[1].b])
        self.recip(s8[1][0:C, :], s8[1][0:C, :], R=[s8[1].b], W=[s8[1].b])
        self.tt('dve', ysb[0:C, :, :], ysb[0:C, :, :], s8[0][0:C, :].unsqueeze(2).to_broadcast([C, 8, 64]), ALU.subtract, R=[ysb.b, s8[0].b], W=[ysb.b])
        yh = W['yh']
        self.tt('dve', yh[0:C, :].rearrange("p (h d) -> p h d", d=64), ysb[0:C, :, :], s8[1][0:C, :].unsqueeze(2).to_broadcast([C, 8, 64]), ALU.mult,
                R=[ysb.b, s8[1].b], W=[yh.b])
        pt = self.pT[1]
        for cb in range(4):
            self.tr(pt[:, cb * 128:cb * 128 + C], yh[0:C, cb * 128:(cb + 1) * 128], idb[0:C, 0:C], R=[yh.b, idb.b], W=[pt.b])
        yT1, yr = W['yT1'], W['yr'][self.nyr % 2]
        self.nyr += 1
        bonus, gT = ho['bonus'], ho['gT']
        for cb in range(4):
            self.ts('dve', yT1[:, cb, 0:C], pt[:, cb * 128:cb * 128 + C], c['gn_g'][:, cb:cb + 1], c['gn_b'][:, cb:cb + 1], ALU.mult, ALU.add,
                    R=[pt.b, c['gn_g'].b, c['gn_b'].b], W=[yT1.b])
        self.tt('dve', yT1[:, :, 0:C], yT1[:, :, 0:C], bonus[:, :, sl], ALU.add, R=[yT1.b, bonus.b], W=[yT1.b])
        self.tt('dve', yr[:, :, 0:C], yT1[:, :, 0:C], gT[:, :, sl], ALU.mult, R=[yT1.b, gT.b], W=[yr.b])
        t0 = tok0 + ch * C
        self.dma('sp', g['Ys'][0:512, t0:t0 + C].rearrange("(a p) t -> p a t", p=128), yr[:, :, 0:C], R=[yr.b])

    def phase2(self, st):
        c = self.c
        maxtot = max(g['tot'] for g in self.G)
        maxT = max(g['T'] for g in self.G)
        maxkb = max(g['nkb'] for g in self.G)
        Kt = [self.sb(st, (67, maxtot), BF16, name='Kt') for _ in range(2)]
        Qt = [self.sb(st, (67, maxT), BF16, name='Qt') for _ in range(2)]
        Vh = [self.sb(st, (128, maxkb, 128), BF16, name='Vh') for _ in range(2)]
        for v in Vh:
            self.memset('pool', v[:, :, 64:128], 1.0, W=[v.b])
        Pt = [self.sb(st, (128, 512), BF16, name='Pt') for _ in range(3)]
        rl = [self.sb(st, (64, 512), F32, name='rl') for _ in range(2)]
        yo = [self.sb(st, (64, 512), BF16, name='yo') for _ in range(2)]
        NPS = 4
        LA = 3
        pS = [self.pA[0], self.pA[1], self.pA[2], self.pA[3]]
        pO = [self.pA[4], self.pA[5]]
        Pt = Pt + [self.sb(st, (128, 512), BF16, name='Pt')]
        heads = [(g, h) for g in self.G for h in range(NH)]
        def load(idx):
            g, h = heads[idx]
            T, tot = g['T'], g['tot']
            kt, qt, vh = Kt[idx % 2], Qt[idx % 2], Vh[idx % 2]
            self.dma('sp', kt[0:64, 0:tot], g['Ks'][h, 0:64, :], W=[kt.b])
            self.dma('sp', kt[64:67, 0:tot], g['Ks'][h, 64:67, :], W=[kt.b])
            self.dma('sp', qt[0:64, 0:T], g['Qs'][h, 0:64, :], W=[qt.b])
            self.dma('sp', qt[64:67, 0:T], g['Qs'][h, 64:67, :], W=[qt.b])
            nfull = tot // 128
            if nfull > 0:
                self.dma('sp', vh[:, 0:nfull, 0:64], g['Vs'][h, :, 0:nfull, :], W=[vh.b])
            rem = tot - nfull * 128
            if rem:
                self.dma('sp', vh[0:rem, nfull, 0:64], g['Vs'][h, 0:rem, nfull, :], W=[vh.b])
        blocks = []
        nq = 0
        for idx, (g, h) in enumerate(heads):
            T, past, tot = g['T'], g['past'], g['tot']
            QT = min(512, T)
            for qi in range(T // QT):
                q0 = qi * QT
                qpos0 = past + q0
                nblk = (qpos0 + QT - 1) // 128 + 1
                for j in range(nblk):
                    k0 = j * 128
                    rows = min(128, tot - k0)
                    if k0 + rows - 1 <= qpos0:
                        c0, diag = 0, False
                    else:
                        c0, diag = k0 - qpos0, True
                    blocks.append(dict(idx=idx, g=g, h=h, q0=q0, QT=QT, j=j, k0=k0, rows=rows, c0=c0, diag=diag,
                                       first=(j == 0), last=(j == nblk - 1), nq=nq, newhead=(qi == 0 and j == 0)))
                nq += 1

        def emit_S(n):
            bl = blocks[n]
            g, h, idx = bl['g'], bl['h'], bl['idx']
            kt, qt = Kt[idx % 2], Qt[idx % 2]
            rows, c0, QT, q0, k0, j = bl['rows'], bl['c0'], bl['QT'], bl['q0'], bl['k0'], bl['j']
            ps_, pt_ = pS[n % NPS], Pt[n % NPS]
            self.mm(ps_[0:rows, c0:QT], kt[:, k0:k0 + rows], qt[:, q0 + c0:q0 + QT], start=True, stop=not bl['diag'],
                    R=[kt.b, qt.b], W=[ps_.b])
            if bl['diag']:
                self.mm(ps_[0:rows, c0:c0 + rows], c['ident_b'][0:rows, 0:rows], c['maskD_b'][0:rows, 0:rows], start=False, stop=True,
                        R=[c['ident_b'].b, c['maskD_b'].b], W=[ps_.b])
            self.act(pt_[0:rows, c0:QT], ps_[0:rows, c0:QT], AF.Exp, bias=g['neglc'][0:rows, j, h:h + 1], scale=1.0,
                     R=[ps_.b, g['neglc'].b], W=[pt_.b])

        def emit_PV(n):
            bl = blocks[n]
            g, h, idx = bl['g'], bl['h'], bl['idx']
            vh = Vh[idx % 2]
            rows, c0, QT, q0, j = bl['rows'], bl['c0'], bl['QT'], bl['q0'], bl['j']
            pt_ = Pt[n % NPS]
            po = pO[bl['nq'] % 2]
            self.mm(po[:, c0:QT], vh[0:rows, j, :], pt_[0:rows, c0:QT], start=bl['first'], stop=bl['last'],
                    R=[vh.b, pt_.b], W=[po.b])
            if bl['last']:
                rlt, yot = rl[bl['nq'] % 2], yo[bl['nq'] % 2]
                self.recip(rlt[:, 0:QT], po[64:128, 0:QT], R=[po.b], W=[rlt.b])
                self.tt('dve', yot[:, 0:QT], po[0:64, 0:QT], rlt[:, 0:QT], ALU.mult, R=[po.b, rlt.b], W=[yot.b])
                self.dma('sp', g['Ys'][512 + h * 64:512 + (h + 1) * 64, q0:q0 + QT], yot[:, 0:QT], R=[yot.b])

        load(0)
        if len(heads) > 1:
            load(1)
        nb_ = len(blocks)
        first_block = {}
        for n, bl in enumerate(blocks):
            first_block.setdefault(bl['idx'], n)
        load_at = {first_block[idx] + LA: idx + 1 for idx in range(1, len(heads) - 1)}
        for n in range(nb_ + LA):
            if n in load_at:
                load(load_at[n])
            if n < nb_:
                emit_S(n)
            if n - LA >= 0:
                emit_PV(n - LA)

    def phase3(self, st):
        I, c = self.I, self.c
        wout = self.sb(st, (128, 8, D), BF16, name='wout')
        wg, wu = self.wg, self.wu
        wd = self.sb(st, (128, NFB, D), BF16, name='wd')
        for (dst, src, kk_, ncol) in [(wout, I['w_out'], 8, D), (wd, I['w_d'], NFB, D)]:
            s3 = src.rearrange("(k p) c -> p k c", p=128)
            for k in range(kk_):
                for a in range(0, ncol, 1408 if ncol == DFF else 1024):
                    b = min(ncol, a + (1408 if ncol == DFF else 1024))
                    self.dma('pool', dst[:, k, a:b], s3[:, k, a:b], W=[dst.b])
        NT = 256
        xt = [self.sb(st, (128, 2, D), F32, name='x3')]
        yT = [self.sb(st, (128, 8, NT), BF16, name='yT')]
        xsb = self.sb(st, (128, 2, D), BF16, name='xsb3')
        h2 = self.sb(st, (128, 8, NT), BF16, name='h2')
        junk = self.sb(st, (128, D), BF16, name='junk3')
        ss = self.sb(st, (128, 4), F32)
        rstd = self.sb(st, (128, 4), F32)
        actT = self.sb(st, (128, NFB, NT), BF16, name='actT')
        sil = [self.sb(st, (128, NT), F32, name='sil') for _ in range(2)]
        yo = [self.sb(st, (128, D), F32, name='yo3')]
        gt1 = self.sb(st, (128, D), F32, name='gt1')
        gt2 = self.sb(st, (128, D), F32, name='gt2')
        n = 0
        hflag = self.sb(st, (1, 1), mybir.dt.int32, name='hflag')
        self.dma('sp', hflag[:], I['half'], W=[hflag.b])
        r_base = st.enter_context(self.nc.gpsimd.register("r_base"))
        r_off = st.enter_context(self.nc.gpsimd.register("r_off"))
        HALF = self.SEQ // 2
        def init_reg(e):
            e.reg_load(r_base, hflag[0:1, 0:1])
            e.reg_mul(r_base, r_base, HALF)
            return e.nop()
        self.P.op('pool', init_reg, [hflag.b], [])
        for g in self.G:
            T = g['T'] if g['gi'] == 1 else HALF
            xsrc = g['x'] if g['gi'] == 1 else I['xp3']
            self.build_gates(g, gt1, gt2, sil)
            ntile = (T + NT - 1) // NT
            for ti in range(ntile):
                tok0 = ti * NT
                nt = min(NT, T - tok0)
                nb = (nt + 127) // 128
                bs = min(128, nt)
                x = xt[n % len(xt)]
                y_ = yT[n % len(yT)]
                n += 1
                self.dma('sp', x[0:bs, 0:nb, :], xsrc[tok0:tok0 + nt, :].rearrange("(b p) d -> p b d", p=bs), W=[x.b])
                if g['gi'] == 1:
                    self.dma('sp', y_[:, :, 0:nt], g['Ys'][:, tok0:tok0 + nt].rearrange("(k p) t -> p k t", p=128), W=[y_.b])
                else:
                    ys = g['Ys']
                    SEQ_ = self.SEQ
                    def dyn(e, y_=y_, tok0=tok0, nt=nt, ys=ys, SEQ_=SEQ_):
                        e.reg_add(r_off, r_base, tok0)
                        src = bass.AP(ys.tensor, r_off, [[SEQ_, 128], [128 * SEQ_, 8], [1, nt]])
                        return e.dma_start(out=y_[:, :, 0:nt], in_=src)
                    self.P.dma_fn('pool', dyn, (), [y_.b])
                tmpm = yo[0]
                for blk in range(nb):
                    for half in range(2):
                        pp = self.pA[half]
                        for k in range(8):
                            self.mm(pp[0:bs, :], y_[:, k, blk * 128:blk * 128 + bs], wout[:, k, half * 512:(half + 1) * 512],
                                    start=(k == 0), stop=(k == 7), R=[y_.b, wout.b], W=[pp.b])
                        self.tt('dve', tmpm[0:bs, half * 512:(half + 1) * 512], pp[0:bs, :], gt1[0:bs, half * 512:(half + 1) * 512], ALU.mult,
                                R=[pp.b, gt1.b], W=[tmpm.b])
                    self.tt('pool', x[0:bs, blk, :], tmpm[0:bs, :], x[0:bs, blk, :], ALU.add, R=[tmpm.b, x.b], W=[x.b])
                x1 = x
                self.norm_transpose(x1, nb, bs, g['gm2'], g['sh2'], xsb, h2, junk, ss, rstd, 0, alt=True)
                for fb in range(NFB):
                    pg, pu = self.pA[2 + (fb % 2)], self.pA[4 + (fb % 2)]
                    for k in range(8):
                        self.mm(pg[:, 0:nt], wg[:, k, fb * 128:(fb + 1) * 128], h2[:, k, 0:nt], start=(k == 0), stop=(k == 7), R=[wg.b, h2.b], W=[pg.b])
                    for k in range(8):
                        self.mm(pu[:, 0:nt], wu[:, k, fb * 128:(fb + 1) * 128], h2[:, k, 0:nt], start=(k == 0), stop=(k == 7), R=[wu.b, h2.b], W=[pu.b])
                    s_ = sil[fb % 2]
                    self.act(s_[:, 0:nt], pg[:, 0:nt], AF.Silu, R=[pg.b], W=[s_.b])
                    self.tt('dve', actT[:, fb, 0:nt], s_[:, 0:nt], pu[:, 0:nt], ALU.mult, R=[s_.b, pu.b], W=[actT.b])
                for blk in range(nb):
                    yo_ = yo[0]
                    for half in range(2):
                        pp = self.pA[half]
                        for fb in range(NFB):
                            self.mm(pp[0:bs, :], actT[:, fb, blk * 128:blk * 128 + bs], wd[:, fb, half * 512:(half + 1) * 512],
                                    start=(fb == 0), stop=(fb == NFB - 1), R=[actT.b, wd.b], W=[pp.b])
                        self.tt('dve', yo_[0:bs, half * 512:(half + 1) * 512], pp[0:bs, :], gt2[0:bs, half * 512:(half + 1) * 512], ALU.mult,
                                R=[pp.b, gt2.b], W=[yo_.b])
                    self.tt('pool', yo_[0:bs, :], yo_[0:bs, :], x1[0:bs, blk, :], ALU.add, R=[yo_.b, x1.b], W=[yo_.b])
                    self.dma('sp', g['o_y'][tok0 + blk * 128:tok0 + blk * 128 + bs, :], yo_[0:bs, :], R=[yo_.b])


def _consts():
    i = np.arange(128)
    s, t = i[:, None], i[None, :]
    cst = {}
    cst['c_ident'] = np.eye(128, dtype=np.float32)
    cst['c_triu'] = (s <= t).astype(np.float32)
    cst['c_ones'] = np.ones((128, 128), np.float32)
    cst['c_blk'] = ((s // 64) == (t // 64)).astype(np.float32)
    cst['c_maskA'] = np.concatenate([-(s < t).astype(np.float32), (s <= t).astype(np.float32)], axis=1)
    cst['c_maskC'] = -(s > t).astype(np.float32)
    cst['c_maskD'] = np.where(s <= t, 0.0, NEG).astype(np.float32)
    r = np.ones((128, 1024), np.float32)
    r[:, ::128] = 0.0
    cst['c_reset'] = r
    cst['c_ipair'] = ((i[:, None] % 64) == np.arange(64)[None, :]).astype(np.float32)
    return cst


def _pk(v, nblk):
    return np.ascontiguousarray(np.asarray(v, np.float32).reshape(nblk, 128).T)


_NC_CACHE = {}


def _get_nc(SEQ, PAST, NS, debug=False, phases=(1, 2, 3), sub=None):
    key = (SEQ, PAST, NS, debug, tuple(phases), str(sub))
    if key not in _NC_CACHE:
        _NC_CACHE[key] = KB(SEQ, PAST, NS, debug, phases, sub).build()
    return _NC_CACHE[key]


def make_in_maps(inp, n_cores=8):
    f = lambda a: np.ascontiguousarray(np.asarray(a, dtype=np.float32))
    xp, xs = f(inp['x_prompt']), f(inp['x_sample'])
    BP = xp.shape[0]
    cst = _consts()
    shared = dict(cst)
    L = 0
    shared['w_ada'] = f(inp['w_ada'][L]); shared['b_ada'] = _pk(inp['b_ada'][L], 48)
    shared['w_in'] = f(inp['w_in'][L]); shared['w_out'] = f(inp['w_out'][L])
    shared['w_g'] = f(inp['w_ffn_gate'][L]); shared['w_u'] = f(inp['w_ffn_up'][L]); shared['w_d'] = f(inp['w_ffn_down'][L])
    shared['n1g'] = _pk(inp['norm1_g'][L], 8); shared['n2g'] = _pk(inp['norm2_g'][L], 8)
    shared['mu'] = _pk(inp['shift_mu'][L], 14)
    for nm, src in [('w0', 'w0'), ('a0', 'a0'), ('k_k', 'k_k'), ('k_a', 'k_a'), ('gn_g', 'gn_g'), ('gn_b', 'gn_b')]:
        shared[nm] = _pk(inp[src][L], 4)
    shared['r_k'] = _pk(np.asarray(inp['r_k'][L]).reshape(512), 4)
    shared['w_dec'] = f(inp['w_decay_up'][L]); shared['w_aaa'] = f(inp['w_aaa_up'][L]); shared['w_gup'] = f(inp['w_gate_up'][L])
    gq = np.tile(np.asarray(inp['fox_q_g'][L], np.float32), 8)
    gk = np.tile(np.asarray(inp['fox_k_g'][L], np.float32), 8)
    shared['gqk'] = np.ascontiguousarray(np.broadcast_to(np.concatenate([gq, gk])[None, :], (128, 1024)))
    shared['f_b'] = np.ascontiguousarray(np.broadcast_to(np.asarray(inp['fox_f_b'][L], np.float32)[None, :], (128, 8)))
    maps = []
    for cidx in range(n_cores):
        b = cidx % BP
        m = dict(shared)
        m['xp'] = xp[b]
        hf = cidx // BP
        H2 = xp.shape[1] // 2
        m['half'] = np.array([[hf]], np.int32)
        m['xp3'] = np.ascontiguousarray(xp[b, hf * H2:(hf + 1) * H2])
        m['xs'] = xs[cidx]
        cv = np.stack([np.asarray(inp['c_prompt'][b], np.float32), np.asarray(inp['c_sample'][cidx], np.float32)], axis=-1)
        m['cvec'] = np.ascontiguousarray(cv.reshape(8, 128, 2).transpose(1, 0, 2))
        m['ck'] = f(inp['cache_fox_k'][L, cidx]).reshape(-1, 512)
        m['cv'] = f(inp['cache_fox_v'][L, cidx]).reshape(-1, 512)
        m['clf'] = f(inp['cache_fox_logf'][L, cidx])
        m['st0'] = np.ascontiguousarray(f(inp['state_rwkv'][L, cidx]).transpose(1, 0, 2))
        m['shp0'] = _pk(inp['state_rwkv_shift'][L, cidx, 0], 14)
        maps.append(m)
    return maps


def assemble(res, BP, SEQ, NSEQ, NS):
    r = res
    def upk(a):
        return np.ascontiguousarray(a.T).reshape(-1)
    y_p = np.stack([np.concatenate([r[b]['y_p'], r[b + BP]['y_p']], axis=0) for b in range(BP)])
    y_s = np.stack([r[c]['y_s'] for c in range(NSEQ)])
    st_p = np.stack([r[b]['st_p'] for b in range(BP)])[None]
    sh_p = np.stack([upk(r[b]['sh_p'])[None, :] for b in range(BP)])[None]
    k_p = np.stack([r[b]['k_p'].reshape(SEQ, 8, 64) for b in range(BP)])[None]
    v_p = np.stack([r[b]['v_p'].reshape(SEQ, 8, 64) for b in range(BP)])[None]
    lf_p = np.stack([r[b]['lf_p'] for b in range(BP)])[None]
    st_s = np.stack([r[c]['st_s'] for c in range(NSEQ)])[None]
    sh_s = np.stack([upk(r[c]['sh_s'])[None, :] for c in range(NSEQ)])[None]
    k_s = np.stack([r[c]['k_s'].reshape(NS, 8, 64) for c in range(NSEQ)])[None]
    v_s = np.stack([r[c]['v_s'].reshape(NS, 8, 64) for c in range(NSEQ)])[None]
    lf_s = np.stack([r[c]['lf_s'] for c in range(NSEQ)])[None]
    outs = (y_p, y_s, st_p, sh_p, k_p, v_p, lf_p, st_s, sh_s, k_s, v_s, lf_s)
    return tuple(np.ascontiguousarray(o, dtype=np.float32) for o in outs)


def kernel(**inputs):
    xp = np.asarray(inputs['x_prompt'])
    xs = np.asarray(inputs['x_sample'])
    BP, SEQ, _ = xp.shape
    NSEQ, NS, _ = xs.shape
    PAST = np.asarray(inputs['cache_fox_k']).shape[2]
    nc = _get_nc(SEQ, PAST, NS)
    maps = make_in_maps(inputs, 8)
    res = run_bass_kernel_spmd(nc, maps, core_ids=list(range(8)))
    return assemble(res.results, BP, SEQ, NSEQ, NS)
```

```python
import contextlib
import math
import numpy as np
import concourse.bass as bass
import concourse.mybir as mybir
from concourse.bass_utils import run_bass_kernel_spmd

F32 = mybir.dt.float32
BF16 = mybir.dt.bfloat16
AF = mybir.ActivationFunctionType
ALU = mybir.AluOpType
AX = mybir.AxisListType

ENGS = ('pe', 'dve', 'act', 'pool', 'sp')

D = 1024
HD = 64
NH = 8
RW = 512
RCOLS = 1792
INC = 3336
DFF = 2816
NFB = DFF // 128
EPS = 1e-6
GN_EPS = 64e-5
C0 = math.exp(-0.5)
NEG = -30000.0


class Buf:
    __slots__ = ('last_write', 'readers')

    def __init__(self):
        self.last_write = None
        self.readers = []


class Prog:
    def __init__(self, nc, st, n_dma_sems=10):
        self.nc = nc
        self.q = {e: [] for e in ENGS}
        self.count = {e: 0 for e in ENGS}
        self.seen = {e: {} for e in ENGS}
        self.n_dma_sems = n_dma_sems
        self.dma_next = {e: 0 for e in ENGS}
        self.dma_val = {}
        names = list(ENGS)
        for e in ('sp', 'pool', 'act'):
            for i in range(n_dma_sems):
                k = f'd_{e}_{i}'
                names.append(k)
                self.dma_val[k] = 0
        self.sems = {n: st.enter_context(nc.semaphore(n)) for n in names}
        self.n_ops = 0
        self.noself = ()
        self.wswap = False

    def _deps(self, q, reads, writes, extra=()):
        need = {}

        def add(tok):
            if tok is None:
                return
            k, v = tok
            if need.get(k, 0) < v:
                need[k] = v
        for b in reads:
            add(b.last_write)
        for b in writes:
            add(b.last_write)
            for r in b.readers:
                add(r)
        for t in extra:
            add(t)
        waits = []
        for k, v in need.items():
            if k == q and (q == 'pe' or q in self.noself):
                continue
            if self.seen[q].get(k, 0) >= v:
                continue
            self.seen[q][k] = v
            waits.append((k, v))
        waits.sort(key=lambda kv: kv[0] == q)
        return waits

    def _mark(self, tok, reads, writes):
        for b in reads:
            if len(b.readers) > 6:
                m = {}
                for k, v in b.readers:
                    if m.get(k, 0) < v:
                        m[k] = v
                b.readers = list(m.items())
            b.readers.append(tok)
        for b in writes:
            b.last_write = tok
            b.readers = []

    def op(self, q, fn, reads=(), writes=()):
        waits = self._deps(q, reads, writes)
        self.count[q] += 1
        tok = (q, self.count[q])
        self.q[q].append((waits, fn, (q, 1)))
        self._mark(tok, reads, writes)
        self.n_ops += 1
        return tok

    def dma(self, q, out, in_, reads=(), writes=(), **kw):
        i = self.dma_next[q]
        self.dma_next[q] = (i + 1) % self.n_dma_sems
        k = f'd_{q}_{i}'
        prev = self.dma_val[k]
        ex = [(k, prev)] if prev > 0 else []
        waits = self._deps(q, reads, writes, ex)
        self.dma_val[k] = prev + 16
        tok = (k, prev + 16)
        self.q[q].append((waits, lambda e: e.dma_start(out=out, in_=in_, **kw), (k, 16)))
        self._mark(tok, reads, writes)
        self.n_ops += 1
        return tok

    def dma_fn(self, q, fn, reads=(), writes=()):
        i = self.dma_next[q]
        self.dma_next[q] = (i + 1) % self.n_dma_sems
        k = f'd_{q}_{i}'
        prev = self.dma_val[k]
        ex = [(k, prev)] if prev > 0 else []
        waits = self._deps(q, reads, writes, ex)
        self.dma_val[k] = prev + 16
        tok = (k, prev + 16)
        self.q[q].append((waits, fn, (k, 16)))
        self._mark(tok, reads, writes)
        self.n_ops += 1
        return tok

    def wait_all_dma(self, q):
        toks = [(k, v) for k, v in self.dma_val.items() if v > 0]
        waits = self._deps(q, (), (), toks)
        self.q[q].append((waits, None, None))

    def emit(self):
        nc = self.nc
        sems = self.sems
        with nc.Block() as block:
            handles = {'pe': block.tensor, 'dve': block.vector, 'act': block.scalar,
                       'pool': block.gpsimd, 'sp': block.sync}
            for e in ENGS:
                ops = self.q[e]
                if not ops:
                    continue

                def body(eng, ops=ops):
                    for waits, fn, inc in ops:
                        if self.wswap:
                            waits = list(reversed(waits))
                        for k, v in waits:
                            eng.wait_ge(sems[k], v)
                        if fn is not None:
                            ins = fn(eng)
                            if inc is not None:
                                ins.then_inc(sems[inc[0]], inc[1])
                handles[e](body)
        self.q = {e: [] for e in ENGS}


class TB:
    def __init__(self, t, n=1):
        self.t = t
        self.bs = [Buf() for _ in range(n)]

    @property
    def b(self):
        return self.bs[0]

    def __getitem__(self, k):
        return self.t[k]


class KB:
    def __init__(self, SEQ, PAST, NS, debug=False, phases=(1, 2, 3), sub=None):
        self.SEQ, self.PAST, self.NS, self.debug = SEQ, PAST, NS, debug
        self.phases = phases
        self.sub = sub or {}
        self.nc = bass.Bass("TRN2", target_bir_lowering=False)
        self.uid = 0

    def bb(self):
        self._bb = (getattr(self, '_bb', -1) + 1) % 4
        return self.pA[2 + self._bb]

    def mm(self, out, lhsT, rhs, start=True, stop=True, R=(), W=()):
        return self.P.op('pe', lambda e: e.matmul(out, lhsT, rhs, start=start, stop=stop), R, W)

    def tr(self, out, in_, ident, R=(), W=()):
        return self.P.op('pe', lambda e: e.transpose(out, in_, ident), R, W)

    def act(self, out, in_, func, bias=None, scale=None, accum=None, R=(), W=()):
        kw = {}
        if bias is not None:
            kw['bias'] = bias
        if scale is not None:
            kw['scale'] = scale
        if accum is not None:
            kw['accum_out'] = accum
        return self.P.op('act', lambda e: e.activation(out, in_, func, **kw), R, W)

    def ts(self, eng, out, in0, s1, s2=None, op0=ALU.mult, op1=None, R=(), W=()):
        if op1 is None:
            return self.P.op(eng, lambda e: e.tensor_scalar(out, in0, s1, None, op0), R, W)
        return self.P.op(eng, lambda e: e.tensor_scalar(out, in0, s1, s2, op0, op1), R, W)

    def tt(self, eng, out, in0, in1, op, R=(), W=()):
        return self.P.op(eng, lambda e: e.tensor_tensor(out, in0, in1, op=op), R, W)

    def stt(self, out, in0, scalar, in1, op0, op1, R=(), W=()):
        return self.P.op('dve', lambda e: e.scalar_tensor_tensor(out, in0, scalar, in1, op0, op1), R, W)

    def cp(self, eng, out, in_, R=(), W=()):
        if eng == 'act':
            return self.P.op('act', lambda e: e.activation(out, in_, AF.Copy), R, W)
        return self.P.op(eng, lambda e: e.tensor_copy(out, in_), R, W)

    def red(self, out, in_, op=ALU.add, R=(), W=()):
        return self.P.op('dve', lambda e: e.tensor_reduce(out, in_, AX.X, op), R, W)

    def recip(self, out, in_, R=(), W=()):
        return self.P.op('dve', lambda e: e.reciprocal(out, in_), R, W)

    def memset(self, eng, ap, val, W=()):
        return self.P.op(eng, lambda e: e.memset(ap, val), (), W)

    def dma(self, q, out, in_, R=(), W=(), **kw):
        return self.P.dma(q, out, in_, R, W, **kw)

    def sb(self, st, shape, dt, n=1, name=None):
        self.uid += 1
        t = st.enter_context(self.nc.sbuf_tensor(f"{name or 's'}_{self.uid}", list(shape), dt))
        return TB(t, n)

    def ps(self, st, shape, dt, n=1, name=None):
        self.uid += 1
        t = st.enter_context(self.nc.psum_tensor(f"{name or 'p'}_{self.uid}", list(shape), dt))
        return TB(t, n)

    def din(self, name, shape, dt=F32):
        return self.nc.dram_tensor(name, list(shape), dt, kind="ExternalInput").ap()

    def dout(self, name, shape, dt=F32):
        return self.nc.dram_tensor(name, list(shape), dt, kind="ExternalOutput").ap()

    def dscr(self, name, shape, dt=BF16):
        kind = "ExternalOutput" if self.debug else "Internal"
        return self.nc.dram_tensor(name, list(shape), dt, kind=kind).ap()

    def build(self):
        nc = self.nc
        SEQ, PAST, NS = self.SEQ, self.PAST, self.NS
        I = {}
        self.I = I
        I['half'] = self.din('half', (1, 1), mybir.dt.int32)
        I['xp3'] = self.din('xp3', (SEQ // 2, D))
        for nm, shp in [('xp', (SEQ, D)), ('xs', (NS, D)), ('cvec', (128, 8, 2)),
                        ('ck', (PAST, 512)), ('cv', (PAST, 512)), ('clf', (PAST, 8)),
                        ('st0', (64, 8, 64)), ('shp0', (128, 14)),
                        ('w_ada', (D, 6 * D)), ('b_ada', (128, 48)), ('w_in', (D, INC)),
                        ('w_out', (D, D)), ('w_g', (D, DFF)), ('w_u', (D, DFF)), ('w_d', (DFF, D)),
                        ('n1g', (128, 8)), ('n2g', (128, 8)), ('mu', (128, 14)),
                        ('w0', (128, 4)), ('a0', (128, 4)), ('k_k', (128, 4)), ('k_a', (128, 4)),
                        ('r_k', (128, 4)), ('gn_g', (128, 4)), ('gn_b', (128, 4)),
                        ('w_dec', (64, 512)), ('w_aaa', (64, 512)), ('w_gup', (128, 512)),
                        ('gqk', (128, 1024)), ('f_b', (128, 8)),
                        ('c_ident', (128, 128)), ('c_triu', (128, 128)), ('c_ones', (128, 128)),
                        ('c_blk', (128, 128)), ('c_maskA', (128, 256)), ('c_maskC', (128, 128)),
                        ('c_maskD', (128, 128)), ('c_reset', (128, 1024)), ('c_ipair', (128, 64))]:
            I[nm] = self.din(nm, shp)
        self.I = I
        O = {}
        for nm, shp in [('y_p', (SEQ // 2, D)), ('y_s', (NS, D)), ('st_p', (8, 64, 64)), ('sh_p', (128, 14)),
                        ('k_p', (SEQ, 512)), ('v_p', (SEQ, 512)), ('lf_p', (SEQ, 8)),
                        ('st_s', (8, 64, 64)), ('sh_s', (128, 14)),
                        ('k_s', (NS, 512)), ('v_s', (NS, 512)), ('lf_s', (NS, 8))]:
            O[nm] = self.dout(nm, shp)
        self.O = O
        self.G = []
        for gi, (T, past) in enumerate([(SEQ, 0), (NS, PAST)]):
            tot = past + T
            nkb = (tot + 127) // 128
            g = dict(gi=gi, T=T, past=past, tot=tot, nkb=nkb,
                     Qs=self.dscr(f'Qs{gi}', (8, 67, T)), Ks=self.dscr(f'Ks{gi}', (8, 67, tot)),
                     Vs=self.dscr(f'Vs{gi}', (8, 128, nkb, 64)), Ys=self.dscr(f'Ys{gi}', (D, T)),
                     x=I['xp'] if gi == 0 else I['xs'])
            g['o_y'], g['o_st'], g['o_sh'], g['o_k'], g['o_v'], g['o_lf'] = (
                (O['y_p'], O['st_p'], O['sh_p'], O['k_p'], O['v_p'], O['lf_p']) if gi == 0 else
                (O['y_s'], O['st_s'], O['sh_s'], O['k_s'], O['v_s'], O['lf_s']))
            self.G.append(g)

        with contextlib.ExitStack() as st:
            self.P = Prog(nc, st)
            self.P.noself = tuple(self.sub.get('noself', ()))
            self.persistent(st)
            ph = self.phases
            with contextlib.ExitStack() as sw1:
                if 1 in ph:
                    self.load_win(sw1)
                with contextlib.ExitStack() as s0:
                    self.phase0(s0)
                    self.P.wait_all_dma('sp')
                    self.P.emit()
                if 1 in ph:
                    with contextlib.ExitStack() as s1:
                        self.phase1(s1)
                        self.P.wait_all_dma('sp')
                        self.P.emit()
            with contextlib.ExitStack() as sw3:
                if 3 in ph:
                    self.load_wgu(sw3)
                if 2 in ph:
                    with contextlib.ExitStack() as s2:
                        self.phase2(s2)
                        self.P.wait_all_dma('sp')
                        self.P.emit()
                with contextlib.ExitStack() as s3:
                    if 3 in ph:
                        self.phase3(s3)
                    self.P.wait_all_dma('sp')
                    self.P.emit()
        return nc

    def persistent(self, st):
        I = self.I
        c = {}
        def ld(nm, shape, dt=F32, q='sp'):
            t = self.sb(st, shape, dt, name=nm)
            self.dma('pool' if dt == BF16 else q, t[:], I[nm], W=[t.b])
            return t
        c['ident_f'] = ld('c_ident', (128, 128))
        c['triu_f'] = ld('c_triu', (128, 128))
        c['ones_f'] = ld('c_ones', (128, 128))
        c['maskA'] = ld('c_maskA', (128, 256))
        c['maskC'] = ld('c_maskC', (128, 128))
        c['reset'] = ld('c_reset', (128, 1024))
        c['ipair'] = ld('c_ipair', (128, 64))
        c['ident_b'] = self.sb(st, (128, 128), BF16, name='identb')
        self.dma('pool', c['ident_b'][:], I['c_ident'], W=[c['ident_b'].b])
        c['blk_b'] = self.sb(st, (128, 128), BF16, name='blkb')
        self.dma('pool', c['blk_b'][:], I['c_blk'], W=[c['blk_b'].b])
        c['maskD_b'] = self.sb(st, (128, 128), BF16, name='maskDb')
        self.dma('pool', c['maskD_b'][:], I['c_maskD'], W=[c['maskD_b'].b])
        for nm in ['n1g', 'n2g']:
            c[nm] = ld(nm, (128, 8))
        c['mu'] = ld('mu', (128, 14))
        for nm in ['w0', 'a0', 'k_k', 'k_a', 'r_k', 'gn_g', 'gn_b']:
            c[nm] = ld(nm, (128, 4))
        c['f_b'] = ld('f_b', (128, 8))
        c['eps'] = self.sb(st, (128, 1), F32, name='eps')
        self.memset('dve', c['eps'][:], EPS, W=[c['eps'].b])
        c['eps64'] = self.sb(st, (128, 1), F32, name='eps64')
        self.memset('dve', c['eps64'][:], 64 * EPS, W=[c['eps64'].b])
        c['gneps'] = self.sb(st, (128, 1), F32, name='gneps')
        self.memset('dve', c['gneps'][:], GN_EPS, W=[c['gneps'].b])
        c['omu'] = self.sb(st, (128, 14), F32, name='omu')
        self.ts('dve', c['omu'][:], c['mu'][:], -1.0, 1.0, ALU.mult, ALU.add, R=[c['mu'].b], W=[c['omu'].b])
        c['omka'] = self.sb(st, (128, 4), F32, name='omka')
        self.ts('dve', c['omka'][:], c['k_a'][:], -1.0, 1.0, ALU.mult, ALU.add, R=[c['k_a'].b], W=[c['omka'].b])
        c['mod'] = self.sb(st, (128, 48, 2), F32, name='mod')
        self.c = c
        for g in self.G:
            g['gm1'] = self.sb(st, (128, 8), F32, name='gm1')
            g['gm2'] = self.sb(st, (128, 8), F32, name='gm2')
            g['sh1'] = self.sb(st, (128, 8), F32, name='sh1')
            g['sh2'] = self.sb(st, (128, 8), F32, name='sh2')
            g['neglc'] = self.sb(st, (128, g['nkb'], 8), F32, name='neglc')
        self.pT = [self.ps(st, (128, 1024), BF16, name='pT') for _ in range(2)]
        self.pA = [self.ps(st, (128, 512), F32, name='pA') for _ in range(6)]

    def load_win(self, st):
        I = self.I
        win = self.sb(st, (128, 8, INC), BF16, name='win')
        wsrc = I['w_in'].rearrange("(k p) c -> p k c", p=128)
        for k in range(8):
            for (a, b) in [(0, 1792), (1792, INC)]:
                self.dma('pool', win[:, k, a:b], wsrc[:, k, a:b], W=[win.b])
        self.win = win

    def load_wgu(self, st):
        I = self.I
        self.wg = self.sb(st, (128, 8, DFF), BF16, name='wg')
        self.wu = self.sb(st, (128, 8, DFF), BF16, name='wu')
        for (dst, src) in [(self.wg, I['w_g']), (self.wu, I['w_u'])]:
            s3 = src.rearrange("(k p) c -> p k c", p=128)
            for k in range(8):
                for a in range(0, DFF, 1408):
                    self.dma('pool', dst[:, k, a:a + 1408], s3[:, k, a:a + 1408], W=[dst.b])

    def phase0(self, st):
        I, c = self.I, self.c
        cv = self.sb(st, (128, 8, 2), F32)
        self.dma('sp', cv[:], I['cvec'], W=[cv.b])
        cs = self.sb(st, (128, 8, 2), F32)
        self.act(cs[:], cv[:], AF.Silu, R=[cv.b], W=[cs.b])
        bada = self.sb(st, (128, 48), F32)
        self.dma('sp', bada[:], I['b_ada'], W=[bada.b])
        wa = [self.sb(st, (128, 8, 512), F32) for _ in range(2)]
        wsrc = I['w_ada'].rearrange("(k p) c -> p k c", p=128)
        pm = self.pA[0]
        for ch in range(12):
            w = wa[ch % 2]
            self.dma('sp', w[:], wsrc[:, :, ch * 512:(ch + 1) * 512], W=[w.b])
            for cbl in range(4):
                gb = ch * 4 + cbl
                for k in range(8):
                    self.mm(pm[:, gb * 2:gb * 2 + 2], w[:, k, cbl * 128:(cbl + 1) * 128], cs[:, k, :],
                            start=(k == 0), stop=(k == 7), R=[w.b, cs.b], W=[pm.b])
        mod = c['mod']
        self.tt('dve', mod[:], pm[:, 0:96].rearrange("p (a b) -> p a b", b=2),
                bada[:].unsqueeze(2).to_broadcast([128, 48, 2]), ALU.add, R=[pm.b, bada.b], W=[mod.b])
        tmp = self.sb(st, (128, 8), F32)
        for g in self.G:
            gi = g['gi']
            for (dst, blk0, ng) in [(g['gm1'], 8, c['n1g']), (g['gm2'], 32, c['n2g'])]:
                self.ts('dve', tmp[:], mod[:, blk0:blk0 + 8, gi], 1.0, None, ALU.add, R=[mod.b], W=[tmp.b])
                self.tt('dve', dst[:], tmp[:], ng[:], ALU.mult, R=[tmp.b, ng.b], W=[dst.b])
            self.cp('dve', g['sh1'][:], mod[:, 0:8, gi], R=[mod.b], W=[g['sh1'].b])
            self.cp('dve', g['sh2'][:], mod[:, 24:32, gi], R=[mod.b], W=[g['sh2'].b])

    def build_gates(self, g, gt1, gt2, tl):
        c = self.c
        mod = c['mod']
        gi = g['gi']
        n = 0
        for (dst, blk0) in [(gt1, 16), (gt2, 40)]:
            for half in range(2):
                pg = self.pA[4 + (n % 2)]
                n += 1
                for j in range(4):
                    blk = half * 4 + j
                    t = tl[blk % 2]
                    self.ts('dve', t[:, 0:128], c['ones_f'][:], mod[:, blk0 + blk, gi:gi + 1], None, ALU.mult,
                            R=[c['ones_f'].b, mod.b], W=[t.b])
                    self.mm(pg[:, j * 128:(j + 1) * 128], t[:, 0:128], c['ident_f'][:], R=[t.b, c['ident_f'].b], W=[pg.b])
                self.cp('act', dst[:, half * 512:(half + 1) * 512], pg[:], R=[pg.b], W=[dst.b])

    def norm_transpose(self, xt, nb, bs, gm, sh, xsb, hT, junk, ss, rstd, pT_sel, alt=False):
        c = self.c
        for blk in range(nb):
            self.act(junk[0:bs, :], xt[0:bs, blk, :], AF.Square, accum=ss[0:bs, blk:blk + 1],
                     R=[xt.b], W=[junk.b, ss.b])
        self.act(rstd[0:bs, 0:nb], ss[0:bs, 0:nb], AF.Sqrt, bias=c['eps'][0:bs, :], scale=1.0 / D,
                 R=[ss.b, c['eps'].b], W=[rstd.b])
        self.recip(rstd[0:bs, 0:nb], rstd[0:bs, 0:nb], R=[rstd.b], W=[rstd.b])
        for blk in range(nb):
            self.act(xsb[0:bs, blk, :], xt[0:bs, blk, :], AF.Copy, scale=rstd[0:bs, blk:blk + 1],
                     R=[xt.b, rstd.b], W=[xsb.b])
        nt = (nb - 1) * 128 + bs
        for k in range(8):
            pt = self.pT[(pT_sel + k) % 2] if alt else self.pT[pT_sel]
            for blk in range(nb):
                self.tr(pt[:, blk * 128:blk * 128 + bs], xsb[0:bs, blk, k * 128:(k + 1) * 128],
                        c['ident_b'][0:bs, 0:bs], R=[xsb.b, c['ident_b'].b], W=[pt.b])
            self.ts('dve', hT[:, k, 0:nt], pt[:, 0:nt], gm[:, k:k + 1], sh[:, k:k + 1], ALU.mult, ALU.add,
                    R=[pt.b, gm.b, sh.b], W=[hT.b])

    def phase1(self, st):
        I, c = self.I, self.c
        win = self.win
        wdec = self.sb(st, (64, 512), BF16, name='wdec')
        self.dma('pool', wdec[:], I['w_dec'], W=[wdec.b])
        waaa = self.sb(st, (128, 512), BF16, name='waaa')
        self.dma('pool', waaa[64:128, :], I['w_aaa'], W=[waaa.b])
        wgup = self.sb(st, (128, 512), BF16, name='wgup')
        self.dma('pool', wgup[:], I['w_gup'], W=[wgup.b])
        gqk = self.sb(st, (128, 1024), F32, name='gqk')
        self.dma('sp', gqk[:], I['gqk'], W=[gqk.b])
        self.win, self.wdec, self.waaa, self.wgup, self.gqk = win, wdec, waaa, wgup, gqk

        NT = 128
        self.NT1 = NT
        W = {}
        W['xt'] = [self.sb(st, (128, 1, D), F32, name='xt') for _ in range(2)]
        W['xsb'] = self.sb(st, (128, 1, D), BF16, name='xsb')
        W['hT'] = self.sb(st, (128, 8, NT), BF16, name='hT')
        W['junk'] = self.sb(st, (128, D), BF16, name='junk')
        W['ss'] = self.sb(st, (128, 4), F32)
        W['rstd'] = self.sb(st, (128, 4), F32)
        W['sq'] = self.sb(st, (128, 1024), F32, name='sq')
        W['ss16'] = self.sb(st, (128, 16), F32)
        W['rs16'] = self.sb(st, (128, 16), F32)
        W['qkn'] = self.sb(st, (128, 1024), F32, name='qkn')
        W['kout'] = [self.sb(st, (128, 512), F32, name='kout') for _ in range(2)]
        W['qkb'] = self.sb(st, (128, 1024), BF16, name='qkb')
        W['QKt'] = [self.sb(st, (128, 8, NT), BF16, name='QKt') for _ in range(2)]
        W['v32'] = [self.sb(st, (128, 512), F32, name='v32') for _ in range(2)]
        W['vb'] = [self.sb(st, (128, 1, 512), BF16, name='vb') for _ in range(2)]
        W['lf'] = [self.sb(st, (128, 8), F32, name='lf') for _ in range(2)]
        W['lft'] = self.sb(st, (128, 8), F32)
        W['lcT'] = self.sb(st, (8, NT), F32)
        W['lcr'] = self.sb(st, (8, NT), F32)
        W['lcs'] = [self.sb(st, (8, 3, NT), BF16) for _ in range(2)]
        W['lchf'] = self.sb(st, (8, NT), F32)
        W['R'] = self.sb(st, (128, 8), F32, name='Racc')
        W['onesb'] = self.sb(st, (3, 2112), BF16, name='onesb')
        self.memset('pool', W['onesb'][:], 1.0, W=[W['onesb'].b])
        for nm in ['z']:
            W[nm] = self.sb(st, (128, 14, NT), F32, name=nm)
        W['t1'] = [self.sb(st, (128, NT), F32, name='t1') for _ in range(2)]
        W['carry'] = self.sb(st, (128, 14), F32, name='carry')
        for nm in ['esig', 'aa', 'kk', 'kp', 'cs', 'gx', 'tmpf', 'beta']:
            W[nm] = self.sb(st, (128, 4, NT), F32, name=nm)
        for nm in ['sqb', 'rkb']:
            W[nm] = self.sb(st, (128, 4, NT), BF16, name=nm)
        W['HO'] = []
        for _ in range(2):
            ho = {}
            for nm in ['BtT', 'KtT', 'BgT', 'KgT', 'vTb', 'gT', 'bonus']:
                ho[nm] = self.sb(st, (128, 4, NT), BF16, name=nm)
            ho['gam'] = self.sb(st, (128, 4, NT), F32, name='gam')
            ho['AR'] = self.sb(st, (128, 4, 1, 2, 128), BF16, name='AR')
            W['HO'].append(ho)
        W['tdw'] = self.sb(st, (128, NT), BF16, name='tdw')
        W['dab'] = self.sb(st, (128, NT), BF16, name='dab')
        W['sg'] = self.sb(st, (128, NT), BF16, name='sg')
        W['nb16'] = self.sb(st, (128, 4, 1), F32, name='nb16')
        W['Bgt'] = self.sb(st, (128, 512), BF16, name='Bgt')
        W['Kgt'] = self.sb(st, (128, 512), BF16, name='Kgt')
        W['Vt'] = self.sb(st, (128, 512), BF16, name='Vt')
        W['MLt'] = self.sb(st, (128, 8, 256), BF16, name='MLt')
        W['MKt'] = self.sb(st, (128, 8, 256), BF16, name='MKt')
        W['Lc'] = [self.sb(st, (128, 8, 128), BF16, name='Lc') for _ in range(2)]
        W['Mc'] = [self.sb(st, (128, 8, 128), BF16, name='Mc') for _ in range(2)]
        W['Xf'] = self.sb(st, (128, 8, 128), F32, name='Xf')
        W['Xb'] = self.sb(st, (128, 8, 128), BF16, name='Xb')
        W['GT'] = self.sb(st, (128, 4, 64), BF16, name='GT')
        W['Hs'] = self.sb(st, (128, 4, 64), F32, name='Hs')
        W['RAT'] = self.sb(st, (128, 4, 128), BF16, name='RAT')
        W['Sf'] = self.sb(st, (128, 4, 64), F32, name='Sf')
        W['Sb'] = self.sb(st, (128, 4, 64), BF16, name='Sb')
        W['ysb'] = self.sb(st, (128, 8, 64), F32, name='ysb')
        W['ysq'] = self.sb(st, (128, 8, 64), F32, name='ysq')
        W['yh'] = self.sb(st, (128, 512), BF16, name='yh')
        W['st8'] = [self.sb(st, (128, 8), F32) for _ in range(4)]
        W['yT1'] = self.sb(st, (128, 4, 128), F32, name='yT1')
        W['yr'] = [self.sb(st, (128, 4, 128), BF16, name='yr') for _ in range(2)]
        self.nyr = 0
        W['Sv'] = self.sb(st, (64, 8, 64), F32, name='Sv')
        W['So'] = self.sb(st, (64, 4, 128), F32, name='So')
        self.W = W

        for g in self.G:
            self.phase1_group(g, NT)

    def phase1_group(self, g, NT):
        I, c, W = self.I, self.c, self.W
        gi, T, past = g['gi'], g['T'], g['past']
        for h in range(NH):
            for a in range(0, g['tot'], 2112):
                b = min(g['tot'], a + 2112)
                self.dma('sp', g['Ks'][h, 64:67, a:b], W['onesb'][:, 0:b - a], R=[W['onesb'].b])
        self.memset('dve', W['R'][:], 0.0, W=[W['R'].b])
        npb = past // 128
        for pb in range(0 if self.sub.get('skip_past') else npb):
            kc = W['kout'][pb % 2]
            self.dma('sp', kc[:], I['ck'][pb * 128:(pb + 1) * 128, :], W=[kc.b])
            self.cp('pool', W['qkb'][:, 512:1024], kc[:], R=[kc.b], W=[W['qkb'].b])
            self.k_transposes(g, pb * 128, 128, 0, only_k=True, blk=pb)
            self.flush_qk(g, pb * 128, 128, only_k=True, blk=pb)
            vc = W['v32'][pb % 2]
            self.dma('sp', vc[:], I['cv'][pb * 128:(pb + 1) * 128, :], W=[vc.b])
            vb = W['vb'][pb % 2]
            self.cp('pool', vb[:, 0, :], vc[:], R=[vc.b], W=[vb.b])
            self.dma('sp', g['Vs'][:, :, pb, :].rearrange("h p d -> p h d"),
                     vb[:, 0, :].rearrange("p (h d) -> p h d", d=64), R=[vb.b])
            lf = W['lf'][pb % 2]
            self.dma('sp', lf[:], I['clf'][pb * 128:(pb + 1) * 128, :], W=[lf.b])
            self.lc_block(g, lf, 128, pb, None, 0)
        if gi == 0:
            self.memset('dve', W['Sf'][:], 0.0, W=[W['Sf'].b])
            self.memset('pool', W['Sb'][:], 0.0, W=[W['Sb'].b])
            self.memset('dve', W['carry'][:], 0.0, W=[W['carry'].b])
        else:
            self.dma('sp', W['carry'][:], I['shp0'], W=[W['carry'].b])
            self.dma('sp', W['Sv'][:], I['st0'], W=[W['Sv'].b])
            for cb in range(4):
                pa = self.pA[cb % 2]
                self.tr(pa[:, 0:64], W['Sv'][:, 2 * cb:2 * cb + 2, :], c['ident_f'][0:64, 0:64],
                        R=[W['Sv'].b, c['ident_f'].b], W=[pa.b])
                self.cp('dve', W['Sf'][:, cb, :], pa[:, 0:64], R=[pa.b], W=[W['Sf'].b])
            self.cp('pool', W['Sb'][:], W['Sf'][:], R=[W['Sf'].b], W=[W['Sb'].b])
        ntile = (T + NT - 1) // NT
        def run(gens):
            gens = [x for x in gens if x[0] is not None]
            while gens:
                for x in list(gens):
                    for _ in range(x[1]):
                        try:
                            next(x[0])
                        except StopIteration:
                            gens.remove(x)
                            break
        genB = None
        for ti in range(ntile):
            nt = min(NT, T - ti * NT)
            genA = self.phase1_tile(g, ti, ti * NT, nt)
            run([(genB, self.sub.get('nb', 1)), (genA, self.sub.get('na', 1))])
            C_ = min(128, nt)
            genB = self.rwkv_chunk(g, ti, ti * NT, 0, C_, nt) if not self.sub.get('skip_rwkv') else None
        run([(genB, 1)])
        if self.sub.get('skip_final'):
            return
        self.dma('sp', g['o_sh'], W['carry'][:], R=[W['carry'].b])
        for cb in range(4):
            pa = self.pA[cb % 2]
            self.tr(pa[0:64, 0:128], W['Sf'][:, cb, :], c['ident_f'][:], R=[W['Sf'].b, c['ident_f'].b], W=[pa.b])
            self.cp('dve', W['So'][:, cb, :], pa[0:64, 0:128], R=[pa.b], W=[W['So'].b])
        self.dma('sp', g['o_st'].rearrange("(cb hh) v k -> v cb hh k", hh=2),
                 W['So'][:].rearrange("v cb (hh k) -> v cb hh k", hh=2), R=[W['So'].b])

    def k_transposes(self, g, tok0, bs, col0, only_k, blk):
        c, W = self.c, self.W
        QKt = W['QKt'][g.get('qkt_sel', 0)]
        for j in range(4 if only_k else 8):
            jj = j + 4 if only_k else j
            pt = self.pT[0]
            self.tr(pt[:, 0:bs], W['qkb'][0:bs, jj * 128:(jj + 1) * 128], c['ident_b'][0:bs, 0:bs],
                    R=[W['qkb'].b, c['ident_b'].b], W=[pt.b])
            self.cp('act', QKt[:, jj, col0:col0 + bs], pt[:, 0:bs], R=[pt.b], W=[QKt.b])

    def flush_qk(self, g, tok0, n, only_k, blk=None, qtok0=None):
        W = self.W
        QKt = W['QKt'][g.get('qkt_sel', 0)]
        for h in range(NH):
            pb = 64 * (h % 2)
            self.dma('sp', g['Ks'][h, 0:64, tok0:tok0 + n], QKt[pb:pb + 64, 4 + h // 2, 0:n], R=[QKt.b])
            if not only_k:
                self.dma('sp', g['Qs'][h, 0:64, qtok0:qtok0 + n], QKt[pb:pb + 64, h // 2, 0:n], R=[QKt.b])
        g['qkt_sel'] = 1 - g.get('qkt_sel', 0)

    def lc_block(self, g, lf, bs, kb, lcT_cols, col0):
        c, W = self.c, self.W
        R = W['R']
        pa = self.pA[1]
        self.mm(pa[0:bs, 0:8], c['triu_f'][0:bs, 0:bs], lf[0:bs, :], start=True, stop=False,
                R=[c['triu_f'].b, lf.b], W=[pa.b])
        self.mm(pa[0:bs, 0:8], c['ones_f'][:, 0:bs], R[:, :], start=False, stop=True, R=[c['ones_f'].b, R.b], W=[pa.b])
        self.ts('dve', g['neglc'][0:bs, kb, :], pa[0:bs, 0:8], -1.0, None, ALU.mult, R=[pa.b], W=[g['neglc'].b])
        if lcT_cols is not None:
            o = 16 + col0
            self.mm(pa[0:8, o:o + bs], lf[0:bs, :], c['triu_f'][0:bs, 0:bs], start=True, stop=False,
                    R=[lf.b, c['triu_f'].b], W=[pa.b])
            self.mm(pa[0:8, o:o + bs], R[:, :], c['ones_f'][:, 0:bs], start=False, stop=True,
                    R=[R.b, c['ones_f'].b], W=[pa.b])
            self.cp('dve', W['lcT'][:, col0:col0 + bs], pa[0:8, o:o + bs], R=[pa.b], W=[W['lcT'].b])
        self.tt('dve', R[0:bs, :], R[0:bs, :], lf[0:bs, :], ALU.add, R=[R.b, lf.b], W=[R.b])

    def phase1_tile(self, g, ti, tok0, nt):
        I, c, W = self.I, self.c, self.W
        gi, past = g['gi'], g['past']
        nb = (nt + 127) // 128
        bs = min(128, nt)
        xt = W['xt'][ti % 2]
        self.dma('sp', xt[0:bs, 0:nb, :], g['x'][tok0:tok0 + nt, :].rearrange("(b p) d -> p b d", p=bs), W=[xt.b])
        hT = W['hT']
        self.norm_transpose(xt, nb, bs, g['gm1'], g['sh1'], W['xsb'], hT, W['junk'], W['ss'], W['rstd'], 0)
        win = self.win
        yield
        for blk in range(0 if self.sub.get('skip_fox') else nb):
            kb = (past + tok0) // 128 + blk
            pq, pk, pv, pf = self.pA[0], self.pA[1], self.pA[0], self.pA[1]
            def proj(pp, c0, wdt):
                for k in range(8):
                    self.mm(pp[0:bs, 0:wdt], hT[:, k, blk * 128:blk * 128 + bs], win[:, k, c0:c0 + wdt],
                            start=(k == 0), stop=(k == 7), R=[hT.b, win.b], W=[pp.b])
            proj(pq, 1792, 512)
            proj(pk, 2304, 512)
            yield
            sq, ss16, rs16, qkn = W['sq'], W['ss16'], W['rs16'], W['qkn']
            self.act(sq[0:bs, 0:512], pq[0:bs, :], AF.Square, R=[pq.b], W=[sq.b])
            self.act(sq[0:bs, 512:1024], pk[0:bs, :], AF.Square, R=[pk.b], W=[sq.b])
            self.red(ss16[0:bs, :], sq[0:bs, :].rearrange("p (g d) -> p g d", d=64), R=[sq.b], W=[ss16.b])
            self.act(rs16[0:bs, 0:8], ss16[0:bs, 0:8], AF.Sqrt, bias=c['eps64'][0:bs, :], scale=1.0,
                     R=[ss16.b, c['eps64'].b], W=[rs16.b])
            self.act(rs16[0:bs, 8:16], ss16[0:bs, 8:16], AF.Sqrt, bias=c['eps'][0:bs, :], scale=1.0 / 64,
                     R=[ss16.b, c['eps'].b], W=[rs16.b])
            self.recip(rs16[0:bs, :], rs16[0:bs, :], R=[rs16.b], W=[rs16.b])
            self.tt('dve', qkn[0:bs, 0:512].rearrange("p (g d) -> p g d", d=64),
                    pq[0:bs, :].rearrange("p (g d) -> p g d", d=64),
                    rs16[0:bs, 0:8].unsqueeze(2).to_broadcast([bs, 8, 64]), ALU.mult, R=[pq.b, rs16.b], W=[qkn.b])
            self.tt('dve', qkn[0:bs, 512:1024].rearrange("p (g d) -> p g d", d=64),
                    pk[0:bs, :].rearrange("p (g d) -> p g d", d=64),
                    rs16[0:bs, 8:16].unsqueeze(2).to_broadcast([bs, 8, 64]), ALU.mult, R=[pk.b, rs16.b], W=[qkn.b])
            kout = W['kout'][blk % 2]
            self.tt('dve', W['qkb'][0:bs, 0:512], qkn[0:bs, 0:512], self.gqk[0:bs, 0:512], ALU.mult,
                    R=[qkn.b, self.gqk.b], W=[W['qkb'].b])
            self.tt('dve', kout[0:bs, :], qkn[0:bs, 512:1024], self.gqk[0:bs, 512:1024], ALU.mult,
                    R=[qkn.b, self.gqk.b], W=[kout.b])
            self.dma('sp', g['o_k'][tok0 + blk * 128:tok0 + blk * 128 + bs, :], kout[0:bs, :], R=[kout.b])
            self.cp('pool', W['qkb'][0:bs, 512:1024], kout[0:bs, :], R=[kout.b], W=[W['qkb'].b])
            self.k_transposes(g, tok0, bs, blk * 128, only_k=False, blk=blk)
            yield
            proj(pv, 2816, 512)
            proj(pf, 3328, 8)
            v32 = W['v32'][blk % 2]
            self.cp('act', v32[0:bs, :], pv[0:bs, :], R=[pv.b], W=[v32.b])
            self.dma('sp', g['o_v'][tok0 + blk * 128:tok0 + blk * 128 + bs, :], v32[0:bs, :], R=[v32.b])
            vb = W['vb'][ti % 2]
            self.cp('pool', vb[0:bs, blk, :], v32[0:bs, :], R=[v32.b], W=[vb.b])
            yield
            lf = W['lf'][blk % 2]
            self.tt('dve', W['lft'][0:bs, :], pf[0:bs, 0:8], c['f_b'][0:bs, :], ALU.add, R=[pf.b, c['f_b'].b], W=[W['lft'].b])
            self.act(W['lft'][0:bs, :], W['lft'][0:bs, :], AF.Exp, scale=-1.0, R=[W['lft'].b], W=[W['lft'].b])
            self.act(W['lft'][0:bs, :], W['lft'][0:bs, :], AF.Ln, bias=1.0, scale=1.0, R=[W['lft'].b], W=[W['lft'].b])
            self.ts('dve', lf[0:bs, :], W['lft'][0:bs, :], -1.0, None, ALU.mult, R=[W['lft'].b], W=[lf.b])
            self.dma('sp', g['o_lf'][tok0 + blk * 128:tok0 + blk * 128 + bs, :], lf[0:bs, :], R=[lf.b])
            self.lc_block(g, lf, bs, kb, True, blk * 128)
        if self.sub.get('skip_fox'):
            if not self.sub.get('skip_rwkv'):
                yield from self.rwkv_tile(g, ti, tok0, nt)
            return
        self.flush_qk(g, past + tok0, nt, only_k=False, qtok0=tok0)
        vb = W['vb'][ti % 2]
        kb0 = (past + tok0) // 128
        self.dma('sp', g['Vs'][:, 0:bs, kb0:kb0 + nb, :].rearrange("h p b d -> p h b d"),
                 vb[0:bs, 0:nb, :].rearrange("p b (h d) -> p h b d", d=64), R=[vb.b])
        yield
        lcs = W['lcs'][ti % 2]
        lcT, lcr, lchf = W['lcT'], W['lcr'], W['lchf']
        self.cp('dve', lcs[:, 0, 0:nt], lcT[:, 0:nt], R=[lcT.b], W=[lcs.b])
        self.tt('dve', lcr[:, 0:nt], lcT[:, 0:nt], lcs[:, 0, 0:nt], ALU.subtract, R=[lcT.b, lcs.b], W=[lcr.b])
        self.cp('dve', lcs[:, 1, 0:nt], lcr[:, 0:nt], R=[lcr.b], W=[lcs.b])
        self.tt('dve', lchf[:, 0:nt], lcr[:, 0:nt], lcs[:, 1, 0:nt], ALU.subtract, R=[lcr.b, lcs.b], W=[lchf.b])
        self.cp('dve', lcs[:, 2, 0:nt], lchf[:, 0:nt], R=[lchf.b], W=[lcs.b])
        self.dma('sp', g['Qs'][:, 64:67, tok0:tok0 + nt], lcs[:, :, 0:nt], R=[lcs.b])
        if not self.sub.get('skip_rwkv'):
            yield from self.rwkv_tile(g, ti, tok0, nt)
        yield

    def rwkv_tile(self, g, ti, tok0, nt):
        I, c, W = self.I, self.c, self.W
        ho = W['HO'][ti % 2]
        win, hT = self.win, W['hT']
        C = min(128, nt)
        nch = nt // C
        z = W['z']
        carry = W['carry']
        for cb in range(14):
            pp = self.pA[cb % 2]
            for k in range(8):
                self.mm(pp[:, 0:nt], win[:, k, cb * 128:(cb + 1) * 128], hT[:, k, 0:nt],
                        start=(k == 0), stop=(k == 7), R=[win.b, hT.b], W=[pp.b])
            t1 = W['t1'][cb % 2]
            v = self.sub.get('v1', 15)
            if v & 1:
                self.act(t1[:, 1:nt], pp[:, 0:nt - 1], AF.Copy, scale=c['mu'][:, cb:cb + 1], R=[pp.b, c['mu'].b], W=[t1.b])
            if v & 2:
                self.ts('dve' if v & 16 else 'pool', t1[:, 0:1], carry[:, cb:cb + 1], c['mu'][:, cb:cb + 1], None, ALU.mult,
                        R=[carry.b, c['mu'].b], W=[t1.b])
            if v & 4:
                self.stt(z[:, cb, 0:nt], pp[:, 0:nt], c['omu'][:, cb:cb + 1], t1[:, 0:nt], ALU.mult, ALU.add,
                         R=[pp.b, c['omu'].b, t1.b], W=[z.b])
            if v & 8:
                self.cp('act', carry[:, cb:cb + 1], pp[:, nt - 1:nt], R=[pp.b], W=[carry.b])
            if cb % 2 == 1:
                yield
        if self.sub.get('rstop', 99) <= 1:
            return
        zr, zk, zv = z[:, 0:4, 0:nt], z[:, 4:8, 0:nt], z[:, 8:12, 0:nt]
        tdw, dab, sg = W['tdw'], W['dab'], W['sg']
        self.act(tdw[0:64, 0:nt], z[0:64, 12, 0:nt], AF.Tanh, R=[z.b], W=[tdw.b])
        self.cp('pool', dab[64:128, 0:nt], z[64:128, 12, 0:nt], R=[z.b], W=[dab.b])
        self.act(sg[:, 0:nt], z[:, 13, 0:nt], AF.Sigmoid, R=[z.b], W=[sg.b])
        esig, aa, gT = W['esig'], W['aa'], ho['gT']
        for cb in range(4):
            p1, p2, p3 = self.pA[0], self.pA[1], self.pA[0]
            o = (cb % 2) * 256
            self.mm(p1[:, o:o + nt], self.wdec[0:64, cb * 128:(cb + 1) * 128], tdw[0:64, 0:nt], R=[self.wdec.b, tdw.b], W=[p1.b])
            self.act(esig[:, cb, 0:nt], p1[:, o:o + nt], AF.Sigmoid, bias=c['w0'][:, cb:cb + 1], R=[p1.b, c['w0'].b], W=[esig.b])
            self.mm(p2[:, o:o + nt], self.waaa[64:128, cb * 128:(cb + 1) * 128], dab[64:128, 0:nt], R=[self.waaa.b, dab.b], W=[p2.b])
            self.act(aa[:, cb, 0:nt], p2[:, o:o + nt], AF.Sigmoid, bias=c['a0'][:, cb:cb + 1], R=[p2.b, c['a0'].b], W=[aa.b])
            self.mm(p3[:, o:o + nt], self.wgup[:, cb * 128:(cb + 1) * 128], sg[:, 0:nt], R=[self.wgup.b, sg.b], W=[p3.b])
            self.cp('dve', gT[:, cb, 0:nt], p3[:, o:o + nt], R=[p3.b], W=[gT.b])
        if self.sub.get('rstop', 99) <= 2:
            return
        yield
        kk, kp, sqb, tmpf = W['kk'], W['kp'], W['sqb'], W['tmpf']
        for cb in range(4):
            self.ts('dve', kk[:, cb, 0:nt], z[:, 4 + cb, 0:nt], c['k_k'][:, cb:cb + 1], None, ALU.mult, R=[z.b, c['k_k'].b], W=[kk.b])
        self.act(sqb[:, :, 0:nt], kk[:, :, 0:nt], AF.Square, R=[kk.b], W=[sqb.b])
        for cb in range(4):
            pp = self.pA[cb // 2]
            o = (cb % 2) * 256
            self.mm(pp[:, o:o + nt], c['blk_b'][:], sqb[:, cb, 0:nt], R=[c['blk_b'].b, sqb.b], W=[pp.b])
            self.act(tmpf[:, cb, 0:nt], pp[:, o:o + nt], AF.Sqrt, R=[pp.b], W=[tmpf.b])
        self.ts('dve', tmpf[:, :, 0:nt], tmpf[:, :, 0:nt], 1e-12, None, ALU.max, R=[tmpf.b], W=[tmpf.b])
        self.recip(tmpf[:, :, 0:nt], tmpf[:, :, 0:nt], R=[tmpf.b], W=[tmpf.b])
        self.tt('dve', kk[:, :, 0:nt], kk[:, :, 0:nt], tmpf[:, :, 0:nt], ALU.mult, R=[kk.b, tmpf.b], W=[kk.b])
        yield
        for cb in range(4):
            self.ts('dve', tmpf[:, cb, 0:nt], aa[:, cb, 0:nt], c['k_a'][:, cb:cb + 1], c['omka'][:, cb:cb + 1], ALU.mult, ALU.add,
                    R=[aa.b, c['k_a'].b, c['omka'].b], W=[tmpf.b])
        self.tt('dve', kp[:, :, 0:nt], zk, tmpf[:, :, 0:nt], ALU.mult, R=[z.b, tmpf.b], W=[kp.b])
        yield
        bonus, rkb = ho['bonus'], W['rkb']
        self.tt('dve', tmpf[:, :, 0:nt], zr, kp[:, :, 0:nt], ALU.mult, R=[z.b, kp.b], W=[tmpf.b])
        for cb in range(4):
            self.ts('dve', rkb[:, cb, 0:nt], tmpf[:, cb, 0:nt], c['r_k'][:, cb:cb + 1], None, ALU.mult, R=[tmpf.b, c['r_k'].b], W=[rkb.b])
        for cb in range(4):
            pp = self.pA[cb // 2]
            o = (cb % 2) * 256
            self.mm(pp[:, o:o + nt], c['blk_b'][:], rkb[:, cb, 0:nt], R=[c['blk_b'].b, rkb.b], W=[pp.b])
            self.tt('dve', bonus[:, cb, 0:nt], pp[:, o:o + nt], z[:, 8 + cb, 0:nt], ALU.mult, R=[pp.b, z.b], W=[bonus.b])
        if self.sub.get('rstop', 99) <= 3:
            return
        yield
        cs, gam, gx, beta = W['cs'], ho['gam'], W['gx'], W['beta']
        NTf = self.NT1
        if nt == NTf:
            self.P.op('dve', lambda e: e.tensor_tensor_scan(cs[:].rearrange("p a t -> p (a t)"), c['reset'][:, 0:4 * nt],
                                                            esig[:].rearrange("p a t -> p (a t)"), 0.0, op0=ALU.mult, op1=ALU.add),
                      [c['reset'].b, esig.b], [cs.b])
        else:
            for cb in range(4):
                self.P.op('dve', lambda e, cb=cb: e.tensor_tensor_scan(cs[:, cb, 0:nt], c['reset'][:, 0:nt], esig[:, cb, 0:nt], 0.0,
                                                                      op0=ALU.mult, op1=ALU.add),
                          [c['reset'].b, esig.b], [cs.b])
        self.act(gam[:, :, 0:nt], cs[:, :, 0:nt], AF.Exp, scale=-C0, R=[cs.b], W=[gam.b])
        AR, BtT, KtT, BgT, KgT, vTb = ho['AR'], ho['BtT'], ho['KtT'], ho['BgT'], ho['KgT'], ho['vTb']
        def chv(ap):
            return ap.rearrange("p a (n c) -> p a n c", c=C)
        self.tt('dve', AR[:, :, 0:nch, 1, 0:C], chv(zr), chv(gam[:, :, 0:nt]), ALU.mult, R=[z.b, gam.b], W=[AR.b])
        self.tt('dve', beta[:, :, 0:nt], kk[:, :, 0:nt], aa[:, :, 0:nt], ALU.mult, R=[kk.b, aa.b], W=[beta.b])
        yield
        self.act(gx[:, :, 0:nt], cs[:, :, 0:nt], AF.Exp, scale=C0, R=[cs.b], W=[gx.b])
        self.tt('dve', BtT[:, :, 0:nt], beta[:, :, 0:nt], gx[:, :, 0:nt], ALU.mult, R=[beta.b, gx.b], W=[BtT.b])
        self.tt('dve', KtT[:, :, 0:nt], kp[:, :, 0:nt], gx[:, :, 0:nt], ALU.mult, R=[kp.b, gx.b], W=[KtT.b])
        yield
        self.tt('dve', tmpf[:, :, 0:nt], cs[:, :, 0:nt], esig[:, :, 0:nt], ALU.subtract, R=[cs.b, esig.b], W=[tmpf.b])
        self.act(gx[:, :, 0:nt], tmpf[:, :, 0:nt], AF.Exp, scale=-C0, R=[tmpf.b], W=[gx.b])
        self.tt('dve', AR[:, :, 0:nch, 0, 0:C], chv(kk[:, :, 0:nt]), chv(gx[:, :, 0:nt]), ALU.mult, R=[kk.b, gx.b], W=[AR.b])
        yield
        nb16 = W['nb16']
        self.ts('dve', nb16[:, :, 0:nch], cs[:, :, C - 1:nt:C], -C0, None, ALU.mult, R=[cs.b], W=[nb16.b])
        for cb in range(4):
            for ch in range(nch):
                self.act(gx[:, cb, ch * C:(ch + 1) * C], cs[:, cb, ch * C:(ch + 1) * C], AF.Exp, bias=nb16[:, cb, ch:ch + 1], scale=C0,
                         R=[cs.b, nb16.b], W=[gx.b])
        self.tt('dve', BgT[:, :, 0:nt], beta[:, :, 0:nt], gx[:, :, 0:nt], ALU.mult, R=[beta.b, gx.b], W=[BgT.b])
        self.tt('dve', KgT[:, :, 0:nt], kp[:, :, 0:nt], gx[:, :, 0:nt], ALU.mult, R=[kp.b, gx.b], W=[KgT.b])
        self.cp('pool', vTb[:, :, 0:nt], zv, R=[z.b], W=[vTb.b])
        if self.sub.get('rstop', 99) <= 4:
            return

    def rwkv_chunk(self, g, ti, tok0, ch, C, nt):
        c, W = self.c, self.W
        ho = W['HO'][ti % 2]
        AR, BtT, KtT, BgT, KgT, vTb = ho['AR'], ho['BtT'], ho['KtT'], ho['BgT'], ho['KgT'], ho['vTb']
        Bgt, Kgt, Vt, MLt, MKt, Xf, Xb = W['Bgt'], W['Kgt'], W['Vt'], W['MLt'], W['MKt'], W['Xf'], W['Xb']
        sl = slice(ch * C, (ch + 1) * C)
        idb = c['ident_b']
        pt = self.pT[1]
        for cb in range(4):
            self.tr(pt[0:C, cb * 128:(cb + 1) * 128], AR[:, cb, ch, 0, 0:C], idb[:], R=[AR.b, idb.b], W=[pt.b])
        self.ts('dve', Xb[0:C, :, 0:64], pt[0:C, 0:512].rearrange("p (h d) -> p h d", d=64), -1.0, None, ALU.mult, R=[pt.b], W=[Xb.b])
        for (src, dst, eng, pi) in [(BgT, Bgt, 'act', 1), (KgT, Kgt, 'act', 0), (vTb, Vt, 'act', 1)]:
            pt = self.pT[1]
            for cb in range(4):
                self.tr(pt[0:C, cb * 128:(cb + 1) * 128], src[:, cb, sl], idb[:], R=[src.b, idb.b], W=[pt.b])
            self.cp(eng, dst[0:C, :], pt[0:C, 0:512], R=[pt.b], W=[dst.b])
        if self.sub.get('rstop', 99) <= 5:
            return
        yield
        mA = c['maskA'][0:C, :].rearrange("p (a c) -> p a c", a=2)[:, :, 0:C]
        Lc0 = W['Lc'][0]
        for par in range(2):
            pb_ = 64 * par
            for half in range(2):
                pA_, pB_, pc = self.bb(), self.bb(), self.bb()
                for j in range(2):
                    cb = 2 * half + j
                    ar = AR[pb_:pb_ + 64, cb, ch, :, 0:C]
                    o = j * 256
                    self.mm(pA_[0:C, o:o + 2 * C].rearrange("p (a c) -> p a c", a=2), BtT[pb_:pb_ + 64, cb, sl], ar,
                            R=[BtT.b, AR.b], W=[pA_.b])
                    self.mm(pB_[0:C, o:o + 2 * C].rearrange("p (a c) -> p a c", a=2), KtT[pb_:pb_ + 64, cb, sl], ar,
                            R=[KtT.b, AR.b], W=[pB_.b])
                    self.mm(pc[0:C, j * 128:j * 128 + C], AR[pb_:pb_ + 64, cb, ch, 0, 0:C], BtT[pb_:pb_ + 64, cb, sl],
                            R=[AR.b, BtT.b], W=[pc.b])
                h0 = 2 * (2 * half) + par
                mA4 = mA.unsqueeze(1).to_broadcast([C, 2, 2, C])
                self.tt('dve', MLt[0:C, h0:h0 + 3:2, :].rearrange("p h (a c) -> p h a c", a=2)[:, :, :, 0:C],
                        pA_[0:C, 0:512].rearrange("p (h x) -> p h x", h=2)[:, :, 0:2 * C].rearrange("p h (a c) -> p h a c", a=2), mA4, ALU.mult,
                        R=[pA_.b, c['maskA'].b], W=[MLt.b])
                self.tt('dve', MKt[0:C, h0:h0 + 3:2, :].rearrange("p h (a c) -> p h a c", a=2)[:, :, :, 0:C],
                        pB_[0:C, 0:512].rearrange("p (h x) -> p h x", h=2)[:, :, 0:2 * C].rearrange("p h (a c) -> p h a c", a=2), mA4, ALU.mult,
                        R=[pB_.b, c['maskA'].b], W=[MKt.b])
                self.tt('dve', Lc0[0:C, h0:h0 + 3:2, 0:C],
                        pc[0:C, 0:256].rearrange("p (h c) -> p h c", h=2)[:, :, 0:C],
                        c['maskC'][0:C, 0:C].unsqueeze(1).to_broadcast([C, 2, C]), ALU.mult,
                        R=[pc.b, c['maskC'].b], W=[Lc0.b])
                yield
        if self.sub.get('rstop', 99) <= 6:
            return
        yield
        pl = self.bb()
        for h in range(NH):
            self.mm(pl[0:C, h * 64:(h + 1) * 64], MKt[0:C, h, 0:C], Vt[0:C, h * 64:(h + 1) * 64], R=[MKt.b, Vt.b], W=[pl.b])
        self.cp('act', Xb[0:C, :, 64:128], pl[0:C, :].rearrange("p (h d) -> p h d", d=64), R=[pl.b], W=[Xb.b])
        if self.sub.get('rstop', 99) <= 7:
            return
        yield
        nlev = int(round(math.log2(C)))
        Lc, Mc = W['Lc'], W['Mc']
        for lev in range(nlev):
            Lcur = Lc[lev % 2]
            Lnx = Lc[(lev + 1) % 2]
            Mnx = Mc[(lev + 1) % 2]
            def Mcur(h):
                return MLt[0:C, h, 0:C] if lev == 0 else Mc[lev % 2][0:C, h, 0:C]
            Mb = MLt.b if lev == 0 else Mc[lev % 2].b
            for half in range(2):
                yield
                px = self.bb()
                for hh in range(4):
                    h = 4 * half + hh
                    self.mm(px[0:C, hh * 128:hh * 128 + 128], Mcur(h), Xb[0:C, h, :], R=[Mb, Xb.b], W=[px.b])
                if lev < nlev - 1:
                    pm_, pl_ = self.bb(), self.bb()
                    for hh in range(4):
                        h = 4 * half + hh
                        self.mm(pm_[0:C, hh * 128:hh * 128 + C], Lcur[0:C, h, 0:C], Mcur(h), R=[Lcur.b, Mb], W=[pm_.b])
                        self.mm(pl_[0:C, hh * 128:hh * 128 + C], Mcur(h), Lcur[0:C, h, 0:C], R=[Lcur.b, Mb], W=[pl_.b])
                self.tt('dve', Xb[0:C, 4 * half:4 * half + 4, :], Xb[0:C, 4 * half:4 * half + 4, :],
                        px[0:C, :].rearrange("p (h d) -> p h d", d=128), ALU.add, R=[px.b, Xb.b], W=[Xb.b])
                if lev < nlev - 1:
                    self.cp('act', Mnx[0:C, 4 * half:4 * half + 4, 0:C],
                            pm_[0:C, :].rearrange("p (h d) -> p h d", d=128)[:, :, 0:C], R=[pm_.b], W=[Mnx.b])
                    self.cp('act', Lnx[0:C, 4 * half:4 * half + 4, 0:C],
                            pl_[0:C, :].rearrange("p (h d) -> p h d", d=128)[:, :, 0:C], R=[pl_.b], W=[Lnx.b])
        if self.sub.get('rstop', 99) <= 8:
            return
        yield
        GT, Hs, RAT, Sf, Sb = W['GT'], W['Hs'], W['RAT'], W['Sf'], W['Sb']
        gam = ho['gam']
        for par in range(2):
            pb_ = 64 * par
            pg, ph, pr = self.bb(), self.bb(), self.bb()
            for cb in range(4):
                h = 2 * cb + par
                self.mm(pg[pb_:pb_ + 64, cb * 64:(cb + 1) * 64], Xb[0:C, h, 0:64], Bgt[0:C, h * 64:(h + 1) * 64], R=[Xb.b, Bgt.b], W=[pg.b])
                self.mm(ph[pb_:pb_ + 64, cb * 64:(cb + 1) * 64], Bgt[0:C, h * 64:(h + 1) * 64], Xb[0:C, h, 64:128], start=True, stop=False,
                        R=[Xb.b, Bgt.b], W=[ph.b])
                self.mm(ph[pb_:pb_ + 64, cb * 64:(cb + 1) * 64], Kgt[0:C, h * 64:(h + 1) * 64], Vt[0:C, h * 64:(h + 1) * 64], start=False, stop=True,
                        R=[Kgt.b, Vt.b], W=[ph.b])
                self.mm(pr[pb_:pb_ + 64, cb * 128:cb * 128 + C], Xb[0:C, h, 0:64], MLt[0:C, h, 128:128 + C], R=[Xb.b, MLt.b], W=[pr.b])
            for cb in range(4):
                gC = gam[pb_:pb_ + 64, cb, ch * C + C - 1:ch * C + C]
                self.stt(GT[pb_:pb_ + 64, cb, :], c['ipair'][pb_:pb_ + 64, :], gC, pg[pb_:pb_ + 64, cb * 64:(cb + 1) * 64], ALU.mult, ALU.add,
                         R=[c['ipair'].b, gam.b, pg.b], W=[GT.b])
            self.cp('act', Hs[pb_:pb_ + 64, :, :], ph[pb_:pb_ + 64, 0:256].rearrange("p (a d) -> p a d", d=64), R=[ph.b], W=[Hs.b])
            self.tt('dve', RAT[pb_:pb_ + 64, :, 0:C], pr[pb_:pb_ + 64, :].rearrange("p (a d) -> p a d", d=128)[:, :, 0:C],
                    AR[pb_:pb_ + 64, :, ch, 1, 0:C], ALU.add, R=[pr.b, AR.b], W=[RAT.b])
        if self.sub.get('rstop', 99) <= 9:
            return
        yield
        ysb, ysq = W['ysb'], W['ysq']
        s10 = self.sub.get('s10', 3)
        if s10 & 1:
            for par in range(2):
                pb_ = 64 * par
                py = self.bb()
                py2 = self.bb() if C != 128 else None
                for cb in range(4):
                    h = 2 * cb + par
                    o = cb * 64
                    self.mm(py[0:C, o:o + 64], MLt[0:C, h, 128:128 + C], Xb[0:C, h, 64:128], start=True, stop=False, R=[MLt.b, Xb.b], W=[py.b])
                    if C == 128:
                        self.mm(py[0:C, o:o + 64], MKt[0:C, h, 128:128 + C], Vt[0:C, h * 64:(h + 1) * 64], start=False, stop=False, R=[MKt.b, Vt.b], W=[py.b])
                        self.mm(py[0:C, o:o + 64], RAT[pb_:pb_ + 64, cb, 0:C], Sb[pb_:pb_ + 64, cb, :], start=False, stop=True, R=[RAT.b, Sb.b], W=[py.b])
                    else:
                        self.mm(py[0:C, o:o + 64], MKt[0:C, h, 128:128 + C], Vt[0:C, h * 64:(h + 1) * 64], start=False, stop=True, R=[MKt.b, Vt.b], W=[py.b])
                        self.mm(py2[0:C, o:o + 64], RAT[pb_:pb_ + 64, cb, 0:C], Sb[pb_:pb_ + 64, cb, :], start=True, stop=True, R=[RAT.b, Sb.b], W=[py2.b])
                if C != 128:
                    self.cp('act', ysq[0:C, 0:4, :], py2[0:C, 0:256].rearrange("p (a d) -> p a d", d=64), R=[py2.b], W=[ysq.b])
                    self.tt('dve', ysb[0:C, par:8:2, :], py[0:C, 0:256].rearrange("p (a d) -> p a d", d=64), ysq[0:C, 0:4, :], ALU.add,
                            R=[py.b, ysq.b], W=[ysb.b])
                    continue
                if s10 & 4:
                    self.cp('dve', ysb[0:C, par:8:2, :], py[0:C, 0:256].rearrange("p (a d) -> p a d", d=64), R=[py.b], W=[ysb.b])
                elif s10 & 8:
                    pass
                else:
                    self.cp('act', ysb[0:C, par:8:2, :], py[0:C, 0:256].rearrange("p (a d) -> p a d", d=64), R=[py.b], W=[ysb.b])
        if s10 & 2:
            pSs = [self.bb(), self.bb()]
            for par in range(2):
                pb_ = 64 * par
                pS = pSs[par]
                for cb in range(4):
                    self.mm(pS[pb_:pb_ + 64, cb * 64:(cb + 1) * 64], GT[pb_:pb_ + 64, cb, :], Sb[pb_:pb_ + 64, cb, :], R=[GT.b, Sb.b], W=[pS.b])
            for par in range(2):
                pb_ = 64 * par
                pS = pSs[par]
                self.tt('dve', Sf[pb_:pb_ + 64, :, :], pS[pb_:pb_ + 64, 0:256].rearrange("p (a d) -> p a d", d=64), Hs[pb_:pb_ + 64, :, :], ALU.add,
                        R=[pS.b, Hs.b], W=[Sf.b])
            self.cp('pool', Sb[:], Sf[:], R=[Sf.b], W=[Sb.b])
        if self.sub.get('rstop', 99) <= 10:
            return
        yield
        s8 = W['st8']
        self.red(s8[0][0:C, :], ysb[0:C, :, :], R=[ysb.b], W=[s8[0].b])
        self.act(ysq[0:C, :, :], ysb[0:C, :, :], AF.Square, R=[ysb.b], W=[ysq.b])
        self.red(s8[1][0:C, :], ysq[0:C, :, :], R=[ysq.b], W=[s8[1].b])
        self.ts('dve', s8[0][0:C, :], s8[0][0:C, :], 1.0 / 64, None, ALU.mult, R=[s8[0].b], W=[s8[0].b])
        self.tt('dve', s8[2][0:C, :], s8[0][0:C, :], s8[0][0:C, :], ALU.mult, R=[s8[0].b], W=[s8[2].b])
        self.stt(s8[1][0:C, :], s8[1][0:C, :], 1.0 / 64, s8[2][0:C, :], ALU.mult, ALU.subtract, R=[s8[1].b, s8[2].b], W=[s8[1].b])
        self.act(s8[1][0:C, :], s8[1][0:C, :], AF.Sqrt, bias=c['gneps'][0:C, :], scale=1.0, R=[s8[1].b, c['gneps'].b], W=[s8[1].b])
        self.recip(s8[1][0:C, :], s8[1][0:C, :], R=[s8[1].b], W=[s8[1].b])
        self.tt('dve', ysb[0:C, :, :], ysb[0:C, :, :], s8[0][0:C, :].unsqueeze(2).to_broadcast([C, 8, 64]), ALU.subtract, R=[ysb.b, s8[0].b], W=[ysb.b])
        yh = W['yh']
        self.tt('dve', yh[0:C, :].rearrange("p (h d) -> p h d", d=64), ysb[0:C, :, :], s8[1][0:C, :].unsqueeze(2).to_broadcast([C, 8, 64]), ALU.mult,
                R=[ysb.b, s8[1].b], W=[yh.b])
        pt = self.pT[1]
        for cb in range(4):
            self.tr(pt[:, cb * 128:cb * 128 + C], yh[0:C, cb * 128:(cb + 1) * 128], idb[0:C, 0:C], R=[yh.b, idb.b], W=[pt.b])
        yT1, yr = W['yT1'], W['yr'][self.nyr % 2]
        self.nyr += 1
        bonus, gT = ho['bonus'], ho['gT']
        for cb in range(4):
            self.ts('dve', yT1[:, cb, 0:C], pt[:, cb * 128:cb * 128 + C], c['gn_g'][:, cb:cb + 1], c['gn_b'][:, cb:cb + 1], ALU.mult, ALU.add,
                    R=[pt.b, c['gn_g'].b, c['gn_b'].b], W=[yT1.b])
        self.tt('dve', yT1[:, :, 0:C], yT1[:, :, 0:C], bonus[:, :, sl], ALU.add, R=[yT1.b, bonus.b], W=[yT1.b])
        self.tt('dve', yr[:, :, 0:C], yT1[:, :, 0:C], gT[:, :, sl], ALU.mult, R=[yT1.b, gT.b], W=[yr.b])
        t0 = tok0 + ch * C
        self.dma('sp', g['Ys'][0:512, t0:t0 + C].rearrange("(a p) t -> p a t", p=128), yr[:, :, 0:C], R=[yr.b])

    def phase2(self, st):
        c = self.c
        maxtot = max(g['tot'] for g in self.G)
        maxT = max(g['T'] for g in self.G)
        maxkb = max(g['nkb'] for g in self.G)
        Kt = [self.sb(st, (67, maxtot), BF16, name='Kt') for _ in range(2)]
        Qt = [self.sb(st, (67, maxT), BF16, name='Qt') for _ in range(2)]
        Vh = [self.sb(st, (128, maxkb, 128), BF16, name='Vh') for _ in range(2)]
        for v in Vh:
            self.memset('pool', v[:, :, 64:128], 1.0, W=[v.b])
        Pt = [self.sb(st, (128, 512), BF16, name='Pt') for _ in range(3)]
        rl = [self.sb(st, (64, 512), F32, name='rl') for _ in range(2)]
        yo = [self.sb(st, (64, 512), BF16, name='yo') for _ in range(2)]
        NPS = 4
        LA = 3
        pS = [self.pA[0], self.pA[1], self.pA[2], self.pA[3]]
        pO = [self.pA[4], self.pA[5]]
        Pt = Pt + [self.sb(st, (128, 512), BF16, name='Pt')]
        heads = [(g, h) for g in self.G for h in range(NH)]
        def load(idx):
            g, h = heads[idx]
            T, tot = g['T'], g['tot']
            kt, qt, vh = Kt[idx % 2], Qt[idx % 2], Vh[idx % 2]
            self.dma('sp', kt[0:64, 0:tot], g['Ks'][h, 0:64, :], W=[kt.b])
            self.dma('sp', kt[64:67, 0:tot], g['Ks'][h, 64:67, :], W=[kt.b])
            self.dma('sp', qt[0:64, 0:T], g['Qs'][h, 0:64, :], W=[qt.b])
            self.dma('sp', qt[64:67, 0:T], g['Qs'][h, 64:67, :], W=[qt.b])
            nfull = tot // 128
            if nfull > 0:
                self.dma('sp', vh[:, 0:nfull, 0:64], g['Vs'][h, :, 0:nfull, :], W=[vh.b])
            rem = tot - nfull * 128
            if rem:
                self.dma('sp', vh[0:rem, nfull, 0:64], g['Vs'][h, 0:rem, nfull, :], W=[vh.b])
        blocks = []
        nq = 0
        for idx, (g, h) in enumerate(heads):
            T, past, tot = g['T'], g['past'], g['tot']
            QT = min(512, T)
            for qi in range(T // QT):
                q0 = qi * QT
                qpos0 = past + q0
                nblk = (qpos0 + QT - 1) // 128 + 1
                for j in range(nblk):
                    k0 = j * 128
                    rows = min(128, tot - k0)
                    if k0 + rows - 1 <= qpos0:
                        c0, diag = 0, False
                    else:
                        c0, diag = k0 - qpos0, True
                    blocks.append(dict(idx=idx, g=g, h=h, q0=q0, QT=QT, j=j, k0=k0, rows=rows, c0=c0, diag=diag,
                                       first=(j == 0), last=(j == nblk - 1), nq=nq, newhead=(qi == 0 and j == 0)))
                nq += 1

        def emit_S(n):
            bl = blocks[n]
            g, h, idx = bl['g'], bl['h'], bl['idx']
            kt, qt = Kt[idx % 2], Qt[idx % 2]
            rows, c0, QT, q0, k0, j = bl['rows'], bl['c0'], bl['QT'], bl['q0'], bl['k0'], bl['j']
            ps_, pt_ = pS[n % NPS], Pt[n % NPS]
            self.mm(ps_[0:rows, c0:QT], kt[:, k0:k0 + rows], qt[:, q0 + c0:q0 + QT], start=True, stop=not bl['diag'],
                    R=[kt.b, qt.b], W=[ps_.b])
            if bl['diag']:
                self.mm(ps_[0:rows, c0:c0 + rows], c['ident_b'][0:rows, 0:rows], c['maskD_b'][0:rows, 0:rows], start=False, stop=True,
                        R=[c['ident_b'].b, c['maskD_b'].b], W=[ps_.b])
            self.act(pt_[0:rows, c0:QT], ps_[0:rows, c0:QT], AF.Exp, bias=g['neglc'][0:rows, j, h:h + 1], scale=1.0,
                     R=[ps_.b, g['neglc'].b], W=[pt_.b])

        def emit_PV(n):
            bl = blocks[n]
            g, h, idx = bl['g'], bl['h'], bl['idx']
            vh = Vh[idx % 2]
            rows, c0, QT, q0, j = bl['rows'], bl['c0'], bl['QT'], bl['q0'], bl['j']
            pt_ = Pt[n % NPS]
            po = pO[bl['nq'] % 2]
            self.mm(po[:, c0:QT], vh[0:rows, j, :], pt_[0:rows, c0:QT], start=bl['first'], stop=bl['last'],
                    R=[vh.b, pt_.b], W=[po.b])
            if bl['last']:
                rlt, yot = rl[bl['nq'] % 2], yo[bl['nq'] % 2]
                self.recip(rlt[:, 0:QT], po[64:128, 0:QT], R=[po.b], W=[rlt.b])
                self.tt('dve', yot[:, 0:QT], po[0:64, 0:QT], rlt[:, 0:QT], ALU.mult, R=[po.b, rlt.b], W=[yot.b])
                self.dma('sp', g['Ys'][512 + h * 64:512 + (h + 1) * 64, q0:q0 + QT], yot[:, 0:QT], R=[yot.b])

        load(0)
        if len(heads) > 1:
            load(1)
        nb_ = len(blocks)
        first_block = {}
        for n, bl in enumerate(blocks):
            first_block.setdefault(bl['idx'], n)
        load_at = {first_block[idx] + LA: idx + 1 for idx in range(1, len(heads) - 1)}
        for n in range(nb_ + LA):
            if n in load_at:
                load(load_at[n])
            if n < nb_:
                emit_S(n)
            if n - LA >= 0:
                emit_PV(n - LA)

    def phase3(self, st):
        I, c = self.I, self.c
        wout = self.sb(st, (128, 8, D), BF16, name='wout')
        wg, wu = self.wg, self.wu
        wd = self.sb(st, (128, NFB, D), BF16, name='wd')
        for (dst, src, kk_, ncol) in [(wout, I['w_out'], 8, D), (wd, I['w_d'], NFB, D)]:
            s3 = src.rearrange("(k p) c -> p k c", p=128)
            for k in range(kk_):
                for a in range(0, ncol, 1408 if ncol == DFF else 1024):
                    b = min(ncol, a + (1408 if ncol == DFF else 1024))
                    self.dma('pool', dst[:, k, a:b], s3[:, k, a:b], W=[dst.b])
        NT = 256
        xt = [self.sb(st, (128, 2, D), F32, name='x3')]
        yT = [self.sb(st, (128, 8, NT), BF16, name='yT')]
        xsb = self.sb(st, (128, 2, D), BF16, name='xsb3')
        h2 = self.sb(st, (128, 8, NT), BF16, name='h2')
        junk = self.sb(st, (128, D), BF16, name='junk3')
        ss = self.sb(st, (128, 4), F32)
        rstd = self.sb(st, (128, 4), F32)
        actT = self.sb(st, (128, NFB, NT), BF16, name='actT')
        sil = [self.sb(st, (128, NT), F32, name='sil') for _ in range(2)]
        yo = [self.sb(st, (128, D), F32, name='yo3')]
        gt1 = self.sb(st, (128, D), F32, name='gt1')
        gt2 = self.sb(st, (128, D), F32, name='gt2')
        n = 0
        hflag = self.sb(st, (1, 1), mybir.dt.int32, name='hflag')
        self.dma('sp', hflag[:], I['half'], W=[hflag.b])
        r_base = st.enter_context(self.nc.gpsimd.register("r_base"))
        r_off = st.enter_context(self.nc.gpsimd.register("r_off"))
        HALF = self.SEQ // 2
        def init_reg(e):
            e.reg_load(r_base, hflag[0:1, 0:1])
            e.reg_mul(r_base, r_base, HALF)
            return e.nop()
        self.P.op('pool', init_reg, [hflag.b], [])
        for g in self.G:
            T = g['T'] if g['gi'] == 1 else HALF
            xsrc = g['x'] if g['gi'] == 1 else I['xp3']
            self.build_gates(g, gt1, gt2, sil)
            ntile = (T + NT - 1) // NT
            for ti in range(ntile):
                tok0 = ti * NT
                nt = min(NT, T - tok0)
                nb = (nt + 127) // 128
                bs = min(128, nt)
                x = xt[n % len(xt)]
                y_ = yT[n % len(yT)]
                n += 1
                self.dma('sp', x[0:bs, 0:nb, :], xsrc[tok0:tok0 + nt, :].rearrange("(b p) d -> p b d", p=bs), W=[x.b])
                if g['gi'] == 1:
                    self.dma('sp', y_[:, :, 0:nt], g['Ys'][:, tok0:tok0 + nt].rearrange("(k p) t -> p k t", p=128), W=[y_.b])
                else:
                    ys = g['Ys']
                    SEQ_ = self.SEQ
                    def dyn(e, y_=y_, tok0=tok0, nt=nt, ys=ys, SEQ_=SEQ_):
                        e.reg_add(r_off, r_base, tok0)
                        src = bass.AP(ys.tensor, r_off, [[SEQ_, 128], [128 * SEQ_, 8], [1, nt]])
                        return e.dma_start(out=y_[:, :, 0:nt], in_=src)
                    self.P.dma_fn('pool', dyn, (), [y_.b])
                tmpm = yo[0]
                for blk in range(nb):
                    for half in range(2):
                        pp = self.pA[half]
                        for k in range(8):
                            self.mm(pp[0:bs, :], y_[:, k, blk * 128:blk * 128 + bs], wout[:, k, half * 512:(half + 1) * 512],
                                    start=(k == 0), stop=(k == 7), R=[y_.b, wout.b], W=[pp.b])
                        self.tt('dve', tmpm[0:bs, half * 512:(half + 1) * 512], pp[0:bs, :], gt1[0:bs, half * 512:(half + 1) * 512], ALU.mult,
                                R=[pp.b, gt1.b], W=[tmpm.b])
                    self.tt('pool', x[0:bs, blk, :], tmpm[0:bs, :], x[0:bs, blk, :], ALU.add, R=[tmpm.b, x.b], W=[x.b])
                x1 = x
                self.norm_transpose(x1, nb, bs, g['gm2'], g['sh2'], xsb, h2, junk, ss, rstd, 0, alt=True)
                for fb in range(NFB):
                    pg, pu = self.pA[2 + (fb % 2)], self.pA[4 + (fb % 2)]
                    for k in range(8):
                        self.mm(pg[:, 0:nt], wg[:, k, fb * 128:(fb + 1) * 128], h2[:, k, 0:nt], start=(k == 0), stop=(k == 7), R=[wg.b, h2.b], W=[pg.b])
                    for k in range(8):
                        self.mm(pu[:, 0:nt], wu[:, k, fb * 128:(fb + 1) * 128], h2[:, k, 0:nt], start=(k == 0), stop=(k == 7), R=[wu.b, h2.b], W=[pu.b])
                    s_ = sil[fb % 2]
                    self.act(s_[:, 0:nt], pg[:, 0:nt], AF.Silu, R=[pg.b], W=[s_.b])
                    self.tt('dve', actT[:, fb, 0:nt], s_[:, 0:nt], pu[:, 0:nt], ALU.mult, R=[s_.b, pu.b], W=[actT.b])
                for blk in range(nb):
                    yo_ = yo[0]
                    for half in range(2):
                        pp = self.pA[half]
                        for fb in range(NFB):
                            self.mm(pp[0:bs, :], actT[:, fb, blk * 128:blk * 128 + bs], wd[:, fb, half * 512:(half + 1) * 512],
                                    start=(fb == 0), stop=(fb == NFB - 1), R=[actT.b, wd.b], W=[pp.b])
                        self.tt('dve', yo_[0:bs, half * 512:(half + 1) * 512], pp[0:bs, :], gt2[0:bs, half * 512:(half + 1) * 512], ALU.mult,
                                R=[pp.b, gt2.b], W=[yo_.b])
                    self.tt('pool', yo_[0:bs, :], yo_[0:bs, :], x1[0:bs, blk, :], ALU.add, R=[yo_.b, x1.b], W=[yo_.b])
                    self.dma('sp', g['o_y'][tok0 + blk * 128:tok0 + blk * 128 + bs, :], yo_[0:bs, :], R=[yo_.b])


def _consts():
    i = np.arange(128)
    s, t = i[:, None], i[None, :]
    cst = {}
    cst['c_ident'] = np.eye(128, dtype=np.float32)
    cst['c_triu'] = (s <= t).astype(np.float32)
    cst['c_ones'] = np.ones((128, 128), np.float32)
    cst['c_blk'] = ((s // 64) == (t // 64)).astype(np.float32)
    cst['c_maskA'] = np.concatenate([-(s < t).astype(np.float32), (s <= t).astype(np.float32)], axis=1)
    cst['c_maskC'] = -(s > t).astype(np.float32)
    cst['c_maskD'] = np.where(s <= t, 0.0, NEG).astype(np.float32)
    r = np.ones((128, 1024), np.float32)
    r[:, ::128] = 0.0
    cst['c_reset'] = r
    cst['c_ipair'] = ((i[:, None] % 64) == np.arange(64)[None, :]).astype(np.float32)
    return cst


def _pk(v, nblk):
    return np.ascontiguousarray(np.asarray(v, np.float32).reshape(nblk, 128).T)


_NC_CACHE = {}


def _get_nc(SEQ, PAST, NS, debug=False, phases=(1, 2, 3), sub=None):
    key = (SEQ, PAST, NS, debug, tuple(phases), str(sub))
    if key not in _NC_CACHE:
        _NC_CACHE[key] = KB(SEQ, PAST, NS, debug, phases, sub).build()
    return _NC_CACHE[key]


def make_in_maps(inp, n_cores=8):
    f = lambda a: np.ascontiguousarray(np.asarray(a, dtype=np.float32))
    xp, xs = f(inp['x_prompt']), f(inp['x_sample'])
    BP = xp.shape[0]
    cst = _consts()
    shared = dict(cst)
    L = 0
    shared['w_ada'] = f(inp['w_ada'][L]); shared['b_ada'] = _pk(inp['b_ada'][L], 48)
    shared['w_in'] = f(inp['w_in'][L]); shared['w_out'] = f(inp['w_out'][L])
    shared['w_g'] = f(inp['w_ffn_gate'][L]); shared['w_u'] = f(inp['w_ffn_up'][L]); shared['w_d'] = f(inp['w_ffn_down'][L])
    shared['n1g'] = _pk(inp['norm1_g'][L], 8); shared['n2g'] = _pk(inp['norm2_g'][L], 8)
    shared['mu'] = _pk(inp['shift_mu'][L], 14)
    for nm, src in [('w0', 'w0'), ('a0', 'a0'), ('k_k', 'k_k'), ('k_a', 'k_a'), ('gn_g', 'gn_g'), ('gn_b', 'gn_b')]:
        shared[nm] = _pk(inp[src][L], 4)
    shared['r_k'] = _pk(np.asarray(inp['r_k'][L]).reshape(512), 4)
    shared['w_dec'] = f(inp['w_decay_up'][L]); shared['w_aaa'] = f(inp['w_aaa_up'][L]); shared['w_gup'] = f(inp['w_gate_up'][L])
    gq = np.tile(np.asarray(inp['fox_q_g'][L], np.float32), 8)
    gk = np.tile(np.asarray(inp['fox_k_g'][L], np.float32), 8)
    shared['gqk'] = np.ascontiguousarray(np.broadcast_to(np.concatenate([gq, gk])[None, :], (128, 1024)))
    shared['f_b'] = np.ascontiguousarray(np.broadcast_to(np.asarray(inp['fox_f_b'][L], np.float32)[None, :], (128, 8)))
    maps = []
    for cidx in range(n_cores):
        b = cidx % BP
        m = dict(shared)
        m['xp'] = xp[b]
        hf = cidx // BP
        H2 = xp.shape[1] // 2
        m['half'] = np.array([[hf]], np.int32)
        m['xp3'] = np.ascontiguousarray(xp[b, hf * H2:(hf + 1) * H2])
        m['xs'] = xs[cidx]
        cv = np.stack([np.asarray(inp['c_prompt'][b], np.float32), np.asarray(inp['c_sample'][cidx], np.float32)], axis=-1)
        m['cvec'] = np.ascontiguousarray(cv.reshape(8, 128, 2).transpose(1, 0, 2))
        m['ck'] = f(inp['cache_fox_k'][L, cidx]).reshape(-1, 512)
        m['cv'] = f(inp['cache_fox_v'][L, cidx]).reshape(-1, 512)
        m['clf'] = f(inp['cache_fox_logf'][L, cidx])
        m['st0'] = np.ascontiguousarray(f(inp['state_rwkv'][L, cidx]).transpose(1, 0, 2))
        m['shp0'] = _pk(inp['state_rwkv_shift'][L, cidx, 0], 14)
        maps.append(m)
    return maps


def assemble(res, BP, SEQ, NSEQ, NS):
    r = res
    def upk(a):
        return np.ascontiguousarray(a.T).reshape(-1)
    y_p = np.stack([np.concatenate([r[b]['y_p'], r[b + BP]['y_p']], axis=0) for b in range(BP)])
    y_s = np.stack([r[c]['y_s'] for c in range(NSEQ)])
    st_p = np.stack([r[b]['st_p'] for b in range(BP)])[None]
    sh_p = np.stack([upk(r[b]['sh_p'])[None, :] for b in range(BP)])[None]
    k_p = np.stack([r[b]['k_p'].reshape(SEQ, 8, 64) for b in range(BP)])[None]
    v_p = np.stack([r[b]['v_p'].reshape(SEQ, 8, 64) for b in range(BP)])[None]
    lf_p = np.stack([r[b]['lf_p'] for b in range(BP)])[None]
    st_s = np.stack([r[c]['st_s'] for c in range(NSEQ)])[None]
    sh_s = np.stack([upk(r[c]['sh_s'])[None, :] for c in range(NSEQ)])[None]
    k_s = np.stack([r[c]['k_s'].reshape(NS, 8, 64) for c in range(NSEQ)])[None]
    v_s = np.stack([r[c]['v_s'].reshape(NS, 8, 64) for c in range(NSEQ)])[None]
    lf_s = np.stack([r[c]['lf_s'] for c in range(NSEQ)])[None]
    outs = (y_p, y_s, st_p, sh_p, k_p, v_p, lf_p, st_s, sh_s, k_s, v_s, lf_s)
    return tuple(np.ascontiguousarray(o, dtype=np.float32) for o in outs)


def kernel(**inputs):
    xp = np.asarray(inputs['x_prompt'])
    xs = np.asarray(inputs['x_sample'])
    BP, SEQ, _ = xp.shape
    NSEQ, NS, _ = xs.shape
    PAST = np.asarray(inputs['cache_fox_k']).shape[2]
    nc = _get_nc(SEQ, PAST, NS)
    maps = make_in_maps(inputs, 8)
    res = run_bass_kernel_spmd(nc, maps, core_ids=list(range(8)))
    return assemble(res.results, BP, SEQ, NSEQ, NS)
```

```python
import contextlib
import math
import numpy as np
import concourse.bass as bass
import concourse.mybir as mybir
from concourse.bass_utils import run_bass_kernel_spmd

F32 = mybir.dt.float32
BF16 = mybir.dt.bfloat16
AF = mybir.ActivationFunctionType
ALU = mybir.AluOpType
AX = mybir.AxisListType

ENGS = ('pe', 'dve', 'act', 'pool', 'sp')

D = 1024
HD = 64
NH = 8
RW = 512
RCOLS = 1792
INC = 3336
DFF = 2816
NFB = DFF // 128
EPS = 1e-6
GN_EPS = 64e-5
C0 = math.exp(-0.5)
NEG = -30000.0


class Buf:
    __slots__ = ('last_write', 'readers')

    def __init__(self):
        self.last_write = None
        self.readers = []


class Prog:
    def __init__(self, nc, st, n_dma_sems=10):
        self.nc = nc
        self.q = {e: [] for e in ENGS}
        self.count = {e: 0 for e in ENGS}
        self.seen = {e: {} for e in ENGS}
        self.n_dma_sems = n_dma_sems
        self.dma_next = {e: 0 for e in ENGS}
        self.dma_val = {}
        names = list(ENGS)
        for e in ('sp', 'pool', 'act'):
            for i in range(n_dma_sems):
                k = f'd_{e}_{i}'
                names.append(k)
                self.dma_val[k] = 0
        self.sems = {n: st.enter_context(nc.semaphore(n)) for n in names}
        self.n_ops = 0
        self.noself = ()
        self.wswap = False

    def _deps(self, q, reads, writes, extra=()):
        need = {}

        def add(tok):
            if tok is None:
                return
            k, v = tok
            if need.get(k, 0) < v:
                need[k] = v
        for b in reads:
            add(b.last_write)
        for b in writes:
            add(b.last_write)
            for r in b.readers:
                add(r)
        for t in extra:
            add(t)
        waits = []
        for k, v in need.items():
            if k == q and (q == 'pe' or q in self.noself):
                continue
            if self.seen[q].get(k, 0) >= v:
                continue
            self.seen[q][k] = v
            waits.append((k, v))
        waits.sort(key=lambda kv: kv[0] == q)
        return waits

    def _mark(self, tok, reads, writes):
        for b in reads:
            if len(b.readers) > 6:
                m = {}
                for k, v in b.readers:
                    if m.get(k, 0) < v:
                        m[k] = v
                b.readers = list(m.items())
            b.readers.append(tok)
        for b in writes:
            b.last_write = tok
            b.readers = []

    def op(self, q, fn, reads=(), writes=()):
        waits = self._deps(q, reads, writes)
        self.count[q] += 1
        tok = (q, self.count[q])
        self.q[q].append((waits, fn, (q, 1)))
        self._mark(tok, reads, writes)
        self.n_ops += 1
        return tok

    def dma(self, q, out, in_, reads=(), writes=(), **kw):
        i = self.dma_next[q]
        self.dma_next[q] = (i + 1) % self.n_dma_sems
        k = f'd_{q}_{i}'
        prev = self.dma_val[k]
        ex = [(k, prev)] if prev > 0 else []
        waits = self._deps(q, reads, writes, ex)
        self.dma_val[k] = prev + 16
        tok = (k, prev + 16)
        self.q[q].append((waits, lambda e: e.dma_start(out=out, in_=in_, **kw), (k, 16)))
        self._mark(tok, reads, writes)
        self.n_ops += 1
        return tok

    def dma_fn(self, q, fn, reads=(), writes=()):
        i = self.dma_next[q]
        self.dma_next[q] = (i + 1) % self.n_dma_sems
        k = f'd_{q}_{i}'
        prev = self.dma_val[k]
        ex = [(k, prev)] if prev > 0 else []
        waits = self._deps(q, reads, writes, ex)
        self.dma_val[k] = prev + 16
        tok = (k, prev + 16)
        self.q[q].append((waits, fn, (k, 16)))
        self._mark(tok, reads, writes)
        self.n_ops += 1
        return tok

    def wait_all_dma(self, q):
        toks = [(k, v) for k, v in self.dma_val.items() if v > 0]
        waits = self._deps(q, (), (), toks)
        self.q[q].append((waits, None, None))

    def emit(self):
        nc = self.nc
        sems = self.sems
        with nc.Block() as block:
            handles = {'pe': block.tensor, 'dve': block.vector, 'act': block.scalar,
                       'pool': block.gpsimd, 'sp': block.sync}
            for e in ENGS:
                ops = self.q[e]
                if not ops:
                    continue

                def body(eng, ops=ops):
                    for waits, fn, inc in ops:
                        if self.wswap:
                            waits = list(reversed(waits))
                        for k, v in waits:
                            eng.wait_ge(sems[k], v)
                        if fn is not None:
                            ins = fn(eng)
                            if inc is not None:
                                ins.then_inc(sems[inc[0]], inc[1])
                handles[e](body)
        self.q = {e: [] for e in ENGS}


class TB:
    def __init__(self, t, n=1):
        self.t = t
        self.bs = [Buf() for _ in range(n)]

    @property
    def b(self):
        return self.bs[0]

    def __getitem__(self, k):
        return self.t[k]


class KB:
    def __init__(self, SEQ, PAST, NS, debug=False, phases=(1, 2, 3), sub=None):
        self.SEQ, self.PAST, self.NS, self.debug = SEQ, PAST, NS, debug
        self.phases = phases
        self.sub = sub or {}
        self.nc = bass.Bass("TRN2", target_bir_lowering=False)
        self.uid = 0

    def bb(self):
        self._bb = (getattr(self, '_bb', -1) + 1) % 4
        return self.pA[2 + self._bb]

    def mm(self, out, lhsT, rhs, start=True, stop=True, R=(), W=()):
        return self.P.op('pe', lambda e: e.matmul(out, lhsT, rhs, start=start, stop=stop), R, W)

    def tr(self, out, in_, ident, R=(), W=()):
        return self.P.op('pe', lambda e: e.transpose(out, in_, ident), R, W)

    def act(self, out, in_, func, bias=None, scale=None, accum=None, R=(), W=()):
        kw = {}
        if bias is not None:
            kw['bias'] = bias
        if scale is not None:
            kw['scale'] = scale
        if accum is not None:
            kw['accum_out'] = accum
        return self.P.op('act', lambda e: e.activation(out, in_, func, **kw), R, W)

    def ts(self, eng, out, in0, s1, s2=None, op0=ALU.mult, op1=None, R=(), W=()):
        if op1 is None:
            return self.P.op(eng, lambda e: e.tensor_scalar(out, in0, s1, None, op0), R, W)
        return self.P.op(eng, lambda e: e.tensor_scalar(out, in0, s1, s2, op0, op1), R, W)

    def tt(self, eng, out, in0, in1, op, R=(), W=()):
        return self.P.op(eng, lambda e: e.tensor_tensor(out, in0, in1, op=op), R, W)

    def stt(self, out, in0, scalar, in1, op0, op1, R=(), W=()):
        return self.P.op('dve', lambda e: e.scalar_tensor_tensor(out, in0, scalar, in1, op0, op1), R, W)

    def cp(self, eng, out, in_, R=(), W=()):
        if eng == 'act':
            return self.P.op('act', lambda e: e.activation(out, in_, AF.Copy), R, W)
        return self.P.op(eng, lambda e: e.tensor_copy(out, in_), R, W)

    def red(self, out, in_, op=ALU.add, R=(), W=()):
        return self.P.op('dve', lambda e: e.tensor_reduce(out, in_, AX.X, op), R, W)

    def recip(self, out, in_, R=(), W=()):
        return self.P.op('dve', lambda e: e.reciprocal(out, in_), R, W)

    def memset(self, eng, ap, val, W=()):
        return self.P.op(eng, lambda e: e.memset(ap, val), (), W)

    def dma(self, q, out, in_, R=(), W=(), **kw):
        return self.P.dma(q, out, in_, R, W, **kw)

    def sb(self, st, shape, dt, n=1, name=None):
        self.uid += 1
        t = st.enter_context(self.nc.sbuf_tensor(f"{name or 's'}_{self.uid}", list(shape), dt))
        return TB(t, n)

    def ps(self, st, shape, dt, n=1, name=None):
        self.uid += 1
        t = st.enter_context(self.nc.psum_tensor(f"{name or 'p'}_{self.uid}", list(shape), dt))
        return TB(t, n)

    def din(self, name, shape, dt=F32):
        return self.nc.dram_tensor(name, list(shape), dt, kind="ExternalInput").ap()

    def dout(self, name, shape, dt=F32):
        return self.nc.dram_tensor(name, list(shape), dt, kind="ExternalOutput").ap()

    def dscr(self, name, shape, dt=BF16):
        kind = "ExternalOutput" if self.debug else "Internal"
        return self.nc.dram_tensor(name, list(shape), dt, kind=kind).ap()

    def build(self):
        nc = self.nc
        SEQ, PAST, NS = self.SEQ, self.PAST, self.NS
        I = {}
        self.I = I
        I['half'] = self.din('half', (1, 1), mybir.dt.int32)
        I['xp3'] = self.din('xp3', (SEQ // 2, D))
        for nm, shp in [('xp', (SEQ, D)), ('xs', (NS, D)), ('cvec', (128, 8, 2)),
                        ('ck', (PAST, 512)), ('cv', (PAST, 512)), ('clf', (PAST, 8)),
                        ('st0', (64, 8, 64)), ('shp0', (128, 14)),
                        ('w_ada', (D, 6 * D)), ('b_ada', (128, 48)), ('w_in', (D, INC)),
                        ('w_out', (D, D)), ('w_g', (D, DFF)), ('w_u', (D, DFF)), ('w_d', (DFF, D)),
                        ('n1g', (128, 8)), ('n2g', (128, 8)), ('mu', (128, 14)),
                        ('w0', (128, 4)), ('a0', (128, 4)), ('k_k', (128, 4)), ('k_a', (128, 4)),
                        ('r_k', (128, 4)), ('gn_g', (128, 4)), ('gn_b', (128, 4)),
                        ('w_dec', (64, 512)), ('w_aaa', (64, 512)), ('w_gup', (128, 512)),
                        ('gqk', (128, 1024)), ('f_b', (128, 8)),
                        ('c_ident', (128, 128)), ('c_triu', (128, 128)), ('c_ones', (128, 128)),
                        ('c_blk', (128, 128)), ('c_maskA', (128, 256)), ('c_maskC', (128, 128)),
                        ('c_maskD', (128, 128)), ('c_reset', (128, 1024)), ('c_ipair', (128, 64))]:
            I[nm] = self.din(nm, shp)
        self.I = I
        O = {}
        for nm, shp in [('y_p', (SEQ // 2, D)), ('y_s', (NS, D)), ('st_p', (8, 64, 64)), ('sh_p', (128, 14)),
                        ('k_p', (SEQ, 512)), ('v_p', (SEQ, 512)), ('lf_p', (SEQ, 8)),
                        ('st_s', (8, 64, 64)), ('sh_s', (128, 14)),
                        ('k_s', (NS, 512)), ('v_s', (NS, 512)), ('lf_s', (NS, 8))]:
            O[nm] = self.dout(nm, shp)
        self.O = O
        self.G = []
        for gi, (T, past) in enumerate([(SEQ, 0), (NS, PAST)]):
            tot = past + T
            nkb = (tot + 127) // 128
            g = dict(gi=gi, T=T, past=past, tot=tot, nkb=nkb,
                     Qs=self.dscr(f'Qs{gi}', (8, 67, T)), Ks=self.dscr(f'Ks{gi}', (8, 67, tot)),
                     Vs=self.dscr(f'Vs{gi}', (8, 128, nkb, 64)), Ys=self.dscr(f'Ys{gi}', (D, T)),
                     x=I['xp'] if gi == 0 else I['xs'])
            g['o_y'], g['o_st'], g['o_sh'], g['o_k'], g['o_v'], g['o_lf'] = (
                (O['y_p'], O['st_p'], O['sh_p'], O['k_p'], O['v_p'], O['lf_p']) if gi == 0 else
                (O['y_s'], O['st_s'], O['sh_s'], O['k_s'], O['v_s'], O['lf_s']))
            self.G.append(g)

        with contextlib.ExitStack() as st:
            self.P = Prog(nc, st)
            self.P.noself = tuple(self.sub.get('noself', ()))
            self.persistent(st)
            ph = self.phases
            with contextlib.ExitStack() as sw1:
                if 1 in ph:
                    self.load_win(sw1)
                with contextlib.ExitStack() as s0:
                    self.phase0(s0)
                    self.P.wait_all_dma('sp')
                    self.P.emit()
                if 1 in ph:
                    with contextlib.ExitStack() as s1:
                        self.phase1(s1)
                        self.P.wait_all_dma('sp')
                        self.P.emit()
            with contextlib.ExitStack() as sw3:
                if 3 in ph:
                    self.load_wgu(sw3)
                if 2 in ph:
                    with contextlib.ExitStack() as s2:
                        self.phase2(s2)
                        self.P.wait_all_dma('sp')
                        self.P.emit()
                with contextlib.ExitStack() as s3:
                    if 3 in ph:
                        self.phase3(s3)
                    self.P.wait_all_dma('sp')
                    self.P.emit()
        return nc

    def persistent(self, st):
        I = self.I
        c = {}
        def ld(nm, shape, dt=F32, q='sp'):
            t = self.sb(st, shape, dt, name=nm)
            self.dma('pool' if dt == BF16 else q, t[:], I[nm], W=[t.b])
            return t
        c['ident_f'] = ld('c_ident', (128, 128))
        c['triu_f'] = ld('c_triu', (128, 128))
        c['ones_f'] = ld('c_ones', (128, 128))
        c['maskA'] = ld('c_maskA', (128, 256))
        c['maskC'] = ld('c_maskC', (128, 128))
        c['reset'] = ld('c_reset', (128, 1024))
        c['ipair'] = ld('c_ipair', (128, 64))
        c['ident_b'] = self.sb(st, (128, 128), BF16, name='identb')
        self.dma('pool', c['ident_b'][:], I['c_ident'], W=[c['ident_b'].b])
        c['blk_b'] = self.sb(st, (128, 128), BF16, name='blkb')
        self.dma('pool', c['blk_b'][:], I['c_blk'], W=[c['blk_b'].b])
        c['maskD_b'] = self.sb(st, (128, 128), BF16, name='maskDb')
        self.dma('pool', c['maskD_b'][:], I['c_maskD'], W=[c['maskD_b'].b])
        for nm in ['n1g', 'n2g']:
            c[nm] = ld(nm, (128, 8))
        c['mu'] = ld('mu', (128, 14))
        for nm in ['w0', 'a0', 'k_k', 'k_a', 'r_k', 'gn_g', 'gn_b']:
            c[nm] = ld(nm, (128, 4))
        c['f_b'] = ld('f_b', (128, 8))
        c['eps'] = self.sb(st, (128, 1), F32, name='eps')
        self.memset('dve', c['eps'][:], EPS, W=[c['eps'].b])
        c['eps64'] = self.sb(st, (128, 1), F32, name='eps64')
        self.memset('dve', c['eps64'][:], 64 * EPS, W=[c['eps64'].b])
        c['gneps'] = self.sb(st, (128, 1), F32, name='gneps')
        self.memset('dve', c['gneps'][:], GN_EPS, W=[c['gneps'].b])
        c['omu'] = self.sb(st, (128, 14), F32, name='omu')
        self.ts('dve', c['omu'][:], c['mu'][:], -1.0, 1.0, ALU.mult, ALU.add, R=[c['mu'].b], W=[c['omu'].b])
        c['omka'] = self.sb(st, (128, 4), F32, name='omka')
        self.ts('dve', c['omka'][:], c['k_a'][:], -1.0, 1.0, ALU.mult, ALU.add, R=[c['k_a'].b], W=[c['omka'].b])
        c['mod'] = self.sb(st, (128, 48, 2), F32, name='mod')
        self.c = c
        for g in self.G:
            g['gm1'] = self.sb(st, (128, 8), F32, name='gm1')
            g['gm2'] = self.sb(st, (128, 8), F32, name='gm2')
            g['sh1'] = self.sb(st, (128, 8), F32, name='sh1')
            g['sh2'] = self.sb(st, (128, 8), F32, name='sh2')
            g['neglc'] = self.sb(st, (128, g['nkb'], 8), F32, name='neglc')
        self.pT = [self.ps(st, (128, 1024), BF16, name='pT') for _ in range(2)]
        self.pA = [self.ps(st, (128, 512), F32, name='pA') for _ in range(6)]

    def load_win(self, st):
        I = self.I
        win = self.sb(st, (128, 8, INC), BF16, name='win')
        wsrc = I['w_in'].rearrange("(k p) c -> p k c", p=128)
        for k in range(8):
            for (a, b) in [(0, 1792), (1792, INC)]:
                self.dma('pool', win[:, k, a:b], wsrc[:, k, a:b], W=[win.b])
        self.win = win

    def load_wgu(self, st):
        I = self.I
        self.wg = self.sb(st, (128, 8, DFF), BF16, name='wg')
        self.wu = self.sb(st, (128, 8, DFF), BF16, name='wu')
        for (dst, src) in [(self.wg, I['w_g']), (self.wu, I['w_u'])]:
            s3 = src.rearrange("(k p) c -> p k c", p=128)
            for k in range(8):
                for a in range(0, DFF, 1408):
                    self.dma('pool', dst[:, k, a:a + 1408], s3[:, k, a:a + 1408], W=[dst.b])

    def phase0(self, st):
        I, c = self.I, self.c
        cv = self.sb(st, (128, 8, 2), F32)
        self.dma('sp', cv[:], I['cvec'], W=[cv.b])
        cs = self.sb(st, (128, 8, 2), F32)
        self.act(cs[:], cv[:], AF.Silu, R=[cv.b], W=[cs.b])
        bada = self.sb(st, (128, 48), F32)
        self.dma('sp', bada[:], I['b_ada'], W=[bada.b])
        wa = [self.sb(st, (128, 8, 512), F32) for _ in range(2)]
        wsrc = I['w_ada'].rearrange("(k p) c -> p k c", p=128)
        pm = self.pA[0]
        for ch in range(12):
            w = wa[ch % 2]
            self.dma('sp', w[:], wsrc[:, :, ch * 512:(ch + 1) * 512], W=[w.b])
            for cbl in range(4):
                gb = ch * 4 + cbl
                for k in range(8):
                    self.mm(pm[:, gb * 2:gb * 2 + 2], w[:, k, cbl * 128:(cbl + 1) * 128], cs[:, k, :],
                            start=(k == 0), stop=(k == 7), R=[w.b, cs.b], W=[pm.b])
        mod = c['mod']
        self.tt('dve', mod[:], pm[:, 0:96].rearrange("p (a b) -> p a b", b=2),
                bada[:].unsqueeze(2).to_broadcast([128, 48, 2]), ALU.add, R=[pm.b, bada.b], W=[mod.b])
        tmp = self.sb(st, (128, 8), F32)
        for g in self.G:
            gi = g['gi']
            for (dst, blk0, ng) in [(g['gm1'], 8, c['n1g']), (g['gm2'], 32, c['n2g'])]:
                self.ts('dve', tmp[:], mod[:, blk0:blk0 + 8, gi], 1.0, None, ALU.add, R=[mod.b], W=[tmp.b])
                self.tt('dve', dst[:], tmp[:], ng[:], ALU.mult, R=[tmp.b, ng.b], W=[dst.b])
            self.cp('dve', g['sh1'][:], mod[:, 0:8, gi], R=[mod.b], W=[g['sh1'].b])
            self.cp('dve', g['sh2'][:], mod[:, 24:32, gi], R=[mod.b], W=[g['sh2'].b])

    def build_gates(self, g, gt1, gt2, tl):
        c = self.c
        mod = c['mod']
        gi = g['gi']
        n = 0
        for (dst, blk0) in [(gt1, 16), (gt2, 40)]:
            for half in range(2):
                pg = self.pA[4 + (n % 2)]
                n += 1
                for j in range(4):
                    blk = half * 4 + j
                    t = tl[blk % 2]
                    self.ts('dve', t[:, 0:128], c['ones_f'][:], mod[:, blk0 + blk, gi:gi + 1], None, ALU.mult,
                            R=[c['ones_f'].b, mod.b], W=[t.b])
                    self.mm(pg[:, j * 128:(j + 1) * 128], t[:, 0:128], c['ident_f'][:], R=[t.b, c['ident_f'].b], W=[pg.b])
                self.cp('act', dst[:, half * 512:(half + 1) * 512], pg[:], R=[pg.b], W=[dst.b])

    def norm_transpose(self, xt, nb, bs, gm, sh, xsb, hT, junk, ss, rstd, pT_sel, alt=False):
        c = self.c
        for blk in range(nb):
            self.act(junk[0:bs, :], xt[0:bs, blk, :], AF.Square, accum=ss[0:bs, blk:blk + 1],
                     R=[xt.b], W=[junk.b, ss.b])
        self.act(rstd[0:bs, 0:nb], ss[0:bs, 0:nb], AF.Sqrt, bias=c['eps'][0:bs, :], scale=1.0 / D,
                 R=[ss.b, c['eps'].b], W=[rstd.b])
        self.recip(rstd[0:bs, 0:nb], rstd[0:bs, 0:nb], R=[rstd.b], W=[rstd.b])
        for blk in range(nb):
            self.act(xsb[0:bs, blk, :], xt[0:bs, blk, :], AF.Copy, scale=rstd[0:bs, blk:blk + 1],
                     R=[xt.b, rstd.b], W=[xsb.b])
        nt = (nb - 1) * 128 + bs
        for k in range(8):
            pt = self.pT[(pT_sel + k) % 2] if alt else self.pT[pT_sel]
            for blk in range(nb):
                self.tr(pt[:, blk * 128:blk * 128 + bs], xsb[0:bs, blk, k * 128:(k + 1) * 128],
                        c['ident_b'][0:bs, 0:bs], R=[xsb.b, c['ident_b'].b], W=[pt.b])
            self.ts('dve', hT[:, k, 0:nt], pt[:, 0:nt], gm[:, k:k + 1], sh[:, k:k + 1], ALU.mult, ALU.add,
                    R=[pt.b, gm.b, sh.b], W=[hT.b])

    def phase1(self, st):
        I, c = self.I, self.c
        win = self.win
        wdec = self.sb(st, (64, 512), BF16, name='wdec')
        self.dma('pool', wdec[:], I['w_dec'], W=[wdec.b])
        waaa = self.sb(st, (128, 512), BF16, name='waaa')
        self.dma('pool', waaa[64:128, :], I['w_aaa'], W=[waaa.b])
        wgup = self.sb(st, (128, 512), BF16, name='wgup')
        self.dma('pool', wgup[:], I['w_gup'], W=[wgup.b])
        gqk = self.sb(st, (128, 1024), F32, name='gqk')
        self.dma('sp', gqk[:], I['gqk'], W=[gqk.b])
        self.win, self.wdec, self.waaa, self.wgup, self.gqk = win, wdec, waaa, wgup, gqk

        NT = 128
        self.NT1 = NT
        W = {}
        W['xt'] = [self.sb(st, (128, 1, D), F32, name='xt') for _ in range(2)]
        W['xsb'] = self.sb(st, (128, 1, D), BF16, name='xsb')
        W['hT'] = self.sb(st, (128, 8, NT), BF16, name='hT')
        W['junk'] = self.sb(st, (128, D), BF16, name='junk')
        W['ss'] = self.sb(st, (128, 4), F32)
        W['rstd'] = self.sb(st, (128, 4), F32)
        W['sq'] = self.sb(st, (128, 1024), F32, name='sq')
        W['ss16'] = self.sb(st, (128, 16), F32)
        W['rs16'] = self.sb(st, (128, 16), F32)
        W['qkn'] = self.sb(st, (128, 1024), F32, name='qkn')
        W['kout'] = [self.sb(st, (128, 512), F32, name='kout') for _ in range(2)]
        W['qkb'] = self.sb(st, (128, 1024), BF16, name='qkb')
        W['QKt'] = [self.sb(st, (128, 8, NT), BF16, name='QKt') for _ in range(2)]
        W['v32'] = [self.sb(st, (128, 512), F32, name='v32') for _ in range(2)]
        W['vb'] = [self.sb(st, (128, 1, 512), BF16, name='vb') for _ in range(2)]
        W['lf'] = [self.sb(st, (128, 8), F32, name='lf') for _ in range(2)]
        W['lft'] = self.sb(st, (128, 8), F32)
        W['lcT'] = self.sb(st, (8, NT), F32)
        W['lcr'] = self.sb(st, (8, NT), F32)
        W['lcs'] = [self.sb(st, (8, 3, NT), BF16) for _ in range(2)]
        W['lchf'] = self.sb(st, (8, NT), F32)
        W['R'] = self.sb(st, (128, 8), F32, name='Racc')
        W['onesb'] = self.sb(st, (3, 2112), BF16, name='onesb')
        self.memset('pool', W['onesb'][:], 1.0, W=[W['onesb'].b])
        for nm in ['z']:
            W[nm] = self.sb(st, (128, 14, NT), F32, name=nm)
        W['t1'] = [self.sb(st, (128, NT), F32, name='t1') for _ in range(2)]
        W['carry'] = self.sb(st, (128, 14), F32, name='carry')
        for nm in ['esig', 'aa', 'kk', 'kp', 'cs', 'gx', 'tmpf', 'beta']:
            W[nm] = self.sb(st, (128, 4, NT), F32, name=nm)
        for nm in ['sqb', 'rkb']:
            W[nm] = self.sb(st, (128, 4, NT), BF16, name=nm)
        W['HO'] = []
        for _ in range(2):
            ho = {}
            for nm in ['BtT', 'KtT', 'BgT', 'KgT', 'vTb', 'gT', 'bonus']:
                ho[nm] = self.sb(st, (128, 4, NT), BF16, name=nm)
            ho['gam'] = self.sb(st, (128, 4, NT), F32, name='gam')
            ho['AR'] = self.sb(st, (128, 4, 1, 2, 128), BF16, name='AR')
            W['HO'].append(ho)
        W['tdw'] = self.sb(st, (128, NT), BF16, name='tdw')
        W['dab'] = self.sb(st, (128, NT), BF16, name='dab')
        W['sg'] = self.sb(st, (128, NT), BF16, name='sg')
        W['nb16'] = self.sb(st, (128, 4, 1), F32, name='nb16')
        W['Bgt'] = self.sb(st, (128, 512), BF16, name='Bgt')
        W['Kgt'] = self.sb(st, (128, 512), BF16, name='Kgt')
        W['Vt'] = self.sb(st, (128, 512), BF16, name='Vt')
        W['MLt'] = self.sb(st, (128, 8, 256), BF16, name='MLt')
        W['MKt'] = self.sb(st, (128, 8, 256), BF16, name='MKt')
        W['Lc'] = [self.sb(st, (128, 8, 128), BF16, name='Lc') for _ in range(2)]
        W['Mc'] = [self.sb(st, (128, 8, 128), BF16, name='Mc') for _ in range(2)]
        W['Xf'] = self.sb(st, (128, 8, 128), F32, name='Xf')
        W['Xb'] = self.sb(st, (128, 8, 128), BF16, name='Xb')
        W['GT'] = self.sb(st, (128, 4, 64), BF16, name='GT')
        W['Hs'] = self.sb(st, (128, 4, 64), F32, name='Hs')
        W['RAT'] = self.sb(st, (128, 4, 128), BF16, name='RAT')
        W['Sf'] = self.sb(st, (128, 4, 64), F32, name='Sf')
        W['Sb'] = self.sb(st, (128, 4, 64), BF16, name='Sb')
        W['ysb'] = self.sb(st, (128, 8, 64), F32, name='ysb')
        W['ysq'] = self.sb(st, (128, 8, 64), F32, name='ysq')
        W['yh'] = self.sb(st, (128, 512), BF16, name='yh')
        W['st8'] = [self.sb(st, (128, 8), F32) for _ in range(4)]
        W['yT1'] = self.sb(st, (128, 4, 128), F32, name='yT1')
        W['yr'] = [self.sb(st, (128, 4, 128), BF16, name='yr') for _ in range(2)]
        self.nyr = 0
        W['Sv'] = self.sb(st, (64, 8, 64), F32, name='Sv')
        W['So'] = self.sb(st, (64, 4, 128), F32, name='So')
        self.W = W

        for g in self.G:
            self.phase1_group(g, NT)

    def phase1_group(self, g, NT):
        I, c, W = self.I, self.c, self.W
        gi, T, past = g['gi'], g['T'], g['past']
        for h in range(NH):
            for a in range(0, g['tot'], 2112):
                b = min(g['tot'], a + 2112)
                self.dma('sp', g['Ks'][h, 64:67, a:b], W['onesb'][:, 0:b - a], R=[W['onesb'].b])
        self.memset('dve', W['R'][:], 0.0, W=[W['R'].b])
        npb = past // 128
        for pb in range(0 if self.sub.get('skip_past') else npb):
            kc = W['kout'][pb % 2]
            self.dma('sp', kc[:], I['ck'][pb * 128:(pb + 1) * 128, :], W=[kc.b])
            self.cp('pool', W['qkb'][:, 512:1024], kc[:], R=[kc.b], W=[W['qkb'].b])
            self.k_transposes(g, pb * 128, 128, 0, only_k=True, blk=pb)
            self.flush_qk(g, pb * 128, 128, only_k=True, blk=pb)
            vc = W['v32'][pb % 2]
            self.dma('sp', vc[:], I['cv'][pb * 128:(pb + 1) * 128, :], W=[vc.b])
            vb = W['vb'][pb % 2]
            self.cp('pool', vb[:, 0, :], vc[:], R=[vc.b], W=[vb.b])
            self.dma('sp', g['Vs'][:, :, pb, :].rearrange("h p d -> p h d"),
                     vb[:, 0, :].rearrange("p (h d) -> p h d", d=64), R=[vb.b])
            lf = W['lf'][pb % 2]
            self.dma('sp', lf[:], I['clf'][pb * 128:(pb + 1) * 128, :], W=[lf.b])
            self.lc_block(g, lf, 128, pb, None, 0)
        if gi == 0:
            self.memset('dve', W['Sf'][:], 0.0, W=[W['Sf'].b])
            self.memset('pool', W['Sb'][:], 0.0, W=[W['Sb'].b])
            self.memset('dve', W['carry'][:], 0.0, W=[W['carry'].b])
        else:
            self.dma('sp', W['carry'][:], I['shp0'], W=[W['carry'].b])
            self.dma('sp', W['Sv'][:], I['st0'], W=[W['Sv'].b])
            for cb in range(4):
                pa = self.pA[cb % 2]
                self.tr(pa[:, 0:64], W['Sv'][:, 2 * cb:2 * cb + 2, :], c['ident_f'][0:64, 0:64],
                        R=[W['Sv'].b, c['ident_f'].b], W=[pa.b])
                self.cp('dve', W['Sf'][:, cb, :], pa[:, 0:64], R=[pa.b], W=[W['Sf'].b])
            self.cp('pool', W['Sb'][:], W['Sf'][:], R=[W['Sf'].b], W=[W['Sb'].b])
        ntile = (T + NT - 1) // NT
        def run(gens):
            gens = [x for x in gens if x[0] is not None]
            while gens:
                for x in list(gens):
                    for _ in range(x[1]):
                        try:
                            next(x[0])
                        except StopIteration:
                            gens.remove(x)
                            break
        genB = None
        for ti in range(ntile):
            nt = min(NT, T - ti * NT)
            genA = self.phase1_tile(g, ti, ti * NT, nt)
            run([(genB, self.sub.get('nb', 1)), (genA, self.sub.get('na', 1))])
            C_ = min(128, nt)
            genB = self.rwkv_chunk(g, ti, ti * NT, 0, C_, nt) if not self.sub.get('skip_rwkv') else None
        run([(genB, 1)])
        if self.sub.get('skip_final'):
            return
        self.dma('sp', g['o_sh'], W['carry'][:], R=[W['carry'].b])
        for cb in range(4):
            pa = self.pA[cb % 2]
            self.tr(pa[0:64, 0:128], W['Sf'][:, cb, :], c['ident_f'][:], R=[W['Sf'].b, c['ident_f'].b], W=[pa.b])
            self.cp('dve', W['So'][:, cb, :], pa[0:64, 0:128], R=[pa.b], W=[W['So'].b])
        self.dma('sp', g['o_st'].rearrange("(cb hh) v k -> v cb hh k", hh=2),
                 W['So'][:].rearrange("v cb (hh k) -> v cb hh k", hh=2), R=[W['So'].b])

    def k_transposes(self, g, tok0, bs, col0, only_k, blk):
        c, W = self.c, self.W
        QKt = W['QKt'][g.get('qkt_sel', 0)]
        for j in range(4 if only_k else 8):
            jj = j + 4 if only_k else j
            pt = self.pT[0]
            self.tr(pt[:, 0:bs], W['qkb'][0:bs, jj * 128:(jj + 1) * 128], c['ident_b'][0:bs, 0:bs],
                    R=[W['qkb'].b, c['ident_b'].b], W=[pt.b])
            self.cp('act', QKt[:, jj, col0:col0 + bs], pt[:, 0:bs], R=[pt.b], W=[QKt.b])

    def flush_qk(self, g, tok0, n, only_k, blk=None, qtok0=None):
        W = self.W
        QKt = W['QKt'][g.get('qkt_sel', 0)]
        for h in range(NH):
            pb = 64 * (h % 2)
            self.dma('sp', g['Ks'][h, 0:64, tok0:tok0 + n], QKt[pb:pb + 64, 4 + h // 2, 0:n], R=[QKt.b])
            if not only_k:
                self.dma('sp', g['Qs'][h, 0:64, qtok0:qtok0 + n], QKt[pb:pb + 64, h // 2, 0:n], R=[QKt.b])
        g['qkt_sel'] = 1 - g.get('qkt_sel', 0)

    def lc_block(self, g, lf, bs, kb, lcT_cols, col0):
        c, W = self.c, self.W
        R = W['R']
        pa = self.pA[1]
        self.mm(pa[0:bs, 0:8], c['triu_f'][0:bs, 0:bs], lf[0:bs, :], start=True, stop=False,
                R=[c['triu_f'].b, lf.b], W=[pa.b])
        self.mm(pa[0:bs, 0:8], c['ones_f'][:, 0:bs], R[:, :], start=False, stop=True, R=[c['ones_f'].b, R.b], W=[pa.b])
        self.ts('dve', g['neglc'][0:bs, kb, :], pa[0:bs, 0:8], -1.0, None, ALU.mult, R=[pa.b], W=[g['neglc'].b])
        if lcT_cols is not None:
            o = 16 + col0
            self.mm(pa[0:8, o:o + bs], lf[0:bs, :], c['triu_f'][0:bs, 0:bs], start=True, stop=False,
                    R=[lf.b, c['triu_f'].b], W=[pa.b])
            self.mm(pa[0:8, o:o + bs], R[:, :], c['ones_f'][:, 0:bs], start=False, stop=True,
                    R=[R.b, c['ones_f'].b], W=[pa.b])
            self.cp('dve', W['lcT'][:, col0:col0 + bs], pa[0:8, o:o + bs], R=[pa.b], W=[W['lcT'].b])
        self.tt('dve', R[0:bs, :], R[0:bs, :], lf[0:bs, :], ALU.add, R=[R.b, lf.b], W=[R.b])

    def phase1_tile(self, g, ti, tok0, nt):
        I, c, W = self.I, self.c, self.W
        gi, past = g['gi'], g['past']
        nb = (nt + 127) // 128
        bs = min(128, nt)
        xt = W['xt'][ti % 2]
        self.dma('sp', xt[0:bs, 0:nb, :], g['x'][tok0:tok0 + nt, :].rearrange("(b p) d -> p b d", p=bs), W=[xt.b])
        hT = W['hT']
        self.norm_transpose(xt, nb, bs, g['gm1'], g['sh1'], W['xsb'], hT, W['junk'], W['ss'], W['rstd'], 0)
        win = self.win
        yield
        for blk in range(0 if self.sub.get('skip_fox') else nb):
            kb = (past + tok0) // 128 + blk
            pq, pk, pv, pf = self.pA[0], self.pA[1], self.pA[0], self.pA[1]
            def proj(pp, c0, wdt):
                for k in range(8):
                    self.mm(pp[0:bs, 0:wdt], hT[:, k, blk * 128:blk * 128 + bs], win[:, k, c0:c0 + wdt],
                            start=(k == 0), stop=(k == 7), R=[hT.b, win.b], W=[pp.b])
            proj(pq, 1792, 512)
            proj(pk, 2304, 512)
            yield
            sq, ss16, rs16, qkn = W['sq'], W['ss16'], W['rs16'], W['qkn']
            self.act(sq[0:bs, 0:512], pq[0:bs, :], AF.Square, R=[pq.b], W=[sq.b])
            self.act(sq[0:bs, 512:1024], pk[0:bs, :], AF.Square, R=[pk.b], W=[sq.b])
            self.red(ss16[0:bs, :], sq[0:bs, :].rearrange("p (g d) -> p g d", d=64), R=[sq.b], W=[ss16.b])
            self.act(rs16[0:bs, 0:8], ss16[0:bs, 0:8], AF.Sqrt, bias=c['eps64'][0:bs, :], scale=1.0,
                     R=[ss16.b, c['eps64'].b], W=[rs16.b])
            self.act(rs16[0:bs, 8:16], ss16[0:bs, 8:16], AF.Sqrt, bias=c['eps'][0:bs, :], scale=1.0 / 64,
                     R=[ss16.b, c['eps'].b], W=[rs16.b])
            self.recip(rs16[0:bs, :], rs16[0:bs, :], R=[rs16.b], W=[rs16.b])
            self.tt('dve', qkn[0:bs, 0:512].rearrange("p (g d) -> p g d", d=64),
                    pq[0:bs, :].rearrange("p (g d) -> p g d", d=64),
                    rs16[0:bs, 0:8].unsqueeze(2).to_broadcast([bs, 8, 64]), ALU.mult, R=[pq.b, rs16.b], W=[qkn.b])
            self.tt('dve', qkn[0:bs, 512:1024].rearrange("p (g d) -> p g d", d=64),
                    pk[0:bs, :].rearrange("p (g d) -> p g d", d=64),
                    rs16[0:bs, 8:16].unsqueeze(2).to_broadcast([bs, 8, 64]), ALU.mult, R=[pk.b, rs16.b], W=[qkn.b])
            kout = W['kout'][blk % 2]
            self.tt('dve', W['qkb'][0:bs, 0:512], qkn[0:bs, 0:512], self.gqk[0:bs, 0:512], ALU.mult,
                    R=[qkn.b, self.gqk.b], W=[W['qkb'].b])
            self.tt('dve', kout[0:bs, :], qkn[0:bs, 512:1024], self.gqk[0:bs, 512:1024], ALU.mult,
                    R=[qkn.b, self.gqk.b], W=[kout.b])
            self.dma('sp', g['o_k'][tok0 + blk * 128:tok0 + blk * 128 + bs, :], kout[0:bs, :], R=[kout.b])
            self.cp('pool', W['qkb'][0:bs, 512:1024], kout[0:bs, :], R=[kout.b], W=[W['qkb'].b])
            self.k_transposes(g, tok0, bs, blk * 128, only_k=False, blk=blk)
            yield
            proj(pv, 2816, 512)
            proj(pf, 3328, 8)
            v32 = W['v32'][blk % 2]
            self.cp('act', v32[0:bs, :], pv[0:bs, :], R=[pv.b], W=[v32.b])
            self.dma('sp', g['o_v'][tok0 + blk * 128:tok0 + blk * 128 + bs, :], v32[0:bs, :], R=[v32.b])
            vb = W['vb'][ti % 2]
            self.cp('pool', vb[0:bs, blk, :], v32[0:bs, :], R=[v32.b], W=[vb.b])
            yield
            lf = W['lf'][blk % 2]
            self.tt('dve', W['lft'][0:bs, :], pf[0:bs, 0:8], c['f_b'][0:bs, :], ALU.add, R=[pf.b, c['f_b'].b], W=[W['lft'].b])
            self.act(W['lft'][0:bs, :], W['lft'][0:bs, :], AF.Exp, scale=-1.0, R=[W['lft'].b], W=[W['lft'].b])
            self.act(W['lft'][0:bs, :], W['lft'][0:bs, :], AF.Ln, bias=1.0, scale=1.0, R=[W['lft'].b], W=[W['lft'].b])
            self.ts('dve', lf[0:bs, :], W['lft'][0:bs, :], -1.0, None, ALU.mult, R=[W['lft'].b], W=[lf.b])
            self.dma('sp', g['o_lf'][tok0 + blk * 128:tok0 + blk * 128 + bs, :], lf[0:bs, :], R=[lf.b])
            self.lc_block(g, lf, bs, kb, True, blk * 128)
        if self.sub.get('skip_fox'):
            if not self.sub.get('skip_rwkv'):
                yield from self.rwkv_tile(g, ti, tok0, nt)
            return
        self.flush_qk(g, past + tok0, nt, only_k=False, qtok0=tok0)
        vb = W['vb'][ti % 2]
        kb0 = (past + tok0) // 128
        self.dma('sp', g['Vs'][:, 0:bs, kb0:kb0 + nb, :].rearrange("h p b d -> p h b d"),
                 vb[0:bs, 0:nb, :].rearrange("p b (h d) -> p h b d", d=64), R=[vb.b])
        yield
        lcs = W['lcs'][ti % 2]
        lcT, lcr, lchf = W['lcT'], W['lcr'], W['lchf']
        self.cp('dve', lcs[:, 0, 0:nt], lcT[:, 0:nt], R=[lcT.b], W=[lcs.b])
        self.tt('dve', lcr[:, 0:nt], lcT[:, 0:nt], lcs[:, 0, 0:nt], ALU.subtract, R=[lcT.b, lcs.b], W=[lcr.b])
        self.cp('dve', lcs[:, 1, 0:nt], lcr[:, 0:nt], R=[lcr.b], W=[lcs.b])
        self.tt('dve', lchf[:, 0:nt], lcr[:, 0:nt], lcs[:, 1, 0:nt], ALU.subtract, R=[lcr.b, lcs.b], W=[lchf.b])
        self.cp('dve', lcs[:, 2, 0:nt], lchf[:, 0:nt], R=[lchf.b], W=[lcs.b])
        self.dma('sp', g['Qs'][:, 64:67, tok0:tok0 + nt], lcs[:, :, 0:nt], R=[lcs.b])
        if not self.sub.get('skip_rwkv'):
            yield from self.rwkv_tile(g, ti, tok0, nt)
        yield

    def rwkv_tile(self, g, ti, tok0, nt):
        I, c, W = self.I, self.c, self.W
        ho = W['HO'][ti % 2]
        win, hT = self.win, W['hT']
        C = min(128, nt)
        nch = nt // C
        z = W['z']
        carry = W['carry']
        for cb in range(14):
            pp = self.pA[cb % 2]
            for k in range(8):
                self.mm(pp[:, 0:nt], win[:, k, cb * 128:(cb + 1) * 128], hT[:, k, 0:nt],
                        start=(k == 0), stop=(k == 7), R=[win.b, hT.b], W=[pp.b])
            t1 = W['t1'][cb % 2]
            v = self.sub.get('v1', 15)
            if v & 1:
                self.act(t1[:, 1:nt], pp[:, 0:nt - 1], AF.Copy, scale=c['mu'][:, cb:cb + 1], R=[pp.b, c['mu'].b], W=[t1.b])
            if v & 2:
                self.ts('dve' if v & 16 else 'pool', t1[:, 0:1], carry[:, cb:cb + 1], c['mu'][:, cb:cb + 1], None, ALU.mult,
                        R=[carry.b, c['mu'].b], W=[t1.b])
            if v & 4:
                self.stt(z[:, cb, 0:nt], pp[:, 0:nt], c['omu'][:, cb:cb + 1], t1[:, 0:nt], ALU.mult, ALU.add,
                         R=[pp.b, c['omu'].b, t1.b], W=[z.b])
            if v & 8:
                self.cp('act', carry[:, cb:cb + 1], pp[:, nt - 1:nt], R=[pp.b], W=[carry.b])
            if cb % 2 == 1:
                yield
        if self.sub.get('rstop', 99) <= 1:
            return
        zr, zk, zv = z[:, 0:4, 0:nt], z[:, 4:8, 0:nt], z[:, 8:12, 0:nt]
        tdw, dab, sg = W['tdw'], W['dab'], W['sg']
        self.act(tdw[0:64, 0:nt], z[0:64, 12, 0:nt], AF.Tanh, R=[z.b], W=[tdw.b])
        self.cp('pool', dab[64:128, 0:nt], z[64:128, 12, 0:nt], R=[z.b], W=[dab.b])
        self.act(sg[:, 0:nt], z[:, 13, 0:nt], AF.Sigmoid, R=[z.b], W=[sg.b])
        esig, aa, gT = W['esig'], W['aa'], ho['gT']
        for cb in range(4):
            p1, p2, p3 = self.pA[0], self.pA[1], self.pA[0]
            o = (cb % 2) * 256
            self.mm(p1[:, o:o + nt], self.wdec[0:64, cb * 128:(cb + 1) * 128], tdw[0:64, 0:nt], R=[self.wdec.b, tdw.b], W=[p1.b])
            self.act(esig[:, cb, 0:nt], p1[:, o:o + nt], AF.Sigmoid, bias=c['w0'][:, cb:cb + 1], R=[p1.b, c['w0'].b], W=[esig.b])
            self.mm(p2[:, o:o + nt], self.waaa[64:128, cb * 128:(cb + 1) * 128], dab[64:128, 0:nt], R=[self.waaa.b, dab.b], W=[p2.b])
            self.act(aa[:, cb, 0:nt], p2[:, o:o + nt], AF.Sigmoid, bias=c['a0'][:, cb:cb + 1], R=[p2.b, c['a0'].b], W=[aa.b])
            self.mm(p3[:, o:o + nt], self.wgup[:, cb * 128:(cb + 1) * 128], sg[:, 0:nt], R=[self.wgup.b, sg.b], W=[p3.b])
            self.cp('dve', gT[:, cb, 0:nt], p3[:, o:o + nt], R=[p3.b], W=[gT.b])
        if self.sub.get('rstop', 99) <= 2:
            return
        yield
        kk, kp, sqb, tmpf = W['kk'], W['kp'], W['sqb'], W['tmpf']
        for cb in range(4):
            self.ts('dve', kk[:, cb, 0:nt], z[:, 4 + cb, 0:nt], c['k_k'][:, cb:cb + 1], None, ALU.mult, R=[z.b, c['k_k'].b], W=[kk.b])
        self.act(sqb[:, :, 0:nt], kk[:, :, 0:nt], AF.Square, R=[kk.b], W=[sqb.b])
        for cb in range(4):
            pp = self.pA[cb // 2]
            o = (cb % 2) * 256
            self.mm(pp[:, o:o + nt], c['blk_b'][:], sqb[:, cb, 0:nt], R=[c['blk_b'].b, sqb.b], W=[pp.b])
            self.act(tmpf[:, cb, 0:nt], pp[:, o:o + nt], AF.Sqrt, R=[pp.b], W=[tmpf.b])
        self.ts('dve', tmpf[:, :, 0:nt], tmpf[:, :, 0:nt], 1e-12, None, ALU.max, R=[tmpf.b], W=[tmpf.b])
        self.recip(tmpf[:, :, 0:nt], tmpf[:, :, 0:nt], R=[tmpf.b], W=[tmpf.b])
        self.tt('dve', kk[:, :, 0:nt], kk[:, :, 0:nt], tmpf[:, :, 0:nt], ALU.mult, R=[kk.b, tmpf.b], W=[kk.b])
        yield
        for cb in range(4):
            self.ts('dve', tmpf[:, cb, 0:nt], aa[:, cb, 0:nt], c['k_a'][:, cb:cb + 1], c['omka'][:, cb:cb + 1], ALU.mult, ALU.add,
                    R=[aa.b, c['k_a'].b, c['omka'].b], W=[tmpf.b])
        self.tt('dve', kp[:, :, 0:nt], zk, tmpf[:, :, 0:nt], ALU.mult, R=[z.b, tmpf.b], W=[kp.b])
        yield
        bonus, rkb = ho['bonus'], W['rkb']
        self.tt('dve', tmpf[:, :, 0:nt], zr, kp[:, :, 0:nt], ALU.mult, R=[z.b, kp.b], W=[tmpf.b])
        for cb in range(4):
            self.ts('dve', rkb[:, cb, 0:nt], tmpf[:, cb, 0:nt], c['r_k'][:, cb:cb + 1], None, ALU.mult, R=[tmpf.b, c['r_k'].b], W=[rkb.b])
        for cb in range(4):
            pp = self.pA[cb // 2]
            o = (cb % 2) * 256
            self.mm(pp[:, o:o + nt], c['blk_b'][:], rkb[:, cb, 0:nt], R=[c['blk_b'].b, rkb.b], W=[pp.b])
            self.tt('dve', bonus[:, cb, 0:nt], pp[:, o:o + nt], z[:, 8 + cb, 0:nt], ALU.mult, R=[pp.b, z.b], W=[bonus.b])
        if self.sub.get('rstop', 99) <= 3:
            return
        yield
        cs, gam, gx, beta = W['cs'], ho['gam'], W['gx'], W['beta']
        NTf = self.NT1
        if nt == NTf:
            self.P.op('dve', lambda e: e.tensor_tensor_scan(cs[:].rearrange("p a t -> p (a t)"), c['reset'][:, 0:4 * nt],
                                                            esig[:].rearrange("p a t -> p (a t)"), 0.0, op0=ALU.mult, op1=ALU.add),
                      [c['reset'].b, esig.b], [cs.b])
        else:
            for cb in range(4):
                self.P.op('dve', lambda e, cb=cb: e.tensor_tensor_scan(cs[:, cb, 0:nt], c['reset'][:, 0:nt], esig[:, cb, 0:nt], 0.0,
                                                                      op0=ALU.mult, op1=ALU.add),
                          [c['reset'].b, esig.b], [cs.b])
        self.act(gam[:, :, 0:nt], cs[:, :, 0:nt], AF.Exp, scale=-C0, R=[cs.b], W=[gam.b])
        AR, BtT, KtT, BgT, KgT, vTb = ho['AR'], ho['BtT'], ho['KtT'], ho['BgT'], ho['KgT'], ho['vTb']
        def chv(ap):
            return ap.rearrange("p a (n c) -> p a n c", c=C)
        self.tt('dve', AR[:, :, 0:nch, 1, 0:C], chv(zr), chv(gam[:, :, 0:nt]), ALU.mult, R=[z.b, gam.b], W=[AR.b])
        self.tt('dve', beta[:, :, 0:nt], kk[:, :, 0:nt], aa[:, :, 0:nt], ALU.mult, R=[kk.b, aa.b], W=[beta.b])
        yield
        self.act(gx[:, :, 0:nt], cs[:, :, 0:nt], AF.Exp, scale=C0, R=[cs.b], W=[gx.b])
        self.tt('dve', BtT[:, :, 0:nt], beta[:, :, 0:nt], gx[:, :, 0:nt], ALU.mult, R=[beta.b, gx.b], W=[BtT.b])
        self.tt('dve', KtT[:, :, 0:nt], kp[:, :, 0:nt], gx[:, :, 0:nt], ALU.mult, R=[kp.b, gx.b], W=[KtT.b])
        yield
        self.tt('dve', tmpf[:, :, 0:nt], cs[:, :, 0:nt], esig[:, :, 0:nt], ALU.subtract, R=[cs.b, esig.b], W=[tmpf.b])
        self.act(gx[:, :, 0:nt], tmpf[:, :, 0:nt], AF.Exp, scale=-C0, R=[tmpf.b], W=[gx.b])
        self.tt('dve', AR[:, :, 0:nch, 0, 0:C], chv(kk[:, :, 0:nt]), chv(gx[:, :, 0:nt]), ALU.mult, R=[kk.b, gx.b], W=[AR.b])
        yield
        nb16 = W['nb16']
        self.ts('dve', nb16[:, :, 0:nch], cs[:, :, C - 1:nt:C], -C0, None, ALU.mult, R=[cs.b], W=[nb16.b])
        for cb in range(4):
            for ch in range(nch):
                self.act(gx[:, cb, ch * C:(ch + 1) * C], cs[:, cb, ch * C:(ch + 1) * C], AF.Exp, bias=nb16[:, cb, ch:ch + 1], scale=C0,
                         R=[cs.b, nb16.b], W=[gx.b])
        self.tt('dve', BgT[:, :, 0:nt], beta[:, :, 0:nt], gx[:, :, 0:nt], ALU.mult, R=[beta.b, gx.b], W=[BgT.b])
        self.tt('dve', KgT[:, :, 0:nt], kp[:, :, 0:nt], gx[:, :, 0:nt], ALU.mult, R=[kp.b, gx.b], W=[KgT.b])
        self.cp('pool', vTb[:, :, 0:nt], zv, R=[z.b], W=[vTb.b])
        if self.sub.get('rstop', 99) <= 4:
            return

    def rwkv_chunk(self, g, ti, tok0, ch, C, nt):
        c, W = self.c, self.W
        ho = W['HO'][ti % 2]
        AR, BtT, KtT, BgT, KgT, vTb = ho['AR'], ho['BtT'], ho['KtT'], ho['BgT'], ho['KgT'], ho['vTb']
        Bgt, Kgt, Vt, MLt, MKt, Xf, Xb = W['Bgt'], W['Kgt'], W['Vt'], W['MLt'], W['MKt'], W['Xf'], W['Xb']
        sl = slice(ch * C, (ch + 1) * C)
        idb = c['ident_b']
        pt = self.pT[1]
        for cb in range(4):
            self.tr(pt[0:C, cb * 128:(cb + 1) * 128], AR[:, cb, ch, 0, 0:C], idb[:], R=[AR.b, idb.b], W=[pt.b])
        self.ts('dve', Xb[0:C, :, 0:64], pt[0:C, 0:512].rearrange("p (h d) -> p h d", d=64), -1.0, None, ALU.mult, R=[pt.b], W=[Xb.b])
        for (src, dst, eng, pi) in [(BgT, Bgt, 'act', 1), (KgT, Kgt, 'act', 0), (vTb, Vt, 'act', 1)]:
            pt = self.pT[1]
            for cb in range(4):
                self.tr(pt[0:C, cb * 128:(cb + 1) * 128], src[:, cb, sl], idb[:], R=[src.b, idb.b], W=[pt.b])
            self.cp(eng, dst[0:C, :], pt[0:C, 0:512], R=[pt.b], W=[dst.b])
        if self.sub.get('rstop', 99) <= 5:
            return
        yield
        mA = c['maskA'][0:C, :].rearrange("p (a c) -> p a c", a=2)[:, :, 0:C]
        Lc0 = W['Lc'][0]
        for par in range(2):
            pb_ = 64 * par
            for half in range(2):
                pA_, pB_, pc = self.bb(), self.bb(), self.bb()
                for j in range(2):
                    cb = 2 * half + j
                    ar = AR[pb_:pb_ + 64, cb, ch, :, 0:C]
                    o = j * 256
                    self.mm(pA_[0:C, o:o + 2 * C].rearrange("p (a c) -> p a c", a=2), BtT[pb_:pb_ + 64, cb, sl], ar,
                            R=[BtT.b, AR.b], W=[pA_.b])
                    self.mm(pB_[0:C, o:o + 2 * C].rearrange("p (a c) -> p a c", a=2), KtT[pb_:pb_ + 64, cb, sl], ar,
                            R=[KtT.b, AR.b], W=[pB_.b])
                    self.mm(pc[0:C, j * 128:j * 128 + C], AR[pb_:pb_ + 64, cb, ch, 0, 0:C], BtT[pb_:pb_ + 64, cb, sl],
                            R=[AR.b, BtT.b], W=[pc.b])
                h0 = 2 * (2 * half) + par
                mA4 = mA.unsqueeze(1).to_broadcast([C, 2, 2, C])
                self.tt('dve', MLt[0:C, h0:h0 + 3:2, :].rearrange("p h (a c) -> p h a c", a=2)[:, :, :, 0:C],
                        pA_[0:C, 0:512].rearrange("p (h x) -> p h x", h=2)[:, :, 0:2 * C].rearrange("p h (a c) -> p h a c", a=2), mA4, ALU.mult,
                        R=[pA_.b, c['maskA'].b], W=[MLt.b])
                self.tt('dve', MKt[0:C, h0:h0 + 3:2, :].rearrange("p h (a c) -> p h a c", a=2)[:, :, :, 0:C],
                        pB_[0:C, 0:512].rearrange("p (h x) -> p h x", h=2)[:, :, 0:2 * C].rearrange("p h (a c) -> p h a c", a=2), mA4, ALU.mult,
                        R=[pB_.b, c['maskA'].b], W=[MKt.b])
                self.tt('dve', Lc0[0:C, h0:h0 + 3:2, 0:C],
                        pc[0:C, 0:256].rearrange("p (h c) -> p h c", h=2)[:, :, 0:C],
                        c['maskC'][0:C, 0:C].unsqueeze(1).to_broadcast([C, 2, C]), ALU.mult,
                        R=[pc.b, c['maskC'].b], W=[Lc0.b])
                yield
        if self.sub.get('rstop', 99) <= 6:
            return
        yield
        pl = self.bb()
        for h in range(NH):
            self.mm(pl[0:C, h * 64:(h + 1) * 64], MKt[0:C, h, 0:C], Vt[0:C, h * 64:(h + 1) * 64], R=[MKt.b, Vt.b], W=[pl.b])
        self.cp('act', Xb[0:C, :, 64:128], pl[0:C, :].rearrange("p (h d) -> p h d", d=64), R=[pl.b], W=[Xb.b])
        if self.sub.get('rstop', 99) <= 7:
            return
        yield
        nlev = int(round(math.log2(C)))
        Lc, Mc = W['Lc'], W['Mc']
        for lev in range(nlev):
            Lcur = Lc[lev % 2]
            Lnx = Lc[(lev + 1) % 2]
            Mnx = Mc[(lev + 1) % 2]
            def Mcur(h):
                return MLt[0:C, h, 0:C] if lev == 0 else Mc[lev % 2][0:C, h, 0:C]
            Mb = MLt.b if lev == 0 else Mc[lev % 2].b
            for half in range(2):
                yield
                px = self.bb()
                for hh in range(4):
                    h = 4 * half + hh
                    self.mm(px[0:C, hh * 128:hh * 128 + 128], Mcur(h), Xb[0:C, h, :], R=[Mb, Xb.b], W=[px.b])
                if lev < nlev - 1:
                    pm_, pl_ = self.bb(), self.bb()
                    for hh in range(4):
                        h = 4 * half + hh
                        self.mm(pm_[0:C, hh * 128:hh * 128 + C], Lcur[0:C, h, 0:C], Mcur(h), R=[Lcur.b, Mb], W=[pm_.b])
                        self.mm(pl_[0:C, hh * 128:hh * 128 + C], Mcur(h), Lcur[0:C, h, 0:C], R=[Lcur.b, Mb], W=[pl_.b])
                self.tt('dve', Xb[0:C, 4 * half:4 * half + 4, :], Xb[0:C, 4 * half:4 * half + 4, :],
                        px[0:C, :].rearrange("p (h d) -> p h d", d=128), ALU.add, R=[px.b, Xb.b], W=[Xb.b])
                if lev < nlev - 1:
                    self.cp('act', Mnx[0:C, 4 * half:4 * half + 4, 0:C],
                            pm_[0:C, :].rearrange("p (h d) -> p h d", d=128)[:, :, 0:C], R=[pm_.b], W=[Mnx.b])
                    self.cp('act', Lnx[0:C, 4 * half:4 * half + 4, 0:C],
                            pl_[0:C, :].rearrange("p (h d) -> p h d", d=128)[:, :, 0:C], R=[pl_.b], W=[Lnx.b])
        if self.sub.get('rstop', 99) <= 8:
            return
        yield
        GT, Hs, RAT, Sf, Sb = W['GT'], W['Hs'], W['RAT'], W['Sf'], W['Sb']
        gam = ho['gam']
        for par in range(2):
            pb_ = 64 * par
            pg, ph, pr = self.bb(), self.bb(), self.bb()
            for cb in range(4):
                h = 2 * cb + par
                self.mm(pg[pb_:pb_ + 64, cb * 64:(cb + 1) * 64], Xb[0:C, h, 0:64], Bgt[0:C, h * 64:(h + 1) * 64], R=[Xb.b, Bgt.b], W=[pg.b])
                self.mm(ph[pb_:pb_ + 64, cb * 64:(cb + 1) * 64], Bgt[0:C, h * 64:(h + 1) * 64], Xb[0:C, h, 64:128], start=True, stop=False,
                        R=[Xb.b, Bgt.b], W=[ph.b])
                self.mm(ph[pb_:pb_ + 64, cb * 64:(cb + 1) * 64], Kgt[0:C, h * 64:(h + 1) * 64], Vt[0:C, h * 64:(h + 1) * 64], start=False, stop=True,
                        R=[Kgt.b, Vt.b], W=[ph.b])
                self.mm(pr[pb_:pb_ + 64, cb * 128:cb * 128 + C], Xb[0:C, h, 0:64], MLt[0:C, h, 128:128 + C], R=[Xb.b, MLt.b], W=[pr.b])
            for cb in range(4):
                gC = gam[pb_:pb_ + 64, cb, ch * C + C - 1:ch * C + C]
                self.stt(GT[pb_:pb_ + 64, cb, :], c['ipair'][pb_:pb_ + 64, :], gC, pg[pb_:pb_ + 64, cb * 64:(cb + 1) * 64], ALU.mult, ALU.add,
                         R=[c['ipair'].b, gam.b, pg.b], W=[GT.b])
            self.cp('act', Hs[pb_:pb_ + 64, :, :], ph[pb_:pb_ + 64, 0:256].rearrange("p (a d) -> p a d", d=64), R=[ph.b], W=[Hs.b])
            self.tt('dve', RAT[pb_:pb_ + 64, :, 0:C], pr[pb_:pb_ + 64, :].rearrange("p (a d) -> p a d", d=128)[:, :, 0:C],
                    AR[pb_:pb_ + 64, :, ch, 1, 0:C], ALU.add, R=[pr.b, AR.b], W=[RAT.b])
        if self.sub.get('rstop', 99) <= 9:
            return
        yield
        ysb, ysq = W['ysb'], W['ysq']
        s10 = self.sub.get('s10', 3)
        if s10 & 1:
            for par in range(2):
                pb_ = 64 * par
                py = self.bb()
                py2 = self.bb() if C != 128 else None
                for cb in range(4):
                    h = 2 * cb + par
                    o = cb * 64
                    self.mm(py[0:C, o:o + 64], MLt[0:C, h, 128:128 + C], Xb[0:C, h, 64:128], start=True, stop=False, R=[MLt.b, Xb.b], W=[py.b])
                    if C == 128:
                        self.mm(py[0:C, o:o + 64], MKt[0:C, h, 128:128 + C], Vt[0:C, h * 64:(h + 1) * 64], start=False, stop=False, R=[MKt.b, Vt.b], W=[py.b])
                        self.mm(py[0:C, o:o + 64], RAT[pb_:pb_ + 64, cb, 0:C], Sb[pb_:pb_ + 64, cb, :], start=False, stop=True, R=[RAT.b, Sb.b], W=[py.b])
                    else:
                        self.mm(py[0:C, o:o + 64], MKt[0:C, h, 128:128 + C], Vt[0:C, h * 64:(h + 1) * 64], start=False, stop=True, R=[MKt.b, Vt.b], W=[py.b])
                        self.mm(py2[0:C, o:o + 64], RAT[pb_:pb_ + 64, cb, 0:C], Sb[pb_:pb_ + 64, cb, :], start=True, stop=True, R=[RAT.b, Sb.b], W=[py2.b])
                if C != 128:
                    self.cp('act', ysq[0:C, 0:4, :], py2[0:C, 0:256].rearrange("p (a d) -> p a d", d=64), R=[py2.b], W=[ysq.b])
                    self.tt('dve', ysb[0:C, par:8:2, :], py[0:C, 0:256].rearrange("p (a d) -> p a d", d=64), ysq[0:C, 0:4, :], ALU.add,
                            R=[py.b, ysq.b], W=[ysb.b])
                    continue
                if s10 & 4:
                    self.cp('dve', ysb[0:C, par:8:2, :], py[0:C, 0:256].rearrange("p (a d) -> p a d", d=64), R=[py.b], W=[ysb.b])
                elif s10 & 8:
                    pass
                else:
                    self.cp('act', ysb[0:C, par:8:2, :], py[0:C, 0:256].rearrange("p (a d) -> p a d", d=64), R=[py.b], W=[ysb.b])
        if s10 & 2:
            pSs = [self.bb(), self.bb()]
            for par in range(2):
                pb_ = 64 * par
                pS = pSs[par]
                for cb in range(4):
                    self.mm(pS[pb_:pb_ + 64, cb * 64:(cb + 1) * 64], GT[pb_:pb_ + 64, cb, :], Sb[pb_:pb_ + 64, cb, :], R=[GT.b, Sb.b], W=[pS.b])
            for par in range(2):
                pb_ = 64 * par
                pS = pSs[par]
                self.tt('dve', Sf[pb_:pb_ + 64, :, :], pS[pb_:pb_ + 64, 0:256].rearrange("p (a d) -> p a d", d=64), Hs[pb_:pb_ + 64, :, :], ALU.add,
                        R=[pS.b, Hs.b], W=[Sf.b])
            self.cp('pool', Sb[:], Sf[:], R=[Sf.b], W=[Sb.b])
        if self.sub.get('rstop', 99) <= 10:
            return
        yield
        s8 = W['st8']
        self.red(s8[0][0:C, :], ysb[0:C, :, :], R=[ysb.b], W=[s8[0].b])
        self.act(ysq[0:C, :, :], ysb[0:C, :, :], AF.Square, R=[ysb.b], W=[ysq.b])
        self.red(s8[1][0:C, :], ysq[0:C, :, :], R=[ysq.b], W=[s8[1].b])
        self.ts('dve', s8[0][0:C, :], s8[0][0:C, :], 1.0 / 64, None, ALU.mult, R=[s8[0].b], W=[s8[0].b])
        self.tt('dve', s8[2][0:C, :], s8[0][0:C, :], s8[0][0:C, :], ALU.mult, R=[s8[0].b], W=[s8[2].b])
        self.stt(s8[1][0:C, :], s8[1][0:C, :], 1.0 / 64, s8[2][0:C, :], ALU.mult, ALU.subtract, R=[s8[1].b, s8[2].b], W=[s8[1].b])
        self.act(s8[1][0:C, :], s8[1][0:C, :], AF.Sqrt, bias=c['gneps'][0:C, :], scale=1.0, R=[s8[1].b, c['gneps'].b], W=[s8[1].b])
        self.recip(s8[1][0:C, :], s8[1][0:C, :], R=[s8[1].b], W=[s8[1].b])
        self.tt('dve', ysb[0:C, :, :], ysb[0:C, :, :], s8[0][0:C, :].unsqueeze(2).to_broadcast([C, 8, 64]), ALU.subtract, R=[ysb.b, s8[0].b], W=[ysb.b])
        yh = W['yh']
        self.tt('dve', yh[0:C, :].rearrange("p (h d) -> p h d", d=64), ysb[0:C, :, :], s8[1][0:C, :].unsqueeze(2).to_broadcast([C, 8, 64]), ALU.mult,
                R=[ysb.b, s8[1].b], W=[yh.b])
        pt = self.pT[1]
        for cb in range(4):
            self.tr(pt[:, cb * 128:cb * 128 + C], yh[0:C, cb * 128:(cb + 1) * 128], idb[0:C, 0:C], R=[yh.b, idb.b], W=[pt.b])
        yT1, yr = W['yT1'], W['yr'][self.nyr % 2]
        self.nyr += 1
        bonus, gT = ho['bonus'], ho['gT']
        for cb in range(4):
            self.ts('dve', yT1[:, cb, 0:C], pt[:, cb * 128:cb * 128 + C], c['gn_g'][:, cb:cb + 1], c['gn_b'][:, cb:cb + 1], ALU.mult, ALU.add,
                    R=[pt.b, c['gn_g'].b, c['gn_b'].b], W=[yT1.b])
        self.tt('dve', yT1[:, :, 0:C], yT1[:, :, 0:C], bonus[:, :, sl], ALU.add, R=[yT1.b, bonus.b], W=[yT1.b])
        self.tt('dve', yr[:, :, 0:C], yT1[:, :, 0:C], gT[:, :, sl], ALU.mult, R=[yT1.b, gT.b], W=[yr.b])
        t0 = tok0 + ch * C
        self.dma('sp', g['Ys'][0:512, t0:t0 + C].rearrange("(a p) t -> p a t", p=128), yr[:, :, 0:C], R=[yr.b])

    def phase2(self, st):
        c = self.c
        maxtot = max(g['tot'] for g in self.G)
        maxT = max(g['T'] for g in self.G)
        maxkb = max(g['nkb'] for g in self.G)
        Kt = [self.sb(st, (67, maxtot), BF16, name='Kt') for _ in range(2)]
        Qt = [self.sb(st, (67, maxT), BF16, name='Qt') for _ in range(2)]
        Vh = [self.sb(st, (128, maxkb, 128), BF16, name='Vh') for _ in range(2)]
        for v in Vh:
            self.memset('pool', v[:, :, 64:128], 1.0, W=[v.b])
        Pt = [self.sb(st, (128, 512), BF16, name='Pt') for _ in range(3)]
        rl = [self.sb(st, (64, 512), F32, name='rl') for _ in range(2)]
        yo = [self.sb(st, (64, 512), BF16, name='yo') for _ in range(2)]
        NPS = 4
        LA = 3
        pS = [self.pA[0], self.pA[1], self.pA[2], self.pA[3]]
        pO = [self.pA[4], self.pA[5]]
        Pt = Pt + [self.sb(st, (128, 512), BF16, name='Pt')]
        heads = [(g, h) for g in self.G for h in range(NH)]
        def load(idx):
            g, h = heads[idx]
            T, tot = g['T'], g['tot']
            kt, qt, vh = Kt[idx % 2], Qt[idx % 2], Vh[idx % 2]
            self.dma('sp', kt[0:64, 0:tot], g['Ks'][h, 0:64, :], W=[kt.b])
            self.dma('sp', kt[64:67, 0:tot], g['Ks'][h, 64:67, :], W=[kt.b])
            self.dma('sp', qt[0:64, 0:T], g['Qs'][h, 0:64, :], W=[qt.b])
            self.dma('sp', qt[64:67, 0:T], g['Qs'][h, 64:67, :], W=[qt.b])
            nfull = tot // 128
            if nfull > 0:
                self.dma('sp', vh[:, 0:nfull, 0:64], g['Vs'][h, :, 0:nfull, :], W=[vh.b])
            rem = tot - nfull * 128
            if rem:
                self.dma('sp', vh[0:rem, nfull, 0:64], g['Vs'][h, 0:rem, nfull, :], W=[vh.b])
        blocks = []
        nq = 0
        for idx, (g, h) in enumerate(heads):
            T, past, tot = g['T'], g['past'], g['tot']
            QT = min(512, T)
            for qi in range(T // QT):
                q0 = qi * QT
                qpos0 = past + q0
                nblk = (qpos0 + QT - 1) // 128 + 1
                for j in range(nblk):
                    k0 = j * 128
                    rows = min(128, tot - k0)
                    if k0 + rows - 1 <= qpos0:
                        c0, diag = 0, False
                    else:
                        c0, diag = k0 - qpos0, True
                    blocks.append(dict(idx=idx, g=g, h=h, q0=q0, QT=QT, j=j, k0=k0, rows=rows, c0=c0, diag=diag,
                                       first=(j == 0), last=(j == nblk - 1), nq=nq, newhead=(qi == 0 and j == 0)))
                nq += 1

        def emit_S(n):
            bl = blocks[n]
            g, h, idx = bl['g'], bl['h'], bl['idx']
            kt, qt = Kt[idx % 2], Qt[idx % 2]
            rows, c0, QT, q0, k0, j = bl['rows'], bl['c0'], bl['QT'], bl['q0'], bl['k0'], bl['j']
            ps_, pt_ = pS[n % NPS], Pt[n % NPS]
            self.mm(ps_[0:rows, c0:QT], kt[:, k0:k0 + rows], qt[:, q0 + c0:q0 + QT], start=True, stop=not bl['diag'],
                    R=[kt.b, qt.b], W=[ps_.b])
            if bl['diag']:
                self.mm(ps_[0:rows, c0:c0 + rows], c['ident_b'][0:rows, 0:rows], c['maskD_b'][0:rows, 0:rows], start=False, stop=True,
                        R=[c['ident_b'].b, c['maskD_b'].b], W=[ps_.b])
            self.act(pt_[0:rows, c0:QT], ps_[0:rows, c0:QT], AF.Exp, bias=g['neglc'][0:rows, j, h:h + 1], scale=1.0,
                     R=[ps_.b, g['neglc'].b], W=[pt_.b])

        def emit_PV(n):
            bl = blocks[n]
            g, h, idx = bl['g'], bl['h'], bl['idx']
            vh = Vh[idx % 2]
            rows, c0, QT, q0, j = bl['rows'], bl['c0'], bl['QT'], bl['q0'], bl['j']
            pt_ = Pt[n % NPS]
            po = pO[bl['nq'] % 2]
            self.mm(po[:, c0:QT], vh[0:rows, j, :], pt_[0:rows, c0:QT], start=bl['first'], stop=bl['last'],
                    R=[vh.b, pt_.b], W=[po.b])
            if bl['last']:
                rlt, yot = rl[bl['nq'] % 2], yo[bl['nq'] % 2]
                self.recip(rlt[:, 0:QT], po[64:128, 0:QT], R=[po.b], W=[rlt.b])
                self.tt('dve', yot[:, 0:QT], po[0:64, 0:QT], rlt[:, 0:QT], ALU.mult, R=[po.b, rlt.b], W=[yot.b])
                self.dma('sp', g['Ys'][512 + h * 64:512 + (h + 1) * 64, q0:q0 + QT], yot[:, 0:QT], R=[yot.b])

        load(0)
        if len(heads) > 1:
            load(1)
        nb_ = len(blocks)
        first_block = {}
        for n, bl in enumerate(blocks):
            first_block.setdefault(bl['idx'], n)
        load_at = {first_block[idx] + LA: idx + 1 for idx in range(1, len(heads) - 1)}
        for n in range(nb_ + LA):
            if n in load_at:
                load(load_at[n])
            if n < nb_:
                emit_S(n)
            if n - LA >= 0:
                emit_PV(n - LA)

    def phase3(self, st):
        I, c = self.I, self.c
        wout = self.sb(st, (128, 8, D), BF16, name='wout')
        wg, wu = self.wg, self.wu
        wd = self.sb(st, (128, NFB, D), BF16, name='wd')
        for (dst, src, kk_, ncol) in [(wout, I['w_out'], 8, D), (wd, I['w_d'], NFB, D)]:
            s3 = src.rearrange("(k p) c -> p k c", p=128)
            for k in range(kk_):
                for a in range(0, ncol, 1408 if ncol == DFF else 1024):
                    b = min(ncol, a + (1408 if ncol == DFF else 1024))
                    self.dma('pool', dst[:, k, a:b], s3[:, k, a:b], W=[dst.b])
        NT = 256
        xt = [self.sb(st, (128, 2, D), F32, name='x3')]
        yT = [self.sb(st, (128, 8, NT), BF16, name='yT')]
        xsb = self.sb(st, (128, 2, D), BF16, name='xsb3')
        h2 = self.sb(st, (128, 8, NT), BF16, name='h2')
        junk = self.sb(st, (128, D), BF16, name='junk3')
        ss = self.sb(st, (128, 4), F32)
        rstd = self.sb(st, (128, 4), F32)
        actT = self.sb(st, (128, NFB, NT), BF16, name='actT')
        sil = [self.sb(st, (128, NT), F32, name='sil') for _ in range(2)]
        yo = [self.sb(st, (128, D), F32, name='yo3')]
        gt1 = self.sb(st, (128, D), F32, name='gt1')
        gt2 = self.sb(st, (128, D), F32, name='gt2')
        n = 0
        hflag = self.sb(st, (1, 1), mybir.dt.int32, name='hflag')
        self.dma('sp', hflag[:], I['half'], W=[hflag.b])
        r_base = st.enter_context(self.nc.gpsimd.register("r_base"))
        r_off = st.enter_context(self.nc.gpsimd.register("r_off"))
        HALF = self.SEQ // 2
        def init_reg(e):
            e.reg_load(r_base, hflag[0:1, 0:1])
            e.reg_mul(r_base, r_base, HALF)
            return e.nop()
        self.P.op('pool', init_reg, [hflag.b], [])
        for g in self.G:
            T = g['T'] if g['gi'] == 1 else HALF
            xsrc = g['x'] if g['gi'] == 1 else I['xp3']
            self.build_gates(g, gt1, gt2, sil)
            ntile = (T + NT - 1) // NT
            for ti in range(ntile):
                tok0 = ti * NT
                nt = min(NT, T - tok0)
                nb = (nt + 127) // 128
                bs = min(128, nt)
                x = xt[n % len(xt)]
                y_ = yT[n % len(yT)]
                n += 1
                self.dma('sp', x[0:bs, 0:nb, :], xsrc[tok0:tok0 + nt, :].rearrange("(b p) d -> p b d", p=bs), W=[x.b])
                if g['gi'] == 1:
                    self.dma('sp', y_[:, :, 0:nt], g['Ys'][:, tok0:tok0 + nt].rearrange("(k p) t -> p k t", p=128), W=[y_.b])
                else:
                    ys = g['Ys']
                    SEQ_ = self.SEQ
                    def dyn(e, y_=y_, tok0=tok0, nt=nt, ys=ys, SEQ_=SEQ_):
                        e.reg_add(r_off, r_base, tok0)
                        src = bass.AP(ys.tensor, r_off, [[SEQ_, 128], [128 * SEQ_, 8], [1, nt]])
                        return e.dma_start(out=y_[:, :, 0:nt], in_=src)
                    self.P.dma_fn('pool', dyn, (), [y_.b])
                tmpm = yo[0]
                for blk in range(nb):
                    for half in range(2):
                        pp = self.pA[half]
                        for k in range(8):
                            self.mm(pp[0:bs, :], y_[:, k, blk * 128:blk * 128 + bs], wout[:, k, half * 512:(half + 1) * 512],
                                    start=(k == 0), stop=(k == 7), R=[y_.b, wout.b], W=[pp.b])
                        self.tt('dve', tmpm[0:bs, half * 512:(half + 1) * 512], pp[0:bs, :], gt1[0:bs, half * 512:(half + 1) * 512], ALU.mult,
                                R=[pp.b, gt1.b], W=[tmpm.b])
                    self.tt('dve', x[0:bs, blk, :], tmpm[0:bs, :], x[0:bs, blk, :], ALU.add, R=[tmpm.b, x.b], W=[x.b])
                x1 = x
                self.norm_transpose(x1, nb, bs, g['gm2'], g['sh2'], xsb, h2, junk, ss, rstd, 0, alt=True)
                for fb in range(NFB):
                    pg, pu = self.pA[2 + (fb % 2)], self.pA[4 + (fb % 2)]
                    for k in range(8):
                        self.mm(pg[:, 0:nt], wg[:, k, fb * 128:(fb + 1) * 128], h2[:, k, 0:nt], start=(k == 0), stop=(k == 7), R=[wg.b, h2.b], W=[pg.b])
                    for k in range(8):
                        self.mm(pu[:, 0:nt], wu[:, k, fb * 128:(fb + 1) * 128], h2[:, k, 0:nt], start=(k == 0), stop=(k == 7), R=[wu.b, h2.b], W=[pu.b])
                    s_ = sil[fb % 2]
                    self.act(s_[:, 0:nt], pg[:, 0:nt], AF.Silu, R=[pg.b], W=[s_.b])
                    self.tt('dve', actT[:, fb, 0:nt], s_[:, 0:nt], pu[:, 0:nt], ALU.mult, R=[s_.b, pu.b], W=[actT.b])
                for blk in range(nb):
                    yo_ = yo[0]
                    for half in range(2):
                        pp = self.pA[half]
                        for fb in range(NFB):
                            self.mm(pp[0:bs, :], actT[:, fb, blk * 128:blk * 128 + bs], wd[:, fb, half * 512:(half + 1) * 512],
                                    start=(fb == 0), stop=(fb == NFB - 1), R=[actT.b, wd.b], W=[pp.b])
                        self.tt('dve', yo_[0:bs, half * 512:(half + 1) * 512], pp[0:bs, :], gt2[0:bs, half * 512:(half + 1) * 512], ALU.mult,
                                R=[pp.b, gt2.b], W=[yo_.b])
                    self.tt('dve', yo_[0:bs, :], yo_[0:bs, :], x1[0:bs, blk, :], ALU.add, R=[yo_.b, x1.b], W=[yo_.b])
                    self.dma('sp', g['o_y'][tok0 + blk * 128:tok0 + blk * 128 + bs, :], yo_[0:bs, :], R=[yo_.b])


def _consts():
    i = np.arange(128)
    s, t = i[:, None], i[None, :]
    cst = {}
    cst['c_ident'] = np.eye(128, dtype=np.float32)
    cst['c_triu'] = (s <= t).astype(np.float32)
    cst['c_ones'] = np.ones((128, 128), np.float32)
    cst['c_blk'] = ((s // 64) == (t // 64)).astype(np.float32)
    cst['c_maskA'] = np.concatenate([-(s < t).astype(np.float32), (s <= t).astype(np.float32)], axis=1)
    cst['c_maskC'] = -(s > t).astype(np.float32)
    cst['c_maskD'] = np.where(s <= t, 0.0, NEG).astype(np.float32)
    r = np.ones((128, 1024), np.float32)
    r[:, ::128] = 0.0
    cst['c_reset'] = r
    cst['c_ipair'] = ((i[:, None] % 64) == np.arange(64)[None, :]).astype(np.float32)
    return cst


def _pk(v, nblk):
    return np.ascontiguousarray(np.asarray(v, np.float32).reshape(nblk, 128).T)


_NC_CACHE = {}


def _get_nc(SEQ, PAST, NS, debug=False, phases=(1, 2, 3), sub=None):
    key = (SEQ, PAST, NS, debug, tuple(phases), str(sub))
    if key not in _NC_CACHE:
        _NC_CACHE[key] = KB(SEQ, PAST, NS, debug, phases, sub).build()
    return _NC_CACHE[key]


def make_in_maps(inp, n_cores=8):
    f = lambda a: np.ascontiguousarray(np.asarray(a, dtype=np.float32))
    xp, xs = f(inp['x_prompt']), f(inp['x_sample'])
    BP = xp.shape[0]
    cst = _consts()
    shared = dict(cst)
    L = 0
    shared['w_ada'] = f(inp['w_ada'][L]); shared['b_ada'] = _pk(inp['b_ada'][L], 48)
    shared['w_in'] = f(inp['w_in'][L]); shared['w_out'] = f(inp['w_out'][L])
    shared['w_g'] = f(inp['w_ffn_gate'][L]); shared['w_u'] = f(inp['w_ffn_up'][L]); shared['w_d'] = f(inp['w_ffn_down'][L])
    shared['n1g'] = _pk(inp['norm1_g'][L], 8); shared['n2g'] = _pk(inp['norm2_g'][L], 8)
    shared['mu'] = _pk(inp['shift_mu'][L], 14)
    for nm, src in [('w0', 'w0'), ('a0', 'a0'), ('k_k', 'k_k'), ('k_a', 'k_a'), ('gn_g', 'gn_g'), ('gn_b', 'gn_b')]:
        shared[nm] = _pk(inp[src][L], 4)
    shared['r_k'] = _pk(np.asarray(inp['r_k'][L]).reshape(512), 4)
    shared['w_dec'] = f(inp['w_decay_up'][L]); shared['w_aaa'] = f(inp['w_aaa_up'][L]); shared['w_gup'] = f(inp['w_gate_up'][L])
    gq = np.tile(np.asarray(inp['fox_q_g'][L], np.float32), 8)
    gk = np.tile(np.asarray(inp['fox_k_g'][L], np.float32), 8)
    shared['gqk'] = np.ascontiguousarray(np.broadcast_to(np.concatenate([gq, gk])[None, :], (128, 1024)))
    shared['f_b'] = np.ascontiguousarray(np.broadcast_to(np.asarray(inp['fox_f_b'][L], np.float32)[None, :], (128, 8)))
    maps = []
    for cidx in range(n_cores):
        b = cidx % BP
        m = dict(shared)
        m['xp'] = xp[b]
        hf = cidx // BP
        H2 = xp.shape[1] // 2
        m['half'] = np.array([[hf]], np.int32)
        m['xp3'] = np.ascontiguousarray(xp[b, hf * H2:(hf + 1) * H2])
        m['xs'] = xs[cidx]
        cv = np.stack([np.asarray(inp['c_prompt'][b], np.float32), np.asarray(inp['c_sample'][cidx], np.float32)], axis=-1)
        m['cvec'] = np.ascontiguousarray(cv.reshape(8, 128, 2).transpose(1, 0, 2))
        m['ck'] = f(inp['cache_fox_k'][L, cidx]).reshape(-1, 512)
        m['cv'] = f(inp['cache_fox_v'][L, cidx]).reshape(-1, 512)
        m['clf'] = f(inp['cache_fox_logf'][L, cidx])
        m['st0'] = np.ascontiguousarray(f(inp['state_rwkv'][L, cidx]).transpose(1, 0, 2))
        m['shp0'] = _pk(inp['state_rwkv_shift'][L, cidx, 0], 14)
        maps.append(m)
    return maps


def assemble(res, BP, SEQ, NSEQ, NS):
    r = res
    def upk(a):
        return np.ascontiguousarray(a.T).reshape(-1)
    y_p = np.stack([np.concatenate([r[b]['y_p'], r[b + BP]['y_p']], axis=0) for b in range(BP)])
    y_s = np.stack([r[c]['y_s'] for c in range(NSEQ)])
    st_p = np.stack([r[b]['st_p'] for b in range(BP)])[None]
    sh_p = np.stack([upk(r[b]['sh_p'])[None, :] for b in range(BP)])[None]
    k_p = np.stack([r[b]['k_p'].reshape(SEQ, 8, 64) for b in range(BP)])[None]
    v_p = np.stack([r[b]['v_p'].reshape(SEQ, 8, 64) for b in range(BP)])[None]
    lf_p = np.stack([r[b]['lf_p'] for b in range(BP)])[None]
    st_s = np.stack([r[c]['st_s'] for c in range(NSEQ)])[None]
    sh_s = np.stack([upk(r[c]['sh_s'])[None, :] for c in range(NSEQ)])[None]
    k_s = np.stack([r[c]['k_s'].reshape(NS, 8, 64) for c in range(NSEQ)])[None]
    v_s = np.stack([r[c]['v_s'].reshape(NS, 8, 64) for c in range(NSEQ)])[None]
    lf_s = np.stack([r[c]['lf_s'] for c in range(NSEQ)])[None]
    outs = (y_p, y_s, st_p, sh_p, k_p, v_p, lf_p, st_s, sh_s, k_s, v_s, lf_s)
    return tuple(np.ascontiguousarray(o, dtype=np.float32) for o in outs)


def kernel(**inputs):
    xp = np.asarray(inputs['x_prompt'])
    xs = np.asarray(inputs['x_sample'])
    BP, SEQ, _ = xp.shape
    NSEQ, NS, _ = xs.shape
    PAST = np.asarray(inputs['cache_fox_k']).shape[2]
    nc = _get_nc(SEQ, PAST, NS)
    maps = make_in_maps(inputs, 8)
    res = run_bass_kernel_spmd(nc, maps, core_ids=list(range(8)))
    return assemble(res.results, BP, SEQ, NSEQ, NS)
```
